# Optimizing a Trainium2 kernel written in Bass

```python
import jax, jax.numpy as jnp
from jax import lax
import numpy as np

D_MODEL = 1024
BATCH = 4
SEQ = 4096
DEPTH = 2

ROPE_THETA = 500000.0
NORM_EPS = 1e-6
Q_BLOCK = 128
D_FF = 4 * D_MODEL
N_EVEN = (DEPTH + 1) // 2
N_ODD = DEPTH // 2
A_HEADS = 8
A_HEAD_DIM = 64
A_WIDTH = A_HEADS * A_HEAD_DIM
A_ROT_DIM = A_HEAD_DIM // 4
IDX_HEADS = 16
IDX_DIM = 64
TOPK_MAX = 256
B_GROUPS = 8
B_GROUP_DIM = 64
B_WIDTH = B_GROUPS * B_GROUP_DIM
CONV_WIDTH = 3
EVEN_COLS = (A_WIDTH, A_WIDTH, A_WIDTH,
             IDX_HEADS * IDX_DIM, IDX_DIM, IDX_HEADS,
             B_WIDTH, B_WIDTH, B_WIDTH)
EVEN_IN_WIDTH = sum(EVEN_COLS)
EVEN_OUT_WIDTH = A_WIDTH + B_WIDTH
C_HEADS = 16
C_NOPE_DIM = 64
C_ROPE_DIM = 32
C_V_DIM = 64
C_Q_RANK = 384
C_KV_RANK = 256

kernel_name = 'hybrid_dsa_shortconv_mla_sandwich'


def _rms_norm(x, gain):
    xf = x.astype(jnp.float32)
    xf = xf * lax.rsqrt(jnp.mean(xf * xf, axis=-1, keepdims=True) + NORM_EPS)
    return xf.astype(x.dtype) * gain


def _rope_cos_sin(positions, rot_dim):
    inv_freq = ROPE_THETA ** (-jnp.arange(0, rot_dim, 2, dtype=jnp.float32) / rot_dim)
    ang = positions.astype(jnp.float32)[..., None] * inv_freq
    return jnp.cos(ang)[:, :, None, :], jnp.sin(ang)[:, :, None, :]


def _rotate(x, cos, sin):
    xf = x.astype(jnp.float32)
    x1, x2 = jnp.split(xf, 2, axis=-1)
    return jnp.concatenate([x1 * cos - x2 * sin, x2 * cos + x1 * sin], axis=-1).astype(x.dtype)


def _partial_rope(x, cos, sin):
    rot = 2 * cos.shape[-1]
    return jnp.concatenate([_rotate(x[..., :rot], cos, sin), x[..., rot:]], axis=-1)


def _to_blocks(a):
    b, s = a.shape[:2]
    return jnp.moveaxis(a.reshape(b, s // Q_BLOCK, Q_BLOCK, *a.shape[2:]), 1, 0)


def _from_blocks(o):
    nb, b, qb = o.shape[:3]
    return jnp.moveaxis(o, 0, 1).reshape(b, nb * qb, *o.shape[3:])


def _dsa_attention(q, k, v, q_idx, k_idx, w_idx):
    bsz, s, h, dh = q.shape
    n_blocks = s // Q_BLOCK
    top_k = min(TOPK_MAX, s // 4)
    key_pos = jnp.arange(s, dtype=jnp.int32)
    gather = jax.vmap(lambda kb, ib: kb[ib])

    def block(args):
        qb, qib, wib, start = args
        q_pos = start + jnp.arange(Q_BLOCK, dtype=jnp.int32)
        causal = key_pos[None, :] <= q_pos[:, None]
        rel = jax.nn.relu(jnp.einsum('bqhd,bsd->bqhs', qib, k_idx).astype(jnp.float32))
        score = jnp.einsum('bqhs,bqh->bqs', rel, wib)
        score = jnp.where(causal[None], score, -jnp.inf)
        _, sel = lax.top_k(score, top_k)
        valid = sel <= q_pos[None, :, None]
        k_sel = gather(k, sel)
        v_sel = gather(v, sel)
        logits = jnp.einsum('bqhd,bqkhd->bhqk', qb, k_sel).astype(jnp.float32) * (dh ** -0.5)
        logits = jnp.where(valid[:, None], logits, -jnp.inf)
        p = jax.nn.softmax(logits, axis=-1).astype(v.dtype)
        return jnp.einsum('bhqk,bqkhd->bqhd', p, v_sel).reshape(bsz, Q_BLOCK, h * dh)

    starts = jnp.arange(n_blocks, dtype=jnp.int32) * Q_BLOCK
    out = lax.map(block, (_to_blocks(q), _to_blocks(q_idx), _to_blocks(w_idx), starts))
    return _from_blocks(out)


def _short_conv(x_in, gate_b, gate_c, conv_w):
    u = gate_c * x_in
    y = lax.conv_general_dilated(u, conv_w[:, None, :].astype(u.dtype), window_strides=(1,),
                                 padding=[(CONV_WIDTH - 1, 0)],
                                 dimension_numbers=('NWC', 'WIO', 'NWC'),
                                 feature_group_count=u.shape[-1])
    return gate_b * y


def _sparse_attn_conv_mixer(h, rope_quarter, w_in, conv_w, w_out):
    cos, sin = rope_quarter
    bsz, s, _ = h.shape
    proj = h @ w_in
    offsets = np.cumsum(EVEN_COLS)[:-1].tolist()
    q, k, v, q_idx, k_idx, w_idx, gate_b, gate_c, x_in = jnp.split(proj, offsets, axis=-1)
    q = _partial_rope(q.reshape(bsz, s, A_HEADS, A_HEAD_DIM), cos, sin)
    k = _partial_rope(k.reshape(bsz, s, A_HEADS, A_HEAD_DIM), cos, sin)
    v = v.reshape(bsz, s, A_HEADS, A_HEAD_DIM)
    q_idx = _partial_rope(q_idx.reshape(bsz, s, IDX_HEADS, IDX_DIM), cos, sin)
    k_idx = _partial_rope(k_idx[:, :, None, :], cos, sin)[:, :, 0, :]
    w_idx = w_idx.astype(jnp.float32) * (IDX_HEADS ** -0.5 * IDX_DIM ** -0.5)
    attn = _dsa_attention(q, k, v, q_idx, k_idx, w_idx)
    conv = _short_conv(x_in, gate_b, gate_c, conv_w)
    return jnp.concatenate([attn, conv], axis=-1) @ w_out


def _mla_mixer(h, rope_mla, w_dq, q_norm, w_uq, w_dkv, kv_norm, w_ukv, w_o):
    cos, sin = rope_mla
    bsz, s, _ = h.shape
    q = (_rms_norm(h @ w_dq, q_norm) @ w_uq).reshape(bsz, s, C_HEADS, C_NOPE_DIM + C_ROPE_DIM)
    q_nope = q[..., :C_NOPE_DIM]
    q_rope = _rotate(q[..., C_NOPE_DIM:], cos, sin)
    kv_a = h @ w_dkv
    c_kv = _rms_norm(kv_a[..., :C_KV_RANK], kv_norm)
    k_rope = _rotate(kv_a[..., C_KV_RANK:][:, :, None, :], cos, sin)[:, :, 0, :]
    kv = (c_kv @ w_ukv).reshape(bsz, s, C_HEADS, C_NOPE_DIM + C_V_DIM)
    k_nope, v = kv[..., :C_NOPE_DIM], kv[..., C_NOPE_DIM:]
    scale = (C_NOPE_DIM + C_ROPE_DIM) ** -0.5
    key_pos = jnp.arange(s, dtype=jnp.int32)

    def block(args):
        qnb, qrb, start = args
        q_pos = start + jnp.arange(Q_BLOCK, dtype=jnp.int32)
        causal = key_pos[None, :] <= q_pos[:, None]
        logits = (jnp.einsum('bqhd,bshd->bhqs', qnb, k_nope)
                  + jnp.einsum('bqhr,bsr->bhqs', qrb, k_rope)).astype(jnp.float32) * scale
        logits = jnp.where(causal[None, None], logits, -jnp.inf)
        p = jax.nn.softmax(logits, axis=-1).astype(v.dtype)
        return jnp.einsum('bhqs,bshd->bqhd', p, v).reshape(bsz, Q_BLOCK, C_HEADS * C_V_DIM)

    starts = jnp.arange(s // Q_BLOCK, dtype=jnp.int32) * Q_BLOCK
    out = lax.map(block, (_to_blocks(q_nope), _to_blocks(q_rope), starts))
    return _from_blocks(out) @ w_o


def _squared_relu_mlp(h, w1, w2):
    return jnp.square(jax.nn.relu(h @ w1)) @ w2


def setup_inputs(seed: int = 0) -> dict:
    key = jax.random.key(seed)
    ks = jax.random.split(key, 20)

    def dense(k, shape, fan_in):
        return jax.random.normal(k, shape, jnp.float32) * (fan_in ** -0.5)

    def gain(k, shape):
        return 1.0 + 0.02 * jax.random.normal(k, shape, jnp.float32)

    return {
        'x': jax.random.normal(ks[0], (BATCH, SEQ, D_MODEL), jnp.float32),
        'positions': jnp.broadcast_to(jnp.arange(SEQ, dtype=jnp.int32), (BATCH, SEQ)),
        'norm_mix_pre': gain(ks[1], (DEPTH, D_MODEL)),
        'norm_mix_post': gain(ks[2], (DEPTH, D_MODEL)),
        'norm_ffn_pre': gain(ks[3], (DEPTH, D_MODEL)),
        'norm_ffn_post': gain(ks[4], (DEPTH, D_MODEL)),
        'even_w_in': dense(ks[5], (N_EVEN, D_MODEL, EVEN_IN_WIDTH), D_MODEL),
        'even_conv_w': dense(ks[6], (N_EVEN, CONV_WIDTH, B_WIDTH), CONV_WIDTH),
        'even_w_out': dense(ks[7], (N_EVEN, EVEN_OUT_WIDTH, D_MODEL), EVEN_OUT_WIDTH),
        'odd_w_dq': dense(ks[8], (N_ODD, D_MODEL, C_Q_RANK), D_MODEL),
        'odd_q_norm': gain(ks[9], (N_ODD, C_Q_RANK)),
        'odd_w_uq': dense(ks[10], (N_ODD, C_Q_RANK, C_HEADS * (C_NOPE_DIM + C_ROPE_DIM)), C_Q_RANK),
        'odd_w_dkv': dense(ks[11], (N_ODD, D_MODEL, C_KV_RANK + C_ROPE_DIM), D_MODEL),
        'odd_kv_norm': gain(ks[12], (N_ODD, C_KV_RANK)),
        'odd_w_ukv': dense(ks[13], (N_ODD, C_KV_RANK, C_HEADS * (C_NOPE_DIM + C_V_DIM)), C_KV_RANK),
        'odd_w_o': dense(ks[14], (N_ODD, C_HEADS * C_V_DIM, D_MODEL), C_HEADS * C_V_DIM),
        'mlp_w1': dense(ks[15], (DEPTH, D_MODEL, D_FF), D_MODEL),
        'mlp_w2': dense(ks[16], (DEPTH, D_FF, D_MODEL), D_FF),
    }


def reference(x, positions, norm_mix_pre, norm_mix_post, norm_ffn_pre, norm_ffn_post,
              even_w_in, even_conv_w, even_w_out,
              odd_w_dq, odd_q_norm, odd_w_uq, odd_w_dkv, odd_kv_norm, odd_w_ukv, odd_w_o,
              mlp_w1, mlp_w2):
    rope_quarter = _rope_cos_sin(positions, A_ROT_DIM)
    rope_mla = _rope_cos_sin(positions, C_ROPE_DIM)
    for layer in range(DEPTH):
        j = layer // 2
        hn = _rms_norm(x, norm_mix_pre[layer])
        if layer % 2 == 0:
            mix = _sparse_attn_conv_mixer(hn, rope_quarter, even_w_in[j], even_conv_w[j], even_w_out[j])
        else:
            mix = _mla_mixer(hn, rope_mla, odd_w_dq[j], odd_q_norm[j], odd_w_uq[j],
                             odd_w_dkv[j], odd_kv_norm[j], odd_w_ukv[j], odd_w_o[j])
        x = x + _rms_norm(mix, norm_mix_post[layer])
        hn = _rms_norm(x, norm_ffn_pre[layer])
        x = x + _rms_norm(_squared_relu_mlp(hn, mlp_w1[layer], mlp_w2[layer]), norm_ffn_post[layer])
    return x
```

```python
import types
import numpy as np
import ml_dtypes
import concourse.bass as bass
import concourse.mybir as mybir
from concourse.bass_utils import run_bass_kernel_spmd

F32 = mybir.dt.float32
BF16 = mybir.dt.bfloat16
I32 = mybir.dt.int32
AF = mybir.ActivationFunctionType
ALU = mybir.AluOpType
AX = mybir.AxisListType
DT_SIZE = {F32: 4, BF16: 2, I32: 4}

T = 4096
NO = 2048
D = 1024
EPS = 1e-6
NBIS = 24
TWO_PI = float(2 * np.pi)


class Op:
    __slots__ = ("eng", "fn", "reads", "writes", "is_dma", "deps", "needed", "ordinal",
                 "dsem", "dval", "barrier")

    def __init__(self, eng, fn, reads, writes, is_dma):
        self.eng = eng
        self.fn = fn
        self.reads = reads
        self.writes = writes
        self.is_dma = is_dma
        self.deps = []
        self.needed = False
        self.ordinal = None
        self.dsem = None
        self.dval = None
        self.barrier = False


class Prog:
    ENGS = ("pe", "act", "dve", "pool", "sp")
    SB_LIMIT = 228352

    def __init__(self, nc, n_dma_sems=12):
        self.nc = nc
        self.ops = []
        self.sb_off = 16896
        self.sb_max = 0
        self.n_dma_sems = n_dma_sems
        self._uid = 0
        self._bank = 0

    def sb(self, name, shape, dtype):
        nbytes = int(np.prod(shape[1:])) * DT_SIZE[dtype]
        nbytes = (nbytes + 63) // 64 * 64
        self._uid += 1
        t = self.nc.alloc_sbuf_tensor_at(f"{name}_{self._uid}", list(shape), dtype, offset=self.sb_off)
        self.sb_off += nbytes
        self.sb_max = max(self.sb_max, self.sb_off)
        assert self.sb_off <= self.SB_LIMIT, f"SBUF overflow {self.sb_off} at {name}"
        return t

    def mark(self):
        return self.sb_off

    def release(self, m):
        self.sb_off = m

    @staticmethod
    def _freeze(fn):
        if getattr(fn, "__closure__", None) is None:
            return fn
        cells = []
        for c in fn.__closure__:
            try:
                cells.append(types.CellType(c.cell_contents))
            except ValueError:
                cells.append(c)
        return types.FunctionType(fn.__code__, fn.__globals__, fn.__name__, fn.__defaults__, tuple(cells))

    def add(self, eng, fn, r=(), w=(), dma=False):
        fn = self._freeze(fn)
        o = Op(eng, fn, tuple(r), tuple(w), dma)
        self.ops.append(o)
        return o

    def pe(self, fn, r=(), w=()):
        return self.add("pe", fn, r, w)

    def act(self, fn, r=(), w=()):
        return self.add("act", fn, r, w)

    def dve(self, fn, r=(), w=()):
        return self.add("dve", fn, r, w)

    def pool(self, fn, r=(), w=()):
        return self.add("pool", fn, r, w)

    def dma(self, out, in_, r=(), w=(), q="sp", **kw):
        return self.add(q, lambda e: e.dma_start(out=out, in_=in_, **kw), r, w, dma=True)

    def final_wait(self, keys):
        return self.add("sp", lambda e: e.nop(), r=keys, w=())

    def barrier(self):
        o = Op(None, None, (), (), False)
        o.barrier = True
        self.ops.append(o)

    def finalize(self):
        last_w = {}
        readers = {}
        since_barrier = []
        pending_barrier = None
        seen_after = set()
        for o in self.ops:
            if o.barrier:
                summ = []
                lastc = {}
                for p in since_barrier:
                    if p.is_dma:
                        summ.append(p)
                    else:
                        lastc[p.eng] = p
                summ.extend(lastc.values())
                if pending_barrier is not None:
                    summ.extend(pending_barrier)
                pending_barrier = summ
                seen_after = set()
                since_barrier = []
                continue
            deps = []
            if pending_barrier is not None and o.eng not in seen_after:
                deps.extend(pending_barrier)
                seen_after.add(o.eng)
            for k in o.reads:
                if k in last_w:
                    deps.append(last_w[k])
            for k in o.writes:
                if k in last_w:
                    deps.append(last_w[k])
                deps.extend(readers.get(k, ()))
            for k in o.reads:
                readers.setdefault(k, []).append(o)
            for k in o.writes:
                last_w[k] = o
                readers[k] = []
            dd = []
            seen = set()
            for d in deps:
                if d is o or id(d) in seen:
                    continue
                seen.add(id(d))
                if (not d.is_dma) and (not o.is_dma) and d.eng == "pe" and o.eng == "pe":
                    continue
                dd.append(d)
            o.deps = dd
            for d in dd:
                d.needed = True
            since_barrier.append(o)
        cnt = {e: 0 for e in self.ENGS}
        dma_rr = {e: 0 for e in self.ENGS}
        dma_uses = {}
        for o in self.ops:
            if o.barrier:
                continue
            if o.is_dma:
                slot = dma_rr[o.eng] % self.n_dma_sems
                dma_rr[o.eng] += 1
                key = (o.eng, slot)
                dma_uses[key] = dma_uses.get(key, 0) + 1
                o.dsem = key
                o.dval = 16 * dma_uses[key]
            elif o.needed:
                cnt[o.eng] += 1
                o.ordinal = cnt[o.eng]
        self.max_ord = dict(cnt)

    def emit(self):
        nc = self.nc
        self.finalize()
        from contextlib import ExitStack
        es = ExitStack()
        sems = {}
        for e in ("pe", "act", "dve", "pool", "sp"):
            sems[e] = es.enter_context(nc.semaphore(f"c_{e}"))
        dsems = {}
        used = sorted({o.dsem for o in self.ops if (not o.barrier) and o.is_dma})
        for key in used:
            dsems[key] = es.enter_context(nc.semaphore(f"d_{key[0]}_{key[1]}"))
        block = es.enter_context(nc.Block())
        per_eng = {e: [o for o in self.ops if (not o.barrier) and o.eng == e] for e in self.ENGS}

        def body(ename, engine):
            known = {}
            for o in per_eng[ename]:
                waits = {}
                for d in o.deps:
                    if d.is_dma:
                        s, v, k = dsems[d.dsem], d.dval, ("d",) + d.dsem
                    else:
                        s, v, k = sems[d.eng], d.ordinal, ("c", d.eng)
                    if v > waits.get(k, (None, 0))[1]:
                        waits[k] = (s, v)
                if o.is_dma and o.dval > 16:
                    k = ("d",) + o.dsem
                    v = o.dval - 16
                    if v > waits.get(k, (None, 0))[1]:
                        waits[k] = (dsems[o.dsem], v)
                for k, (s, v) in waits.items():
                    if known.get(k, 0) >= v:
                        continue
                    engine.wait_ge(s, v)
                    known[k] = v
                ins = o.fn(engine)
                if o.is_dma:
                    ins.then_inc(dsems[o.dsem], 16)
                elif o.needed:
                    ins.then_inc(sems[ename], 1)

        @block.tensor
        def _(e):
            body("pe", e)

        @block.scalar
        def _(e):
            body("act", e)

        @block.vector
        def _(e):
            body("dve", e)

        @block.gpsimd
        def _(e):
            body("pool", e)

        @block.sync
        def _(e):
            body("sp", e)

        es.close()


def blocks_for(par):
    lo = list(range(par, 16, 2))
    hi = sorted(31 - j for j in lo)
    return lo + hi


C_G = 0
C_QN = 64
C_KVN = 67
C_CW = 69
C_FR0 = 81
C_SG0 = 82
C_FR1 = 83
C_SG1 = 84
NCST = 96


def gcol(layer, kind):
    return C_G + (layer * 4 + kind) * 8


def build_program(n_cores, dbg=(), no_cc=False, stop_after=None):
    nc = bass.Bass("TRN2", target_bir_lowering=False)
    P = Prog(nc)

    def stop(name):
        if stop_after == name:
            P.barrier()
            P.final_wait([])
            P.emit()
            return True
        return False

    def din(name, shape, dt=F32):
        return nc.dram_tensor(name, list(shape), dt, kind="ExternalInput").ap()

    def dscr(name, shape, dt):
        kind = "ExternalOutput" if name in dbg else "Internal"
        return nc.dram_tensor(name, list(shape), dt, kind=kind).ap()

    xT_seq = din("xT_seq", [D, T])
    xT_own = din("xT_own", [2, D, NO + 32])
    pos_seq = din("pos_seq", [1, T], I32)
    pos_own = din("pos_own", [2, 1, NO], I32)
    cst_d = din("cst", [128, NCST])
    kq_d = din("kq", [128, 32])
    cb_d = din("cb", [2, 8, 128, 1024])
    mT_d = din("mT", [2, 128, 8, 512], BF16)
    w0k_d = din("w0k", [D, 1280])
    w0v_d = din("w0v", [D, 512])
    w0q_d = din("w0q", [D, 4608])
    w0wi_d = din("w0wi", [D, 16])
    wout_d = din("w_out", [D, D])
    w1_d = din("w1", [2, D, 4096])
    w2_d = din("w2", [2, 4096, D])
    wdq_d = din("w_dq", [D, 384])
    wuq_d = din("w_uq", [384, 2048])
    wdkv_d = din("w_dkv", [D, 320])
    wukvk_d = din("w_ukv_k", [256, 1024])
    wukvv_d = din("w_ukv_v", [256, 1024])
    wo_d = din("w_o", [D, D])
    outT = nc.dram_tensor("outT", [D, NO], F32, kind="ExternalOutput").ap()

    kT_d = dscr("kT_d", [4, 128, T], BF16)
    kidxT_d = dscr("kidxT_d", [128, T], BF16)
    vaug_d = dscr("vaug_d", [32, 128, 1024], BF16)
    qT_d = dscr("qT_d", [4, 128, NO], BF16)
    qidxT_d = dscr("qidxT_d", [8, 128, NO], BF16)
    convT_d = dscr("convT_d", [4, 128, NO], BF16)
    maskT_d = dscr("maskT_d", [4, 128, 32, 512], BF16)
    attnT_d = dscr("attnT_d", [8, 128, NO], BF16)
    x1T_d = dscr("x1T_d", [D, NO], F32)
    x2T_d = dscr("x2T_d", [2, D, NO], F32)
    x3T_d = dscr("x3T_d", [D, NO], F32)
    kva_sets_d = dscr("kva_sets_d", [2, 288, NO], F32)
    qnT_d = dscr("qnT_d", [8, 128, NO], BF16)
    qrT_d = dscr("qrT_d", [8, 64, NO], BF16)
    knT_d = dscr("knT_d", [8, 128, T], BF16)
    vaug1_d = dscr("vaug1_d", [32, 128, 2048], BF16)
    dbg_d = dscr("dbg_d", [128, 4096], F32)
    kr4_d = dscr("kr4_d", [64, T], BF16)

    ps = [nc.alloc_psum_tensor(f"ps{i}", [128, 512], F32) for i in range(8)]

    def PK(i):
        return ("ps", i)

    cst = P.sb("cst", [128, NCST], F32)
    kq = P.sb("kq", [128, 32], F32)
    widx = P.sb("widx", [128, 16, 16], F32)
    ones_b = P.sb("ones_b", [128, 128], BF16)
    ones_q = P.sb("ones_q", [128, 128], BF16)
    ident = P.sb("ident", [128, 128], F32)
    P.dma(cst[:], cst_d, w=["cst"])
    P.dma(kq[:], kq_d, w=["kq"])
    P.pool(lambda e: e.memset(ones_b[:], 1.0 / 1024), w=["ones_b"])
    P.pool(lambda e: e.memset(ones_q[:], 1.0 / 256), w=["ones_q"])
    P.pool(lambda e: e.memset(ident[:], 1.0), w=["ident"])
    P.pool(lambda e: e.affine_select(out=ident[:], in_=ident[:], pattern=[[-1, 128]], compare_op=ALU.is_equal,
                                     fill=0.0, base=0, channel_multiplier=1), r=["ident"], w=["ident"])
    base_mark = P.mark()

    uid = [0]

    def U(s):
        uid[0] += 1
        return f"{s}#{uid[0]}"

    def rope_tables(pos_d, n, fr_col, sg_col, tag):
        C = P.sb(tag + "C", [128, n], F32)
        S = P.sb(tag + "S", [128, n], F32)
        m = P.mark()
        pi_ = P.sb("posi", [128, n], I32)
        pf = P.sb("posf", [128, n], F32)
        tmp = P.sb("rtmp", [128, n], F32)
        ki = P.sb("rki", [128, n], I32)
        kpi, kpf, kt, kk = U("posi"), U("posf"), U("rtmp"), U("rki")
        kC, kS = tag + "C", tag + "S"
        P.dma(pi_[:], pos_d.to_broadcast([128, n]), w=[kpi])
        P.dve(lambda e: e.tensor_copy(out=pf[:], in_=pi_[:]), r=[kpi], w=[kpf])
        P.dve(lambda e: e.tensor_scalar(out=pf[:], in0=pf[:], scalar1=cst[:, fr_col:fr_col + 1], scalar2=None,
                                        op0=ALU.mult), r=[kpf, "cst"], w=[kpf])
        for which, dst, kd in (("s", S, kS), ("c", C, kC)):
            off = 0.0 if which == "s" else float(np.pi / 2)
            P.dve(lambda e, off=off: e.tensor_scalar(out=tmp[:], in0=pf[:], scalar1=off, scalar2=1.0 / TWO_PI,
                                                     op0=ALU.add, op1=ALU.mult), r=[kpf], w=[kt])
            P.dve(lambda e: e.tensor_copy(out=ki[:], in_=tmp[:]), r=[kt], w=[kk])
            P.dve(lambda e: e.tensor_copy(out=tmp[:], in_=ki[:]), r=[kk], w=[kt])
            P.dve(lambda e: e.scalar_tensor_tensor(out=tmp[:], in0=tmp[:], scalar=-TWO_PI, in1=pf[:],
                                                   op0=ALU.mult, op1=ALU.add), r=[kt, kpf], w=[kt])
            P.dve(lambda e, off=off: e.tensor_scalar(out=tmp[:], in0=tmp[:], scalar1=off, scalar2=None,
                                                     op0=ALU.add), r=[kt], w=[kt])
            P.dve(lambda e, dst=dst: e.tensor_scalar(out=dst[:], in0=tmp[:], scalar1=float(np.pi), scalar2=-TWO_PI,
                                                     op0=ALU.is_gt, op1=ALU.mult), r=[kt], w=[kd])
            P.dve(lambda e, dst=dst: e.tensor_tensor(out=tmp[:], in0=tmp[:], in1=dst[:], op=ALU.add), r=[kt, kd], w=[kt])
            P.dve(lambda e, dst=dst: e.tensor_scalar(out=dst[:], in0=tmp[:], scalar1=-float(np.pi), scalar2=TWO_PI,
                                                     op0=ALU.is_lt, op1=ALU.mult), r=[kt], w=[kd])
            P.dve(lambda e, dst=dst: e.tensor_tensor(out=tmp[:], in0=tmp[:], in1=dst[:], op=ALU.add), r=[kt, kd], w=[kt])
            P.dve(lambda e: e.tensor_scalar(out=tmp[:], in0=tmp[:], scalar1=-3.14159, scalar2=3.14159,
                                            op0=ALU.max, op1=ALU.min), r=[kt], w=[kt])
            P.act(lambda e, dst=dst: e.activation(out=dst[:], in_=tmp[:], func=AF.Sin), r=[kt], w=[kd])
        P.dve(lambda e: e.tensor_scalar(out=S[:], in0=S[:], scalar1=cst[:, sg_col:sg_col + 1], scalar2=None,
                                        op0=ALU.mult), r=[kS, "cst"], w=[kS])
        P.barrier()
        P.release(m)
        return C, S

    def load_w(dst, src_ap, key, nsplit=1):
        P.dma(dst, src_ap, w=[key], q="pool")

    def rmsnorm_fm(xs, kx, nchunk, n, gain_col, hout, kh, onesm, sq, ksq, rstd, krs, ssbank, eps=EPS):
        P.act(lambda e: e.activation(out=sq[:, 0:nchunk, 0:n], in_=xs[:, 0:nchunk, 0:n], func=AF.Square),
              r=[kx], w=[ksq])
        for c in range(nchunk):
            P.pe(lambda e, c=c: e.matmul(ps[ssbank][:, 0:n], lhsT=onesm[:], rhs=sq[:, c, 0:n],
                                         start=(c == 0), stop=(c == nchunk - 1)),
                 r=[ksq, "ones_b", "ones_q"], w=[PK(ssbank)])
        P.act(lambda e: e.activation(out=rstd[:, 0:n], in_=ps[ssbank][:, 0:n], func=AF.Sqrt, bias=eps, scale=1.0),
              r=[PK(ssbank)], w=[krs])
        P.dve(lambda e: e.reciprocal(out=rstd[:, 0:n], in_=rstd[:, 0:n]), r=[krs], w=[krs])
        for c in range(nchunk):
            P.dve(lambda e, c=c: e.scalar_tensor_tensor(out=hout[:, c, 0:n], in0=xs[:, c, 0:n],
                                                      scalar=cst[:, gain_col + c:gain_col + c + 1],
                                                      in1=rstd[:, 0:n], op0=ALU.mult, op1=ALU.mult),
                r=[kx, krs, "cst"], w=[(kh, c)])

    bankrot = [0]

    def nb():
        b = bankrot[0] % 4
        bankrot[0] += 1
        return b

    def proj_fm(wt, kw, oc0, h, kh, nk, n, bank, m=128):
        for kc in range(nk):
            P.pe(lambda e, kc=kc: e.matmul(ps[bank][0:m, 0:n], lhsT=wt[:, kc, oc0:oc0 + m], rhs=h[:, kc, 0:n],
                                           start=(kc == 0), stop=(kc == nk - 1)),
                 r=[kw, (kh, kc)], w=[PK(bank)])

    rope_mark = P.mark()
    C0s, S0s = rope_tables(pos_seq, T, C_FR0, C_SG0, "r0s")

    wk = P.sb("wk", [128, 8, 1280], BF16)
    wv = P.sb("wv", [128, 8, 512], BF16)
    load_w(wk[:], w0k_d.rearrange("(c p) n -> p c n", p=128), "wk")
    load_w(wv[:], w0v_d.rearrange("(c p) n -> p c n", p=128), "wv")
    xs2 = [P.sb(f"xs{i}", [128, 8, 512], F32) for i in range(2)]
    sq = P.sb("sq", [128, 8, 512], BF16)
    hb = P.sb("hb", [128, 8, 512], BF16)
    rstd = P.sb("rstd", [128, 512], F32)
    t1 = [P.sb(f"t1_{i}", [128, 512], F32) for i in range(2)]
    t2 = [P.sb(f"t2_{i}", [128, 512], F32) for i in range(2)]
    kst = [P.sb(f"kst{i}", [128, 5, 512], BF16) for i in range(2)]
    vst = [P.sb(f"vst{i}", [128, 8, 128], BF16) for i in range(2)]
    for i in range(2):
        P.pool(lambda e, i=i: e.memset(vst[i][:], 1.0), w=[("vst", i)])
    xseq_v = xT_seq.rearrange("(c p) t -> p c t", p=128)
    tcnt = [0]

    def rope_evac(bA, bB, Ct, St, col0, n, dst, kdst, kC="r0sC", kS="r0sS"):
        i = tcnt[0] % 2
        tcnt[0] += 1
        P.dve(lambda e: e.tensor_tensor(out=t1[i][:, 0:n], in0=ps[bA][:, 0:n], in1=Ct[:, col0:col0 + n], op=ALU.mult),
              r=[PK(bA), kC], w=[("t1", i)])
        P.dve(lambda e: e.tensor_tensor(out=t2[i][:, 0:n], in0=ps[bB][:, 0:n], in1=St[:, col0:col0 + n], op=ALU.mult),
              r=[PK(bB), kS], w=[("t2", i)])
        P.pool(lambda e: e.tensor_tensor(out=dst, in0=t1[i][:, 0:n], in1=t2[i][:, 0:n], op=ALU.add),
               r=[("t1", i), ("t2", i)], w=[kdst])

    for tc in range(8):
        xs = xs2[tc % 2]
        kx = ("xs", tc % 2)
        P.dma(xs[:], xseq_v[:, :, tc * 512:(tc + 1) * 512], w=[kx])
        rmsnorm_fm(xs, kx, 8, 512, gcol(0, 0), hb, "hb", ones_b, sq, "sq", rstd, "rstd", 7)
        if tc == 0 and "dbg_d" in dbg:
            dtmp = P.sb("dtmp", [128, 1024], F32)
            P.dma(dbg_d[:, 0:512], xs[:, 0, :], r=[kx], w=["dbg0"])
            P.dma(dbg_d[:, 512:1024], rstd[:], r=["rstd"], w=["dbg1"])
            P.dve(lambda e: e.tensor_copy(out=dtmp[:, 0:512], in_=hb[:, 0, :]), r=[("hb", 0)], w=["dtmp"])
            P.dve(lambda e: e.tensor_copy(out=dtmp[:, 512:1024], in_=sq[:, 0, :]), r=["sq"], w=["dtmp"])
            P.dma(dbg_d[:, 1024:2048], dtmp[:], r=["dtmp"], w=["dbg2"])
            dt2 = P.sb("dt2", [128, 512], F32)
            P.act(lambda e: e.activation(out=dt2[:], in_=ps[7][:, :], func=AF.Copy), r=[PK(7)], w=["dt2"])
            P.dma(dbg_d[:, 2048:2560], dt2[:], r=["dt2"], w=["dbg3"])
            P.dma(dbg_d[:, 2560:3072], S0s[:, 0:512], r=["r0sS"], w=["dbg4"])
            P.dma(dbg_d[:, 3072:3584], C0s[:, 3584:4096], r=["r0sC"], w=["dbg5"])
            P.dma(dbg_d[:, 3584:4096], S0s[:, 3584:4096], r=["r0sS"], w=["dbg6"])
        ks = kst[tc % 2]
        for oc in range(5):
            bA, bB = nb(), nb()
            colA = oc * 128 if oc < 4 else 1024
            colB = 512 + oc * 128 if oc < 4 else 1152
            proj_fm(wk, "wk", colA, hb, "hb", 8, 512, bA)
            proj_fm(wk, "wk", colB, hb, "hb", 8, 512, bB)
            rope_evac(bA, bB, C0s, S0s, tc * 512, 512, ks[:, oc, :], ("kst", tc % 2, oc))
        for oc in range(4):
            P.dma(kT_d[oc, :, tc * 512:(tc + 1) * 512], ks[:, oc, :], r=[("kst", tc % 2, oc)], w=[("kT_d", oc, tc)])
        P.dma(kidxT_d[:, tc * 512:(tc + 1) * 512], ks[:, 4, :], r=[("kst", tc % 2, 4)], w=[("kidxT_d", tc)])
        for tb in range(4):
            kb = tc * 4 + tb
            b = nb()
            for kc in range(8):
                P.pe(lambda e, kc=kc, tb=tb: e.matmul(ps[b][:, :], lhsT=hb[:, kc, tb * 128:(tb + 1) * 128],
                                                      rhs=wv[:, kc, :], start=(kc == 0), stop=(kc == 7)),
                     r=["wv", ("hb", kc)], w=[PK(b)])
            vs = vst[kb % 2]
            pv = ps[b][:, :].rearrange("p (h two d) -> p h two d", two=2, d=64)
            vv = vs[:].rearrange("p (h two) d -> p h two d", two=2)
            P.act(lambda e, pv=pv, vv=vv: e.activation(out=vv[:, :, 0, 0:64], in_=pv[:, :, 0, :], func=AF.Copy),
                  r=[PK(b)], w=[("vst", kb % 2)])
            P.act(lambda e, pv=pv, vv=vv: e.activation(out=vv[:, :, 1, 64:128], in_=pv[:, :, 1, :], func=AF.Copy),
                  r=[PK(b)], w=[("vst", kb % 2)])
            P.dma(vaug_d[kb], vs[:].rearrange("p h d -> p (h d)"), r=[("vst", kb % 2)], w=[("vaug_d", kb)])
    P.barrier()
    P.release(rope_mark)

    for s_ in range(2):
        C0o, S0o = rope_tables(pos_own[s_], NO, C_FR0, C_SG0, "r0o")
        set_mark = P.mark()
        if stop("K"):
            return nc
        wq = P.sb("wq", [128, 8, 4608], BF16)
        wwi = P.sb("wwi", [128, 8, 16], BF16)
        for c in range(8):
            P.dma(wq[:, c, :], w0q_d[c * 128:(c + 1) * 128, :], w=["wq"], q="pool")
        load_w(wwi[:], w0wi_d.rearrange("(c p) n -> p c n", p=128), "wwi")
        uext = P.sb("uext", [128, 4, 16, 130], F32)
        gbs = P.sb("gbs", [128, 4, NO], BF16)
        q_mark = P.mark()
        xs2 = [P.sb("xsq", [128, 8, 512], F32)] * 2
        sq = P.sb("sq", [128, 8, 512], BF16)
        hb = P.sb("hb", [128, 8, 512], BF16)
        rstd = P.sb("rstd", [128, 512], F32)
        t1 = [P.sb(f"t1_{i}", [128, 512], F32) for i in range(2)]
        t2 = [P.sb(f"t2_{i}", [128, 512], F32) for i in range(2)]
        qrot = [P.sb(f"qrot{i}", [128, 512], BF16) for i in range(6)]
        gcs = [P.sb(f"gcs{i}", [128, 512], F32) for i in range(2)]
        qrc = [0]
        xown_v = xT_own[s_].rearrange("(c p) t -> p c t", p=128)
        for tc in range(5):
            n = 512 if tc < 4 else 32
            xs = xs2[0]
            kx = ("xs", 0)
            P.dma(xs[:, :, 0:n], xown_v[:, :, tc * 512:tc * 512 + n], w=[kx])
            rmsnorm_fm(xs, kx, 8, n, gcol(0, 0), hb, "hb", ones_b, sq, "sq", rstd, "rstd", 7)
            if tc < 4:
                for oc in range(12):
                    bA, bB = nb(), nb()
                    colA = oc * 128 if oc < 4 else 1024 + (oc - 4) * 128
                    colB = 512 + oc * 128 if oc < 4 else 2048 + (oc - 4) * 128
                    proj_fm(wq, "wq", colA, hb, "hb", 8, 512, bA)
                    proj_fm(wq, "wq", colB, hb, "hb", 8, 512, bB)
                    qi_ = qrc[0] % 6
                    qrc[0] += 1
                    rope_evac(bA, bB, C0o, S0o, tc * 512, 512, qrot[qi_][:], ("qrot", qi_), "r0oC", "r0oS")
                    if oc < 4:
                        P.dma(qT_d[oc, :, tc * 512:(tc + 1) * 512], qrot[qi_][:], r=[("qrot", qi_)], w=[("qT_d", oc, tc)])
                    else:
                        P.dma(qidxT_d[oc - 4, :, tc * 512:(tc + 1) * 512], qrot[qi_][:], r=[("qrot", qi_)],
                              w=[("qidxT_d", oc - 4, tc)])
                for cc in range(4):
                    b = nb()
                    proj_fm(wq, "wq", 3072 + cc * 128, hb, "hb", 8, 512, b)
                    P.act(lambda e, b=b, cc=cc: e.activation(out=gbs[:, cc, tc * 512:(tc + 1) * 512], in_=ps[b][:, :], func=AF.Copy),
                          r=[PK(b)], w=[("gbs", cc)])
                for tb in range(4):
                    b = nb()
                    for kc in range(8):
                        P.pe(lambda e, kc=kc, tb=tb, b=b: e.matmul(ps[b][:, 0:16], lhsT=hb[:, kc, tb * 128:(tb + 1) * 128],
                                                                   rhs=wwi[:, kc, :], start=(kc == 0), stop=(kc == 7)),
                             r=["wwi", ("hb", kc)], w=[PK(b)])
                    P.act(lambda e, b=b, tb=tb: e.activation(out=widx[:, tc * 4 + tb, :], in_=ps[b][:, 0:16], func=AF.Copy),
                          r=[PK(b)], w=["widx"])
            for cc in range(4):
                bA, bB = nb(), nb()
                proj_fm(wq, "wq", 3584 + cc * 128, hb, "hb", 8, n, bA)
                proj_fm(wq, "wq", 4096 + cc * 128, hb, "hb", 8, n, bB)
                g = gcs[cc % 2]
                P.act(lambda e, g=g, bA=bA: e.activation(out=g[:, 0:n], in_=ps[bA][:, 0:n], func=AF.Copy),
                      r=[PK(bA)], w=[("gcs", cc % 2)])
                if tc < 4:
                    o_ap = uext[:, cc, tc * 4:(tc + 1) * 4, 2:130]
                    i0 = ps[bB][:, :].rearrange("p (b t) -> p b t", t=128)
                    i1 = g[:].rearrange("p (b t) -> p b t", t=128)
                else:
                    o_ap = uext[:, cc, :, 0:2]
                    i0 = ps[bB][:, 0:32].rearrange("p (b t) -> p b t", t=2)
                    i1 = g[:, 0:32].rearrange("p (b t) -> p b t", t=2)
                P.dve(lambda e, o_ap=o_ap, i0=i0, i1=i1: e.tensor_tensor(out=o_ap, in0=i0, in1=i1, op=ALU.mult),
                      r=[PK(bB), ("gcs", cc % 2)], w=[("uext", cc)])
        P.barrier()
        P.release(q_mark)
        cacc = [P.sb(f"cacc{i}", [128, 16, 128], F32) for i in range(2)]
        cvo = [P.sb(f"cvo{i}", [128, NO], BF16) for i in range(2)]
        for cc in range(4):
            a = cacc[cc % 2]
            ka = ("cacc", cc % 2)
            P.dve(lambda e, a=a, cc=cc: e.tensor_scalar(out=a[:], in0=uext[:, cc, :, 2:130],
                                                        scalar1=cst[:, C_CW + 8 + cc:C_CW + 9 + cc], scalar2=None, op0=ALU.mult),
                  r=[("uext", cc), "cst"], w=[ka])
            P.dve(lambda e, a=a, cc=cc: e.scalar_tensor_tensor(out=a[:], in0=uext[:, cc, :, 1:129],
                                                               scalar=cst[:, C_CW + 4 + cc:C_CW + 5 + cc], in1=a[:],
                                                               op0=ALU.mult, op1=ALU.add), r=[("uext", cc), "cst", ka], w=[ka])
            P.dve(lambda e, a=a, cc=cc: e.scalar_tensor_tensor(out=a[:], in0=uext[:, cc, :, 0:128],
                                                               scalar=cst[:, C_CW + cc:C_CW + 1 + cc], in1=a[:],
                                                               op0=ALU.mult, op1=ALU.add), r=[("uext", cc), "cst", ka], w=[ka])
            co = cvo[cc % 2]
            P.dve(lambda e, a=a, cc=cc, co=co: e.tensor_tensor(out=co[:], in0=a[:].rearrange("p b t -> p (b t)"),
                                                               in1=gbs[:, cc, :], op=ALU.mult),
                  r=[ka, ("gbs", cc)], w=[("cvo", cc % 2)])
            P.dma(convT_d[cc], co[:], r=[("cvo", cc % 2)], w=[("convT_d", cc)])
        P.barrier()
        P.release(base_mark)

        if stop("Q"):
            return nc
        kidx = P.sb("kidx", [128, T], BF16)
        qidx = P.sb("qidx", [128, 8, NO], BF16)
        P.dma(kidx[:], kidxT_d, r=[("kidxT_d", t_) for t_ in range(8)], w=["kidx"])
        for oc in range(8):
            P.dma(qidx[:, oc, :], qidxT_d[oc], r=[("qidxT_d", oc, t_) for t_ in range(4)], w=[("qidx", oc)])
        sc2 = [P.sb(f"sc{i}", [128, T], F32) for i in range(2)]
        junk = P.sb("junk", [128, T], BF16)
        m01 = P.sb("m01", [128, T], F32)
        cbt = [P.sb(f"cbt{i}", [128, 1024], F32) for i in range(2)]
        rr = [P.sb(f"rr{i}", [128, 512], F32) for i in range(4)]
        mTs = [P.sb(f"mTs{i}", [128, 32, 512], BF16) for i in range(1)]
        sm = P.sb("sm", [128, 64], F32)
        rcnt = [0]
        for g in range(4):
            nk = 1024 * (g + 1)
            nch = nk // 512
            mt = mTs[0]
            kmt = ("mTs", g)
            for j in range(4):
                qi = 4 * g + j
                sc = sc2[qi % 2]
                ksc = ("sc", qi % 2)
                cb = cbt[qi % 2]
                P.dma(cb[:], cb_d[s_, (g // 2) * 4 + j], w=[("cbt", qi % 2)])
                for ch in range(nch):
                    for h in range(16):
                        b = nb()
                        base = (h % 2) * 64
                        P.pe(lambda e, b=b, h=h, base=base, ch=ch: e.matmul(
                            ps[b][:, :], lhsT=qidx[base:base + 64, h // 2, qi * 128:(qi + 1) * 128],
                            rhs=kidx[base:base + 64, ch * 512:(ch + 1) * 512], start=True, stop=True),
                            r=["kidx", ("qidx", h // 2)], w=[PK(b)])
                        ri = rcnt[0] % 4
                        rcnt[0] += 1
                        r_ = rr[ri]
                        P.act(lambda e, b=b, r_=r_: e.activation(out=r_[:], in_=ps[b][:, :], func=AF.Relu),
                              r=[PK(b)], w=[("rr", ri)])
                        dst = sc[:, ch * 512:(ch + 1) * 512]
                        wcol = widx[:, qi, h:h + 1]
                        if h == 0:
                            P.dve(lambda e, r_=r_, dst=dst, wcol=wcol: e.tensor_scalar(out=dst, in0=r_[:], scalar1=wcol,
                                                                                       scalar2=None, op0=ALU.mult),
                                  r=[("rr", ri), "widx"], w=[(ksc, ch)])
                        else:
                            P.dve(lambda e, r_=r_, dst=dst, wcol=wcol: e.scalar_tensor_tensor(
                                out=dst, in0=r_[:], scalar=wcol, in1=dst, op0=ALU.mult, op1=ALU.add),
                                r=[("rr", ri), "widx", (ksc, ch)], w=[(ksc, ch)])
                allsc = [(ksc, ch) for ch in range(nch)]
                P.dve(lambda e, sc=sc: e.tensor_reduce(out=sm[:, 0:1], in_=sc[:, 0:nk], axis=AX.X, op=ALU.max,
                                                       apply_absolute_value=True), r=allsc, w=["sm0"])
                P.dve(lambda e: e.tensor_single_scalar(out=sm[:, 0:1].bitcast(I32), in_=sm[:, 0:1].bitcast(I32),
                                                       scalar=0x7F800000, op=ALU.bitwise_and), r=["sm0"], w=["sm0"])
                P.dve(lambda e: e.tensor_scalar(out=sm[:, 0:1], in0=sm[:, 0:1], scalar1=2.0, scalar2=1e-30,
                                                op0=ALU.mult, op1=ALU.max), r=["sm0"], w=["sm0"])
                for i in range(NBIS + 1):
                    P.pool(lambda e, i=i: e.tensor_scalar(out=sm[:, 8 + i:9 + i], in0=sm[:, 0:1], scalar1=float(2.0 ** -i),
                                                          scalar2=None, op0=ALU.mult), r=["sm0"], w=[("smst", i)])
                P.pool(lambda e: e.memset(sm[:, 1:2], 0.0), w=["sm1"])
                P.dve(lambda e, sc=sc, cb=cb: e.tensor_tensor(out=sc[:, nk - 1024:nk], in0=sc[:, nk - 1024:nk], in1=cb[:],
                                                              op=ALU.add), r=allsc + [("cbt", qi % 2)], w=allsc)
                for i in range(NBIS):
                    P.dve(lambda e, sc=sc: e.tensor_scalar(out=junk[:, 0:nk], in0=sc[:, 0:nk], scalar1=sm[:, 1:2], scalar2=None,
                                                           op0=ALU.is_ge, op1=ALU.add, accum_out=sm[:, 2:3]),
                          r=allsc + ["sm1"], w=["sm2", "junk"])
                    P.dve(lambda e: e.tensor_scalar(out=sm[:, 3:4], in0=sm[:, 2:3], scalar1=kq[:, s_ * 16 + qi:s_ * 16 + qi + 1], scalar2=0.5,
                                                    op0=ALU.is_ge, op1=ALU.subtract), r=["sm2", "kq"], w=["sm3"])
                    P.dve(lambda e, i=i: e.scalar_tensor_tensor(out=sm[:, 1:2], in0=sm[:, 3:4], scalar=sm[:, 8 + i:9 + i],
                                                                in1=sm[:, 1:2], op0=ALU.mult, op1=ALU.add),
                          r=["sm3", ("smst", i), "sm1"], w=["sm1"])
                P.dve(lambda e: e.tensor_tensor(out=sm[:, 4:5], in0=sm[:, 1:2], in1=sm[:, 8 + NBIS:9 + NBIS], op=ALU.subtract),
                      r=["sm1", ("smst", NBIS)], w=["sm4"])
                P.dve(lambda e, sc=sc: e.tensor_scalar(out=m01[:, 0:nk], in0=sc[:, 0:nk], scalar1=sm[:, 4:5], scalar2=None,
                                                       op0=ALU.is_ge), r=allsc + ["sm4"], w=["m01"])
                for k4 in range(nk // 512):
                    b = 4 + (k4 % 2)
                    for kk in range(4):
                        kb = k4 * 4 + kk
                        P.pe(lambda e, b=b, kk=kk, kb=kb: e.transpose(out=ps[b][:, kk * 128:(kk + 1) * 128],
                                                                      in_=m01[:, kb * 128:(kb + 1) * 128], identity=ident[:]),
                             r=["m01", "ident"], w=[PK(b)])
                    P.act(lambda e, b=b, k4=k4, j=j: e.activation(
                        out=mt[:, k4 * 4:(k4 + 1) * 4, j * 128:(j + 1) * 128],
                        in_=ps[b][:, :].rearrange("p (k t) -> p k t", t=128), func=AF.Copy),
                        r=[PK(b)], w=[kmt])
            P.dma(maskT_d[g, :, 0:nk // 128, :], mt[:, 0:nk // 128, :], r=[kmt], w=[("maskT_d", g)])
        P.barrier()
        P.release(base_mark)

        if stop("I"):
            return nc
        def attention(nheads, hp_loader, st_emit, mask_for, scale, out_d, okey):
            pts = [P.sb(f"pt{i}", [128, 512], BF16) for i in range(6)]
            rdn = [P.sb(f"rdn{i}", [128, 512], F32) for i in range(2)]
            ost = [P.sb(f"ost{i}", [128, NO], BF16) for i in range(2)]
            ucnt = [0]
            acnt = [0]
            for hp in range(nheads // 2):
                bufs = hp_loader(hp)
                o_t = ost[hp % 2]
                for hh in range(2):
                    h = hp * 2 + hh
                    for g in range(4):
                        nkb = 8 * (g + 1)
                        accb = 4 + (acnt[0] % 2)
                        acnt[0] += 1
                        pend = []
                        for kb in range(nkb):
                            sb_ = nb()
                            st_emit(bufs, h, hh, g, kb, sb_)
                            pi = ucnt[0] % 6
                            ucnt[0] += 1
                            pt = pts[pi]
                            P.act(lambda e, sb_=sb_, pt=pt: e.activation(out=pt[:], in_=ps[sb_][:, :], func=AF.Exp, scale=scale),
                                  r=[PK(sb_)], w=[("pt", pi)])
                            mk = mask_for(bufs, g, kb)
                            if mk is not None:
                                map_, mkey = mk
                                P.dve(lambda e, pt=pt, map_=map_: e.tensor_tensor(out=pt[:], in0=pt[:], in1=map_, op=ALU.mult),
                                      r=[("pt", pi), mkey], w=[("pt", pi)])
                            pend.append((kb, pt, pi))
                            if len(pend) > 2:
                                kb0, pt0, pi0 = pend.pop(0)
                                P.pe(lambda e, kb0=kb0, pt0=pt0, accb=accb: e.matmul(
                                    ps[accb][:, :], lhsT=bufs["v"][:, kb0, hh * 128:(hh + 1) * 128], rhs=pt0[:],
                                    start=(kb0 == 0), stop=(kb0 == nkb - 1)), r=[("pt", pi0), bufs["kv"]], w=[PK(accb)])
                        for kb0, pt0, pi0 in pend:
                            P.pe(lambda e, kb0=kb0, pt0=pt0, accb=accb: e.matmul(
                                ps[accb][:, :], lhsT=bufs["v"][:, kb0, hh * 128:(hh + 1) * 128], rhs=pt0[:],
                                start=(kb0 == 0), stop=(kb0 == nkb - 1)), r=[("pt", pi0), bufs["kv"]], w=[PK(accb)])
                        rd = rdn[acnt[0] % 2]
                        krd = ("rdn", acnt[0] % 2)
                        nlo, dlo = (0, 64) if hh == 0 else (64, 0)
                        P.dve(lambda e, rd=rd, accb=accb, dlo=dlo: e.reciprocal(out=rd[dlo:dlo + 64, :], in_=ps[accb][dlo:dlo + 64, :]),
                              r=[PK(accb)], w=[krd])
                        P.dve(lambda e, rd=rd, accb=accb, dlo=dlo, nlo=nlo, g=g, o_t=o_t: e.tensor_tensor(
                            out=o_t[nlo:nlo + 64, g * 512:(g + 1) * 512], in0=ps[accb][nlo:nlo + 64, :],
                            in1=rd[dlo:dlo + 64, :], op=ALU.mult), r=[PK(accb), krd], w=[("ost", hp % 2)])
                P.dma(out_d[hp], o_t[:], r=[("ost", hp % 2)], w=[(okey, hp)])

        mres = P.sb("mres", [128, 80, 512], BF16)
        goff = [0, 8, 24, 48]
        for g in range(4):
            nkb = 8 * (g + 1)
            P.dma(mres[:, goff[g]:goff[g] + nkb, :], maskT_d[g, :, 0:nkb, :], r=[("maskT_d", g)], w=[("mres", g)])
        kb2 = [P.sb(f"kb2_{i}", [128, T], BF16) for i in range(2)]
        qb2 = [P.sb(f"qb2_{i}", [128, NO], BF16) for i in range(2)]
        vb2 = [P.sb(f"vb2_{i}", [128, 32, 256], BF16) for i in range(2)]

        def hp_loader0(hp):
            i = hp % 2
            key = ("kvq0", i)
            P.dma(kb2[i][:], kT_d[hp], r=[("kT_d", hp, t_) for t_ in range(8)], w=[key])
            P.dma(qb2[i][:], qT_d[hp], r=[("qT_d", hp, t_) for t_ in range(4)], w=[key])
            P.dma(vb2[i][:], vaug_d[:, :, hp * 256:(hp + 1) * 256].rearrange("k p c -> p k c"),
                  r=[("vaug_d", k_) for k_ in range(32)], w=[key])
            return {"k": kb2[i], "q": qb2[i], "v": vb2[i], "kv": key}

        def st_emit0(bufs, h, hh, g, kb, sb_):
            base = hh * 64
            P.pe(lambda e: e.matmul(ps[sb_][:, :], lhsT=bufs["k"][base:base + 64, kb * 128:(kb + 1) * 128],
                                    rhs=bufs["q"][base:base + 64, g * 512:(g + 1) * 512], start=True, stop=True),
                 r=[bufs["kv"]], w=[PK(sb_)])

        def mask_for0(bufs, g, kb):
            return mres[:, goff[g] + kb, :], ("mres", g)

        attention(8, hp_loader0, st_emit0, mask_for0, 0.125, attnT_d, "attnT_d")
        P.barrier()
        P.release(base_mark)

        if stop("A0"):
            return nc
        def out_phase(in_chunks, wd, layer, x_src, xkey_src, x_dst, xkey_dst):
            wo = P.sb("wo", [128, 8, D], BF16)
            load_w(wo[:], wd.rearrange("(c p) n -> p c n", p=128), "wo")
            ain = P.sb("ain", [128, 8, NO], BF16)
            for c, (ap_, rk) in enumerate(in_chunks):
                P.dma(ain[:, c, :], ap_, r=rk, w=[("ain", c)])
            xo2 = [P.sb(f"xo{i}", [128, 8, 512], F32) for i in range(2)]
            mx = P.sb("mx", [128, 8, 512], F32)
            sq_ = P.sb("sqo", [128, 8, 512], BF16)
            rs = P.sb("rso", [128, 512], F32)
            xv = x_src.rearrange("(c p) t -> p c t", p=128)
            xdv = x_dst.rearrange("(c p) t -> p c t", p=128)
            for tc in range(4):
                xo = xo2[tc % 2]
                kxo = ("xo", tc % 2)
                P.dma(xo[:], xv[:, :, tc * 512:(tc + 1) * 512], r=[(xkey_src, tc)], w=[kxo])
                for oc in range(8):
                    b = nb()
                    for kc in range(8):
                        P.pe(lambda e, kc=kc, oc=oc, b=b: e.matmul(ps[b][:, :], lhsT=wo[:, kc, oc * 128:(oc + 1) * 128],
                                                                   rhs=ain[:, kc, tc * 512:(tc + 1) * 512],
                                                                   start=(kc == 0), stop=(kc == 7)),
                             r=["wo", ("ain", kc)], w=[PK(b)])
                    P.act(lambda e, b=b, oc=oc: e.activation(out=mx[:, oc, :], in_=ps[b][:, :], func=AF.Copy),
                          r=[PK(b)], w=[("mx", oc)])
                    P.dve(lambda e, b=b, oc=oc: e.tensor_tensor(out=sq_[:, oc, :], in0=mx[:, oc, :], in1=mx[:, oc, :], op=ALU.mult),
                          r=[("mx", oc)], w=[("sqo", oc)])
                for c in range(8):
                    P.pe(lambda e, c=c: e.matmul(ps[7][:, :], lhsT=ones_b[:], rhs=sq_[:, c, :], start=(c == 0), stop=(c == 7)),
                         r=[("sqo", c), "ones_b"], w=[PK(7)])
                P.act(lambda e: e.activation(out=rs[:], in_=ps[7][:, :], func=AF.Sqrt, bias=EPS, scale=1.0), r=[PK(7)], w=["rso"])
                P.dve(lambda e: e.reciprocal(out=rs[:], in_=rs[:]), r=["rso"], w=["rso"])
                for c in range(8):
                    gc_ = gcol(layer, 1) + c
                    P.dve(lambda e, c=c, gc_=gc_: e.scalar_tensor_tensor(out=mx[:, c, :], in0=mx[:, c, :], scalar=cst[:, gc_:gc_ + 1],
                                                                          in1=rs[:], op0=ALU.mult, op1=ALU.mult),
                           r=[("mx", c), "rso", "cst"], w=[("mx", c)])
                    P.dve(lambda e, c=c, xo=xo: e.tensor_tensor(out=xo[:, c, :], in0=xo[:, c, :], in1=mx[:, c, :], op=ALU.add),
                          r=[("mx", c), kxo], w=[kxo])
                P.dma(xdv[:, :, tc * 512:(tc + 1) * 512], xo[:], r=[kxo], w=[(xkey_dst, tc)])

        in0 = [(attnT_d[c], [("attnT_d", c)]) for c in range(4)] + [(convT_d[c], [("convT_d", c)]) for c in range(4)]
        out_phase(in0, wout_d, 0, xT_own[s_, :, 0:NO], "xown", x1T_d, "x1T_d")
        P.barrier()
        P.release(base_mark)

        def ffn_phase(layer, x_src, xkey_src, x_dst, xkey_dst):
            w1s = P.sb("w1s", [128, 8, 4096], BF16)
            w2s = P.sb("w2s", [128, 32, D], BF16)
            w1v = w1_d[layer].rearrange("(c p) n -> p c n", p=128)
            w2v = w2_d[layer].rearrange("(c p) n -> p c n", p=128)
            for c in range(8):
                P.dma(w1s[:, c, :], w1v[:, c, :], w=[("w1s", c)], q="pool")
            for c in range(0, 32, 4):
                P.dma(w2s[:, c:c + 4, :], w2v[:, c:c + 4, :], w=[("w2s", c // 4)], q="pool")
            xf = P.sb("xf", [128, 8, 512], F32)
            off3 = P.mark()
            yb = P.sb("yb", [128, 8, 512], F32)
            end3 = P.mark()
            P.release(off3)
            sq_f = P.sb("sqf", [128, 8, 512], BF16)
            hf = P.sb("hf", [128, 8, 512], BF16)
            assert P.mark() == end3
            h1 = P.sb("h1", [128, 32, 512], BF16)
            sqy = [P.sb(f"sqy{i}", [128, 512], BF16) for i in range(2)]
            rt = [P.sb(f"rt{i}", [128, 512], F32) for i in range(2)]
            alias_keys = ["sqf"] + [("hf", c) for c in range(8)]
            rsf = P.sb("rsf", [128, 512], F32)
            xv = x_src.rearrange("(c p) t -> p c t", p=128)
            xdv = x_dst.rearrange("(c p) t -> p c t", p=128)
            rc = [0]
            for tc in range(4):
                P.dma(xf[:], xv[:, :, tc * 512:(tc + 1) * 512], r=[(xkey_src, tc)], w=["xf"])
                P.act(lambda e: e.activation(out=rsf[:, 0:1], in_=rsf[:, 0:1], func=AF.Copy), r=["rsf"],
                      w=alias_keys + ["rsf"] + [("yb", c) for c in range(8)])
                rmsnorm_fm(xf, "xf", 8, 512, gcol(layer, 2), hf, "hf", ones_b, sq_f, "sqf", rsf, "rsf", 7)
                for oc in range(32):
                    b = nb()
                    for kc in range(8):
                        P.pe(lambda e, kc=kc, oc=oc, b=b: e.matmul(ps[b][:, :], lhsT=w1s[:, kc, oc * 128:(oc + 1) * 128],
                                                                   rhs=hf[:, kc, :], start=(kc == 0), stop=(kc == 7)),
                             r=[("w1s", kc), ("hf", kc)], w=[PK(b)])
                    ri = rc[0] % 2
                    rc[0] += 1
                    P.act(lambda e, b=b, ri=ri: e.activation(out=rt[ri][:], in_=ps[b][:, :], func=AF.Relu), r=[PK(b)], w=[("rt", ri)])
                    eng = P.dve if oc % 2 == 0 else P.pool
                    eng(lambda e, ri=ri, oc=oc: e.tensor_tensor(out=h1[:, oc, :], in0=rt[ri][:], in1=rt[ri][:], op=ALU.mult),
                        r=[("rt", ri)], w=[("h1", oc)])
                for oc in range(8):
                    b = nb()
                    for kc in range(32):
                        P.pe(lambda e, kc=kc, oc=oc, b=b: e.matmul(ps[b][:, :], lhsT=w2s[:, kc, oc * 128:(oc + 1) * 128],
                                                                   rhs=h1[:, kc, :], start=(kc == 0), stop=(kc == 31)),
                             r=[("w2s", kc // 4), ("h1", kc)], w=[PK(b)])
                    P.act(lambda e, b=b, oc=oc: e.activation(out=yb[:, oc, :], in_=ps[b][:, :], func=AF.Copy), r=[PK(b)],
                          w=[("yb", oc)] + (alias_keys if oc == 0 else []))
                    P.dve(lambda e, oc=oc: e.tensor_tensor(out=sqy[oc % 2][:], in0=yb[:, oc, :], in1=yb[:, oc, :], op=ALU.mult),
                          r=[("yb", oc)], w=[("sqy", oc % 2)])
                    P.pe(lambda e, oc=oc: e.matmul(ps[7][:, :], lhsT=ones_b[:], rhs=sqy[oc % 2][:], start=(oc == 0), stop=(oc == 7)),
                         r=[("sqy", oc % 2), "ones_b"], w=[PK(7)])
                P.act(lambda e: e.activation(out=rsf[:], in_=ps[7][:, :], func=AF.Sqrt, bias=EPS, scale=1.0), r=[PK(7)], w=["rsf"])
                P.dve(lambda e: e.reciprocal(out=rsf[:], in_=rsf[:]), r=["rsf"], w=["rsf"])
                for c in range(8):
                    gc_ = gcol(layer, 3) + c
                    P.dve(lambda e, c=c, gc_=gc_: e.scalar_tensor_tensor(out=yb[:, c, :], in0=yb[:, c, :], scalar=cst[:, gc_:gc_ + 1],
                                                                          in1=rsf[:], op0=ALU.mult, op1=ALU.mult),
                           r=[("yb", c), "rsf", "cst"], w=[("yb", c)])
                    P.dve(lambda e, c=c: e.tensor_tensor(out=yb[:, c, :], in0=yb[:, c, :], in1=xf[:, c, :], op=ALU.add),
                          r=[("yb", c), "xf"], w=[("yb", c)])
                P.dma(xdv[:, :, tc * 512:(tc + 1) * 512], yb[:], r=[("yb", c) for c in range(8)], w=[(xkey_dst, tc)])

        if stop("O0"):
            return nc
        ffn_phase(0, x1T_d, "x1T_d", x2T_d[s_], ("x2T_d", s_))
        P.barrier()
        P.release(base_mark)

    if stop("F0"):
        return nc
    tabs = [rope_tables(pos_own[s1], NO, C_FR1, C_SG1, f"r1o{s1}") for s1 in range(2)]
    C1S1 = [None, None]
    m1 = P.mark()
    wdq = P.sb("wdq", [128, 8, 384], BF16)
    wuq = P.sb("wuq", [128, 3, 2048], BF16)
    wdkv = P.sb("wdkv", [128, 8, 320], BF16)
    load_w(wdq[:], wdq_d.rearrange("(c p) n -> p c n", p=128), "wdq")
    load_w(wuq[:], wuq_d.rearrange("(c p) n -> p c n", p=128), "wuq")
    load_w(wdkv[:], wdkv_d.rearrange("(c p) n -> p c n", p=128), "wdkv")
    xs2 = [P.sb(f"xs{i}", [128, 8, 512], F32) for i in range(2)]
    sq = P.sb("sq", [128, 8, 512], BF16)
    hb = P.sb("hb", [128, 8, 512], BF16)
    rstd = P.sb("rstd", [128, 512], F32)
    t1 = [P.sb(f"t1_{i}", [128, 512], F32) for i in range(2)]
    t2 = [P.sb(f"t2_{i}", [128, 512], F32) for i in range(2)]
    cq = P.sb("cq", [128, 3, 512], F32)
    cqn = P.sb("cqn", [128, 3, 512], BF16)
    ckv = P.sb("ckv", [128, 2, 512], F32)
    ckvn = P.sb("ckvn", [128, 2, 512], F32)
    krs = P.sb("krs", [32, 512], F32)
    qst = [P.sb(f"qst1_{i}", [128, 16, 512], BF16) for i in range(2)]
    sq3 = P.sb("sq3", [128, 3, 512], BF16)
    rs3 = P.sb("rs3", [128, 512], F32)

    def rope_evac1(bA, bB, m, col0, dst, kdst, eng_out="pool"):
        Cc, Ss = C1S1[0], C1S1[1]
        i = tcnt[0] % 2
        tcnt[0] += 1
        P.dve(lambda e: e.tensor_tensor(out=t1[i][0:m, :], in0=ps[bA][0:m, :], in1=Cc[0:m, col0:col0 + 512], op=ALU.mult),
              r=[PK(bA), "r1o0C", "r1o1C"], w=[("t1", i)])
        P.dve(lambda e: e.tensor_tensor(out=t2[i][0:m, :], in0=ps[bB][0:m, :], in1=Ss[0:m, col0:col0 + 512], op=ALU.mult),
              r=[PK(bB), "r1o0S", "r1o1S"], w=[("t2", i)])
        P.pool(lambda e: e.tensor_tensor(out=dst, in0=t1[i][0:m, :], in1=t2[i][0:m, :], op=ALU.add),
               r=[("t1", i), ("t2", i)], w=[kdst])

    for s1 in range(2):
        C1S1[0], C1S1[1] = tabs[s1]
        x2v = x2T_d[s1].rearrange("(c p) t -> p c t", p=128)
        for tc in range(4):
            xs = xs2[tc % 2]
            kx = ("xs", tc % 2)
            P.dma(xs[:], x2v[:, :, tc * 512:(tc + 1) * 512], r=[(("x2T_d", s1), tc)], w=[kx])
            rmsnorm_fm(xs, kx, 8, 512, gcol(1, 0), hb, "hb", ones_b, sq, "sq", rstd, "rstd", 7)
            if s1 == 0:
                for oc in range(3):
                    b = nb()
                    proj_fm(wdq, "wdq", oc * 128, hb, "hb", 8, 512, b)
                    P.act(lambda e, b=b, oc=oc: e.activation(out=cq[:, oc, :], in_=ps[b][:, :], func=AF.Copy), r=[PK(b)], w=["cq"])
                P.act(lambda e: e.activation(out=sq3[:], in_=cq[:], func=AF.Square), r=["cq"], w=["sq3"])
                for c in range(3):
                    P.pe(lambda e, c=c: e.matmul(ps[7][:, :], lhsT=ones_q[:], rhs=sq3[:, c, :], start=(c == 0), stop=(c == 2)),
                         r=["sq3", "ones_q"], w=[PK(7)])
                P.act(lambda e: e.activation(out=rs3[:], in_=ps[7][:, :], func=AF.Sqrt, bias=EPS, scale=256.0 / 384.0), r=[PK(7)], w=["rs3"])
                P.dve(lambda e: e.reciprocal(out=rs3[:], in_=rs3[:]), r=["rs3"], w=["rs3"])
                for c in range(3):
                    P.dve(lambda e, c=c: e.scalar_tensor_tensor(out=cqn[:, c, :], in0=cq[:, c, :], scalar=cst[:, C_QN + c:C_QN + c + 1],
                                                                in1=rs3[:], op0=ALU.mult, op1=ALU.mult), r=["cq", "rs3", "cst"], w=[("cqn", c)])
                qs = qst[tc % 2]
                for oc in range(8):
                    b = nb()
                    proj_fm(wuq, "wuq", oc * 128, cqn, "cqn", 3, 512, b)
                    P.act(lambda e, b=b, oc=oc: e.activation(out=qs[:, oc, :], in_=ps[b][:, :], func=AF.Copy), r=[PK(b)], w=[("qst", tc % 2, oc)])
                    P.dma(qnT_d[oc, :, tc * 512:(tc + 1) * 512], qs[:, oc, :], r=[("qst", tc % 2, oc)], w=[("qnT_d", oc, tc)])
                for oc in range(8):
                    bA, bB = nb(), nb()
                    proj_fm(wuq, "wuq", 1024 + oc * 64, cqn, "cqn", 3, 512, bA, m=64)
                    proj_fm(wuq, "wuq", 1536 + oc * 64, cqn, "cqn", 3, 512, bB, m=64)
                    rope_evac1(bA, bB, 64, tc * 512, qs[0:64, 8 + oc, :], ("qst", tc % 2, 8 + oc))
                    P.dma(qrT_d[oc, :, tc * 512:(tc + 1) * 512], qs[0:64, 8 + oc, :], r=[("qst", tc % 2, 8 + oc)], w=[("qrT_d", oc, tc)])
            for oc in range(2):
                b = nb()
                proj_fm(wdkv, "wdkv", oc * 128, hb, "hb", 8, 512, b)
                P.act(lambda e, b=b, oc=oc: e.activation(out=ckv[:, oc, :], in_=ps[b][:, :], func=AF.Copy), r=[PK(b)], w=["ckv"])
            P.act(lambda e: e.activation(out=sq3[:, 0:2, :], in_=ckv[:], func=AF.Square), r=["ckv"], w=["sq3"])
            for c in range(2):
                P.pe(lambda e, c=c: e.matmul(ps[7][:, :], lhsT=ones_q[:], rhs=sq3[:, c, :], start=(c == 0), stop=(c == 1)),
                     r=["sq3", "ones_q"], w=[PK(7)])
            P.act(lambda e: e.activation(out=rs3[:], in_=ps[7][:, :], func=AF.Sqrt, bias=EPS, scale=1.0), r=[PK(7)], w=["rs3"])
            P.dve(lambda e: e.reciprocal(out=rs3[:], in_=rs3[:]), r=["rs3"], w=["rs3"])
            for c in range(2):
                P.dve(lambda e, c=c: e.scalar_tensor_tensor(out=ckvn[:, c, :], in0=ckv[:, c, :], scalar=cst[:, C_KVN + c:C_KVN + c + 1],
                                                            in1=rs3[:], op0=ALU.mult, op1=ALU.mult), r=["ckv", "rs3", "cst"], w=[("ckvn", c)])
                P.dma(kva_sets_d[s1, c * 128:(c + 1) * 128, tc * 512:(tc + 1) * 512], ckvn[:, c, :], r=[("ckvn", c)], w=[("kva", s1, tc, c)])
            bA, bB = nb(), nb()
            proj_fm(wdkv, "wdkv", 256, hb, "hb", 8, 512, bA, m=32)
            proj_fm(wdkv, "wdkv", 288, hb, "hb", 8, 512, bB, m=32)
            rope_evac1(bA, bB, 32, tc * 512, krs[:, :], "krs")
            P.dma(kva_sets_d[s1, 256:288, tc * 512:(tc + 1) * 512], krs[:, :], r=["krs"], w=[("kva", s1, tc, 2)])
    P.barrier()
    P.release(base_mark)

    if stop("Q1"):
        return nc
    wkk = P.sb("wkk", [128, 2, 1024], BF16)
    wkv = P.sb("wkv", [128, 2, 1024], BF16)
    load_w(wkk[:], wukvk_d.rearrange("(c p) n -> p c n", p=128), "wkk")
    load_w(wkv[:], wukvv_d.rearrange("(c p) n -> p c n", p=128), "wkv")
    ckf = [P.sb(f"ckf{i}", [128, 2, 512], F32) for i in range(2)]
    ckb = [P.sb(f"ckb{i}", [128, 2, 512], BF16) for i in range(2)]
    kns = [P.sb(f"kns{i}", [128, 8, 512], BF16) for i in range(2)]
    vst1 = [P.sb(f"vst1_{i}", [128, 16, 128], BF16) for i in range(2)]
    kr4 = P.sb("kr4", [64, T], BF16)
    krf = P.sb("krf", [64, T], F32)
    for i in range(2):
        P.pool(lambda e, i=i: e.memset(vst1[i][:], 1.0), w=[("vst1", i)])
    for sl in range(32):
        s1, m_ = sl % 2, sl // 2
        for rep_ in range(2):
            P.dma(krf[rep_ * 32:(rep_ + 1) * 32, sl * 128:(sl + 1) * 128],
                  kva_sets_d[s1, 256:288, m_ * 128:(m_ + 1) * 128], w=[("krf", sl // 8)])
    for q4 in range(4):
        P.dve(lambda e, q4=q4: e.tensor_copy(out=kr4[:, q4 * 1024:(q4 + 1) * 1024], in_=krf[:, q4 * 1024:(q4 + 1) * 1024]),
              r=[("krf", q4)], w=["kr4"])
    for tc in range(8):
        cf = ckf[tc % 2]
        cb_ = ckb[tc % 2]
        for tb in range(4):
            sl = tc * 4 + tb
            s1, m_ = sl % 2, sl // 2
            for c in range(2):
                P.dma(cf[:, c, tb * 128:(tb + 1) * 128],
                      kva_sets_d[s1, c * 128:(c + 1) * 128, m_ * 128:(m_ + 1) * 128], w=[("ckf", tc % 2)])
        P.dve(lambda e, cf=cf, cb_=cb_: e.tensor_copy(out=cb_[:], in_=cf[:]), r=[("ckf", tc % 2)], w=[(("ckb", tc % 2), 0), (("ckb", tc % 2), 1)])
        kn = kns[tc % 2]
        for oc in range(8):
            b = nb()
            proj_fm(wkk, "wkk", oc * 128, cb_, ("ckb", tc % 2), 2, 512, b)
            P.act(lambda e, b=b, oc=oc, kn=kn: e.activation(out=kn[:, oc, :], in_=ps[b][:, :], func=AF.Copy), r=[PK(b)], w=[("kns", tc % 2, oc)])
            P.dma(knT_d[oc, :, tc * 512:(tc + 1) * 512], kn[:, oc, :], r=[("kns", tc % 2, oc)], w=[("knT_d", oc, tc)])
        for tb in range(4):
            kb = tc * 4 + tb
            vs = vst1[kb % 2]
            vv = vs[:].rearrange("p (h two) d -> p h two d", two=2)
            for half in range(2):
                b = nb()
                for kc in range(2):
                    P.pe(lambda e, kc=kc, tb=tb, b=b, half=half, cb_=cb_: e.matmul(
                        ps[b][:, :], lhsT=cb_[:, kc, tb * 128:(tb + 1) * 128], rhs=wkv[:, kc, half * 512:(half + 1) * 512],
                        start=(kc == 0), stop=(kc == 1)), r=["wkv", (("ckb", tc % 2), kc)], w=[PK(b)])
                pv = ps[b][:, :].rearrange("p (h two d) -> p h two d", two=2, d=64)
                P.act(lambda e, pv=pv, vv=vv, half=half: e.activation(out=vv[:, half * 4:(half + 1) * 4, 0, 0:64], in_=pv[:, :, 0, :], func=AF.Copy),
                      r=[PK(b)], w=[("vst1", kb % 2)])
                P.act(lambda e, pv=pv, vv=vv, half=half: e.activation(out=vv[:, half * 4:(half + 1) * 4, 1, 64:128], in_=pv[:, :, 1, :], func=AF.Copy),
                      r=[PK(b)], w=[("vst1", kb % 2)])
            P.dma(vaug1_d[kb], vs[:].rearrange("p h d -> p (h d)"), r=[("vst1", kb % 2)], w=[("vaug1_d", kb)])
    P.dma(kr4_d, kr4[:], r=["kr4"], w=["kr4_d"])
    P.barrier()
    P.release(base_mark)
    kr4b = P.sb("kr4b", [64, T], BF16)
    P.dma(kr4b[:], kr4_d, r=["kr4_d"], w=["kr4b"])

    if stop("K1"):
        return nc
    mTs1 = P.sb("mTs1", [128, 2, 8, 512], BF16)
    for i in range(2):
        P.dma(mTs1[:, i], mT_d[i], w=["mTs1"])
    kb2 = [P.sb(f"kn2_{i}", [128, T], BF16) for i in range(2)]
    qb2 = [P.sb(f"qn2_{i}", [128, NO], BF16) for i in range(2)]
    qr2 = [P.sb(f"qr2_{i}", [64, NO], BF16) for i in range(2)]
    vb2 = [P.sb(f"vb21_{i}", [128, 32, 256], BF16) for i in range(2)]

    def hp_loader1(hp):
        i = hp % 2
        key = ("kvq1", i)
        P.dma(kb2[i][:], knT_d[hp], r=[("knT_d", hp, t_) for t_ in range(8)], w=[key])
        P.dma(qb2[i][:], qnT_d[hp], r=[("qnT_d", hp, t_) for t_ in range(4)], w=[key])
        P.dma(qr2[i][:], qrT_d[hp], r=[("qrT_d", hp, t_) for t_ in range(4)], w=[key])
        P.dma(vb2[i][:], vaug1_d[:, :, hp * 256:(hp + 1) * 256].rearrange("k p c -> p k c"),
              r=[("vaug1_d", k_) for k_ in range(32)], w=[key])
        return {"k": kb2[i], "q": qb2[i], "v": vb2[i], "kv": key, "qr": qr2[i], "qrk": key}

    def st_emit1(bufs, h, hh, g, kb, sb_):
        base = hh * 64
        rb = hh * 32
        P.pe(lambda e: e.matmul(ps[sb_][:, :], lhsT=bufs["k"][base:base + 64, kb * 128:(kb + 1) * 128],
                                rhs=bufs["q"][base:base + 64, g * 512:(g + 1) * 512], start=True, stop=False),
             r=[bufs["kv"]], w=[PK(sb_)])
        P.pe(lambda e: e.matmul(ps[sb_][:, :], lhsT=kr4b[rb:rb + 32, kb * 128:(kb + 1) * 128],
                                rhs=bufs["qr"][rb:rb + 32, g * 512:(g + 1) * 512], start=False, stop=True),
             r=["kr4b", bufs["qrk"]], w=[PK(sb_)])

    def mask_for1(bufs, g, kb):
        rel = kb - 8 * g
        if rel < 0:
            return None
        return mTs1[:, g // 2, rel, :], "mTs1"

    attention(16, hp_loader1, st_emit1, mask_for1, float(96 ** -0.5), attnT_d, "attnT1_d")
    P.barrier()
    P.release(base_mark)

    if stop("A1"):
        return nc
    in1 = [(attnT_d[c], [("attnT1_d", c)]) for c in range(8)]
    out_phase(in1, wo_d, 1, x2T_d[0], ("x2T_d", 0), x3T_d, "x3T_d")
    P.barrier()
    P.release(base_mark)
    ffn_phase(1, x3T_d, "x3T_d", outT, "outT")
    P.final_wait([("outT", tc) for tc in range(4)])
    P.emit()
    return nc


def _swap_cols(w, head, a, b_):
    n = w.shape[1]
    idx = np.arange(n).reshape(-1, head)
    perm = np.concatenate([idx[:, a:b_], idx[:, 0:a], idx[:, b_:]], axis=1).reshape(-1)
    return w[:, perm]


def prepare_inputs(inp, n_batch=4):
    f32 = np.float32
    x = np.asarray(inp["x"], f32)
    pos = np.asarray(inp["positions"]).astype(np.int32)
    w_in = np.asarray(inp["even_w_in"], f32)[0]
    offs = np.cumsum([0, 512, 512, 512, 1024, 64, 16, 512, 512, 512])
    wq_, wk_, wv_, wqi, wki, wwi, wgb, wgc, wxi = [w_in[:, offs[i]:offs[i + 1]] for i in range(9)]
    w0k = np.concatenate([wk_, _swap_cols(wk_, 64, 8, 16), wki, wki, _swap_cols(wki, 64, 8, 16), _swap_cols(wki, 64, 8, 16)], axis=1)
    w0q = np.concatenate([wq_, _swap_cols(wq_, 64, 8, 16), wqi, _swap_cols(wqi, 64, 8, 16), wgb, wgc, wxi], axis=1)
    w_uq = np.asarray(inp["odd_w_uq"], f32)[0]
    cols = np.arange(1536).reshape(16, 96)
    wuq_n = w_uq[:, cols[:, :64].reshape(-1)]
    wuq_r = w_uq[:, cols[:, 64:].reshape(-1)]
    wuq = np.concatenate([wuq_n, wuq_r, _swap_cols(wuq_r, 32, 16, 32)], axis=1)
    w_dkv = np.asarray(inp["odd_w_dkv"], f32)[0]
    wdkv = np.concatenate([w_dkv[:, :256], w_dkv[:, 256:], _swap_cols(w_dkv[:, 256:], 32, 16, 32)], axis=1)
    w_ukv = np.asarray(inp["odd_w_ukv"], f32)[0]
    c2 = np.arange(2048).reshape(16, 128)
    wukv_k = w_ukv[:, c2[:, :64].reshape(-1)]
    wukv_v = w_ukv[:, c2[:, 64:].reshape(-1)]

    cst = np.zeros((128, NCST), f32)
    kinds = ["norm_mix_pre", "norm_mix_post", "norm_ffn_pre", "norm_ffn_post"]
    for l in range(2):
        for k, nm in enumerate(kinds):
            cst[:, gcol(l, k):gcol(l, k) + 8] = np.asarray(inp[nm], f32)[l].reshape(8, 128).T
    cst[:, C_QN:C_QN + 3] = np.asarray(inp["odd_q_norm"], f32)[0].reshape(3, 128).T
    cst[:, C_KVN:C_KVN + 2] = np.asarray(inp["odd_kv_norm"], f32)[0].reshape(2, 128).T
    cw = np.asarray(inp["even_conv_w"], f32)[0]
    for j in range(3):
        cst[:, C_CW + j * 4:C_CW + j * 4 + 4] = cw[j].reshape(4, 128).T
    theta = 500000.0
    if0 = (theta ** (-np.arange(0, 16, 2, dtype=np.float32) / 16)).astype(f32)
    if1 = (theta ** (-np.arange(0, 32, 2, dtype=np.float32) / 32)).astype(f32)
    for p in range(128):
        r = p % 64
        if r < 16:
            cst[p, C_FR0] = if0[r % 8]
            cst[p, C_SG0] = -1.0 if r < 8 else 1.0
        r = p % 32
        cst[p, C_FR1] = if1[r % 16]
        cst[p, C_SG1] = -1.0 if r < 16 else 1.0

    shared = {
        "cst": cst, "w0k": w0k, "w0v": np.ascontiguousarray(wv_), "w0q": w0q, "w0wi": np.ascontiguousarray(wwi),
        "w_out": np.asarray(inp["even_w_out"], f32)[0], "w1": np.asarray(inp["mlp_w1"], f32),
        "w2": np.asarray(inp["mlp_w2"], f32), "w_dq": np.asarray(inp["odd_w_dq"], f32)[0], "w_uq": wuq,
        "w_dkv": wdkv, "w_ukv_k": np.ascontiguousarray(wukv_k), "w_ukv_v": np.ascontiguousarray(wukv_v),
        "w_o": np.asarray(inp["odd_w_o"], f32)[0],
    }
    shared = {k: np.ascontiguousarray(v, dtype=f32) for k, v in shared.items()}
    in_maps = []
    own_idx_all = []
    qi = np.arange(128)
    s_ = np.arange(1024)
    for b in range(n_batch):
        for par in range(2):
            xb = x[b]
            sets = [blocks_for(par), blocks_for(1 - par)]
            xT_sets, pos_sets, cbs = [], [], []
            kq = np.zeros((128, 32), f32)
            for si, blks in enumerate(sets):
                idx = np.concatenate([np.arange(128 * p, 128 * p + 128) for p in blks])
                halo = np.zeros((32, D), f32)
                for i, p in enumerate(blks):
                    if p > 0:
                        halo[2 * i:2 * i + 2] = xb[128 * p - 2:128 * p]
                xT_sets.append(np.concatenate([xb[idx], halo], axis=0).T)
                pos_sets.append(pos[b][idx][None, :])
                cb = np.zeros((8, 128, 1024), f32)
                for i, p in enumerate(blks):
                    kq[:, si * 16 + i] = np.minimum(256, 128 * p + qi + 1)
                for g in range(4):
                    for j in range(4):
                        rel = blks[4 * g + j] % 8
                        vis = s_[None, :] <= (rel * 128 + qi)[:, None]
                        cb[(g // 2) * 4 + j] = np.where(vis, 0.0, -1e30)
                cbs.append(cb)
            own = np.concatenate([np.arange(128 * p, 128 * p + 128) for p in sets[0]])
            own_idx_all.append(own)
            mT = np.zeros((2, 128, 8, 512), f32)
            for g in (0, 2):
                for j in range(4):
                    pq = sets[0][4 * g + j]
                    qpos = 128 * pq + qi
                    for rel in range(8):
                        sl = 8 * g + rel
                        pk = sets[sl % 2][sl // 2]
                        kpos = 128 * pk + qi
                        mT[g // 2, :, rel, j * 128:(j + 1) * 128] = (kpos[:, None] <= qpos[None, :])
            m = dict(shared)
            m.update({
                "xT_seq": np.ascontiguousarray(xb.T), "xT_own": np.ascontiguousarray(np.stack(xT_sets)),
                "pos_seq": np.ascontiguousarray(pos[b][None, :]), "pos_own": np.ascontiguousarray(np.stack(pos_sets)),
                "kq": kq, "cb": np.stack(cbs), "mT": mT.astype(ml_dtypes.bfloat16),
            })
            in_maps.append(m)
    return in_maps, own_idx_all


_NC_CACHE = {}


def kernel(**inputs):
    in_maps, own_idx = prepare_inputs(inputs, 4)
    if 8 not in _NC_CACHE:
        _NC_CACHE[8] = build_program(8)
    nc = _NC_CACHE[8]
    res = run_bass_kernel_spmd(nc, in_maps, core_ids=list(range(8)))
    out = np.zeros((4, T, D), np.float32)
    for c in range(8):
        b = c // 2
        out[b, own_idx[c], :] = np.asarray(res.results[c]["outT"], np.float32).T
    return out
```

```python
import types
import numpy as np
import ml_dtypes
import concourse.bass as bass
import concourse.mybir as mybir
from concourse.bass_utils import run_bass_kernel_spmd

F32 = mybir.dt.float32
BF16 = mybir.dt.bfloat16
I32 = mybir.dt.int32
AF = mybir.ActivationFunctionType
ALU = mybir.AluOpType
AX = mybir.AxisListType
DT_SIZE = {F32: 4, BF16: 2, I32: 4}

T = 4096
NO = 2048
D = 1024
EPS = 1e-6
NBIS = 24
TWO_PI = float(2 * np.pi)


class Op:
    __slots__ = ("eng", "fn", "reads", "writes", "is_dma", "deps", "needed", "ordinal",
                 "dsem", "dval", "barrier")

    def __init__(self, eng, fn, reads, writes, is_dma):
        self.eng = eng
        self.fn = fn
        self.reads = reads
        self.writes = writes
        self.is_dma = is_dma
        self.deps = []
        self.needed = False
        self.ordinal = None
        self.dsem = None
        self.dval = None
        self.barrier = False


class Prog:
    ENGS = ("pe", "act", "dve", "pool", "sp")
    SB_LIMIT = 228352

    def __init__(self, nc, n_dma_sems=12):
        self.nc = nc
        self.ops = []
        self.sb_off = 16896
        self.sb_max = 0
        self.n_dma_sems = n_dma_sems
        self._uid = 0
        self._bank = 0

    def sb(self, name, shape, dtype):
        nbytes = int(np.prod(shape[1:])) * DT_SIZE[dtype]
        nbytes = (nbytes + 63) // 64 * 64
        self._uid += 1
        t = self.nc.alloc_sbuf_tensor_at(f"{name}_{self._uid}", list(shape), dtype, offset=self.sb_off)
        self.sb_off += nbytes
        self.sb_max = max(self.sb_max, self.sb_off)
        assert self.sb_off <= self.SB_LIMIT, f"SBUF overflow {self.sb_off} at {name}"
        return t

    def mark(self):
        return self.sb_off

    def release(self, m):
        self.sb_off = m

    @staticmethod
    def _freeze(fn):
        if getattr(fn, "__closure__", None) is None:
            return fn
        cells = []
        for c in fn.__closure__:
            try:
                cells.append(types.CellType(c.cell_contents))
            except ValueError:
                cells.append(c)
        return types.FunctionType(fn.__code__, fn.__globals__, fn.__name__, fn.__defaults__, tuple(cells))

    def add(self, eng, fn, r=(), w=(), dma=False):
        fn = self._freeze(fn)
        o = Op(eng, fn, tuple(r), tuple(w), dma)
        self.ops.append(o)
        return o

    def pe(self, fn, r=(), w=()):
        return self.add("pe", fn, r, w)

    def act(self, fn, r=(), w=()):
        return self.add("act", fn, r, w)

    def dve(self, fn, r=(), w=()):
        return self.add("dve", fn, r, w)

    def pool(self, fn, r=(), w=()):
        return self.add("pool", fn, r, w)

    def dma(self, out, in_, r=(), w=(), q="sp", **kw):
        return self.add(q, lambda e: e.dma_start(out=out, in_=in_, **kw), r, w, dma=True)

    def final_wait(self, keys):
        return self.add("sp", lambda e: e.nop(), r=keys, w=())

    def barrier(self):
        o = Op(None, None, (), (), False)
        o.barrier = True
        self.ops.append(o)

    def finalize(self):
        last_w = {}
        readers = {}
        since_barrier = []
        pending_barrier = None
        seen_after = set()
        for o in self.ops:
            if o.barrier:
                summ = []
                lastc = {}
                for p in since_barrier:
                    if p.is_dma:
                        summ.append(p)
                    else:
                        lastc[p.eng] = p
                summ.extend(lastc.values())
                if pending_barrier is not None:
                    summ.extend(pending_barrier)
                pending_barrier = summ
                seen_after = set()
                since_barrier = []
                continue
            deps = []
            if pending_barrier is not None and o.eng not in seen_after:
                deps.extend(pending_barrier)
                seen_after.add(o.eng)
            for k in o.reads:
                if k in last_w:
                    deps.append(last_w[k])
            for k in o.writes:
                if k in last_w:
                    deps.append(last_w[k])
                deps.extend(readers.get(k, ()))
            for k in o.reads:
                readers.setdefault(k, []).append(o)
            for k in o.writes:
                last_w[k] = o
                readers[k] = []
            dd = []
            seen = set()
            for d in deps:
                if d is o or id(d) in seen:
                    continue
                seen.add(id(d))
                if (not d.is_dma) and (not o.is_dma) and d.eng == "pe" and o.eng == "pe":
                    continue
                dd.append(d)
            o.deps = dd
            for d in dd:
                d.needed = True
            since_barrier.append(o)
        cnt = {e: 0 for e in self.ENGS}
        dma_rr = {e: 0 for e in self.ENGS}
        dma_uses = {}
        for o in self.ops:
            if o.barrier:
                continue
            if o.is_dma:
                slot = dma_rr[o.eng] % self.n_dma_sems
                dma_rr[o.eng] += 1
                key = (o.eng, slot)
                dma_uses[key] = dma_uses.get(key, 0) + 1
                o.dsem = key
                o.dval = 16 * dma_uses[key]
            elif o.needed:
                cnt[o.eng] += 1
                o.ordinal = cnt[o.eng]
        self.max_ord = dict(cnt)

    def emit(self):
        nc = self.nc
        self.finalize()
        from contextlib import ExitStack
        es = ExitStack()
        sems = {}
        for e in ("pe", "act", "dve", "pool", "sp"):
            sems[e] = es.enter_context(nc.semaphore(f"c_{e}"))
        dsems = {}
        used = sorted({o.dsem for o in self.ops if (not o.barrier) and o.is_dma})
        for key in used:
            dsems[key] = es.enter_context(nc.semaphore(f"d_{key[0]}_{key[1]}"))
        block = es.enter_context(nc.Block())
        per_eng = {e: [o for o in self.ops if (not o.barrier) and o.eng == e] for e in self.ENGS}

        def body(ename, engine):
            known = {}
            for o in per_eng[ename]:
                waits = {}
                for d in o.deps:
                    if d.is_dma:
                        s, v, k = dsems[d.dsem], d.dval, ("d",) + d.dsem
                    else:
                        s, v, k = sems[d.eng], d.ordinal, ("c", d.eng)
                    if v > waits.get(k, (None, 0))[1]:
                        waits[k] = (s, v)
                if o.is_dma and o.dval > 16:
                    k = ("d",) + o.dsem
                    v = o.dval - 16
                    if v > waits.get(k, (None, 0))[1]:
                        waits[k] = (dsems[o.dsem], v)
                for k, (s, v) in waits.items():
                    if known.get(k, 0) >= v:
                        continue
                    engine.wait_ge(s, v)
                    known[k] = v
                ins = o.fn(engine)
                if o.is_dma:
                    ins.then_inc(dsems[o.dsem], 16)
                elif o.needed:
                    ins.then_inc(sems[ename], 1)

        @block.tensor
        def _(e):
            body("pe", e)

        @block.scalar
        def _(e):
            body("act", e)

        @block.vector
        def _(e):
            body("dve", e)

        @block.gpsimd
        def _(e):
            body("pool", e)

        @block.sync
        def _(e):
            body("sp", e)

        es.close()


def blocks_for(par):
    lo = list(range(par, 16, 2))
    hi = sorted(31 - j for j in lo)
    return lo + hi


C_G = 0
C_QN = 64
C_KVN = 67
C_CW = 69
C_FR0 = 81
C_SG0 = 82
C_FR1 = 83
C_SG1 = 84
NCST = 96


def gcol(layer, kind):
    return C_G + (layer * 4 + kind) * 8


def build_program(n_cores, dbg=(), no_cc=False, stop_after=None):
    nc = bass.Bass("TRN2", target_bir_lowering=False)
    P = Prog(nc)

    def stop(name):
        if stop_after == name:
            P.barrier()
            P.final_wait([])
            P.emit()
            return True
        return False

    def din(name, shape, dt=F32):
        return nc.dram_tensor(name, list(shape), dt, kind="ExternalInput").ap()

    def dscr(name, shape, dt):
        kind = "ExternalOutput" if name in dbg else "Internal"
        return nc.dram_tensor(name, list(shape), dt, kind=kind).ap()

    xT_seq = din("xT_seq", [D, T])
    xT_own = din("xT_own", [2, D, NO + 32])
    pos_seq = din("pos_seq", [1, T], I32)
    pos_own = din("pos_own", [2, 1, NO], I32)
    cst_d = din("cst", [128, NCST])
    kq_d = din("kq", [128, 32])
    cb_d = din("cb", [2, 8, 128, 1024])
    mT_d = din("mT", [2, 128, 8, 512], BF16)
    w0k_d = din("w0k", [D, 1280])
    w0v_d = din("w0v", [D, 512])
    w0q_d = din("w0q", [D, 4608])
    w0wi_d = din("w0wi", [D, 16])
    wout_d = din("w_out", [D, D])
    w1_d = din("w1", [2, D, 4096])
    w2_d = din("w2", [2, 4096, D])
    wdq_d = din("w_dq", [D, 384])
    wuq_d = din("w_uq", [384, 2048])
    wdkv_d = din("w_dkv", [D, 320])
    wukvk_d = din("w_ukv_k", [256, 1024])
    wukvv_d = din("w_ukv_v", [256, 1024])
    wo_d = din("w_o", [D, D])
    outT = nc.dram_tensor("outT", [D, NO], F32, kind="ExternalOutput").ap()

    kT_d = dscr("kT_d", [4, 128, T], BF16)
    kidxT_d = dscr("kidxT_d", [128, T], BF16)
    vaug_d = dscr("vaug_d", [32, 128, 1024], BF16)
    qT_d = dscr("qT_d", [4, 128, NO], BF16)
    qidxT_d = dscr("qidxT_d", [8, 128, NO], BF16)
    convT_d = dscr("convT_d", [4, 128, NO], BF16)
    maskT_d = dscr("maskT_d", [4, 128, 32, 512], BF16)
    attnT_d = dscr("attnT_d", [8, 128, NO], BF16)
    x1T_d = dscr("x1T_d", [D, NO], F32)
    x2T_d = dscr("x2T_d", [2, D, NO], F32)
    x3T_d = dscr("x3T_d", [D, NO], F32)
    kva_sets_d = dscr("kva_sets_d", [2, 288, NO], F32)
    qnT_d = dscr("qnT_d", [8, 128, NO], BF16)
    qrT_d = dscr("qrT_d", [8, 64, NO], BF16)
    knT_d = dscr("knT_d", [8, 128, T], BF16)
    vaug1_d = dscr("vaug1_d", [32, 128, 2048], BF16)
    dbg_d = dscr("dbg_d", [128, 4096], F32)
    kr4_d = dscr("kr4_d", [64, T], BF16)

    ps = [nc.alloc_psum_tensor(f"ps{i}", [128, 512], F32) for i in range(8)]

    def PK(i):
        return ("ps", i)

    cst = P.sb("cst", [128, NCST], F32)
    kq = P.sb("kq", [128, 32], F32)
    widx = P.sb("widx", [128, 16, 16], F32)
    ones_b = P.sb("ones_b", [128, 128], BF16)
    ones_q = P.sb("ones_q", [128, 128], BF16)
    ident = P.sb("ident", [128, 128], F32)
    P.dma(cst[:], cst_d, w=["cst"])
    P.dma(kq[:], kq_d, w=["kq"])
    P.pool(lambda e: e.memset(ones_b[:], 1.0 / 1024), w=["ones_b"])
    P.pool(lambda e: e.memset(ones_q[:], 1.0 / 256), w=["ones_q"])
    P.pool(lambda e: e.memset(ident[:], 1.0), w=["ident"])
    P.pool(lambda e: e.affine_select(out=ident[:], in_=ident[:], pattern=[[-1, 128]], compare_op=ALU.is_equal,
                                     fill=0.0, base=0, channel_multiplier=1), r=["ident"], w=["ident"])
    base_mark = P.mark()

    uid = [0]

    def U(s):
        uid[0] += 1
        return f"{s}#{uid[0]}"

    def rope_tables(pos_d, n, fr_col, sg_col, tag):
        C = P.sb(tag + "C", [128, n], F32)
        S = P.sb(tag + "S", [128, n], F32)
        m = P.mark()
        pi_ = P.sb("posi", [128, n], I32)
        pf = P.sb("posf", [128, n], F32)
        tmp = P.sb("rtmp", [128, n], F32)
        ki = P.sb("rki", [128, n], I32)
        kpi, kpf, kt, kk = U("posi"), U("posf"), U("rtmp"), U("rki")
        kC, kS = tag + "C", tag + "S"
        P.dma(pi_[:], pos_d.to_broadcast([128, n]), w=[kpi])
        P.dve(lambda e: e.tensor_copy(out=pf[:], in_=pi_[:]), r=[kpi], w=[kpf])
        P.dve(lambda e: e.tensor_scalar(out=pf[:], in0=pf[:], scalar1=cst[:, fr_col:fr_col + 1], scalar2=None,
                                        op0=ALU.mult), r=[kpf, "cst"], w=[kpf])
        for which, dst, kd in (("s", S, kS), ("c", C, kC)):
            off = 0.0 if which == "s" else float(np.pi / 2)
            P.dve(lambda e, off=off: e.tensor_scalar(out=tmp[:], in0=pf[:], scalar1=off, scalar2=1.0 / TWO_PI,
                                                     op0=ALU.add, op1=ALU.mult), r=[kpf], w=[kt])
            P.dve(lambda e: e.tensor_copy(out=ki[:], in_=tmp[:]), r=[kt], w=[kk])
            P.dve(lambda e: e.tensor_copy(out=tmp[:], in_=ki[:]), r=[kk], w=[kt])
            P.dve(lambda e: e.scalar_tensor_tensor(out=tmp[:], in0=tmp[:], scalar=-TWO_PI, in1=pf[:],
                                                   op0=ALU.mult, op1=ALU.add), r=[kt, kpf], w=[kt])
            P.dve(lambda e, off=off: e.tensor_scalar(out=tmp[:], in0=tmp[:], scalar1=off, scalar2=None,
                                                     op0=ALU.add), r=[kt], w=[kt])
            P.dve(lambda e, dst=dst: e.tensor_scalar(out=dst[:], in0=tmp[:], scalar1=float(np.pi), scalar2=-TWO_PI,
                                                     op0=ALU.is_gt, op1=ALU.mult), r=[kt], w=[kd])
            P.dve(lambda e, dst=dst: e.tensor_tensor(out=tmp[:], in0=tmp[:], in1=dst[:], op=ALU.add), r=[kt, kd], w=[kt])
            P.dve(lambda e, dst=dst: e.tensor_scalar(out=dst[:], in0=tmp[:], scalar1=-float(np.pi), scalar2=TWO_PI,
                                                     op0=ALU.is_lt, op1=ALU.mult), r=[kt], w=[kd])
            P.dve(lambda e, dst=dst: e.tensor_tensor(out=tmp[:], in0=tmp[:], in1=dst[:], op=ALU.add), r=[kt, kd], w=[kt])
            P.dve(lambda e: e.tensor_scalar(out=tmp[:], in0=tmp[:], scalar1=-3.14159, scalar2=3.14159,
                                            op0=ALU.max, op1=ALU.min), r=[kt], w=[kt])
            P.act(lambda e, dst=dst: e.activation(out=dst[:], in_=tmp[:], func=AF.Sin), r=[kt], w=[kd])
        P.dve(lambda e: e.tensor_scalar(out=S[:], in0=S[:], scalar1=cst[:, sg_col:sg_col + 1], scalar2=None,
                                        op0=ALU.mult), r=[kS, "cst"], w=[kS])
        P.barrier()
        P.release(m)
        return C, S

    def load_w(dst, src_ap, key, nsplit=1):
        P.dma(dst, src_ap, w=[key], q="pool")

    def rmsnorm_fm(xs, kx, nchunk, n, gain_col, hout, kh, onesm, sq, ksq, rstd, krs, ssbank, eps=EPS):
        P.act(lambda e: e.activation(out=sq[:, 0:nchunk, 0:n], in_=xs[:, 0:nchunk, 0:n], func=AF.Square),
              r=[kx], w=[ksq])
        for c in range(nchunk):
            P.pe(lambda e, c=c: e.matmul(ps[ssbank][:, 0:n], lhsT=onesm[:], rhs=sq[:, c, 0:n],
                                         start=(c == 0), stop=(c == nchunk - 1)),
                 r=[ksq, "ones_b", "ones_q"], w=[PK(ssbank)])
        P.act(lambda e: e.activation(out=rstd[:, 0:n], in_=ps[ssbank][:, 0:n], func=AF.Sqrt, bias=eps, scale=1.0),
              r=[PK(ssbank)], w=[krs])
        P.dve(lambda e: e.reciprocal(out=rstd[:, 0:n], in_=rstd[:, 0:n]), r=[krs], w=[krs])
        for c in range(nchunk):
            P.dve(lambda e, c=c: e.scalar_tensor_tensor(out=hout[:, c, 0:n], in0=xs[:, c, 0:n],
                                                      scalar=cst[:, gain_col + c:gain_col + c + 1],
                                                      in1=rstd[:, 0:n], op0=ALU.mult, op1=ALU.mult),
                r=[kx, krs, "cst"], w=[(kh, c)])

    bankrot = [0]

    def nb():
        b = bankrot[0] % 4
        bankrot[0] += 1
        return b

    def proj_fm(wt, kw, oc0, h, kh, nk, n, bank, m=128):
        for kc in range(nk):
            P.pe(lambda e, kc=kc: e.matmul(ps[bank][0:m, 0:n], lhsT=wt[:, kc, oc0:oc0 + m], rhs=h[:, kc, 0:n],
                                           start=(kc == 0), stop=(kc == nk - 1)),
                 r=[kw, (kh, kc)], w=[PK(bank)])

    rope_mark = P.mark()
    C0s, S0s = rope_tables(pos_seq, T, C_FR0, C_SG0, "r0s")

    wk = P.sb("wk", [128, 8, 1280], BF16)
    wv = P.sb("wv", [128, 8, 512], BF16)
    load_w(wk[:], w0k_d.rearrange("(c p) n -> p c n", p=128), "wk")
    load_w(wv[:], w0v_d.rearrange("(c p) n -> p c n", p=128), "wv")
    xs2 = [P.sb(f"xs{i}", [128, 8, 512], F32) for i in range(2)]
    sq = P.sb("sq", [128, 8, 512], BF16)
    hb = P.sb("hb", [128, 8, 512], BF16)
    rstd = P.sb("rstd", [128, 512], F32)
    t1 = [P.sb(f"t1_{i}", [128, 512], F32) for i in range(2)]
    t2 = [P.sb(f"t2_{i}", [128, 512], F32) for i in range(2)]
    kst = [P.sb(f"kst{i}", [128, 5, 512], BF16) for i in range(2)]
    vst = [P.sb(f"vst{i}", [128, 8, 128], BF16) for i in range(2)]
    for i in range(2):
        P.pool(lambda e, i=i: e.memset(vst[i][:], 1.0), w=[("vst", i)])
    xseq_v = xT_seq.rearrange("(c p) t -> p c t", p=128)
    tcnt = [0]

    def rope_evac(bA, bB, Ct, St, col0, n, dst, kdst, kC="r0sC", kS="r0sS"):
        i = tcnt[0] % 2
        tcnt[0] += 1
        P.dve(lambda e: e.tensor_tensor(out=t1[i][:, 0:n], in0=ps[bA][:, 0:n], in1=Ct[:, col0:col0 + n], op=ALU.mult),
              r=[PK(bA), kC], w=[("t1", i)])
        P.dve(lambda e: e.tensor_tensor(out=t2[i][:, 0:n], in0=ps[bB][:, 0:n], in1=St[:, col0:col0 + n], op=ALU.mult),
              r=[PK(bB), kS], w=[("t2", i)])
        P.pool(lambda e: e.tensor_tensor(out=dst, in0=t1[i][:, 0:n], in1=t2[i][:, 0:n], op=ALU.add),
               r=[("t1", i), ("t2", i)], w=[kdst])

    for tc in range(8):
        xs = xs2[tc % 2]
        kx = ("xs", tc % 2)
        P.dma(xs[:], xseq_v[:, :, tc * 512:(tc + 1) * 512], w=[kx])
        rmsnorm_fm(xs, kx, 8, 512, gcol(0, 0), hb, "hb", ones_b, sq, "sq", rstd, "rstd", 7)
        if tc == 0 and "dbg_d" in dbg:
            dtmp = P.sb("dtmp", [128, 1024], F32)
            P.dma(dbg_d[:, 0:512], xs[:, 0, :], r=[kx], w=["dbg0"])
            P.dma(dbg_d[:, 512:1024], rstd[:], r=["rstd"], w=["dbg1"])
            P.dve(lambda e: e.tensor_copy(out=dtmp[:, 0:512], in_=hb[:, 0, :]), r=[("hb", 0)], w=["dtmp"])
            P.dve(lambda e: e.tensor_copy(out=dtmp[:, 512:1024], in_=sq[:, 0, :]), r=["sq"], w=["dtmp"])
            P.dma(dbg_d[:, 1024:2048], dtmp[:], r=["dtmp"], w=["dbg2"])
            dt2 = P.sb("dt2", [128, 512], F32)
            P.act(lambda e: e.activation(out=dt2[:], in_=ps[7][:, :], func=AF.Copy), r=[PK(7)], w=["dt2"])
            P.dma(dbg_d[:, 2048:2560], dt2[:], r=["dt2"], w=["dbg3"])
            P.dma(dbg_d[:, 2560:3072], S0s[:, 0:512], r=["r0sS"], w=["dbg4"])
            P.dma(dbg_d[:, 3072:3584], C0s[:, 3584:4096], r=["r0sC"], w=["dbg5"])
            P.dma(dbg_d[:, 3584:4096], S0s[:, 3584:4096], r=["r0sS"], w=["dbg6"])
        ks = kst[tc % 2]
        for oc in range(5):
            bA, bB = nb(), nb()
            colA = oc * 128 if oc < 4 else 1024
            colB = 512 + oc * 128 if oc < 4 else 1152
            proj_fm(wk, "wk", colA, hb, "hb", 8, 512, bA)
            proj_fm(wk, "wk", colB, hb, "hb", 8, 512, bB)
            rope_evac(bA, bB, C0s, S0s, tc * 512, 512, ks[:, oc, :], ("kst", tc % 2, oc))
        for oc in range(4):
            P.dma(kT_d[oc, :, tc * 512:(tc + 1) * 512], ks[:, oc, :], r=[("kst", tc % 2, oc)], w=[("kT_d", oc, tc)])
        P.dma(kidxT_d[:, tc * 512:(tc + 1) * 512], ks[:, 4, :], r=[("kst", tc % 2, 4)], w=[("kidxT_d", tc)])
        for tb in range(4):
            kb = tc * 4 + tb
            b = nb()
            for kc in range(8):
                P.pe(lambda e, kc=kc, tb=tb: e.matmul(ps[b][:, :], lhsT=hb[:, kc, tb * 128:(tb + 1) * 128],
                                                      rhs=wv[:, kc, :], start=(kc == 0), stop=(kc == 7)),
                     r=["wv", ("hb", kc)], w=[PK(b)])
            vs = vst[kb % 2]
            pv = ps[b][:, :].rearrange("p (h two d) -> p h two d", two=2, d=64)
            vv = vs[:].rearrange("p (h two) d -> p h two d", two=2)
            P.act(lambda e, pv=pv, vv=vv: e.activation(out=vv[:, :, 0, 0:64], in_=pv[:, :, 0, :], func=AF.Copy),
                  r=[PK(b)], w=[("vst", kb % 2)])
            P.act(lambda e, pv=pv, vv=vv: e.activation(out=vv[:, :, 1, 64:128], in_=pv[:, :, 1, :], func=AF.Copy),
                  r=[PK(b)], w=[("vst", kb % 2)])
            P.dma(vaug_d[kb], vs[:].rearrange("p h d -> p (h d)"), r=[("vst", kb % 2)], w=[("vaug_d", kb)])
    P.barrier()
    P.release(rope_mark)

    for s_ in range(2):
        C0o, S0o = rope_tables(pos_own[s_], NO, C_FR0, C_SG0, "r0o")
        set_mark = P.mark()
        if stop("K"):
            return nc
        wq = P.sb("wq", [128, 8, 4608], BF16)
        wwi = P.sb("wwi", [128, 8, 16], BF16)
        for c in range(8):
            P.dma(wq[:, c, :], w0q_d[c * 128:(c + 1) * 128, :], w=["wq"], q="pool")
        load_w(wwi[:], w0wi_d.rearrange("(c p) n -> p c n", p=128), "wwi")
        uext = P.sb("uext", [128, 4, 16, 130], F32)
        gbs = P.sb("gbs", [128, 4, NO], BF16)
        q_mark = P.mark()
        xs2 = [P.sb("xsq", [128, 8, 512], F32)] * 2
        sq = P.sb("sq", [128, 8, 512], BF16)
        hb = P.sb("hb", [128, 8, 512], BF16)
        rstd = P.sb("rstd", [128, 512], F32)
        t1 = [P.sb(f"t1_{i}", [128, 512], F32) for i in range(2)]
        t2 = [P.sb(f"t2_{i}", [128, 512], F32) for i in range(2)]
        qrot = [P.sb(f"qrot{i}", [128, 512], BF16) for i in range(6)]
        gcs = [P.sb(f"gcs{i}", [128, 512], F32) for i in range(2)]
        qrc = [0]
        xown_v = xT_own[s_].rearrange("(c p) t -> p c t", p=128)
        for tc in range(5):
            n = 512 if tc < 4 else 32
            xs = xs2[0]
            kx = ("xs", 0)
            P.dma(xs[:, :, 0:n], xown_v[:, :, tc * 512:tc * 512 + n], w=[kx])
            rmsnorm_fm(xs, kx, 8, n, gcol(0, 0), hb, "hb", ones_b, sq, "sq", rstd, "rstd", 7)
            if tc < 4:
                for oc in range(12):
                    bA, bB = nb(), nb()
                    colA = oc * 128 if oc < 4 else 1024 + (oc - 4) * 128
                    colB = 512 + oc * 128 if oc < 4 else 2048 + (oc - 4) * 128
                    proj_fm(wq, "wq", colA, hb, "hb", 8, 512, bA)
                    proj_fm(wq, "wq", colB, hb, "hb", 8, 512, bB)
                    qi_ = qrc[0] % 6
                    qrc[0] += 1
                    rope_evac(bA, bB, C0o, S0o, tc * 512, 512, qrot[qi_][:], ("qrot", qi_), "r0oC", "r0oS")
                    if oc < 4:
                        P.dma(qT_d[oc, :, tc * 512:(tc + 1) * 512], qrot[qi_][:], r=[("qrot", qi_)], w=[("qT_d", oc, tc)])
                    else:
                        P.dma(qidxT_d[oc - 4, :, tc * 512:(tc + 1) * 512], qrot[qi_][:], r=[("qrot", qi_)],
                              w=[("qidxT_d", oc - 4, tc)])
                for cc in range(4):
                    b = nb()
                    proj_fm(wq, "wq", 3072 + cc * 128, hb, "hb", 8, 512, b)
                    P.act(lambda e, b=b, cc=cc: e.activation(out=gbs[:, cc, tc * 512:(tc + 1) * 512], in_=ps[b][:, :], func=AF.Copy),
                          r=[PK(b)], w=[("gbs", cc)])
                for tb in range(4):
                    b = nb()
                    for kc in range(8):
                        P.pe(lambda e, kc=kc, tb=tb, b=b: e.matmul(ps[b][:, 0:16], lhsT=hb[:, kc, tb * 128:(tb + 1) * 128],
                                                                   rhs=wwi[:, kc, :], start=(kc == 0), stop=(kc == 7)),
                             r=["wwi", ("hb", kc)], w=[PK(b)])
                    P.act(lambda e, b=b, tb=tb: e.activation(out=widx[:, tc * 4 + tb, :], in_=ps[b][:, 0:16], func=AF.Copy),
                          r=[PK(b)], w=["widx"])
            for cc in range(4):
                bA, bB = nb(), nb()
                proj_fm(wq, "wq", 3584 + cc * 128, hb, "hb", 8, n, bA)
                proj_fm(wq, "wq", 4096 + cc * 128, hb, "hb", 8, n, bB)
                g = gcs[cc % 2]
                P.act(lambda e, g=g, bA=bA: e.activation(out=g[:, 0:n], in_=ps[bA][:, 0:n], func=AF.Copy),
                      r=[PK(bA)], w=[("gcs", cc % 2)])
                if tc < 4:
                    o_ap = uext[:, cc, tc * 4:(tc + 1) * 4, 2:130]
                    i0 = ps[bB][:, :].rearrange("p (b t) -> p b t", t=128)
                    i1 = g[:].rearrange("p (b t) -> p b t", t=128)
                else:
                    o_ap = uext[:, cc, :, 0:2]
                    i0 = ps[bB][:, 0:32].rearrange("p (b t) -> p b t", t=2)
                    i1 = g[:, 0:32].rearrange("p (b t) -> p b t", t=2)
                P.dve(lambda e, o_ap=o_ap, i0=i0, i1=i1: e.tensor_tensor(out=o_ap, in0=i0, in1=i1, op=ALU.mult),
                      r=[PK(bB), ("gcs", cc % 2)], w=[("uext", cc)])
        P.barrier()
        P.release(q_mark)
        cacc = [P.sb(f"cacc{i}", [128, 16, 128], F32) for i in range(2)]
        cvo = [P.sb(f"cvo{i}", [128, NO], BF16) for i in range(2)]
        for cc in range(4):
            a = cacc[cc % 2]
            ka = ("cacc", cc % 2)
            P.dve(lambda e, a=a, cc=cc: e.tensor_scalar(out=a[:], in0=uext[:, cc, :, 2:130],
                                                        scalar1=cst[:, C_CW + 8 + cc:C_CW + 9 + cc], scalar2=None, op0=ALU.mult),
                  r=[("uext", cc), "cst"], w=[ka])
            P.dve(lambda e, a=a, cc=cc: e.scalar_tensor_tensor(out=a[:], in0=uext[:, cc, :, 1:129],
                                                               scalar=cst[:, C_CW + 4 + cc:C_CW + 5 + cc], in1=a[:],
                                                               op0=ALU.mult, op1=ALU.add), r=[("uext", cc), "cst", ka], w=[ka])
            P.dve(lambda e, a=a, cc=cc: e.scalar_tensor_tensor(out=a[:], in0=uext[:, cc, :, 0:128],
                                                               scalar=cst[:, C_CW + cc:C_CW + 1 + cc], in1=a[:],
                                                               op0=ALU.mult, op1=ALU.add), r=[("uext", cc), "cst", ka], w=[ka])
            co = cvo[cc % 2]
            P.dve(lambda e, a=a, cc=cc, co=co: e.tensor_tensor(out=co[:], in0=a[:].rearrange("p b t -> p (b t)"),
                                                               in1=gbs[:, cc, :], op=ALU.mult),
                  r=[ka, ("gbs", cc)], w=[("cvo", cc % 2)])
            P.dma(convT_d[cc], co[:], r=[("cvo", cc % 2)], w=[("convT_d", cc)])
        P.barrier()
        P.release(base_mark)

        if stop("Q"):
            return nc
        kidx = P.sb("kidx", [128, T], BF16)
        qidx = P.sb("qidx", [128, 8, NO], BF16)
        P.dma(kidx[:], kidxT_d, r=[("kidxT_d", t_) for t_ in range(8)], w=["kidx"])
        for oc in range(8):
            P.dma(qidx[:, oc, :], qidxT_d[oc], r=[("qidxT_d", oc, t_) for t_ in range(4)], w=[("qidx", oc)])
        sc4 = [P.sb(f"sc{i}", [128, T], F32) for i in range(4)]
        junk = P.sb("junk", [128, T], BF16)
        m01 = P.sb("m01", [128, T], F32)
        cbt = [P.sb(f"cbt{i}", [128, 1024], F32) for i in range(2)]
        rr = [P.sb(f"rr{i}", [128, 512], BF16) for i in range(6)]
        mTs = [P.sb(f"mTs{i}", [128, 32, 512], BF16) for i in range(1)]
        dg2 = [P.sb(f"dg{i}", [128, 16, 128], BF16) for i in range(2)]
        identb = P.sb("identb", [128, 128], BF16)
        sm2 = [P.sb(f"sm{i}", [128, 16], F32) for i in range(2)]
        stepT2 = [P.sb(f"stepT{i}", [128, 2, NBIS + 1], F32) for i in range(2)]
        P.dve(lambda e: e.tensor_copy(out=identb[:], in_=ident[:]), r=["ident"], w=["identb"])
        rcnt = [0]
        acnt_i = [0]
        mt = mTs[0]

        def acc_half(g, hf, hi):
            nk = 1024 * (g + 1)
            nch = nk // 512
            for jj in range(2):
                j = 2 * hf + jj
                qi = 4 * g + j
                sc = sc4[j]
                ksc = ("sc", j)
                dg = dg2[qi % 2]
                kdg = ("dg", qi % 2)
                for h in range(16):
                    P.pool(lambda e, dg=dg, h=h, qi=qi: e.tensor_scalar(out=dg[:, h, :], in0=identb[:], scalar1=widx[:, qi, h:h + 1],
                                                                        scalar2=None, op0=ALU.mult),
                           r=["identb", "widx"], w=[kdg])
                for ch in range(nch):
                    accb = 4 + (acnt_i[0] % 2)
                    acnt_i[0] += 1
                    pend = []
                    for h in range(16):
                        b = nb()
                        base = (h % 2) * 64
                        P.pe(lambda e, b=b, h=h, base=base, ch=ch, qi=qi: e.matmul(
                            ps[b][:, :], lhsT=qidx[base:base + 64, h // 2, qi * 128:(qi + 1) * 128],
                            rhs=kidx[base:base + 64, ch * 512:(ch + 1) * 512], start=True, stop=True),
                            r=["kidx", ("qidx", h // 2)], w=[PK(b)])
                        ri = rcnt[0] % 6
                        rcnt[0] += 1
                        r_ = rr[ri]
                        P.act(lambda e, b=b, r_=r_: e.activation(out=r_[:], in_=ps[b][:, :], func=AF.Relu),
                              r=[PK(b)], w=[("rr", ri)])
                        pend.append((h, r_, ri))
                        if len(pend) > 2:
                            h0, r0, ri0 = pend.pop(0)
                            P.pe(lambda e, h0=h0, r0=r0, accb=accb, dg=dg: e.matmul(ps[accb][:, :], lhsT=dg[:, h0, :], rhs=r0[:],
                                                                                    start=(h0 == 0), stop=(h0 == 15)),
                                 r=[("rr", ri0), kdg], w=[PK(accb)])
                    for h0, r0, ri0 in pend:
                        P.pe(lambda e, h0=h0, r0=r0, accb=accb, dg=dg: e.matmul(ps[accb][:, :], lhsT=dg[:, h0, :], rhs=r0[:],
                                                                                start=(h0 == 0), stop=(h0 == 15)),
                             r=[("rr", ri0), kdg], w=[PK(accb)])
                    P.act(lambda e, sc=sc, ch=ch, accb=accb: e.activation(out=sc[:, ch * 512:(ch + 1) * 512], in_=ps[accb][:, :], func=AF.Copy),
                          r=[PK(accb)], w=[(ksc, ch)])

        def prep_half(g, hf, hi):
            nk = 1024 * (g + 1)
            nch = nk // 512
            sm = sm2[hi % 2]
            stepT = stepT2[hi % 2]
            for jj in range(2):
                j = 2 * hf + jj
                qi = 4 * g + j
                sc = sc4[j]
                allsc = [(("sc", j), ch) for ch in range(nch)]
                cb = cbt[jj]
                P.dma(cb[:], cb_d[s_, (g // 2) * 4 + j], w=[("cbt", jj)])
                P.dve(lambda e, sc=sc, jj=jj, sm=sm: e.tensor_reduce(out=sm[:, jj:jj + 1], in_=sc[:, 0:nk], axis=AX.X, op=ALU.max,
                                                                     apply_absolute_value=True), r=allsc, w=[("smA", hi % 2, jj)])
                P.dve(lambda e, sc=sc, cb=cb: e.tensor_tensor(out=sc[:, nk - 1024:nk], in0=sc[:, nk - 1024:nk], in1=cb[:],
                                                              op=ALU.add), r=allsc + [("cbt", jj), ("smA", hi % 2, jj)], w=allsc)
            allA = [("smA", hi % 2, jj) for jj in range(2)]
            P.dve(lambda e, sm=sm: e.tensor_single_scalar(out=sm[:, 0:2].bitcast(I32), in_=sm[:, 0:2].bitcast(I32),
                                                          scalar=0x7F800000, op=ALU.bitwise_and), r=allA, w=[("smA2", hi % 2)])
            P.dve(lambda e, sm=sm: e.tensor_scalar(out=sm[:, 0:2], in0=sm[:, 0:2], scalar1=2.0, scalar2=1e-30,
                                                   op0=ALU.mult, op1=ALU.max), r=[("smA2", hi % 2)], w=[("smA2", hi % 2)])
            for i in range(NBIS + 1):
                P.pool(lambda e, i=i, sm=sm, stepT=stepT: e.tensor_scalar(out=stepT[:, :, i], in0=sm[:, 0:2], scalar1=float(2.0 ** -i),
                                                                          scalar2=None, op0=ALU.mult),
                       r=[("smA2", hi % 2)], w=[("stepT", hi % 2, i)])
            P.pool(lambda e, sm=sm: e.memset(sm[:, 2:4], 0.0), w=[("mid", hi % 2)])

        def bis_half(g, hf, hi):
            nk = 1024 * (g + 1)
            nch = nk // 512
            sm = sm2[hi % 2]
            stepT = stepT2[hi % 2]
            kmid = ("mid", hi % 2)
            qi0 = 4 * g + 2 * hf
            kq2 = kq[:, s_ * 16 + qi0:s_ * 16 + qi0 + 2]
            for i in range(NBIS):
                for jj in range(2):
                    j = 2 * hf + jj
                    P.dve(lambda e, j=j, jj=jj, sm=sm: e.tensor_scalar(out=junk[:, 0:nk], in0=sc4[j][:, 0:nk], scalar1=sm[:, 2 + jj:3 + jj],
                                                                       scalar2=None, op0=ALU.is_ge, op1=ALU.add, accum_out=sm[:, 4 + jj:5 + jj]),
                          r=[(("sc", j), ch) for ch in range(nch)] + [kmid], w=[("cnt", hi % 2, jj)])
                P.dve(lambda e, sm=sm: e.tensor_tensor(out=sm[:, 6:8], in0=sm[:, 4:6], in1=kq2, op=ALU.is_ge),
                      r=[("cnt", hi % 2, jj) for jj in range(2)] + ["kq"], w=[("s4", hi % 2)])
                P.dve(lambda e, i=i, sm=sm, stepT=stepT: e.scalar_tensor_tensor(out=sm[:, 8:10], in0=sm[:, 6:8], scalar=0.5, in1=stepT[:, :, i],
                                                                                op0=ALU.subtract, op1=ALU.mult),
                      r=[("s4", hi % 2), ("stepT", hi % 2, i)], w=[("d4", hi % 2)])
                P.dve(lambda e, sm=sm: e.tensor_tensor(out=sm[:, 2:4], in0=sm[:, 2:4], in1=sm[:, 8:10], op=ALU.add),
                      r=[("d4", hi % 2), kmid], w=[kmid])
            P.dve(lambda e, sm=sm, stepT=stepT: e.tensor_tensor(out=sm[:, 10:12], in0=sm[:, 2:4], in1=stepT[:, :, NBIS], op=ALU.subtract),
                  r=[kmid, ("stepT", hi % 2, NBIS)], w=[("thr", hi % 2)])
            for jj in range(2):
                j = 2 * hf + jj
                P.dve(lambda e, j=j, jj=jj, sm=sm: e.tensor_scalar(out=m01[:, 0:nk], in0=sc4[j][:, 0:nk], scalar1=sm[:, 10 + jj:11 + jj], scalar2=None,
                                                                   op0=ALU.is_ge), r=[(("sc", j), ch) for ch in range(nch)] + [("thr", hi % 2)], w=["m01"])
                for k4 in range(nk // 512):
                    b = 6 + (k4 % 2)
                    for kk in range(4):
                        kb = k4 * 4 + kk
                        P.pe(lambda e, b=b, kk=kk, kb=kb: e.transpose(out=ps[b][:, kk * 128:(kk + 1) * 128],
                                                                      in_=m01[:, kb * 128:(kb + 1) * 128], identity=ident[:]),
                             r=["m01", "ident"], w=[PK(b)])
                    P.act(lambda e, b=b, k4=k4, j=j: e.activation(
                        out=mt[:, k4 * 4:(k4 + 1) * 4, j * 128:(j + 1) * 128],
                        in_=ps[b][:, :].rearrange("p (k t) -> p k t", t=128), func=AF.Copy),
                        r=[PK(b)], w=["mTs"])
            if hf == 1:
                P.dma(maskT_d[g, :, 0:nk // 128, :], mt[:, 0:nk // 128, :], r=["mTs"], w=[("maskT_d", g)])

        halves = [(g, hf) for g in range(4) for hf in range(2)]
        for hi, (g, hf) in enumerate(halves):
            acc_half(g, hf, hi)
            if hi > 0:
                bis_half(halves[hi - 1][0], halves[hi - 1][1], hi - 1)
            prep_half(g, hf, hi)
        bis_half(halves[-1][0], halves[-1][1], len(halves) - 1)
        P.barrier()
        P.release(base_mark)

        if stop("I"):
            return nc
        def attention(nheads, hp_loader, st_emit, mask_for, scale, out_d, okey):
            pts = [P.sb(f"pt{i}", [128, 512], BF16) for i in range(6)]
            rdn = [P.sb(f"rdn{i}", [128, 512], F32) for i in range(2)]
            ost = [P.sb(f"ost{i}", [128, NO], BF16) for i in range(2)]
            ucnt = [0]
            acnt = [0]
            for hp in range(nheads // 2):
                bufs = hp_loader(hp)
                o_t = ost[hp % 2]
                for hh in range(2):
                    h = hp * 2 + hh
                    for g in range(4):
                        nkb = 8 * (g + 1)
                        accb = 4 + (acnt[0] % 2)
                        acnt[0] += 1
                        pend = []
                        for kb in range(nkb):
                            sb_ = nb()
                            st_emit(bufs, h, hh, g, kb, sb_)
                            pi = ucnt[0] % 6
                            ucnt[0] += 1
                            pt = pts[pi]
                            P.act(lambda e, sb_=sb_, pt=pt: e.activation(out=pt[:], in_=ps[sb_][:, :], func=AF.Exp, scale=scale),
                                  r=[PK(sb_)], w=[("pt", pi)])
                            mk = mask_for(bufs, g, kb)
                            if mk is not None:
                                map_, mkey = mk
                                P.dve(lambda e, pt=pt, map_=map_: e.tensor_tensor(out=pt[:], in0=pt[:], in1=map_, op=ALU.mult),
                                      r=[("pt", pi), mkey], w=[("pt", pi)])
                            pend.append((kb, pt, pi))
                            if len(pend) > 2:
                                kb0, pt0, pi0 = pend.pop(0)
                                P.pe(lambda e, kb0=kb0, pt0=pt0, accb=accb: e.matmul(
                                    ps[accb][:, :], lhsT=bufs["v"][:, kb0, hh * 128:(hh + 1) * 128], rhs=pt0[:],
                                    start=(kb0 == 0), stop=(kb0 == nkb - 1)), r=[("pt", pi0), bufs["kv"]], w=[PK(accb)])
                        for kb0, pt0, pi0 in pend:
                            P.pe(lambda e, kb0=kb0, pt0=pt0, accb=accb: e.matmul(
                                ps[accb][:, :], lhsT=bufs["v"][:, kb0, hh * 128:(hh + 1) * 128], rhs=pt0[:],
                                start=(kb0 == 0), stop=(kb0 == nkb - 1)), r=[("pt", pi0), bufs["kv"]], w=[PK(accb)])
                        rd = rdn[acnt[0] % 2]
                        krd = ("rdn", acnt[0] % 2)
                        nlo, dlo = (0, 64) if hh == 0 else (64, 0)
                        P.dve(lambda e, rd=rd, accb=accb, dlo=dlo: e.reciprocal(out=rd[dlo:dlo + 64, :], in_=ps[accb][dlo:dlo + 64, :]),
                              r=[PK(accb)], w=[krd])
                        P.dve(lambda e, rd=rd, accb=accb, dlo=dlo, nlo=nlo, g=g, o_t=o_t: e.tensor_tensor(
                            out=o_t[nlo:nlo + 64, g * 512:(g + 1) * 512], in0=ps[accb][nlo:nlo + 64, :],
                            in1=rd[dlo:dlo + 64, :], op=ALU.mult), r=[PK(accb), krd], w=[("ost", hp % 2)])
                P.dma(out_d[hp], o_t[:], r=[("ost", hp % 2)], w=[(okey, hp)])

        mres = P.sb("mres", [128, 80, 512], BF16)
        goff = [0, 8, 24, 48]
        for g in range(4):
            nkb = 8 * (g + 1)
            P.dma(mres[:, goff[g]:goff[g] + nkb, :], maskT_d[g, :, 0:nkb, :], r=[("maskT_d", g)], w=[("mres", g)])
        kb2 = [P.sb(f"kb2_{i}", [128, T], BF16) for i in range(2)]
        qb2 = [P.sb(f"qb2_{i}", [128, NO], BF16) for i in range(2)]
        vb2 = [P.sb(f"vb2_{i}", [128, 32, 256], BF16) for i in range(2)]

        def hp_loader0(hp):
            i = hp % 2
            key = ("kvq0", i)
            P.dma(kb2[i][:], kT_d[hp], r=[("kT_d", hp, t_) for t_ in range(8)], w=[key])
            P.dma(qb2[i][:], qT_d[hp], r=[("qT_d", hp, t_) for t_ in range(4)], w=[key])
            P.dma(vb2[i][:], vaug_d[:, :, hp * 256:(hp + 1) * 256].rearrange("k p c -> p k c"),
                  r=[("vaug_d", k_) for k_ in range(32)], w=[key])
            return {"k": kb2[i], "q": qb2[i], "v": vb2[i], "kv": key}

        def st_emit0(bufs, h, hh, g, kb, sb_):
            base = hh * 64
            P.pe(lambda e: e.matmul(ps[sb_][:, :], lhsT=bufs["k"][base:base + 64, kb * 128:(kb + 1) * 128],
                                    rhs=bufs["q"][base:base + 64, g * 512:(g + 1) * 512], start=True, stop=True),
                 r=[bufs["kv"]], w=[PK(sb_)])

        def mask_for0(bufs, g, kb):
            return mres[:, goff[g] + kb, :], ("mres", g)

        attention(8, hp_loader0, st_emit0, mask_for0, 0.125, attnT_d, "attnT_d")
        P.barrier()
        P.release(base_mark)

        if stop("A0"):
            return nc
        def out_phase(in_chunks, wd, layer, x_src, xkey_src, x_dst, xkey_dst):
            wo = P.sb("wo", [128, 8, D], BF16)
            load_w(wo[:], wd.rearrange("(c p) n -> p c n", p=128), "wo")
            ain = P.sb("ain", [128, 8, NO], BF16)
            for c, (ap_, rk) in enumerate(in_chunks):
                P.dma(ain[:, c, :], ap_, r=rk, w=[("ain", c)])
            xo2 = [P.sb(f"xo{i}", [128, 8, 512], F32) for i in range(2)]
            mx = P.sb("mx", [128, 8, 512], F32)
            sq_ = P.sb("sqo", [128, 8, 512], BF16)
            rs = P.sb("rso", [128, 512], F32)
            xv = x_src.rearrange("(c p) t -> p c t", p=128)
            xdv = x_dst.rearrange("(c p) t -> p c t", p=128)
            for tc in range(4):
                xo = xo2[tc % 2]
                kxo = ("xo", tc % 2)
                P.dma(xo[:], xv[:, :, tc * 512:(tc + 1) * 512], r=[(xkey_src, tc)], w=[kxo])
                for oc in range(8):
                    b = nb()
                    for kc in range(8):
                        P.pe(lambda e, kc=kc, oc=oc, b=b: e.matmul(ps[b][:, :], lhsT=wo[:, kc, oc * 128:(oc + 1) * 128],
                                                                   rhs=ain[:, kc, tc * 512:(tc + 1) * 512],
                                                                   start=(kc == 0), stop=(kc == 7)),
                             r=["wo", ("ain", kc)], w=[PK(b)])
                    P.act(lambda e, b=b, oc=oc: e.activation(out=mx[:, oc, :], in_=ps[b][:, :], func=AF.Copy),
                          r=[PK(b)], w=[("mx", oc)])
                    P.dve(lambda e, b=b, oc=oc: e.tensor_tensor(out=sq_[:, oc, :], in0=mx[:, oc, :], in1=mx[:, oc, :], op=ALU.mult),
                          r=[("mx", oc)], w=[("sqo", oc)])
                for c in range(8):
                    P.pe(lambda e, c=c: e.matmul(ps[7][:, :], lhsT=ones_b[:], rhs=sq_[:, c, :], start=(c == 0), stop=(c == 7)),
                         r=[("sqo", c), "ones_b"], w=[PK(7)])
                P.act(lambda e: e.activation(out=rs[:], in_=ps[7][:, :], func=AF.Sqrt, bias=EPS, scale=1.0), r=[PK(7)], w=["rso"])
                P.dve(lambda e: e.reciprocal(out=rs[:], in_=rs[:]), r=["rso"], w=["rso"])
                for c in range(8):
                    gc_ = gcol(layer, 1) + c
                    P.dve(lambda e, c=c, gc_=gc_: e.scalar_tensor_tensor(out=mx[:, c, :], in0=mx[:, c, :], scalar=cst[:, gc_:gc_ + 1],
                                                                          in1=rs[:], op0=ALU.mult, op1=ALU.mult),
                           r=[("mx", c), "rso", "cst"], w=[("mx", c)])
                    P.dve(lambda e, c=c, xo=xo: e.tensor_tensor(out=xo[:, c, :], in0=xo[:, c, :], in1=mx[:, c, :], op=ALU.add),
                          r=[("mx", c), kxo], w=[kxo])
                P.dma(xdv[:, :, tc * 512:(tc + 1) * 512], xo[:], r=[kxo], w=[(xkey_dst, tc)])

        in0 = [(attnT_d[c], [("attnT_d", c)]) for c in range(4)] + [(convT_d[c], [("convT_d", c)]) for c in range(4)]
        out_phase(in0, wout_d, 0, xT_own[s_, :, 0:NO], "xown", x1T_d, "x1T_d")
        P.barrier()
        P.release(base_mark)

        def ffn_phase(layer, x_src, xkey_src, x_dst, xkey_dst):
            w1s = P.sb("w1s", [128, 8, 4096], BF16)
            w2s = P.sb("w2s", [128, 32, D], BF16)
            w1v = w1_d[layer].rearrange("(c p) n -> p c n", p=128)
            w2v = w2_d[layer].rearrange("(c p) n -> p c n", p=128)
            for c in range(8):
                P.dma(w1s[:, c, :], w1v[:, c, :], w=[("w1s", c)], q="pool")
            for c in range(0, 32, 4):
                P.dma(w2s[:, c:c + 4, :], w2v[:, c:c + 4, :], w=[("w2s", c // 4)], q="pool")
            xf = P.sb("xf", [128, 8, 512], F32)
            off3 = P.mark()
            yb = P.sb("yb", [128, 8, 512], F32)
            end3 = P.mark()
            P.release(off3)
            sq_f = P.sb("sqf", [128, 8, 512], BF16)
            hf = P.sb("hf", [128, 8, 512], BF16)
            assert P.mark() == end3
            h1 = P.sb("h1", [128, 32, 512], BF16)
            sqy = [P.sb(f"sqy{i}", [128, 512], BF16) for i in range(2)]
            rt = [P.sb(f"rt{i}", [128, 512], F32) for i in range(2)]
            alias_keys = ["sqf"] + [("hf", c) for c in range(8)]
            rsf = P.sb("rsf", [128, 512], F32)
            xv = x_src.rearrange("(c p) t -> p c t", p=128)
            xdv = x_dst.rearrange("(c p) t -> p c t", p=128)
            rc = [0]
            for tc in range(4):
                P.dma(xf[:], xv[:, :, tc * 512:(tc + 1) * 512], r=[(xkey_src, tc)], w=["xf"])
                P.act(lambda e: e.activation(out=rsf[:, 0:1], in_=rsf[:, 0:1], func=AF.Copy), r=["rsf"],
                      w=alias_keys + ["rsf"] + [("yb", c) for c in range(8)])
                rmsnorm_fm(xf, "xf", 8, 512, gcol(layer, 2), hf, "hf", ones_b, sq_f, "sqf", rsf, "rsf", 7)
                for oc in range(32):
                    b = nb()
                    for kc in range(8):
                        P.pe(lambda e, kc=kc, oc=oc, b=b: e.matmul(ps[b][:, :], lhsT=w1s[:, kc, oc * 128:(oc + 1) * 128],
                                                                   rhs=hf[:, kc, :], start=(kc == 0), stop=(kc == 7)),
                             r=[("w1s", kc), ("hf", kc)], w=[PK(b)])
                    ri = rc[0] % 2
                    rc[0] += 1
                    P.act(lambda e, b=b, ri=ri: e.activation(out=rt[ri][:], in_=ps[b][:, :], func=AF.Relu), r=[PK(b)], w=[("rt", ri)])
                    eng = P.dve if oc % 2 == 0 else P.pool
                    eng(lambda e, ri=ri, oc=oc: e.tensor_tensor(out=h1[:, oc, :], in0=rt[ri][:], in1=rt[ri][:], op=ALU.mult),
                        r=[("rt", ri)], w=[("h1", oc)])
                for oc in range(8):
                    b = nb()
                    for kc in range(32):
                        P.pe(lambda e, kc=kc, oc=oc, b=b: e.matmul(ps[b][:, :], lhsT=w2s[:, kc, oc * 128:(oc + 1) * 128],
                                                                   rhs=h1[:, kc, :], start=(kc == 0), stop=(kc == 31)),
                             r=[("w2s", kc // 4), ("h1", kc)], w=[PK(b)])
                    P.act(lambda e, b=b, oc=oc: e.activation(out=yb[:, oc, :], in_=ps[b][:, :], func=AF.Copy), r=[PK(b)],
                          w=[("yb", oc)] + (alias_keys if oc == 0 else []))
                    P.dve(lambda e, oc=oc: e.tensor_tensor(out=sqy[oc % 2][:], in0=yb[:, oc, :], in1=yb[:, oc, :], op=ALU.mult),
                          r=[("yb", oc)], w=[("sqy", oc % 2)])
                    P.pe(lambda e, oc=oc: e.matmul(ps[7][:, :], lhsT=ones_b[:], rhs=sqy[oc % 2][:], start=(oc == 0), stop=(oc == 7)),
                         r=[("sqy", oc % 2), "ones_b"], w=[PK(7)])
                P.act(lambda e: e.activation(out=rsf[:], in_=ps[7][:, :], func=AF.Sqrt, bias=EPS, scale=1.0), r=[PK(7)], w=["rsf"])
                P.dve(lambda e: e.reciprocal(out=rsf[:], in_=rsf[:]), r=["rsf"], w=["rsf"])
                for c in range(8):
                    gc_ = gcol(layer, 3) + c
                    P.dve(lambda e, c=c, gc_=gc_: e.scalar_tensor_tensor(out=yb[:, c, :], in0=yb[:, c, :], scalar=cst[:, gc_:gc_ + 1],
                                                                          in1=rsf[:], op0=ALU.mult, op1=ALU.mult),
                           r=[("yb", c), "rsf", "cst"], w=[("yb", c)])
                    P.dve(lambda e, c=c: e.tensor_tensor(out=yb[:, c, :], in0=yb[:, c, :], in1=xf[:, c, :], op=ALU.add),
                          r=[("yb", c), "xf"], w=[("yb", c)])
                P.dma(xdv[:, :, tc * 512:(tc + 1) * 512], yb[:], r=[("yb", c) for c in range(8)], w=[(xkey_dst, tc)])

        if stop("O0"):
            return nc
        ffn_phase(0, x1T_d, "x1T_d", x2T_d[s_], ("x2T_d", s_))
        P.barrier()
        P.release(base_mark)

    if stop("F0"):
        return nc
    tabs = [rope_tables(pos_own[s1], NO, C_FR1, C_SG1, f"r1o{s1}") for s1 in range(2)]
    C1S1 = [None, None]
    m1 = P.mark()
    wdq = P.sb("wdq", [128, 8, 384], BF16)
    wuq = P.sb("wuq", [128, 3, 2048], BF16)
    wdkv = P.sb("wdkv", [128, 8, 320], BF16)
    load_w(wdq[:], wdq_d.rearrange("(c p) n -> p c n", p=128), "wdq")
    load_w(wuq[:], wuq_d.rearrange("(c p) n -> p c n", p=128), "wuq")
    load_w(wdkv[:], wdkv_d.rearrange("(c p) n -> p c n", p=128), "wdkv")
    xs2 = [P.sb(f"xs{i}", [128, 8, 512], F32) for i in range(2)]
    sq = P.sb("sq", [128, 8, 512], BF16)
    hb = P.sb("hb", [128, 8, 512], BF16)
    rstd = P.sb("rstd", [128, 512], F32)
    t1 = [P.sb(f"t1_{i}", [128, 512], F32) for i in range(2)]
    t2 = [P.sb(f"t2_{i}", [128, 512], F32) for i in range(2)]
    cq = P.sb("cq", [128, 3, 512], F32)
    cqn = P.sb("cqn", [128, 3, 512], BF16)
    ckv = P.sb("ckv", [128, 2, 512], F32)
    ckvn = P.sb("ckvn", [128, 2, 512], F32)
    krs = P.sb("krs", [32, 512], F32)
    qst = [P.sb(f"qst1_{i}", [128, 16, 512], BF16) for i in range(2)]
    sq3 = P.sb("sq3", [128, 3, 512], BF16)
    rs3 = P.sb("rs3", [128, 512], F32)

    def rope_evac1(bA, bB, m, col0, dst, kdst, eng_out="pool"):
        Cc, Ss = C1S1[0], C1S1[1]
        i = tcnt[0] % 2
        tcnt[0] += 1
        P.dve(lambda e: e.tensor_tensor(out=t1[i][0:m, :], in0=ps[bA][0:m, :], in1=Cc[0:m, col0:col0 + 512], op=ALU.mult),
              r=[PK(bA), "r1o0C", "r1o1C"], w=[("t1", i)])
        P.dve(lambda e: e.tensor_tensor(out=t2[i][0:m, :], in0=ps[bB][0:m, :], in1=Ss[0:m, col0:col0 + 512], op=ALU.mult),
              r=[PK(bB), "r1o0S", "r1o1S"], w=[("t2", i)])
        P.pool(lambda e: e.tensor_tensor(out=dst, in0=t1[i][0:m, :], in1=t2[i][0:m, :], op=ALU.add),
               r=[("t1", i), ("t2", i)], w=[kdst])

    for s1 in range(2):
        C1S1[0], C1S1[1] = tabs[s1]
        x2v = x2T_d[s1].rearrange("(c p) t -> p c t", p=128)
        for tc in range(4):
            xs = xs2[tc % 2]
            kx = ("xs", tc % 2)
            P.dma(xs[:], x2v[:, :, tc * 512:(tc + 1) * 512], r=[(("x2T_d", s1), tc)], w=[kx])
            rmsnorm_fm(xs, kx, 8, 512, gcol(1, 0), hb, "hb", ones_b, sq, "sq", rstd, "rstd", 7)
            if s1 == 0:
                for oc in range(3):
                    b = nb()
                    proj_fm(wdq, "wdq", oc * 128, hb, "hb", 8, 512, b)
                    P.act(lambda e, b=b, oc=oc: e.activation(out=cq[:, oc, :], in_=ps[b][:, :], func=AF.Copy), r=[PK(b)], w=["cq"])
                P.act(lambda e: e.activation(out=sq3[:], in_=cq[:], func=AF.Square), r=["cq"], w=["sq3"])
                for c in range(3):
                    P.pe(lambda e, c=c: e.matmul(ps[7][:, :], lhsT=ones_q[:], rhs=sq3[:, c, :], start=(c == 0), stop=(c == 2)),
                         r=["sq3", "ones_q"], w=[PK(7)])
                P.act(lambda e: e.activation(out=rs3[:], in_=ps[7][:, :], func=AF.Sqrt, bias=EPS, scale=256.0 / 384.0), r=[PK(7)], w=["rs3"])
                P.dve(lambda e: e.reciprocal(out=rs3[:], in_=rs3[:]), r=["rs3"], w=["rs3"])
                for c in range(3):
                    P.dve(lambda e, c=c: e.scalar_tensor_tensor(out=cqn[:, c, :], in0=cq[:, c, :], scalar=cst[:, C_QN + c:C_QN + c + 1],
                                                                in1=rs3[:], op0=ALU.mult, op1=ALU.mult), r=["cq", "rs3", "cst"], w=[("cqn", c)])
                qs = qst[tc % 2]
                for oc in range(8):
                    b = nb()
                    proj_fm(wuq, "wuq", oc * 128, cqn, "cqn", 3, 512, b)
                    P.act(lambda e, b=b, oc=oc: e.activation(out=qs[:, oc, :], in_=ps[b][:, :], func=AF.Copy), r=[PK(b)], w=[("qst", tc % 2, oc)])
                    P.dma(qnT_d[oc, :, tc * 512:(tc + 1) * 512], qs[:, oc, :], r=[("qst", tc % 2, oc)], w=[("qnT_d", oc, tc)])
                for oc in range(8):
                    bA, bB = nb(), nb()
                    proj_fm(wuq, "wuq", 1024 + oc * 64, cqn, "cqn", 3, 512, bA, m=64)
                    proj_fm(wuq, "wuq", 1536 + oc * 64, cqn, "cqn", 3, 512, bB, m=64)
                    rope_evac1(bA, bB, 64, tc * 512, qs[0:64, 8 + oc, :], ("qst", tc % 2, 8 + oc))
                    P.dma(qrT_d[oc, :, tc * 512:(tc + 1) * 512], qs[0:64, 8 + oc, :], r=[("qst", tc % 2, 8 + oc)], w=[("qrT_d", oc, tc)])
            for oc in range(2):
                b = nb()
                proj_fm(wdkv, "wdkv", oc * 128, hb, "hb", 8, 512, b)
                P.act(lambda e, b=b, oc=oc: e.activation(out=ckv[:, oc, :], in_=ps[b][:, :], func=AF.Copy), r=[PK(b)], w=["ckv"])
            P.act(lambda e: e.activation(out=sq3[:, 0:2, :], in_=ckv[:], func=AF.Square), r=["ckv"], w=["sq3"])
            for c in range(2):
                P.pe(lambda e, c=c: e.matmul(ps[7][:, :], lhsT=ones_q[:], rhs=sq3[:, c, :], start=(c == 0), stop=(c == 1)),
                     r=["sq3", "ones_q"], w=[PK(7)])
            P.act(lambda e: e.activation(out=rs3[:], in_=ps[7][:, :], func=AF.Sqrt, bias=EPS, scale=1.0), r=[PK(7)], w=["rs3"])
            P.dve(lambda e: e.reciprocal(out=rs3[:], in_=rs3[:]), r=["rs3"], w=["rs3"])
            for c in range(2):
                P.dve(lambda e, c=c: e.scalar_tensor_tensor(out=ckvn[:, c, :], in0=ckv[:, c, :], scalar=cst[:, C_KVN + c:C_KVN + c + 1],
                                                            in1=rs3[:], op0=ALU.mult, op1=ALU.mult), r=["ckv", "rs3", "cst"], w=[("ckvn", c)])
                P.dma(kva_sets_d[s1, c * 128:(c + 1) * 128, tc * 512:(tc + 1) * 512], ckvn[:, c, :], r=[("ckvn", c)], w=[("kva", s1, tc, c)])
            bA, bB = nb(), nb()
            proj_fm(wdkv, "wdkv", 256, hb, "hb", 8, 512, bA, m=32)
            proj_fm(wdkv, "wdkv", 288, hb, "hb", 8, 512, bB, m=32)
            rope_evac1(bA, bB, 32, tc * 512, krs[:, :], "krs")
            P.dma(kva_sets_d[s1, 256:288, tc * 512:(tc + 1) * 512], krs[:, :], r=["krs"], w=[("kva", s1, tc, 2)])
    P.barrier()
    P.release(base_mark)

    if stop("Q1"):
        return nc
    wkk = P.sb("wkk", [128, 2, 1024], BF16)
    wkv = P.sb("wkv", [128, 2, 1024], BF16)
    load_w(wkk[:], wukvk_d.rearrange("(c p) n -> p c n", p=128), "wkk")
    load_w(wkv[:], wukvv_d.rearrange("(c p) n -> p c n", p=128), "wkv")
    ckf = [P.sb(f"ckf{i}", [128, 2, 512], F32) for i in range(2)]
    ckb = [P.sb(f"ckb{i}", [128, 2, 512], BF16) for i in range(2)]
    kns = [P.sb(f"kns{i}", [128, 8, 512], BF16) for i in range(2)]
    vst1 = [P.sb(f"vst1_{i}", [128, 16, 128], BF16) for i in range(2)]
    kr4 = P.sb("kr4", [64, T], BF16)
    krf = P.sb("krf", [64, T], F32)
    for i in range(2):
        P.pool(lambda e, i=i: e.memset(vst1[i][:], 1.0), w=[("vst1", i)])
    for sl in range(32):
        s1, m_ = sl % 2, sl // 2
        for rep_ in range(2):
            P.dma(krf[rep_ * 32:(rep_ + 1) * 32, sl * 128:(sl + 1) * 128],
                  kva_sets_d[s1, 256:288, m_ * 128:(m_ + 1) * 128], w=[("krf", sl // 8)])
    for q4 in range(4):
        P.dve(lambda e, q4=q4: e.tensor_copy(out=kr4[:, q4 * 1024:(q4 + 1) * 1024], in_=krf[:, q4 * 1024:(q4 + 1) * 1024]),
              r=[("krf", q4)], w=["kr4"])
    for tc in range(8):
        cf = ckf[tc % 2]
        cb_ = ckb[tc % 2]
        for tb in range(4):
            sl = tc * 4 + tb
            s1, m_ = sl % 2, sl // 2
            for c in range(2):
                P.dma(cf[:, c, tb * 128:(tb + 1) * 128],
                      kva_sets_d[s1, c * 128:(c + 1) * 128, m_ * 128:(m_ + 1) * 128], w=[("ckf", tc % 2)])
        P.dve(lambda e, cf=cf, cb_=cb_: e.tensor_copy(out=cb_[:], in_=cf[:]), r=[("ckf", tc % 2)], w=[(("ckb", tc % 2), 0), (("ckb", tc % 2), 1)])
        kn = kns[tc % 2]
        for oc in range(8):
            b = nb()
            proj_fm(wkk, "wkk", oc * 128, cb_, ("ckb", tc % 2), 2, 512, b)
            P.act(lambda e, b=b, oc=oc, kn=kn: e.activation(out=kn[:, oc, :], in_=ps[b][:, :], func=AF.Copy), r=[PK(b)], w=[("kns", tc % 2, oc)])
            P.dma(knT_d[oc, :, tc * 512:(tc + 1) * 512], kn[:, oc, :], r=[("kns", tc % 2, oc)], w=[("knT_d", oc, tc)])
        for tb in range(4):
            kb = tc * 4 + tb
            vs = vst1[kb % 2]
            vv = vs[:].rearrange("p (h two) d -> p h two d", two=2)
            for half in range(2):
                b = nb()
                for kc in range(2):
                    P.pe(lambda e, kc=kc, tb=tb, b=b, half=half, cb_=cb_: e.matmul(
                        ps[b][:, :], lhsT=cb_[:, kc, tb * 128:(tb + 1) * 128], rhs=wkv[:, kc, half * 512:(half + 1) * 512],
                        start=(kc == 0), stop=(kc == 1)), r=["wkv", (("ckb", tc % 2), kc)], w=[PK(b)])
                pv = ps[b][:, :].rearrange("p (h two d) -> p h two d", two=2, d=64)
                P.act(lambda e, pv=pv, vv=vv, half=half: e.activation(out=vv[:, half * 4:(half + 1) * 4, 0, 0:64], in_=pv[:, :, 0, :], func=AF.Copy),
                      r=[PK(b)], w=[("vst1", kb % 2)])
                P.act(lambda e, pv=pv, vv=vv, half=half: e.activation(out=vv[:, half * 4:(half + 1) * 4, 1, 64:128], in_=pv[:, :, 1, :], func=AF.Copy),
                      r=[PK(b)], w=[("vst1", kb % 2)])
            P.dma(vaug1_d[kb], vs[:].rearrange("p h d -> p (h d)"), r=[("vst1", kb % 2)], w=[("vaug1_d", kb)])
    P.dma(kr4_d, kr4[:], r=["kr4"], w=["kr4_d"])
    P.barrier()
    P.release(base_mark)
    kr4b = P.sb("kr4b", [64, T], BF16)
    P.dma(kr4b[:], kr4_d, r=["kr4_d"], w=["kr4b"])

    if stop("K1"):
        return nc
    mTs1 = P.sb("mTs1", [128, 2, 8, 512], BF16)
    for i in range(2):
        P.dma(mTs1[:, i], mT_d[i], w=["mTs1"])
    kb2 = [P.sb(f"kn2_{i}", [128, T], BF16) for i in range(2)]
    qb2 = [P.sb(f"qn2_{i}", [128, NO], BF16) for i in range(2)]
    qr2 = [P.sb(f"qr2_{i}", [64, NO], BF16) for i in range(2)]
    vb2 = [P.sb(f"vb21_{i}", [128, 32, 256], BF16) for i in range(2)]

    def hp_loader1(hp):
        i = hp % 2
        key = ("kvq1", i)
        P.dma(kb2[i][:], knT_d[hp], r=[("knT_d", hp, t_) for t_ in range(8)], w=[key])
        P.dma(qb2[i][:], qnT_d[hp], r=[("qnT_d", hp, t_) for t_ in range(4)], w=[key])
        P.dma(qr2[i][:], qrT_d[hp], r=[("qrT_d", hp, t_) for t_ in range(4)], w=[key])
        P.dma(vb2[i][:], vaug1_d[:, :, hp * 256:(hp + 1) * 256].rearrange("k p c -> p k c"),
              r=[("vaug1_d", k_) for k_ in range(32)], w=[key])
        return {"k": kb2[i], "q": qb2[i], "v": vb2[i], "kv": key, "qr": qr2[i], "qrk": key}

    def st_emit1(bufs, h, hh, g, kb, sb_):
        base = hh * 64
        rb = hh * 32
        P.pe(lambda e: e.matmul(ps[sb_][:, :], lhsT=bufs["k"][base:base + 64, kb * 128:(kb + 1) * 128],
                                rhs=bufs["q"][base:base + 64, g * 512:(g + 1) * 512], start=True, stop=False),
             r=[bufs["kv"]], w=[PK(sb_)])
        P.pe(lambda e: e.matmul(ps[sb_][:, :], lhsT=kr4b[rb:rb + 32, kb * 128:(kb + 1) * 128],
                                rhs=bufs["qr"][rb:rb + 32, g * 512:(g + 1) * 512], start=False, stop=True),
             r=["kr4b", bufs["qrk"]], w=[PK(sb_)])

    def mask_for1(bufs, g, kb):
        rel = kb - 8 * g
        if rel < 0:
            return None
        return mTs1[:, g // 2, rel, :], "mTs1"

    attention(16, hp_loader1, st_emit1, mask_for1, float(96 ** -0.5), attnT_d, "attnT1_d")
    P.barrier()
    P.release(base_mark)

    if stop("A1"):
        return nc
    in1 = [(attnT_d[c], [("attnT1_d", c)]) for c in range(8)]
    out_phase(in1, wo_d, 1, x2T_d[0], ("x2T_d", 0), x3T_d, "x3T_d")
    P.barrier()
    P.release(base_mark)
    ffn_phase(1, x3T_d, "x3T_d", outT, "outT")
    P.final_wait([("outT", tc) for tc in range(4)])
    P.emit()
    return nc


def _swap_cols(w, head, a, b_):
    n = w.shape[1]
    idx = np.arange(n).reshape(-1, head)
    perm = np.concatenate([idx[:, a:b_], idx[:, 0:a], idx[:, b_:]], axis=1).reshape(-1)
    return w[:, perm]


def prepare_inputs(inp, n_batch=4):
    f32 = np.float32
    x = np.asarray(inp["x"], f32)
    pos = np.asarray(inp["positions"]).astype(np.int32)
    w_in = np.asarray(inp["even_w_in"], f32)[0]
    offs = np.cumsum([0, 512, 512, 512, 1024, 64, 16, 512, 512, 512])
    wq_, wk_, wv_, wqi, wki, wwi, wgb, wgc, wxi = [w_in[:, offs[i]:offs[i + 1]] for i in range(9)]
    w0k = np.concatenate([wk_, _swap_cols(wk_, 64, 8, 16), wki, wki, _swap_cols(wki, 64, 8, 16), _swap_cols(wki, 64, 8, 16)], axis=1)
    w0q = np.concatenate([wq_, _swap_cols(wq_, 64, 8, 16), wqi, _swap_cols(wqi, 64, 8, 16), wgb, wgc, wxi], axis=1)
    w_uq = np.asarray(inp["odd_w_uq"], f32)[0]
    cols = np.arange(1536).reshape(16, 96)
    wuq_n = w_uq[:, cols[:, :64].reshape(-1)]
    wuq_r = w_uq[:, cols[:, 64:].reshape(-1)]
    wuq = np.concatenate([wuq_n, wuq_r, _swap_cols(wuq_r, 32, 16, 32)], axis=1)
    w_dkv = np.asarray(inp["odd_w_dkv"], f32)[0]
    wdkv = np.concatenate([w_dkv[:, :256], w_dkv[:, 256:], _swap_cols(w_dkv[:, 256:], 32, 16, 32)], axis=1)
    w_ukv = np.asarray(inp["odd_w_ukv"], f32)[0]
    c2 = np.arange(2048).reshape(16, 128)
    wukv_k = w_ukv[:, c2[:, :64].reshape(-1)]
    wukv_v = w_ukv[:, c2[:, 64:].reshape(-1)]

    cst = np.zeros((128, NCST), f32)
    kinds = ["norm_mix_pre", "norm_mix_post", "norm_ffn_pre", "norm_ffn_post"]
    for l in range(2):
        for k, nm in enumerate(kinds):
            cst[:, gcol(l, k):gcol(l, k) + 8] = np.asarray(inp[nm], f32)[l].reshape(8, 128).T
    cst[:, C_QN:C_QN + 3] = np.asarray(inp["odd_q_norm"], f32)[0].reshape(3, 128).T
    cst[:, C_KVN:C_KVN + 2] = np.asarray(inp["odd_kv_norm"], f32)[0].reshape(2, 128).T
    cw = np.asarray(inp["even_conv_w"], f32)[0]
    for j in range(3):
        cst[:, C_CW + j * 4:C_CW + j * 4 + 4] = cw[j].reshape(4, 128).T
    theta = 500000.0
    if0 = (theta ** (-np.arange(0, 16, 2, dtype=np.float32) / 16)).astype(f32)
    if1 = (theta ** (-np.arange(0, 32, 2, dtype=np.float32) / 32)).astype(f32)
    for p in range(128):
        r = p % 64
        if r < 16:
            cst[p, C_FR0] = if0[r % 8]
            cst[p, C_SG0] = -1.0 if r < 8 else 1.0
        r = p % 32
        cst[p, C_FR1] = if1[r % 16]
        cst[p, C_SG1] = -1.0 if r < 16 else 1.0

    shared = {
        "cst": cst, "w0k": w0k, "w0v": np.ascontiguousarray(wv_), "w0q": w0q, "w0wi": np.ascontiguousarray(wwi),
        "w_out": np.asarray(inp["even_w_out"], f32)[0], "w1": np.asarray(inp["mlp_w1"], f32),
        "w2": np.asarray(inp["mlp_w2"], f32), "w_dq": np.asarray(inp["odd_w_dq"], f32)[0], "w_uq": wuq,
        "w_dkv": wdkv, "w_ukv_k": np.ascontiguousarray(wukv_k), "w_ukv_v": np.ascontiguousarray(wukv_v),
        "w_o": np.asarray(inp["odd_w_o"], f32)[0],
    }
    shared = {k: np.ascontiguousarray(v, dtype=f32) for k, v in shared.items()}
    in_maps = []
    own_idx_all = []
    qi = np.arange(128)
    s_ = np.arange(1024)
    for b in range(n_batch):
        for par in range(2):
            xb = x[b]
            sets = [blocks_for(par), blocks_for(1 - par)]
            xT_sets, pos_sets, cbs = [], [], []
            kq = np.zeros((128, 32), f32)
            for si, blks in enumerate(sets):
                idx = np.concatenate([np.arange(128 * p, 128 * p + 128) for p in blks])
                halo = np.zeros((32, D), f32)
                for i, p in enumerate(blks):
                    if p > 0:
                        halo[2 * i:2 * i + 2] = xb[128 * p - 2:128 * p]
                xT_sets.append(np.concatenate([xb[idx], halo], axis=0).T)
                pos_sets.append(pos[b][idx][None, :])
                cb = np.zeros((8, 128, 1024), f32)
                for i, p in enumerate(blks):
                    kq[:, si * 16 + i] = np.minimum(256, 128 * p + qi + 1)
                for g in range(4):
                    for j in range(4):
                        rel = blks[4 * g + j] % 8
                        vis = s_[None, :] <= (rel * 128 + qi)[:, None]
                        cb[(g // 2) * 4 + j] = np.where(vis, 0.0, -1e30)
                cbs.append(cb)
            own = np.concatenate([np.arange(128 * p, 128 * p + 128) for p in sets[0]])
            own_idx_all.append(own)
            mT = np.zeros((2, 128, 8, 512), f32)
            for g in (0, 2):
                for j in range(4):
                    pq = sets[0][4 * g + j]
                    qpos = 128 * pq + qi
                    for rel in range(8):
                        sl = 8 * g + rel
                        pk = sets[sl % 2][sl // 2]
                        kpos = 128 * pk + qi
                        mT[g // 2, :, rel, j * 128:(j + 1) * 128] = (kpos[:, None] <= qpos[None, :])
            m = dict(shared)
            m.update({
                "xT_seq": np.ascontiguousarray(xb.T), "xT_own": np.ascontiguousarray(np.stack(xT_sets)),
                "pos_seq": np.ascontiguousarray(pos[b][None, :]), "pos_own": np.ascontiguousarray(np.stack(pos_sets)),
                "kq": kq, "cb": np.stack(cbs), "mT": mT.astype(ml_dtypes.bfloat16),
            })
            in_maps.append(m)
    return in_maps, own_idx_all


_NC_CACHE = {}


def kernel(**inputs):
    in_maps, own_idx = prepare_inputs(inputs, 4)
    if 8 not in _NC_CACHE:
        _NC_CACHE[8] = build_program(8)
    nc = _NC_CACHE[8]
    res = run_bass_kernel_spmd(nc, in_maps, core_ids=list(range(8)))
    out = np.zeros((4, T, D), np.float32)
    for c in range(8):
        b = c // 2
        out[b, own_idx[c], :] = np.asarray(res.results[c]["outT"], np.float32).T
    return out
```

```python
import types
import numpy as np
import ml_dtypes
import concourse.bass as bass
import concourse.mybir as mybir
from concourse.bass_utils import run_bass_kernel_spmd

F32 = mybir.dt.float32
BF16 = mybir.dt.bfloat16
I32 = mybir.dt.int32
AF = mybir.ActivationFunctionType
ALU = mybir.AluOpType
AX = mybir.AxisListType
DT_SIZE = {F32: 4, BF16: 2, I32: 4}

T = 4096
NO = 2048
D = 1024
EPS = 1e-6
NBIS = 18
TWO_PI = float(2 * np.pi)


class Op:
    __slots__ = ("eng", "fn", "reads", "writes", "is_dma", "deps", "needed", "ordinal",
                 "dsem", "dval", "barrier")

    def __init__(self, eng, fn, reads, writes, is_dma):
        self.eng = eng
        self.fn = fn
        self.reads = reads
        self.writes = writes
        self.is_dma = is_dma
        self.deps = []
        self.needed = False
        self.ordinal = None
        self.dsem = None
        self.dval = None
        self.barrier = False


class Prog:
    ENGS = ("pe", "act", "dve", "pool", "sp")
    SB_LIMIT = 228352

    def __init__(self, nc, n_dma_sems=12):
        self.nc = nc
        self.ops = []
        self.sb_off = 16896
        self.sb_max = 0
        self.n_dma_sems = n_dma_sems
        self._uid = 0
        self._bank = 0

    def sb(self, name, shape, dtype):
        nbytes = int(np.prod(shape[1:])) * DT_SIZE[dtype]
        nbytes = (nbytes + 63) // 64 * 64
        self._uid += 1
        t = self.nc.alloc_sbuf_tensor_at(f"{name}_{self._uid}", list(shape), dtype, offset=self.sb_off)
        self.sb_off += nbytes
        self.sb_max = max(self.sb_max, self.sb_off)
        assert self.sb_off <= self.SB_LIMIT, f"SBUF overflow {self.sb_off} at {name}"
        return t

    def mark(self):
        return self.sb_off

    def release(self, m):
        self.sb_off = m

    @staticmethod
    def _freeze(fn):
        if getattr(fn, "__closure__", None) is None:
            return fn
        cells = []
        for c in fn.__closure__:
            try:
                cells.append(types.CellType(c.cell_contents))
            except ValueError:
                cells.append(c)
        return types.FunctionType(fn.__code__, fn.__globals__, fn.__name__, fn.__defaults__, tuple(cells))

    def add(self, eng, fn, r=(), w=(), dma=False):
        fn = self._freeze(fn)
        o = Op(eng, fn, tuple(r), tuple(w), dma)
        self.ops.append(o)
        return o

    def pe(self, fn, r=(), w=()):
        return self.add("pe", fn, r, w)

    def act(self, fn, r=(), w=()):
        return self.add("act", fn, r, w)

    def dve(self, fn, r=(), w=()):
        return self.add("dve", fn, r, w)

    def pool(self, fn, r=(), w=()):
        return self.add("pool", fn, r, w)

    def dma(self, out, in_, r=(), w=(), q="sp", **kw):
        return self.add(q, lambda e: e.dma_start(out=out, in_=in_, **kw), r, w, dma=True)

    def final_wait(self, keys):
        return self.add("sp", lambda e: e.nop(), r=keys, w=())

    def barrier(self):
        o = Op(None, None, (), (), False)
        o.barrier = True
        self.ops.append(o)

    def finalize(self):
        last_w = {}
        readers = {}
        since_barrier = []
        pending_barrier = None
        seen_after = set()
        for o in self.ops:
            if o.barrier:
                summ = []
                lastc = {}
                for p in since_barrier:
                    if p.is_dma:
                        summ.append(p)
                    else:
                        lastc[p.eng] = p
                summ.extend(lastc.values())
                if pending_barrier is not None:
                    summ.extend(pending_barrier)
                pending_barrier = summ
                seen_after = set()
                since_barrier = []
                continue
            deps = []
            if pending_barrier is not None and o.eng not in seen_after:
                deps.extend(pending_barrier)
                seen_after.add(o.eng)
            for k in o.reads:
                if k in last_w:
                    deps.append(last_w[k])
            for k in o.writes:
                if k in last_w:
                    deps.append(last_w[k])
                deps.extend(readers.get(k, ()))
            for k in o.reads:
                readers.setdefault(k, []).append(o)
            for k in o.writes:
                last_w[k] = o
                readers[k] = []
            dd = []
            seen = set()
            for d in deps:
                if d is o or id(d) in seen:
                    continue
                seen.add(id(d))
                if (not d.is_dma) and (not o.is_dma) and d.eng == "pe" and o.eng == "pe":
                    continue
                dd.append(d)
            o.deps = dd
            for d in dd:
                d.needed = True
            since_barrier.append(o)
        cnt = {e: 0 for e in self.ENGS}
        dma_rr = {e: 0 for e in self.ENGS}
        dma_uses = {}
        for o in self.ops:
            if o.barrier:
                continue
            if o.is_dma:
                slot = dma_rr[o.eng] % self.n_dma_sems
                dma_rr[o.eng] += 1
                key = (o.eng, slot)
                dma_uses[key] = dma_uses.get(key, 0) + 1
                o.dsem = key
                o.dval = 16 * dma_uses[key]
            elif o.needed:
                cnt[o.eng] += 1
                o.ordinal = cnt[o.eng]
        self.max_ord = dict(cnt)

    def emit(self):
        nc = self.nc
        self.finalize()
        from contextlib import ExitStack
        es = ExitStack()
        sems = {}
        for e in ("pe", "act", "dve", "pool", "sp"):
            sems[e] = es.enter_context(nc.semaphore(f"c_{e}"))
        dsems = {}
        used = sorted({o.dsem for o in self.ops if (not o.barrier) and o.is_dma})
        for key in used:
            dsems[key] = es.enter_context(nc.semaphore(f"d_{key[0]}_{key[1]}"))
        block = es.enter_context(nc.Block())
        per_eng = {e: [o for o in self.ops if (not o.barrier) and o.eng == e] for e in self.ENGS}

        def body(ename, engine):
            known = {}
            for o in per_eng[ename]:
                waits = {}
                for d in o.deps:
                    if d.is_dma:
                        s, v, k = dsems[d.dsem], d.dval, ("d",) + d.dsem
                    else:
                        s, v, k = sems[d.eng], d.ordinal, ("c", d.eng)
                    if v > waits.get(k, (None, 0))[1]:
                        waits[k] = (s, v)
                if o.is_dma and o.dval > 16:
                    k = ("d",) + o.dsem
                    v = o.dval - 16
                    if v > waits.get(k, (None, 0))[1]:
                        waits[k] = (dsems[o.dsem], v)
                for k, (s, v) in waits.items():
                    if known.get(k, 0) >= v:
                        continue
                    engine.wait_ge(s, v)
                    known[k] = v
                ins = o.fn(engine)
                if o.is_dma:
                    ins.then_inc(dsems[o.dsem], 16)
                elif o.needed:
                    ins.then_inc(sems[ename], 1)

        @block.tensor
        def _(e):
            body("pe", e)

        @block.scalar
        def _(e):
            body("act", e)

        @block.vector
        def _(e):
            body("dve", e)

        @block.gpsimd
        def _(e):
            body("pool", e)

        @block.sync
        def _(e):
            body("sp", e)

        es.close()


def blocks_for(par):
    lo = list(range(par, 16, 2))
    hi = sorted(31 - j for j in lo)
    return lo + hi


C_G = 0
C_QN = 64
C_KVN = 67
C_CW = 69
C_FR0 = 81
C_SG0 = 82
C_FR1 = 83
C_SG1 = 84
NCST = 96


def gcol(layer, kind):
    return C_G + (layer * 4 + kind) * 8


def build_program(n_cores, dbg=(), no_cc=False, stop_after=None):
    nc = bass.Bass("TRN2", target_bir_lowering=False)
    P = Prog(nc)

    def stop(name):
        if stop_after == name:
            P.barrier()
            P.final_wait([])
            P.emit()
            return True
        return False

    def din(name, shape, dt=F32):
        return nc.dram_tensor(name, list(shape), dt, kind="ExternalInput").ap()

    def dscr(name, shape, dt):
        kind = "ExternalOutput" if name in dbg else "Internal"
        return nc.dram_tensor(name, list(shape), dt, kind=kind).ap()

    xT_seq = din("xT_seq", [D, T])
    xT_own = din("xT_own", [2, D, NO + 32])
    pos_seq = din("pos_seq", [1, T], I32)
    pos_own = din("pos_own", [2, 1, NO], I32)
    cst_d = din("cst", [128, NCST])
    kq_d = din("kq", [128, 32])
    cb_d = din("cb", [2, 8, 128, 1024])
    mT_d = din("mT", [2, 128, 8, 512], BF16)
    w0k_d = din("w0k", [D, 1280])
    w0v_d = din("w0v", [D, 512])
    w0q_d = din("w0q", [D, 4608])
    w0wi_d = din("w0wi", [D, 16])
    wout_d = din("w_out", [D, D])
    w1_d = din("w1", [2, D, 4096])
    w2_d = din("w2", [2, 4096, D])
    wdq_d = din("w_dq", [D, 384])
    wuq_d = din("w_uq", [384, 2048])
    wdkv_d = din("w_dkv", [D, 320])
    wukvk_d = din("w_ukv_k", [256, 1024])
    wukvv_d = din("w_ukv_v", [256, 1024])
    wo_d = din("w_o", [D, D])
    outT = nc.dram_tensor("outT", [D, NO], F32, kind="ExternalOutput").ap()

    kT_d = dscr("kT_d", [4, 128, T], BF16)
    kidxT_d = dscr("kidxT_d", [128, T], BF16)
    vaug_d = dscr("vaug_d", [32, 128, 1024], BF16)
    qT_d = dscr("qT_d", [4, 128, NO], BF16)
    qidxT_d = dscr("qidxT_d", [8, 128, NO], BF16)
    convT_d = dscr("convT_d", [4, 128, NO], BF16)
    maskT_d = dscr("maskT_d", [4, 128, 32, 512], BF16)
    attnT_d = dscr("attnT_d", [8, 128, NO], BF16)
    x1T_d = dscr("x1T_d", [D, NO], F32)
    x2T_d = dscr("x2T_d", [2, D, NO], F32)
    x3T_d = dscr("x3T_d", [D, NO], F32)
    kva_sets_d = dscr("kva_sets_d", [2, 288, NO], F32)
    qnT_d = dscr("qnT_d", [8, 128, NO], BF16)
    qrT_d = dscr("qrT_d", [8, 64, NO], BF16)
    knT_d = dscr("knT_d", [8, 128, T], BF16)
    vaug1_d = dscr("vaug1_d", [32, 128, 2048], BF16)
    dbg_d = dscr("dbg_d", [128, 4096], F32)
    kr4_d = dscr("kr4_d", [64, T], BF16)

    ps = [nc.alloc_psum_tensor(f"ps{i}", [128, 512], F32) for i in range(8)]

    def PK(i):
        return ("ps", i)

    cst = P.sb("cst", [128, NCST], F32)
    kq = P.sb("kq", [128, 32], F32)
    widx = P.sb("widx", [128, 16, 16], F32)
    ones_b = P.sb("ones_b", [128, 128], BF16)
    ones_q = P.sb("ones_q", [128, 128], BF16)
    ident = P.sb("ident", [128, 128], F32)
    P.dma(cst[:], cst_d, w=["cst"])
    P.dma(kq[:], kq_d, w=["kq"])
    P.pool(lambda e: e.memset(ones_b[:], 1.0 / 1024), w=["ones_b"])
    P.pool(lambda e: e.memset(ones_q[:], 1.0 / 256), w=["ones_q"])
    P.pool(lambda e: e.memset(ident[:], 1.0), w=["ident"])
    P.pool(lambda e: e.affine_select(out=ident[:], in_=ident[:], pattern=[[-1, 128]], compare_op=ALU.is_equal,
                                     fill=0.0, base=0, channel_multiplier=1), r=["ident"], w=["ident"])
    base_mark = P.mark()

    uid = [0]

    def U(s):
        uid[0] += 1
        return f"{s}#{uid[0]}"

    def rope_tables(pos_d, n, fr_col, sg_col, tag):
        C = P.sb(tag + "C", [128, n], F32)
        S = P.sb(tag + "S", [128, n], F32)
        m = P.mark()
        pi_ = P.sb("posi", [128, n], I32)
        pf = P.sb("posf", [128, n], F32)
        tmp = P.sb("rtmp", [128, n], F32)
        ki = P.sb("rki", [128, n], I32)
        kpi, kpf, kt, kk = U("posi"), U("posf"), U("rtmp"), U("rki")
        kC, kS = tag + "C", tag + "S"
        P.dma(pi_[:], pos_d.to_broadcast([128, n]), w=[kpi])
        P.dve(lambda e: e.tensor_copy(out=pf[:], in_=pi_[:]), r=[kpi], w=[kpf])
        P.dve(lambda e: e.tensor_scalar(out=pf[:], in0=pf[:], scalar1=cst[:, fr_col:fr_col + 1], scalar2=None,
                                        op0=ALU.mult), r=[kpf, "cst"], w=[kpf])
        for which, dst, kd in (("s", S, kS), ("c", C, kC)):
            off = 0.0 if which == "s" else float(np.pi / 2)
            P.dve(lambda e, off=off: e.tensor_scalar(out=tmp[:], in0=pf[:], scalar1=off, scalar2=1.0 / TWO_PI,
                                                     op0=ALU.add, op1=ALU.mult), r=[kpf], w=[kt])
            P.dve(lambda e: e.tensor_copy(out=ki[:], in_=tmp[:]), r=[kt], w=[kk])
            P.dve(lambda e: e.tensor_copy(out=tmp[:], in_=ki[:]), r=[kk], w=[kt])
            P.dve(lambda e: e.scalar_tensor_tensor(out=tmp[:], in0=tmp[:], scalar=-TWO_PI, in1=pf[:],
                                                   op0=ALU.mult, op1=ALU.add), r=[kt, kpf], w=[kt])
            P.dve(lambda e, off=off: e.tensor_scalar(out=tmp[:], in0=tmp[:], scalar1=off, scalar2=None,
                                                     op0=ALU.add), r=[kt], w=[kt])
            P.dve(lambda e, dst=dst: e.tensor_scalar(out=dst[:], in0=tmp[:], scalar1=float(np.pi), scalar2=-TWO_PI,
                                                     op0=ALU.is_gt, op1=ALU.mult), r=[kt], w=[kd])
            P.dve(lambda e, dst=dst: e.tensor_tensor(out=tmp[:], in0=tmp[:], in1=dst[:], op=ALU.add), r=[kt, kd], w=[kt])
            P.dve(lambda e, dst=dst: e.tensor_scalar(out=dst[:], in0=tmp[:], scalar1=-float(np.pi), scalar2=TWO_PI,
                                                     op0=ALU.is_lt, op1=ALU.mult), r=[kt], w=[kd])
            P.dve(lambda e, dst=dst: e.tensor_tensor(out=tmp[:], in0=tmp[:], in1=dst[:], op=ALU.add), r=[kt, kd], w=[kt])
            P.dve(lambda e: e.tensor_scalar(out=tmp[:], in0=tmp[:], scalar1=-3.14159, scalar2=3.14159,
                                            op0=ALU.max, op1=ALU.min), r=[kt], w=[kt])
            P.act(lambda e, dst=dst: e.activation(out=dst[:], in_=tmp[:], func=AF.Sin), r=[kt], w=[kd])
        P.dve(lambda e: e.tensor_scalar(out=S[:], in0=S[:], scalar1=cst[:, sg_col:sg_col + 1], scalar2=None,
                                        op0=ALU.mult), r=[kS, "cst"], w=[kS])
        P.barrier()
        P.release(m)
        return C, S

    def load_w(dst, src_ap, key, nsplit=1):
        P.dma(dst, src_ap, w=[key], q="pool")

    def rmsnorm_fm(xs, kx, nchunk, n, gain_col, hout, kh, onesm, sq, ksq, rstd, krs, ssbank, eps=EPS):
        P.act(lambda e: e.activation(out=sq[:, 0:nchunk, 0:n], in_=xs[:, 0:nchunk, 0:n], func=AF.Square),
              r=[kx], w=[ksq])
        for c in range(nchunk):
            P.pe(lambda e, c=c: e.matmul(ps[ssbank][:, 0:n], lhsT=onesm[:], rhs=sq[:, c, 0:n],
                                         start=(c == 0), stop=(c == nchunk - 1)),
                 r=[ksq, "ones_b", "ones_q"], w=[PK(ssbank)])
        P.act(lambda e: e.activation(out=rstd[:, 0:n], in_=ps[ssbank][:, 0:n], func=AF.Sqrt, bias=eps, scale=1.0),
              r=[PK(ssbank)], w=[krs])
        P.dve(lambda e: e.reciprocal(out=rstd[:, 0:n], in_=rstd[:, 0:n]), r=[krs], w=[krs])
        for c in range(nchunk):
            P.dve(lambda e, c=c: e.scalar_tensor_tensor(out=hout[:, c, 0:n], in0=xs[:, c, 0:n],
                                                      scalar=cst[:, gain_col + c:gain_col + c + 1],
                                                      in1=rstd[:, 0:n], op0=ALU.mult, op1=ALU.mult),
                r=[kx, krs, "cst"], w=[(kh, c)])

    bankrot = [0]

    def nb():
        b = bankrot[0] % 4
        bankrot[0] += 1
        return b

    def proj_fm(wt, kw, oc0, h, kh, nk, n, bank, m=128):
        for kc in range(nk):
            P.pe(lambda e, kc=kc: e.matmul(ps[bank][0:m, 0:n], lhsT=wt[:, kc, oc0:oc0 + m], rhs=h[:, kc, 0:n],
                                           start=(kc == 0), stop=(kc == nk - 1)),
                 r=[kw, (kh, kc)], w=[PK(bank)])

    rope_mark = P.mark()
    C0s, S0s = rope_tables(pos_seq, T, C_FR0, C_SG0, "r0s")

    wk = P.sb("wk", [128, 8, 1280], BF16)
    wv = P.sb("wv", [128, 8, 512], BF16)
    load_w(wk[:], w0k_d.rearrange("(c p) n -> p c n", p=128), "wk")
    load_w(wv[:], w0v_d.rearrange("(c p) n -> p c n", p=128), "wv")
    xs2 = [P.sb(f"xs{i}", [128, 8, 512], F32) for i in range(2)]
    sq = P.sb("sq", [128, 8, 512], BF16)
    hb = P.sb("hb", [128, 8, 512], BF16)
    rstd = P.sb("rstd", [128, 512], F32)
    t1 = [P.sb(f"t1_{i}", [128, 512], F32) for i in range(2)]
    t2 = [P.sb(f"t2_{i}", [128, 512], F32) for i in range(2)]
    kst = [P.sb(f"kst{i}", [128, 5, 512], BF16) for i in range(2)]
    vst = [P.sb(f"vst{i}", [128, 8, 128], BF16) for i in range(2)]
    for i in range(2):
        P.pool(lambda e, i=i: e.memset(vst[i][:], 1.0), w=[("vst", i)])
    xseq_v = xT_seq.rearrange("(c p) t -> p c t", p=128)
    tcnt = [0]

    def rope_evac(bA, bB, Ct, St, col0, n, dst, kdst, kC="r0sC", kS="r0sS"):
        i = tcnt[0] % 2
        tcnt[0] += 1
        P.dve(lambda e: e.tensor_tensor(out=t1[i][:, 0:n], in0=ps[bA][:, 0:n], in1=Ct[:, col0:col0 + n], op=ALU.mult),
              r=[PK(bA), kC], w=[("t1", i)])
        P.dve(lambda e: e.tensor_tensor(out=t2[i][:, 0:n], in0=ps[bB][:, 0:n], in1=St[:, col0:col0 + n], op=ALU.mult),
              r=[PK(bB), kS], w=[("t2", i)])
        P.pool(lambda e: e.tensor_tensor(out=dst, in0=t1[i][:, 0:n], in1=t2[i][:, 0:n], op=ALU.add),
               r=[("t1", i), ("t2", i)], w=[kdst])

    for tc in range(8):
        xs = xs2[tc % 2]
        kx = ("xs", tc % 2)
        P.dma(xs[:], xseq_v[:, :, tc * 512:(tc + 1) * 512], w=[kx])
        rmsnorm_fm(xs, kx, 8, 512, gcol(0, 0), hb, "hb", ones_b, sq, "sq", rstd, "rstd", 7)
        if tc == 0 and "dbg_d" in dbg:
            dtmp = P.sb("dtmp", [128, 1024], F32)
            P.dma(dbg_d[:, 0:512], xs[:, 0, :], r=[kx], w=["dbg0"])
            P.dma(dbg_d[:, 512:1024], rstd[:], r=["rstd"], w=["dbg1"])
            P.dve(lambda e: e.tensor_copy(out=dtmp[:, 0:512], in_=hb[:, 0, :]), r=[("hb", 0)], w=["dtmp"])
            P.dve(lambda e: e.tensor_copy(out=dtmp[:, 512:1024], in_=sq[:, 0, :]), r=["sq"], w=["dtmp"])
            P.dma(dbg_d[:, 1024:2048], dtmp[:], r=["dtmp"], w=["dbg2"])
            dt2 = P.sb("dt2", [128, 512], F32)
            P.act(lambda e: e.activation(out=dt2[:], in_=ps[7][:, :], func=AF.Copy), r=[PK(7)], w=["dt2"])
            P.dma(dbg_d[:, 2048:2560], dt2[:], r=["dt2"], w=["dbg3"])
            P.dma(dbg_d[:, 2560:3072], S0s[:, 0:512], r=["r0sS"], w=["dbg4"])
            P.dma(dbg_d[:, 3072:3584], C0s[:, 3584:4096], r=["r0sC"], w=["dbg5"])
            P.dma(dbg_d[:, 3584:4096], S0s[:, 3584:4096], r=["r0sS"], w=["dbg6"])
        ks = kst[tc % 2]
        for oc in range(5):
            bA, bB = nb(), nb()
            colA = oc * 128 if oc < 4 else 1024
            colB = 512 + oc * 128 if oc < 4 else 1152
            proj_fm(wk, "wk", colA, hb, "hb", 8, 512, bA)
            proj_fm(wk, "wk", colB, hb, "hb", 8, 512, bB)
            rope_evac(bA, bB, C0s, S0s, tc * 512, 512, ks[:, oc, :], ("kst", tc % 2, oc))
        for oc in range(4):
            P.dma(kT_d[oc, :, tc * 512:(tc + 1) * 512], ks[:, oc, :], r=[("kst", tc % 2, oc)], w=[("kT_d", oc, tc)])
        P.dma(kidxT_d[:, tc * 512:(tc + 1) * 512], ks[:, 4, :], r=[("kst", tc % 2, 4)], w=[("kidxT_d", tc)])
        for tb in range(4):
            kb = tc * 4 + tb
            b = nb()
            for kc in range(8):
                P.pe(lambda e, kc=kc, tb=tb: e.matmul(ps[b][:, :], lhsT=hb[:, kc, tb * 128:(tb + 1) * 128],
                                                      rhs=wv[:, kc, :], start=(kc == 0), stop=(kc == 7)),
                     r=["wv", ("hb", kc)], w=[PK(b)])
            vs = vst[kb % 2]
            pv = ps[b][:, :].rearrange("p (h two d) -> p h two d", two=2, d=64)
            vv = vs[:].rearrange("p (h two) d -> p h two d", two=2)
            P.act(lambda e, pv=pv, vv=vv: e.activation(out=vv[:, :, 0, 0:64], in_=pv[:, :, 0, :], func=AF.Copy),
                  r=[PK(b)], w=[("vst", kb % 2)])
            P.act(lambda e, pv=pv, vv=vv: e.activation(out=vv[:, :, 1, 64:128], in_=pv[:, :, 1, :], func=AF.Copy),
                  r=[PK(b)], w=[("vst", kb % 2)])
            P.dma(vaug_d[kb], vs[:].rearrange("p h d -> p (h d)"), r=[("vst", kb % 2)], w=[("vaug_d", kb)])
    P.barrier()
    P.release(rope_mark)

    for s_ in range(2):
        C0o, S0o = rope_tables(pos_own[s_], NO, C_FR0, C_SG0, "r0o")
        set_mark = P.mark()
        if stop("K"):
            return nc
        wq = P.sb("wq", [128, 8, 4608], BF16)
        wwi = P.sb("wwi", [128, 8, 16], BF16)
        for c in range(8):
            P.dma(wq[:, c, :], w0q_d[c * 128:(c + 1) * 128, :], w=["wq"], q="pool")
        load_w(wwi[:], w0wi_d.rearrange("(c p) n -> p c n", p=128), "wwi")
        uext = P.sb("uext", [128, 4, 16, 130], F32)
        gbs = P.sb("gbs", [128, 4, NO], BF16)
        q_mark = P.mark()
        xs2 = [P.sb("xsq", [128, 8, 512], F32)] * 2
        sq = P.sb("sq", [128, 8, 512], BF16)
        hb = P.sb("hb", [128, 8, 512], BF16)
        rstd = P.sb("rstd", [128, 512], F32)
        t1 = [P.sb(f"t1_{i}", [128, 512], F32) for i in range(2)]
        t2 = [P.sb(f"t2_{i}", [128, 512], F32) for i in range(2)]
        qrot = [P.sb(f"qrot{i}", [128, 512], BF16) for i in range(6)]
        gcs = [P.sb(f"gcs{i}", [128, 512], F32) for i in range(2)]
        qrc = [0]
        xown_v = xT_own[s_].rearrange("(c p) t -> p c t", p=128)
        for tc in range(5):
            n = 512 if tc < 4 else 32
            xs = xs2[0]
            kx = ("xs", 0)
            P.dma(xs[:, :, 0:n], xown_v[:, :, tc * 512:tc * 512 + n], w=[kx])
            rmsnorm_fm(xs, kx, 8, n, gcol(0, 0), hb, "hb", ones_b, sq, "sq", rstd, "rstd", 7)
            if tc < 4:
                for oc in range(12):
                    bA, bB = nb(), nb()
                    colA = oc * 128 if oc < 4 else 1024 + (oc - 4) * 128
                    colB = 512 + oc * 128 if oc < 4 else 2048 + (oc - 4) * 128
                    proj_fm(wq, "wq", colA, hb, "hb", 8, 512, bA)
                    proj_fm(wq, "wq", colB, hb, "hb", 8, 512, bB)
                    qi_ = qrc[0] % 6
                    qrc[0] += 1
                    rope_evac(bA, bB, C0o, S0o, tc * 512, 512, qrot[qi_][:], ("qrot", qi_), "r0oC", "r0oS")
                    if oc < 4:
                        P.dma(qT_d[oc, :, tc * 512:(tc + 1) * 512], qrot[qi_][:], r=[("qrot", qi_)], w=[("qT_d", oc, tc)])
                    else:
                        P.dma(qidxT_d[oc - 4, :, tc * 512:(tc + 1) * 512], qrot[qi_][:], r=[("qrot", qi_)],
                              w=[("qidxT_d", oc - 4, tc)])
                for cc in range(4):
                    b = nb()
                    proj_fm(wq, "wq", 3072 + cc * 128, hb, "hb", 8, 512, b)
                    P.act(lambda e, b=b, cc=cc: e.activation(out=gbs[:, cc, tc * 512:(tc + 1) * 512], in_=ps[b][:, :], func=AF.Copy),
                          r=[PK(b)], w=[("gbs", cc)])
                for tb in range(4):
                    b = nb()
                    for kc in range(8):
                        P.pe(lambda e, kc=kc, tb=tb, b=b: e.matmul(ps[b][:, 0:16], lhsT=hb[:, kc, tb * 128:(tb + 1) * 128],
                                                                   rhs=wwi[:, kc, :], start=(kc == 0), stop=(kc == 7)),
                             r=["wwi", ("hb", kc)], w=[PK(b)])
                    P.act(lambda e, b=b, tb=tb: e.activation(out=widx[:, tc * 4 + tb, :], in_=ps[b][:, 0:16], func=AF.Copy),
                          r=[PK(b)], w=["widx"])
            for cc in range(4):
                bA, bB = nb(), nb()
                proj_fm(wq, "wq", 3584 + cc * 128, hb, "hb", 8, n, bA)
                proj_fm(wq, "wq", 4096 + cc * 128, hb, "hb", 8, n, bB)
                g = gcs[cc % 2]
                P.act(lambda e, g=g, bA=bA: e.activation(out=g[:, 0:n], in_=ps[bA][:, 0:n], func=AF.Copy),
                      r=[PK(bA)], w=[("gcs", cc % 2)])
                if tc < 4:
                    o_ap = uext[:, cc, tc * 4:(tc + 1) * 4, 2:130]
                    i0 = ps[bB][:, :].rearrange("p (b t) -> p b t", t=128)
                    i1 = g[:].rearrange("p (b t) -> p b t", t=128)
                else:
                    o_ap = uext[:, cc, :, 0:2]
                    i0 = ps[bB][:, 0:32].rearrange("p (b t) -> p b t", t=2)
                    i1 = g[:, 0:32].rearrange("p (b t) -> p b t", t=2)
                P.dve(lambda e, o_ap=o_ap, i0=i0, i1=i1: e.tensor_tensor(out=o_ap, in0=i0, in1=i1, op=ALU.mult),
                      r=[PK(bB), ("gcs", cc % 2)], w=[("uext", cc)])
        P.barrier()
        P.release(q_mark)
        cacc = [P.sb(f"cacc{i}", [128, 16, 128], F32) for i in range(2)]
        cvo = [P.sb(f"cvo{i}", [128, NO], BF16) for i in range(2)]
        for cc in range(4):
            a = cacc[cc % 2]
            ka = ("cacc", cc % 2)
            P.dve(lambda e, a=a, cc=cc: e.tensor_scalar(out=a[:], in0=uext[:, cc, :, 2:130],
                                                        scalar1=cst[:, C_CW + 8 + cc:C_CW + 9 + cc], scalar2=None, op0=ALU.mult),
                  r=[("uext", cc), "cst"], w=[ka])
            P.dve(lambda e, a=a, cc=cc: e.scalar_tensor_tensor(out=a[:], in0=uext[:, cc, :, 1:129],
                                                               scalar=cst[:, C_CW + 4 + cc:C_CW + 5 + cc], in1=a[:],
                                                               op0=ALU.mult, op1=ALU.add), r=[("uext", cc), "cst", ka], w=[ka])
            P.dve(lambda e, a=a, cc=cc: e.scalar_tensor_tensor(out=a[:], in0=uext[:, cc, :, 0:128],
                                                               scalar=cst[:, C_CW + cc:C_CW + 1 + cc], in1=a[:],
                                                               op0=ALU.mult, op1=ALU.add), r=[("uext", cc), "cst", ka], w=[ka])
            co = cvo[cc % 2]
            P.dve(lambda e, a=a, cc=cc, co=co: e.tensor_tensor(out=co[:], in0=a[:].rearrange("p b t -> p (b t)"),
                                                               in1=gbs[:, cc, :], op=ALU.mult),
                  r=[ka, ("gbs", cc)], w=[("cvo", cc % 2)])
            P.dma(convT_d[cc], co[:], r=[("cvo", cc % 2)], w=[("convT_d", cc)])
        P.barrier()
        P.release(base_mark)

        if stop("Q"):
            return nc
        kidx = P.sb("kidx", [128, T], BF16)
        qidx = P.sb("qidx", [128, 8, NO], BF16)
        P.dma(kidx[:], kidxT_d, r=[("kidxT_d", t_) for t_ in range(8)], w=["kidx"])
        for oc in range(8):
            P.dma(qidx[:, oc, :], qidxT_d[oc], r=[("qidxT_d", oc, t_) for t_ in range(4)], w=[("qidx", oc)])
        sc4 = [P.sb(f"sc{i}", [128, T], F32) for i in range(4)]
        junk = P.sb("junk", [128, T], BF16)
        m01 = P.sb("m01", [128, T], F32)
        cbt = [P.sb(f"cbt{i}", [128, 1024], F32) for i in range(2)]
        rr = [P.sb(f"rr{i}", [128, 512], BF16) for i in range(6)]
        mTs = [P.sb(f"mTs{i}", [128, 32, 512], BF16) for i in range(1)]
        dg2 = [P.sb(f"dg{i}", [128, 16, 128], BF16) for i in range(2)]
        identb = P.sb("identb", [128, 128], BF16)
        sm2 = [P.sb(f"sm{i}", [128, 16], F32) for i in range(2)]
        stepT2 = [P.sb(f"stepT{i}", [128, 2, NBIS + 1], F32) for i in range(2)]
        P.dve(lambda e: e.tensor_copy(out=identb[:], in_=ident[:]), r=["ident"], w=["identb"])
        rcnt = [0]
        acnt_i = [0]
        mt = mTs[0]

        def acc_half(g, hf, hi):
            nk = 1024 * (g + 1)
            nch = nk // 512
            for jj in range(2):
                j = 2 * hf + jj
                qi = 4 * g + j
                sc = sc4[j]
                ksc = ("sc", j)
                dg = dg2[qi % 2]
                kdg = ("dg", qi % 2)
                for h in range(16):
                    P.pool(lambda e, dg=dg, h=h, qi=qi: e.tensor_scalar(out=dg[:, h, :], in0=identb[:], scalar1=widx[:, qi, h:h + 1],
                                                                        scalar2=None, op0=ALU.mult),
                           r=["identb", "widx"], w=[kdg])
                for ch in range(nch):
                    accb = 4 + (acnt_i[0] % 2)
                    acnt_i[0] += 1
                    pend = []
                    for h in range(16):
                        b = nb()
                        base = (h % 2) * 64
                        P.pe(lambda e, b=b, h=h, base=base, ch=ch, qi=qi: e.matmul(
                            ps[b][:, :], lhsT=qidx[base:base + 64, h // 2, qi * 128:(qi + 1) * 128],
                            rhs=kidx[base:base + 64, ch * 512:(ch + 1) * 512], start=True, stop=True),
                            r=["kidx", ("qidx", h // 2)], w=[PK(b)])
                        ri = rcnt[0] % 6
                        rcnt[0] += 1
                        r_ = rr[ri]
                        P.act(lambda e, b=b, r_=r_: e.activation(out=r_[:], in_=ps[b][:, :], func=AF.Relu),
                              r=[PK(b)], w=[("rr", ri)])
                        pend.append((h, r_, ri))
                        if len(pend) > 2:
                            h0, r0, ri0 = pend.pop(0)
                            P.pe(lambda e, h0=h0, r0=r0, accb=accb, dg=dg: e.matmul(ps[accb][:, :], lhsT=dg[:, h0, :], rhs=r0[:],
                                                                                    start=(h0 == 0), stop=(h0 == 15)),
                                 r=[("rr", ri0), kdg], w=[PK(accb)])
                    for h0, r0, ri0 in pend:
                        P.pe(lambda e, h0=h0, r0=r0, accb=accb, dg=dg: e.matmul(ps[accb][:, :], lhsT=dg[:, h0, :], rhs=r0[:],
                                                                                start=(h0 == 0), stop=(h0 == 15)),
                             r=[("rr", ri0), kdg], w=[PK(accb)])
                    P.act(lambda e, sc=sc, ch=ch, accb=accb: e.activation(out=sc[:, ch * 512:(ch + 1) * 512], in_=ps[accb][:, :], func=AF.Copy),
                          r=[PK(accb)], w=[(ksc, ch)])

        def prep_half(g, hf, hi):
            nk = 1024 * (g + 1)
            nch = nk // 512
            sm = sm2[hi % 2]
            stepT = stepT2[hi % 2]
            for jj in range(2):
                j = 2 * hf + jj
                qi = 4 * g + j
                sc = sc4[j]
                allsc = [(("sc", j), ch) for ch in range(nch)]
                cb = cbt[jj]
                P.dma(cb[:], cb_d[s_, (g // 2) * 4 + j], w=[("cbt", jj)])
                P.dve(lambda e, sc=sc, jj=jj, sm=sm: e.tensor_reduce(out=sm[:, jj:jj + 1], in_=sc[:, 0:nk], axis=AX.X, op=ALU.max,
                                                                     apply_absolute_value=True), r=allsc, w=[("smA", hi % 2, jj)])
                P.dve(lambda e, sc=sc, cb=cb: e.tensor_tensor(out=sc[:, nk - 1024:nk], in0=sc[:, nk - 1024:nk], in1=cb[:],
                                                              op=ALU.add), r=allsc + [("cbt", jj), ("smA", hi % 2, jj)], w=allsc)
            allA = [("smA", hi % 2, jj) for jj in range(2)]
            P.dve(lambda e, sm=sm: e.tensor_single_scalar(out=sm[:, 0:2].bitcast(I32), in_=sm[:, 0:2].bitcast(I32),
                                                          scalar=0x7F800000, op=ALU.bitwise_and), r=allA, w=[("smA2", hi % 2)])
            P.dve(lambda e, sm=sm: e.tensor_scalar(out=sm[:, 0:2], in0=sm[:, 0:2], scalar1=2.0, scalar2=1e-30,
                                                   op0=ALU.mult, op1=ALU.max), r=[("smA2", hi % 2)], w=[("smA2", hi % 2)])
            for i in range(NBIS + 1):
                P.pool(lambda e, i=i, sm=sm, stepT=stepT: e.tensor_scalar(out=stepT[:, :, i], in0=sm[:, 0:2], scalar1=float(2.0 ** -i),
                                                                          scalar2=None, op0=ALU.mult),
                       r=[("smA2", hi % 2)], w=[("stepT", hi % 2, i)])
            P.pool(lambda e, sm=sm: e.memset(sm[:, 2:4], 0.0), w=[("mid", hi % 2)])

        def bis_half(g, hf, hi):
            nk = 1024 * (g + 1)
            nch = nk // 512
            sm = sm2[hi % 2]
            stepT = stepT2[hi % 2]
            kmid = ("mid", hi % 2)
            qi0 = 4 * g + 2 * hf
            kq2 = kq[:, s_ * 16 + qi0:s_ * 16 + qi0 + 2]
            for i in range(NBIS):
                for jj in range(2):
                    j = 2 * hf + jj
                    P.dve(lambda e, j=j, jj=jj, sm=sm: e.tensor_scalar(out=junk[:, 0:nk], in0=sc4[j][:, 0:nk], scalar1=sm[:, 2 + jj:3 + jj],
                                                                       scalar2=None, op0=ALU.is_ge, op1=ALU.add, accum_out=sm[:, 4 + jj:5 + jj]),
                          r=[(("sc", j), ch) for ch in range(nch)] + [kmid], w=[("cnt", hi % 2, jj)])
                P.dve(lambda e, sm=sm: e.tensor_tensor(out=sm[:, 6:8], in0=sm[:, 4:6], in1=kq2, op=ALU.is_ge),
                      r=[("cnt", hi % 2, jj) for jj in range(2)] + ["kq"], w=[("s4", hi % 2)])
                P.dve(lambda e, i=i, sm=sm, stepT=stepT: e.scalar_tensor_tensor(out=sm[:, 8:10], in0=sm[:, 6:8], scalar=0.5, in1=stepT[:, :, i],
                                                                                op0=ALU.subtract, op1=ALU.mult),
                      r=[("s4", hi % 2), ("stepT", hi % 2, i)], w=[("d4", hi % 2)])
                P.dve(lambda e, sm=sm: e.tensor_tensor(out=sm[:, 2:4], in0=sm[:, 2:4], in1=sm[:, 8:10], op=ALU.add),
                      r=[("d4", hi % 2), kmid], w=[kmid])
            P.dve(lambda e, sm=sm, stepT=stepT: e.tensor_tensor(out=sm[:, 10:12], in0=sm[:, 2:4], in1=stepT[:, :, NBIS], op=ALU.subtract),
                  r=[kmid, ("stepT", hi % 2, NBIS)], w=[("thr", hi % 2)])
            for jj in range(2):
                j = 2 * hf + jj
                P.dve(lambda e, j=j, jj=jj, sm=sm: e.tensor_scalar(out=m01[:, 0:nk], in0=sc4[j][:, 0:nk], scalar1=sm[:, 10 + jj:11 + jj], scalar2=None,
                                                                   op0=ALU.is_ge), r=[(("sc", j), ch) for ch in range(nch)] + [("thr", hi % 2)], w=["m01"])
                for k4 in range(nk // 512):
                    b = 6 + (k4 % 2)
                    for kk in range(4):
                        kb = k4 * 4 + kk
                        P.pe(lambda e, b=b, kk=kk, kb=kb: e.transpose(out=ps[b][:, kk * 128:(kk + 1) * 128],
                                                                      in_=m01[:, kb * 128:(kb + 1) * 128], identity=ident[:]),
                             r=["m01", "ident"], w=[PK(b)])
                    P.act(lambda e, b=b, k4=k4, j=j: e.activation(
                        out=mt[:, k4 * 4:(k4 + 1) * 4, j * 128:(j + 1) * 128],
                        in_=ps[b][:, :].rearrange("p (k t) -> p k t", t=128), func=AF.Copy),
                        r=[PK(b)], w=["mTs"])
            if hf == 1:
                P.dma(maskT_d[g, :, 0:nk // 128, :], mt[:, 0:nk // 128, :], r=["mTs"], w=[("maskT_d", g)])

        halves = [(g, hf) for g in range(4) for hf in range(2)]
        for hi, (g, hf) in enumerate(halves):
            acc_half(g, hf, hi)
            if hi > 0:
                bis_half(halves[hi - 1][0], halves[hi - 1][1], hi - 1)
            prep_half(g, hf, hi)
        bis_half(halves[-1][0], halves[-1][1], len(halves) - 1)
        P.barrier()
        P.release(base_mark)

        if stop("I"):
            return nc
        def attention(nheads, hp_loader, st_emit, mask_for, scale, out_d, okey):
            pts = [P.sb(f"pt{i}", [128, 512], BF16) for i in range(6)]
            rdn = [P.sb(f"rdn{i}", [128, 512], F32) for i in range(2)]
            ost = [P.sb(f"ost{i}", [128, NO], BF16) for i in range(2)]
            ucnt = [0]
            acnt = [0]
            for hp in range(nheads // 2):
                bufs = hp_loader(hp)
                o_t = ost[hp % 2]
                for hh in range(2):
                    h = hp * 2 + hh
                    for g in range(4):
                        nkb = 8 * (g + 1)
                        accb = 4 + (acnt[0] % 2)
                        acnt[0] += 1
                        pend = []
                        for kb in range(nkb):
                            sb_ = nb()
                            st_emit(bufs, h, hh, g, kb, sb_)
                            pi = ucnt[0] % 6
                            ucnt[0] += 1
                            pt = pts[pi]
                            P.act(lambda e, sb_=sb_, pt=pt: e.activation(out=pt[:], in_=ps[sb_][:, :], func=AF.Exp, scale=scale),
                                  r=[PK(sb_)], w=[("pt", pi)])
                            mk = mask_for(bufs, g, kb)
                            if mk is not None:
                                map_, mkey = mk
                                P.dve(lambda e, pt=pt, map_=map_: e.tensor_tensor(out=pt[:], in0=pt[:], in1=map_, op=ALU.mult),
                                      r=[("pt", pi), mkey], w=[("pt", pi)])
                            pend.append((kb, pt, pi))
                            if len(pend) > 2:
                                kb0, pt0, pi0 = pend.pop(0)
                                P.pe(lambda e, kb0=kb0, pt0=pt0, accb=accb: e.matmul(
                                    ps[accb][:, :], lhsT=bufs["v"][:, kb0, hh * 128:(hh + 1) * 128], rhs=pt0[:],
                                    start=(kb0 == 0), stop=(kb0 == nkb - 1)), r=[("pt", pi0), bufs["kv"]], w=[PK(accb)])
                        for kb0, pt0, pi0 in pend:
                            P.pe(lambda e, kb0=kb0, pt0=pt0, accb=accb: e.matmul(
                                ps[accb][:, :], lhsT=bufs["v"][:, kb0, hh * 128:(hh + 1) * 128], rhs=pt0[:],
                                start=(kb0 == 0), stop=(kb0 == nkb - 1)), r=[("pt", pi0), bufs["kv"]], w=[PK(accb)])
                        rd = rdn[acnt[0] % 2]
                        krd = ("rdn", acnt[0] % 2)
                        nlo, dlo = (0, 64) if hh == 0 else (64, 0)
                        P.dve(lambda e, rd=rd, accb=accb, dlo=dlo: e.reciprocal(out=rd[dlo:dlo + 64, :], in_=ps[accb][dlo:dlo + 64, :]),
                              r=[PK(accb)], w=[krd])
                        P.dve(lambda e, rd=rd, accb=accb, dlo=dlo, nlo=nlo, g=g, o_t=o_t: e.tensor_tensor(
                            out=o_t[nlo:nlo + 64, g * 512:(g + 1) * 512], in0=ps[accb][nlo:nlo + 64, :],
                            in1=rd[dlo:dlo + 64, :], op=ALU.mult), r=[PK(accb), krd], w=[("ost", hp % 2)])
                P.dma(out_d[hp], o_t[:], r=[("ost", hp % 2)], w=[(okey, hp)])

        mres = P.sb("mres", [128, 80, 512], BF16)
        goff = [0, 8, 24, 48]
        for g in range(4):
            nkb = 8 * (g + 1)
            P.dma(mres[:, goff[g]:goff[g] + nkb, :], maskT_d[g, :, 0:nkb, :], r=[("maskT_d", g)], w=[("mres", g)])
        kb2 = [P.sb(f"kb2_{i}", [128, T], BF16) for i in range(2)]
        qb2 = [P.sb(f"qb2_{i}", [128, NO], BF16) for i in range(2)]
        vb2 = [P.sb(f"vb2_{i}", [128, 32, 256], BF16) for i in range(2)]

        def hp_loader0(hp):
            i = hp % 2
            key = ("kvq0", i)
            P.dma(kb2[i][:], kT_d[hp], r=[("kT_d", hp, t_) for t_ in range(8)], w=[key])
            P.dma(qb2[i][:], qT_d[hp], r=[("qT_d", hp, t_) for t_ in range(4)], w=[key])
            P.dma(vb2[i][:], vaug_d[:, :, hp * 256:(hp + 1) * 256].rearrange("k p c -> p k c"),
                  r=[("vaug_d", k_) for k_ in range(32)], w=[key])
            return {"k": kb2[i], "q": qb2[i], "v": vb2[i], "kv": key}

        def st_emit0(bufs, h, hh, g, kb, sb_):
            base = hh * 64
            P.pe(lambda e: e.matmul(ps[sb_][:, :], lhsT=bufs["k"][base:base + 64, kb * 128:(kb + 1) * 128],
                                    rhs=bufs["q"][base:base + 64, g * 512:(g + 1) * 512], start=True, stop=True),
                 r=[bufs["kv"]], w=[PK(sb_)])

        def mask_for0(bufs, g, kb):
            return mres[:, goff[g] + kb, :], ("mres", g)

        attention(8, hp_loader0, st_emit0, mask_for0, 0.125, attnT_d, "attnT_d")
        P.barrier()
        P.release(base_mark)

        if stop("A0"):
            return nc
        def out_phase(in_chunks, wd, layer, x_src, xkey_src, x_dst, xkey_dst):
            wo = P.sb("wo", [128, 8, D], BF16)
            load_w(wo[:], wd.rearrange("(c p) n -> p c n", p=128), "wo")
            ain = P.sb("ain", [128, 8, NO], BF16)
            for c, (ap_, rk) in enumerate(in_chunks):
                P.dma(ain[:, c, :], ap_, r=rk, w=[("ain", c)])
            xo2 = [P.sb(f"xo{i}", [128, 8, 512], F32) for i in range(2)]
            mx = P.sb("mx", [128, 8, 512], F32)
            sq_ = P.sb("sqo", [128, 8, 512], BF16)
            rs = P.sb("rso", [128, 512], F32)
            xv = x_src.rearrange("(c p) t -> p c t", p=128)
            xdv = x_dst.rearrange("(c p) t -> p c t", p=128)
            for tc in range(4):
                xo = xo2[tc % 2]
                kxo = ("xo", tc % 2)
                P.dma(xo[:], xv[:, :, tc * 512:(tc + 1) * 512], r=[(xkey_src, tc)], w=[kxo])
                for oc in range(8):
                    b = nb()
                    for kc in range(8):
                        P.pe(lambda e, kc=kc, oc=oc, b=b: e.matmul(ps[b][:, :], lhsT=wo[:, kc, oc * 128:(oc + 1) * 128],
                                                                   rhs=ain[:, kc, tc * 512:(tc + 1) * 512],
                                                                   start=(kc == 0), stop=(kc == 7)),
                             r=["wo", ("ain", kc)], w=[PK(b)])
                    P.act(lambda e, b=b, oc=oc: e.activation(out=mx[:, oc, :], in_=ps[b][:, :], func=AF.Copy),
                          r=[PK(b)], w=[("mx", oc)])
                    P.dve(lambda e, b=b, oc=oc: e.tensor_tensor(out=sq_[:, oc, :], in0=mx[:, oc, :], in1=mx[:, oc, :], op=ALU.mult),
                          r=[("mx", oc)], w=[("sqo", oc)])
                for c in range(8):
                    P.pe(lambda e, c=c: e.matmul(ps[7][:, :], lhsT=ones_b[:], rhs=sq_[:, c, :], start=(c == 0), stop=(c == 7)),
                         r=[("sqo", c), "ones_b"], w=[PK(7)])
                P.act(lambda e: e.activation(out=rs[:], in_=ps[7][:, :], func=AF.Sqrt, bias=EPS, scale=1.0), r=[PK(7)], w=["rso"])
                P.dve(lambda e: e.reciprocal(out=rs[:], in_=rs[:]), r=["rso"], w=["rso"])
                for c in range(8):
                    gc_ = gcol(layer, 1) + c
                    P.dve(lambda e, c=c, gc_=gc_: e.scalar_tensor_tensor(out=mx[:, c, :], in0=mx[:, c, :], scalar=cst[:, gc_:gc_ + 1],
                                                                          in1=rs[:], op0=ALU.mult, op1=ALU.mult),
                           r=[("mx", c), "rso", "cst"], w=[("mx", c)])
                    P.dve(lambda e, c=c, xo=xo: e.tensor_tensor(out=xo[:, c, :], in0=xo[:, c, :], in1=mx[:, c, :], op=ALU.add),
                          r=[("mx", c), kxo], w=[kxo])
                P.dma(xdv[:, :, tc * 512:(tc + 1) * 512], xo[:], r=[kxo], w=[(xkey_dst, tc)])

        in0 = [(attnT_d[c], [("attnT_d", c)]) for c in range(4)] + [(convT_d[c], [("convT_d", c)]) for c in range(4)]
        out_phase(in0, wout_d, 0, xT_own[s_, :, 0:NO], "xown", x1T_d, "x1T_d")
        P.barrier()
        P.release(base_mark)

        def ffn_phase(layer, x_src, xkey_src, x_dst, xkey_dst):
            w1s = P.sb("w1s", [128, 8, 4096], BF16)
            w2s = P.sb("w2s", [128, 32, D], BF16)
            w1v = w1_d[layer].rearrange("(c p) n -> p c n", p=128)
            w2v = w2_d[layer].rearrange("(c p) n -> p c n", p=128)
            for c in range(8):
                P.dma(w1s[:, c, :], w1v[:, c, :], w=[("w1s", c)], q="pool")
            for c in range(0, 32, 4):
                P.dma(w2s[:, c:c + 4, :], w2v[:, c:c + 4, :], w=[("w2s", c // 4)], q="pool")
            xf = P.sb("xf", [128, 8, 512], F32)
            off3 = P.mark()
            yb = P.sb("yb", [128, 8, 512], F32)
            end3 = P.mark()
            P.release(off3)
            sq_f = P.sb("sqf", [128, 8, 512], BF16)
            hf = P.sb("hf", [128, 8, 512], BF16)
            assert P.mark() == end3
            h1 = P.sb("h1", [128, 32, 512], BF16)
            sqy = [P.sb(f"sqy{i}", [128, 512], BF16) for i in range(2)]
            rt = [P.sb(f"rt{i}", [128, 512], F32) for i in range(2)]
            alias_keys = ["sqf"] + [("hf", c) for c in range(8)]
            rsf = P.sb("rsf", [128, 512], F32)
            xv = x_src.rearrange("(c p) t -> p c t", p=128)
            xdv = x_dst.rearrange("(c p) t -> p c t", p=128)
            rc = [0]
            for tc in range(4):
                P.dma(xf[:], xv[:, :, tc * 512:(tc + 1) * 512], r=[(xkey_src, tc)], w=["xf"])
                P.act(lambda e: e.activation(out=rsf[:, 0:1], in_=rsf[:, 0:1], func=AF.Copy), r=["rsf"],
                      w=alias_keys + ["rsf"] + [("yb", c) for c in range(8)])
                rmsnorm_fm(xf, "xf", 8, 512, gcol(layer, 2), hf, "hf", ones_b, sq_f, "sqf", rsf, "rsf", 7)
                for oc in range(32):
                    b = nb()
                    for kc in range(8):
                        P.pe(lambda e, kc=kc, oc=oc, b=b: e.matmul(ps[b][:, :], lhsT=w1s[:, kc, oc * 128:(oc + 1) * 128],
                                                                   rhs=hf[:, kc, :], start=(kc == 0), stop=(kc == 7)),
                             r=[("w1s", kc), ("hf", kc)], w=[PK(b)])
                    ri = rc[0] % 2
                    rc[0] += 1
                    P.act(lambda e, b=b, ri=ri: e.activation(out=rt[ri][:], in_=ps[b][:, :], func=AF.Relu), r=[PK(b)], w=[("rt", ri)])
                    eng = P.dve if oc % 2 == 0 else P.pool
                    eng(lambda e, ri=ri, oc=oc: e.tensor_tensor(out=h1[:, oc, :], in0=rt[ri][:], in1=rt[ri][:], op=ALU.mult),
                        r=[("rt", ri)], w=[("h1", oc)])
                for oc in range(8):
                    b = nb()
                    for kc in range(32):
                        P.pe(lambda e, kc=kc, oc=oc, b=b: e.matmul(ps[b][:, :], lhsT=w2s[:, kc, oc * 128:(oc + 1) * 128],
                                                                   rhs=h1[:, kc, :], start=(kc == 0), stop=(kc == 31)),
                             r=[("w2s", kc // 4), ("h1", kc)], w=[PK(b)])
                    P.act(lambda e, b=b, oc=oc: e.activation(out=yb[:, oc, :], in_=ps[b][:, :], func=AF.Copy), r=[PK(b)],
                          w=[("yb", oc)] + (alias_keys if oc == 0 else []))
                    P.dve(lambda e, oc=oc: e.tensor_tensor(out=sqy[oc % 2][:], in0=yb[:, oc, :], in1=yb[:, oc, :], op=ALU.mult),
                          r=[("yb", oc)], w=[("sqy", oc % 2)])
                    P.pe(lambda e, oc=oc: e.matmul(ps[7][:, :], lhsT=ones_b[:], rhs=sqy[oc % 2][:], start=(oc == 0), stop=(oc == 7)),
                         r=[("sqy", oc % 2), "ones_b"], w=[PK(7)])
                P.act(lambda e: e.activation(out=rsf[:], in_=ps[7][:, :], func=AF.Sqrt, bias=EPS, scale=1.0), r=[PK(7)], w=["rsf"])
                P.dve(lambda e: e.reciprocal(out=rsf[:], in_=rsf[:]), r=["rsf"], w=["rsf"])
                for c in range(8):
                    gc_ = gcol(layer, 3) + c
                    P.dve(lambda e, c=c, gc_=gc_: e.scalar_tensor_tensor(out=yb[:, c, :], in0=yb[:, c, :], scalar=cst[:, gc_:gc_ + 1],
                                                                          in1=rsf[:], op0=ALU.mult, op1=ALU.mult),
                           r=[("yb", c), "rsf", "cst"], w=[("yb", c)])
                    P.dve(lambda e, c=c: e.tensor_tensor(out=yb[:, c, :], in0=yb[:, c, :], in1=xf[:, c, :], op=ALU.add),
                          r=[("yb", c), "xf"], w=[("yb", c)])
                P.dma(xdv[:, :, tc * 512:(tc + 1) * 512], yb[:], r=[("yb", c) for c in range(8)], w=[(xkey_dst, tc)])

        if stop("O0"):
            return nc
        ffn_phase(0, x1T_d, "x1T_d", x2T_d[s_], ("x2T_d", s_))
        P.barrier()
        P.release(base_mark)

    if stop("F0"):
        return nc
    tabs = [rope_tables(pos_own[s1], NO, C_FR1, C_SG1, f"r1o{s1}") for s1 in range(2)]
    C1S1 = [None, None]
    m1 = P.mark()
    wdq = P.sb("wdq", [128, 8, 384], BF16)
    wuq = P.sb("wuq", [128, 3, 2048], BF16)
    wdkv = P.sb("wdkv", [128, 8, 320], BF16)
    load_w(wdq[:], wdq_d.rearrange("(c p) n -> p c n", p=128), "wdq")
    load_w(wuq[:], wuq_d.rearrange("(c p) n -> p c n", p=128), "wuq")
    load_w(wdkv[:], wdkv_d.rearrange("(c p) n -> p c n", p=128), "wdkv")
    xs2 = [P.sb(f"xs{i}", [128, 8, 512], F32) for i in range(2)]
    sq = P.sb("sq", [128, 8, 512], BF16)
    hb = P.sb("hb", [128, 8, 512], BF16)
    rstd = P.sb("rstd", [128, 512], F32)
    t1 = [P.sb(f"t1_{i}", [128, 512], F32) for i in range(2)]
    t2 = [P.sb(f"t2_{i}", [128, 512], F32) for i in range(2)]
    cq = P.sb("cq", [128, 3, 512], F32)
    cqn = P.sb("cqn", [128, 3, 512], BF16)
    ckv = P.sb("ckv", [128, 2, 512], F32)
    ckvn = P.sb("ckvn", [128, 2, 512], F32)
    krs = P.sb("krs", [32, 512], F32)
    qst = [P.sb(f"qst1_{i}", [128, 16, 512], BF16) for i in range(2)]
    sq3 = P.sb("sq3", [128, 3, 512], BF16)
    rs3 = P.sb("rs3", [128, 512], F32)

    def rope_evac1(bA, bB, m, col0, dst, kdst, eng_out="pool"):
        Cc, Ss = C1S1[0], C1S1[1]
        i = tcnt[0] % 2
        tcnt[0] += 1
        P.dve(lambda e: e.tensor_tensor(out=t1[i][0:m, :], in0=ps[bA][0:m, :], in1=Cc[0:m, col0:col0 + 512], op=ALU.mult),
              r=[PK(bA), "r1o0C", "r1o1C"], w=[("t1", i)])
        P.dve(lambda e: e.tensor_tensor(out=t2[i][0:m, :], in0=ps[bB][0:m, :], in1=Ss[0:m, col0:col0 + 512], op=ALU.mult),
              r=[PK(bB), "r1o0S", "r1o1S"], w=[("t2", i)])
        P.pool(lambda e: e.tensor_tensor(out=dst, in0=t1[i][0:m, :], in1=t2[i][0:m, :], op=ALU.add),
               r=[("t1", i), ("t2", i)], w=[kdst])

    for s1 in range(2):
        C1S1[0], C1S1[1] = tabs[s1]
        x2v = x2T_d[s1].rearrange("(c p) t -> p c t", p=128)
        for tc in range(4):
            xs = xs2[tc % 2]
            kx = ("xs", tc % 2)
            P.dma(xs[:], x2v[:, :, tc * 512:(tc + 1) * 512], r=[(("x2T_d", s1), tc)], w=[kx])
            rmsnorm_fm(xs, kx, 8, 512, gcol(1, 0), hb, "hb", ones_b, sq, "sq", rstd, "rstd", 7)
            if s1 == 0:
                for oc in range(3):
                    b = nb()
                    proj_fm(wdq, "wdq", oc * 128, hb, "hb", 8, 512, b)
                    P.act(lambda e, b=b, oc=oc: e.activation(out=cq[:, oc, :], in_=ps[b][:, :], func=AF.Copy), r=[PK(b)], w=["cq"])
                P.act(lambda e: e.activation(out=sq3[:], in_=cq[:], func=AF.Square), r=["cq"], w=["sq3"])
                for c in range(3):
                    P.pe(lambda e, c=c: e.matmul(ps[7][:, :], lhsT=ones_q[:], rhs=sq3[:, c, :], start=(c == 0), stop=(c == 2)),
                         r=["sq3", "ones_q"], w=[PK(7)])
                P.act(lambda e: e.activation(out=rs3[:], in_=ps[7][:, :], func=AF.Sqrt, bias=EPS, scale=256.0 / 384.0), r=[PK(7)], w=["rs3"])
                P.dve(lambda e: e.reciprocal(out=rs3[:], in_=rs3[:]), r=["rs3"], w=["rs3"])
                for c in range(3):
                    P.dve(lambda e, c=c: e.scalar_tensor_tensor(out=cqn[:, c, :], in0=cq[:, c, :], scalar=cst[:, C_QN + c:C_QN + c + 1],
                                                                in1=rs3[:], op0=ALU.mult, op1=ALU.mult), r=["cq", "rs3", "cst"], w=[("cqn", c)])
                qs = qst[tc % 2]
                for oc in range(8):
                    b = nb()
                    proj_fm(wuq, "wuq", oc * 128, cqn, "cqn", 3, 512, b)
                    P.act(lambda e, b=b, oc=oc: e.activation(out=qs[:, oc, :], in_=ps[b][:, :], func=AF.Copy), r=[PK(b)], w=[("qst", tc % 2, oc)])
                    P.dma(qnT_d[oc, :, tc * 512:(tc + 1) * 512], qs[:, oc, :], r=[("qst", tc % 2, oc)], w=[("qnT_d", oc, tc)])
                for oc in range(8):
                    bA, bB = nb(), nb()
                    proj_fm(wuq, "wuq", 1024 + oc * 64, cqn, "cqn", 3, 512, bA, m=64)
                    proj_fm(wuq, "wuq", 1536 + oc * 64, cqn, "cqn", 3, 512, bB, m=64)
                    rope_evac1(bA, bB, 64, tc * 512, qs[0:64, 8 + oc, :], ("qst", tc % 2, 8 + oc))
                    P.dma(qrT_d[oc, :, tc * 512:(tc + 1) * 512], qs[0:64, 8 + oc, :], r=[("qst", tc % 2, 8 + oc)], w=[("qrT_d", oc, tc)])
            for oc in range(2):
                b = nb()
                proj_fm(wdkv, "wdkv", oc * 128, hb, "hb", 8, 512, b)
                P.act(lambda e, b=b, oc=oc: e.activation(out=ckv[:, oc, :], in_=ps[b][:, :], func=AF.Copy), r=[PK(b)], w=["ckv"])
            P.act(lambda e: e.activation(out=sq3[:, 0:2, :], in_=ckv[:], func=AF.Square), r=["ckv"], w=["sq3"])
            for c in range(2):
                P.pe(lambda e, c=c: e.matmul(ps[7][:, :], lhsT=ones_q[:], rhs=sq3[:, c, :], start=(c == 0), stop=(c == 1)),
                     r=["sq3", "ones_q"], w=[PK(7)])
            P.act(lambda e: e.activation(out=rs3[:], in_=ps[7][:, :], func=AF.Sqrt, bias=EPS, scale=1.0), r=[PK(7)], w=["rs3"])
            P.dve(lambda e: e.reciprocal(out=rs3[:], in_=rs3[:]), r=["rs3"], w=["rs3"])
            for c in range(2):
                P.dve(lambda e, c=c: e.scalar_tensor_tensor(out=ckvn[:, c, :], in0=ckv[:, c, :], scalar=cst[:, C_KVN + c:C_KVN + c + 1],
                                                            in1=rs3[:], op0=ALU.mult, op1=ALU.mult), r=["ckv", "rs3", "cst"], w=[("ckvn", c)])
                P.dma(kva_sets_d[s1, c * 128:(c + 1) * 128, tc * 512:(tc + 1) * 512], ckvn[:, c, :], r=[("ckvn", c)], w=[("kva", s1, tc, c)])
            bA, bB = nb(), nb()
            proj_fm(wdkv, "wdkv", 256, hb, "hb", 8, 512, bA, m=32)
            proj_fm(wdkv, "wdkv", 288, hb, "hb", 8, 512, bB, m=32)
            rope_evac1(bA, bB, 32, tc * 512, krs[:, :], "krs")
            P.dma(kva_sets_d[s1, 256:288, tc * 512:(tc + 1) * 512], krs[:, :], r=["krs"], w=[("kva", s1, tc, 2)])
    P.barrier()
    P.release(base_mark)

    if stop("Q1"):
        return nc
    wkk = P.sb("wkk", [128, 2, 1024], BF16)
    wkv = P.sb("wkv", [128, 2, 1024], BF16)
    load_w(wkk[:], wukvk_d.rearrange("(c p) n -> p c n", p=128), "wkk")
    load_w(wkv[:], wukvv_d.rearrange("(c p) n -> p c n", p=128), "wkv")
    ckf = [P.sb(f"ckf{i}", [128, 2, 512], F32) for i in range(2)]
    ckb = [P.sb(f"ckb{i}", [128, 2, 512], BF16) for i in range(2)]
    kns = [P.sb(f"kns{i}", [128, 8, 512], BF16) for i in range(2)]
    vst1 = [P.sb(f"vst1_{i}", [128, 16, 128], BF16) for i in range(2)]
    kr4 = P.sb("kr4", [64, T], BF16)
    krf = P.sb("krf", [64, T], F32)
    for i in range(2):
        P.pool(lambda e, i=i: e.memset(vst1[i][:], 1.0), w=[("vst1", i)])
    for sl in range(32):
        s1, m_ = sl % 2, sl // 2
        for rep_ in range(2):
            P.dma(krf[rep_ * 32:(rep_ + 1) * 32, sl * 128:(sl + 1) * 128],
                  kva_sets_d[s1, 256:288, m_ * 128:(m_ + 1) * 128], w=[("krf", sl // 8)])
    for q4 in range(4):
        P.dve(lambda e, q4=q4: e.tensor_copy(out=kr4[:, q4 * 1024:(q4 + 1) * 1024], in_=krf[:, q4 * 1024:(q4 + 1) * 1024]),
              r=[("krf", q4)], w=["kr4"])
    for tc in range(8):
        cf = ckf[tc % 2]
        cb_ = ckb[tc % 2]
        for tb in range(4):
            sl = tc * 4 + tb
            s1, m_ = sl % 2, sl // 2
            for c in range(2):
                P.dma(cf[:, c, tb * 128:(tb + 1) * 128],
                      kva_sets_d[s1, c * 128:(c + 1) * 128, m_ * 128:(m_ + 1) * 128], w=[("ckf", tc % 2)])
        P.dve(lambda e, cf=cf, cb_=cb_: e.tensor_copy(out=cb_[:], in_=cf[:]), r=[("ckf", tc % 2)], w=[(("ckb", tc % 2), 0), (("ckb", tc % 2), 1)])
        kn = kns[tc % 2]
        for oc in range(8):
            b = nb()
            proj_fm(wkk, "wkk", oc * 128, cb_, ("ckb", tc % 2), 2, 512, b)
            P.act(lambda e, b=b, oc=oc, kn=kn: e.activation(out=kn[:, oc, :], in_=ps[b][:, :], func=AF.Copy), r=[PK(b)], w=[("kns", tc % 2, oc)])
            P.dma(knT_d[oc, :, tc * 512:(tc + 1) * 512], kn[:, oc, :], r=[("kns", tc % 2, oc)], w=[("knT_d", oc, tc)])
        for tb in range(4):
            kb = tc * 4 + tb
            vs = vst1[kb % 2]
            vv = vs[:].rearrange("p (h two) d -> p h two d", two=2)
            for half in range(2):
                b = nb()
                for kc in range(2):
                    P.pe(lambda e, kc=kc, tb=tb, b=b, half=half, cb_=cb_: e.matmul(
                        ps[b][:, :], lhsT=cb_[:, kc, tb * 128:(tb + 1) * 128], rhs=wkv[:, kc, half * 512:(half + 1) * 512],
                        start=(kc == 0), stop=(kc == 1)), r=["wkv", (("ckb", tc % 2), kc)], w=[PK(b)])
                pv = ps[b][:, :].rearrange("p (h two d) -> p h two d", two=2, d=64)
                P.act(lambda e, pv=pv, vv=vv, half=half: e.activation(out=vv[:, half * 4:(half + 1) * 4, 0, 0:64], in_=pv[:, :, 0, :], func=AF.Copy),
                      r=[PK(b)], w=[("vst1", kb % 2)])
                P.act(lambda e, pv=pv, vv=vv, half=half: e.activation(out=vv[:, half * 4:(half + 1) * 4, 1, 64:128], in_=pv[:, :, 1, :], func=AF.Copy),
                      r=[PK(b)], w=[("vst1", kb % 2)])
            P.dma(vaug1_d[kb], vs[:].rearrange("p h d -> p (h d)"), r=[("vst1", kb % 2)], w=[("vaug1_d", kb)])
    P.dma(kr4_d, kr4[:], r=["kr4"], w=["kr4_d"])
    P.barrier()
    P.release(base_mark)

    if stop("K1"):
        return nc
    mTs1 = P.sb("mTs1", [128, 2, 8, 512], BF16)
    for i in range(2):
        P.dma(mTs1[:, i], mT_d[i], w=["mTs1"])
    kh2 = [[P.sb(f"kh_{i}_{hh}", [96, T], BF16) for hh in range(2)] for i in range(2)]
    qh2 = [[P.sb(f"qh_{i}_{hh}", [96, NO], BF16) for hh in range(2)] for i in range(2)]
    vb2 = [P.sb(f"vb21_{i}", [128, 32, 256], BF16) for i in range(2)]

    def hp_loader1(hp):
        i = hp % 2
        key = ("kvq1", i)
        for hh in range(2):
            P.dma(kh2[i][hh][0:64, :], knT_d[hp, hh * 64:(hh + 1) * 64, :], r=[("knT_d", hp, t_) for t_ in range(8)], w=[key])
            P.dma(kh2[i][hh][64:96, :], kr4_d[0:32, :], r=["kr4_d"], w=[key])
            P.dma(qh2[i][hh][0:64, :], qnT_d[hp, hh * 64:(hh + 1) * 64, :], r=[("qnT_d", hp, t_) for t_ in range(4)], w=[key])
            P.dma(qh2[i][hh][64:96, :], qrT_d[hp, hh * 32:(hh + 1) * 32, :], r=[("qrT_d", hp, t_) for t_ in range(4)], w=[key])
        P.dma(vb2[i][:], vaug1_d[:, :, hp * 256:(hp + 1) * 256].rearrange("k p c -> p k c"),
              r=[("vaug1_d", k_) for k_ in range(32)], w=[key])
        return {"k": kh2[i], "q": qh2[i], "v": vb2[i], "kv": key}

    def st_emit1(bufs, h, hh, g, kb, sb_):
        P.pe(lambda e: e.matmul(ps[sb_][:, :], lhsT=bufs["k"][hh][0:96, kb * 128:(kb + 1) * 128],
                                rhs=bufs["q"][hh][0:96, g * 512:(g + 1) * 512], start=True, stop=True),
             r=[bufs["kv"]], w=[PK(sb_)])

    def mask_for1(bufs, g, kb):
        rel = kb - 8 * g
        if rel < 0:
            return None
        return mTs1[:, g // 2, rel, :], "mTs1"

    attention(16, hp_loader1, st_emit1, mask_for1, float(96 ** -0.5), attnT_d, "attnT1_d")
    P.barrier()
    P.release(base_mark)

    if stop("A1"):
        return nc
    in1 = [(attnT_d[c], [("attnT1_d", c)]) for c in range(8)]
    out_phase(in1, wo_d, 1, x2T_d[0], ("x2T_d", 0), x3T_d, "x3T_d")
    P.barrier()
    P.release(base_mark)
    ffn_phase(1, x3T_d, "x3T_d", outT, "outT")
    P.final_wait([("outT", tc) for tc in range(4)])
    P.emit()
    return nc


def _swap_cols(w, head, a, b_):
    n = w.shape[1]
    idx = np.arange(n).reshape(-1, head)
    perm = np.concatenate([idx[:, a:b_], idx[:, 0:a], idx[:, b_:]], axis=1).reshape(-1)
    return w[:, perm]


def prepare_inputs(inp, n_batch=4):
    f32 = np.float32
    x = np.asarray(inp["x"], f32)
    pos = np.asarray(inp["positions"]).astype(np.int32)
    w_in = np.asarray(inp["even_w_in"], f32)[0]
    offs = np.cumsum([0, 512, 512, 512, 1024, 64, 16, 512, 512, 512])
    wq_, wk_, wv_, wqi, wki, wwi, wgb, wgc, wxi = [w_in[:, offs[i]:offs[i + 1]] for i in range(9)]
    w0k = np.concatenate([wk_, _swap_cols(wk_, 64, 8, 16), wki, wki, _swap_cols(wki, 64, 8, 16), _swap_cols(wki, 64, 8, 16)], axis=1)
    w0q = np.concatenate([wq_, _swap_cols(wq_, 64, 8, 16), wqi, _swap_cols(wqi, 64, 8, 16), wgb, wgc, wxi], axis=1)
    w_uq = np.asarray(inp["odd_w_uq"], f32)[0]
    cols = np.arange(1536).reshape(16, 96)
    wuq_n = w_uq[:, cols[:, :64].reshape(-1)]
    wuq_r = w_uq[:, cols[:, 64:].reshape(-1)]
    wuq = np.concatenate([wuq_n, wuq_r, _swap_cols(wuq_r, 32, 16, 32)], axis=1)
    w_dkv = np.asarray(inp["odd_w_dkv"], f32)[0]
    wdkv = np.concatenate([w_dkv[:, :256], w_dkv[:, 256:], _swap_cols(w_dkv[:, 256:], 32, 16, 32)], axis=1)
    w_ukv = np.asarray(inp["odd_w_ukv"], f32)[0]
    c2 = np.arange(2048).reshape(16, 128)
    wukv_k = w_ukv[:, c2[:, :64].reshape(-1)]
    wukv_v = w_ukv[:, c2[:, 64:].reshape(-1)]

    cst = np.zeros((128, NCST), f32)
    kinds = ["norm_mix_pre", "norm_mix_post", "norm_ffn_pre", "norm_ffn_post"]
    for l in range(2):
        for k, nm in enumerate(kinds):
            cst[:, gcol(l, k):gcol(l, k) + 8] = np.asarray(inp[nm], f32)[l].reshape(8, 128).T
    cst[:, C_QN:C_QN + 3] = np.asarray(inp["odd_q_norm"], f32)[0].reshape(3, 128).T
    cst[:, C_KVN:C_KVN + 2] = np.asarray(inp["odd_kv_norm"], f32)[0].reshape(2, 128).T
    cw = np.asarray(inp["even_conv_w"], f32)[0]
    for j in range(3):
        cst[:, C_CW + j * 4:C_CW + j * 4 + 4] = cw[j].reshape(4, 128).T
    theta = 500000.0
    if0 = (theta ** (-np.arange(0, 16, 2, dtype=np.float32) / 16)).astype(f32)
    if1 = (theta ** (-np.arange(0, 32, 2, dtype=np.float32) / 32)).astype(f32)
    for p in range(128):
        r = p % 64
        if r < 16:
            cst[p, C_FR0] = if0[r % 8]
            cst[p, C_SG0] = -1.0 if r < 8 else 1.0
        r = p % 32
        cst[p, C_FR1] = if1[r % 16]
        cst[p, C_SG1] = -1.0 if r < 16 else 1.0

    shared = {
        "cst": cst, "w0k": w0k, "w0v": np.ascontiguousarray(wv_), "w0q": w0q, "w0wi": np.ascontiguousarray(wwi),
        "w_out": np.asarray(inp["even_w_out"], f32)[0], "w1": np.asarray(inp["mlp_w1"], f32),
        "w2": np.asarray(inp["mlp_w2"], f32), "w_dq": np.asarray(inp["odd_w_dq"], f32)[0], "w_uq": wuq,
        "w_dkv": wdkv, "w_ukv_k": np.ascontiguousarray(wukv_k), "w_ukv_v": np.ascontiguousarray(wukv_v),
        "w_o": np.asarray(inp["odd_w_o"], f32)[0],
    }
    shared = {k: np.ascontiguousarray(v, dtype=f32) for k, v in shared.items()}
    in_maps = []
    own_idx_all = []
    qi = np.arange(128)
    s_ = np.arange(1024)
    for b in range(n_batch):
        for par in range(2):
            xb = x[b]
            sets = [blocks_for(par), blocks_for(1 - par)]
            xT_sets, pos_sets, cbs = [], [], []
            kq = np.zeros((128, 32), f32)
            for si, blks in enumerate(sets):
                idx = np.concatenate([np.arange(128 * p, 128 * p + 128) for p in blks])
                halo = np.zeros((32, D), f32)
                for i, p in enumerate(blks):
                    if p > 0:
                        halo[2 * i:2 * i + 2] = xb[128 * p - 2:128 * p]
                xT_sets.append(np.concatenate([xb[idx], halo], axis=0).T)
                pos_sets.append(pos[b][idx][None, :])
                cb = np.zeros((8, 128, 1024), f32)
                for i, p in enumerate(blks):
                    kq[:, si * 16 + i] = np.minimum(256, 128 * p + qi + 1)
                for g in range(4):
                    for j in range(4):
                        rel = blks[4 * g + j] % 8
                        vis = s_[None, :] <= (rel * 128 + qi)[:, None]
                        cb[(g // 2) * 4 + j] = np.where(vis, 0.0, -1e30)
                cbs.append(cb)
            own = np.concatenate([np.arange(128 * p, 128 * p + 128) for p in sets[0]])
            own_idx_all.append(own)
            mT = np.zeros((2, 128, 8, 512), f32)
            for g in (0, 2):
                for j in range(4):
                    pq = sets[0][4 * g + j]
                    qpos = 128 * pq + qi
                    for rel in range(8):
                        sl = 8 * g + rel
                        pk = sets[sl % 2][sl // 2]
                        kpos = 128 * pk + qi
                        mT[g // 2, :, rel, j * 128:(j + 1) * 128] = (kpos[:, None] <= qpos[None, :])
            m = dict(shared)
            m.update({
                "xT_seq": np.ascontiguousarray(xb.T), "xT_own": np.ascontiguousarray(np.stack(xT_sets)),
                "pos_seq": np.ascontiguousarray(pos[b][None, :]), "pos_own": np.ascontiguousarray(np.stack(pos_sets)),
                "kq": kq, "cb": np.stack(cbs), "mT": mT.astype(ml_dtypes.bfloat16),
            })
            in_maps.append(m)
    return in_maps, own_idx_all


_NC_CACHE = {}


def kernel(**inputs):
    in_maps, own_idx = prepare_inputs(inputs, 4)
    if 8 not in _NC_CACHE:
        _NC_CACHE[8] = build_program(8)
    nc = _NC_CACHE[8]
    res = run_bass_kernel_spmd(nc, in_maps, core_ids=list(range(8)))
    out = np.zeros((4, T, D), np.float32)
    for c in range(8):
        b = c // 2
        out[b, own_idx[c], :] = np.asarray(res.results[c]["outT"], np.float32).T
    return out
```

```python
import types
import numpy as np
import ml_dtypes
import concourse.bass as bass
import concourse.mybir as mybir
from concourse.bass_utils import run_bass_kernel_spmd

F32 = mybir.dt.float32
BF16 = mybir.dt.bfloat16
I32 = mybir.dt.int32
AF = mybir.ActivationFunctionType
ALU = mybir.AluOpType
AX = mybir.AxisListType
DT_SIZE = {F32: 4, BF16: 2, I32: 4}

T = 4096
NO = 2048
D = 1024
EPS = 1e-6
NBIS = 18
NFILL = 1
NBURST = 10
TWO_PI = float(2 * np.pi)


class Op:
    __slots__ = ("eng", "fn", "reads", "writes", "is_dma", "deps", "needed", "ordinal",
                 "dsem", "dval", "barrier")

    def __init__(self, eng, fn, reads, writes, is_dma):
        self.eng = eng
        self.fn = fn
        self.reads = reads
        self.writes = writes
        self.is_dma = is_dma
        self.deps = []
        self.needed = False
        self.ordinal = None
        self.dsem = None
        self.dval = None
        self.barrier = False


class Prog:
    ENGS = ("pe", "act", "dve", "pool", "sp")
    SB_LIMIT = 228352

    def __init__(self, nc, n_dma_sems=12):
        self.nc = nc
        self.ops = []
        self.sb_off = 16896
        self.sb_max = 0
        self.n_dma_sems = n_dma_sems
        self._uid = 0
        self._bank = 0

    def sb(self, name, shape, dtype):
        nbytes = int(np.prod(shape[1:])) * DT_SIZE[dtype]
        nbytes = (nbytes + 63) // 64 * 64
        self._uid += 1
        t = self.nc.alloc_sbuf_tensor_at(f"{name}_{self._uid}", list(shape), dtype, offset=self.sb_off)
        self.sb_off += nbytes
        self.sb_max = max(self.sb_max, self.sb_off)
        assert self.sb_off <= self.SB_LIMIT, f"SBUF overflow {self.sb_off} at {name}"
        return t

    def mark(self):
        return self.sb_off

    def release(self, m):
        self.sb_off = m

    @staticmethod
    def _freeze(fn):
        if getattr(fn, "__closure__", None) is None:
            return fn
        cells = []
        for c in fn.__closure__:
            try:
                cells.append(types.CellType(c.cell_contents))
            except ValueError:
                cells.append(c)
        return types.FunctionType(fn.__code__, fn.__globals__, fn.__name__, fn.__defaults__, tuple(cells))

    def add(self, eng, fn, r=(), w=(), dma=False):
        fn = self._freeze(fn)
        o = Op(eng, fn, tuple(r), tuple(w), dma)
        self.ops.append(o)
        return o

    def pe(self, fn, r=(), w=()):
        return self.add("pe", fn, r, w)

    def act(self, fn, r=(), w=()):
        return self.add("act", fn, r, w)

    def dve(self, fn, r=(), w=()):
        return self.add("dve", fn, r, w)

    def pool(self, fn, r=(), w=()):
        return self.add("pool", fn, r, w)

    def dma(self, out, in_, r=(), w=(), q="sp", **kw):
        return self.add(q, lambda e: e.dma_start(out=out, in_=in_, **kw), r, w, dma=True)

    def final_wait(self, keys):
        return self.add("sp", lambda e: e.nop(), r=keys, w=())

    def barrier(self):
        o = Op(None, None, (), (), False)
        o.barrier = True
        self.ops.append(o)

    def finalize(self):
        last_w = {}
        readers = {}
        since_barrier = []
        pending_barrier = None
        seen_after = set()
        for o in self.ops:
            if o.barrier:
                summ = []
                lastc = {}
                for p in since_barrier:
                    if p.is_dma:
                        summ.append(p)
                    else:
                        lastc[p.eng] = p
                summ.extend(lastc.values())
                if pending_barrier is not None:
                    summ.extend(pending_barrier)
                pending_barrier = summ
                seen_after = set()
                since_barrier = []
                continue
            deps = []
            if pending_barrier is not None and o.eng not in seen_after:
                deps.extend(pending_barrier)
                seen_after.add(o.eng)
            for k in o.reads:
                if k in last_w:
                    deps.append(last_w[k])
            for k in o.writes:
                if k in last_w:
                    deps.append(last_w[k])
                deps.extend(readers.get(k, ()))
            for k in o.reads:
                readers.setdefault(k, []).append(o)
            for k in o.writes:
                last_w[k] = o
                readers[k] = []
            dd = []
            seen = set()
            for d in deps:
                if d is o or id(d) in seen:
                    continue
                seen.add(id(d))
                if (not d.is_dma) and (not o.is_dma) and d.eng == "pe" and o.eng == "pe":
                    continue
                dd.append(d)
            o.deps = dd
            for d in dd:
                d.needed = True
            since_barrier.append(o)
        cnt = {e: 0 for e in self.ENGS}
        dma_rr = {e: 0 for e in self.ENGS}
        dma_uses = {}
        for o in self.ops:
            if o.barrier:
                continue
            if o.is_dma:
                slot = dma_rr[o.eng] % self.n_dma_sems
                dma_rr[o.eng] += 1
                key = (o.eng, slot)
                dma_uses[key] = dma_uses.get(key, 0) + 1
                o.dsem = key
                o.dval = 16 * dma_uses[key]
            elif o.needed:
                cnt[o.eng] += 1
                o.ordinal = cnt[o.eng]
        self.max_ord = dict(cnt)

    def emit(self):
        nc = self.nc
        self.finalize()
        from contextlib import ExitStack
        es = ExitStack()
        sems = {}
        for e in ("pe", "act", "dve", "pool", "sp"):
            sems[e] = es.enter_context(nc.semaphore(f"c_{e}"))
        dsems = {}
        used = sorted({o.dsem for o in self.ops if (not o.barrier) and o.is_dma})
        for key in used:
            dsems[key] = es.enter_context(nc.semaphore(f"d_{key[0]}_{key[1]}"))
        block = es.enter_context(nc.Block())
        per_eng = {e: [o for o in self.ops if (not o.barrier) and o.eng == e] for e in self.ENGS}

        def body(ename, engine):
            known = {}
            for o in per_eng[ename]:
                waits = {}
                for d in o.deps:
                    if d.is_dma:
                        s, v, k = dsems[d.dsem], d.dval, ("d",) + d.dsem
                    else:
                        s, v, k = sems[d.eng], d.ordinal, ("c", d.eng)
                    if v > waits.get(k, (None, 0))[1]:
                        waits[k] = (s, v)
                if o.is_dma and o.dval > 16:
                    k = ("d",) + o.dsem
                    v = o.dval - 16
                    if v > waits.get(k, (None, 0))[1]:
                        waits[k] = (dsems[o.dsem], v)
                for k, (s, v) in waits.items():
                    if known.get(k, 0) >= v:
                        continue
                    engine.wait_ge(s, v)
                    known[k] = v
                ins = o.fn(engine)
                if o.is_dma:
                    ins.then_inc(dsems[o.dsem], 16)
                elif o.needed:
                    ins.then_inc(sems[ename], 1)

        @block.tensor
        def _(e):
            body("pe", e)

        @block.scalar
        def _(e):
            body("act", e)

        @block.vector
        def _(e):
            body("dve", e)

        @block.gpsimd
        def _(e):
            body("pool", e)

        @block.sync
        def _(e):
            body("sp", e)

        es.close()


def blocks_for(par):
    lo = list(range(par, 16, 2))
    hi = sorted(31 - j for j in lo)
    return lo + hi


C_G = 0
C_QN = 64
C_KVN = 67
C_CW = 69
C_FR0 = 81
C_SG0 = 82
C_FR1 = 83
C_SG1 = 84
NCST = 96


def gcol(layer, kind):
    return C_G + (layer * 4 + kind) * 8


def build_program(n_cores, dbg=(), no_cc=False, stop_after=None):
    nc = bass.Bass("TRN2", target_bir_lowering=False)
    P = Prog(nc)

    def stop(name):
        if stop_after == name:
            P.barrier()
            P.final_wait([])
            P.emit()
            return True
        return False

    def din(name, shape, dt=F32):
        return nc.dram_tensor(name, list(shape), dt, kind="ExternalInput").ap()

    def dscr(name, shape, dt):
        kind = "ExternalOutput" if name in dbg else "Internal"
        return nc.dram_tensor(name, list(shape), dt, kind=kind).ap()

    xT_seq = din("xT_seq", [D, T])
    xT_own = din("xT_own", [2, D, NO + 32])
    pos_seq = din("pos_seq", [1, T], I32)
    pos_own = din("pos_own", [2, 1, NO], I32)
    cst_d = din("cst", [128, NCST])
    kq_d = din("kq", [128, 32])
    cb_d = din("cb", [2, 8, 128, 1024])
    mT_d = din("mT", [2, 128, 8, 512], BF16)
    w0k_d = din("w0k", [D, 1280])
    w0v_d = din("w0v", [D, 512])
    w0q_d = din("w0q", [D, 4608])
    w0wi_d = din("w0wi", [D, 16])
    wout_d = din("w_out", [D, D])
    w1_d = din("w1", [2, D, 4096])
    w2_d = din("w2", [2, 4096, D])
    wdq_d = din("w_dq", [D, 384])
    wuq_d = din("w_uq", [384, 2048])
    wdkv_d = din("w_dkv", [D, 320])
    wukvk_d = din("w_ukv_k", [256, 1024])
    wukvv_d = din("w_ukv_v", [256, 1024])
    wo_d = din("w_o", [D, D])
    outT = nc.dram_tensor("outT", [D, NO], F32, kind="ExternalOutput").ap()

    kT_d = dscr("kT_d", [4, 128, T], BF16)
    kidxT_d = dscr("kidxT_d", [128, T], BF16)
    vaug_d = dscr("vaug_d", [32, 128, 1024], BF16)
    qT_d = dscr("qT_d", [4, 128, NO], BF16)
    qidxT_d = dscr("qidxT_d", [8, 128, NO], BF16)
    convT_d = dscr("convT_d", [4, 128, NO], BF16)
    maskT_d = dscr("maskT_d", [4, 128, 32, 512], BF16)
    attnT_d = dscr("attnT_d", [8, 128, NO], BF16)
    x1T_d = dscr("x1T_d", [D, NO], F32)
    x2T_d = dscr("x2T_d", [2, D, NO], F32)
    x3T_d = dscr("x3T_d", [D, NO], F32)
    kva_sets_d = dscr("kva_sets_d", [2, 288, NO], F32)
    qnT_d = dscr("qnT_d", [8, 128, NO], BF16)
    qrT_d = dscr("qrT_d", [8, 64, NO], BF16)
    knT_d = dscr("knT_d", [8, 128, T], BF16)
    vaug1_d = dscr("vaug1_d", [32, 128, 2048], BF16)
    dbg_d = dscr("dbg_d", [128, 4096], F32)
    kr4_d = dscr("kr4_d", [64, T], BF16)

    ps = [nc.alloc_psum_tensor(f"ps{i}", [128, 512], F32) for i in range(8)]

    def PK(i):
        return ("ps", i)

    cst = P.sb("cst", [128, NCST], F32)
    kq = P.sb("kq", [128, 32], F32)
    widx = P.sb("widx", [128, 16, 16], F32)
    ones_b = P.sb("ones_b", [128, 128], BF16)
    ones_q = P.sb("ones_q", [128, 128], BF16)
    ident = P.sb("ident", [128, 128], F32)
    P.dma(cst[:], cst_d, w=["cst"])
    P.dma(kq[:], kq_d, w=["kq"])
    P.pool(lambda e: e.memset(ones_b[:], 1.0 / 1024), w=["ones_b"])
    P.pool(lambda e: e.memset(ones_q[:], 1.0 / 256), w=["ones_q"])
    P.pool(lambda e: e.memset(ident[:], 1.0), w=["ident"])
    P.pool(lambda e: e.affine_select(out=ident[:], in_=ident[:], pattern=[[-1, 128]], compare_op=ALU.is_equal,
                                     fill=0.0, base=0, channel_multiplier=1), r=["ident"], w=["ident"])
    base_mark = P.mark()

    uid = [0]

    def U(s):
        uid[0] += 1
        return f"{s}#{uid[0]}"

    def rope_tables(pos_d, n, fr_col, sg_col, tag):
        C = P.sb(tag + "C", [128, n], F32)
        S = P.sb(tag + "S", [128, n], F32)
        m = P.mark()
        pi_ = P.sb("posi", [128, n], I32)
        pf = P.sb("posf", [128, n], F32)
        tmp = P.sb("rtmp", [128, n], F32)
        ki = P.sb("rki", [128, n], I32)
        kpi, kpf, kt, kk = U("posi"), U("posf"), U("rtmp"), U("rki")
        kC, kS = tag + "C", tag + "S"
        P.dma(pi_[:], pos_d.to_broadcast([128, n]), w=[kpi])
        P.dve(lambda e: e.tensor_copy(out=pf[:], in_=pi_[:]), r=[kpi], w=[kpf])
        P.dve(lambda e: e.tensor_scalar(out=pf[:], in0=pf[:], scalar1=cst[:, fr_col:fr_col + 1], scalar2=None,
                                        op0=ALU.mult), r=[kpf, "cst"], w=[kpf])
        for which, dst, kd in (("s", S, kS), ("c", C, kC)):
            off = 0.0 if which == "s" else float(np.pi / 2)
            P.dve(lambda e, off=off: e.tensor_scalar(out=tmp[:], in0=pf[:], scalar1=off, scalar2=1.0 / TWO_PI,
                                                     op0=ALU.add, op1=ALU.mult), r=[kpf], w=[kt])
            P.dve(lambda e: e.tensor_copy(out=ki[:], in_=tmp[:]), r=[kt], w=[kk])
            P.dve(lambda e: e.tensor_copy(out=tmp[:], in_=ki[:]), r=[kk], w=[kt])
            P.dve(lambda e: e.scalar_tensor_tensor(out=tmp[:], in0=tmp[:], scalar=-TWO_PI, in1=pf[:],
                                                   op0=ALU.mult, op1=ALU.add), r=[kt, kpf], w=[kt])
            P.dve(lambda e, off=off: e.tensor_scalar(out=tmp[:], in0=tmp[:], scalar1=off, scalar2=None,
                                                     op0=ALU.add), r=[kt], w=[kt])
            P.dve(lambda e, dst=dst: e.tensor_scalar(out=dst[:], in0=tmp[:], scalar1=float(np.pi), scalar2=-TWO_PI,
                                                     op0=ALU.is_gt, op1=ALU.mult), r=[kt], w=[kd])
            P.dve(lambda e, dst=dst: e.tensor_tensor(out=tmp[:], in0=tmp[:], in1=dst[:], op=ALU.add), r=[kt, kd], w=[kt])
            P.dve(lambda e, dst=dst: e.tensor_scalar(out=dst[:], in0=tmp[:], scalar1=-float(np.pi), scalar2=TWO_PI,
                                                     op0=ALU.is_lt, op1=ALU.mult), r=[kt], w=[kd])
            P.dve(lambda e, dst=dst: e.tensor_tensor(out=tmp[:], in0=tmp[:], in1=dst[:], op=ALU.add), r=[kt, kd], w=[kt])
            P.dve(lambda e: e.tensor_scalar(out=tmp[:], in0=tmp[:], scalar1=-3.14159, scalar2=3.14159,
                                            op0=ALU.max, op1=ALU.min), r=[kt], w=[kt])
            P.act(lambda e, dst=dst: e.activation(out=dst[:], in_=tmp[:], func=AF.Sin), r=[kt], w=[kd])
        P.dve(lambda e: e.tensor_scalar(out=S[:], in0=S[:], scalar1=cst[:, sg_col:sg_col + 1], scalar2=None,
                                        op0=ALU.mult), r=[kS, "cst"], w=[kS])
        P.barrier()
        P.release(m)
        return C, S

    def load_w(dst, src_ap, key, nsplit=1):
        P.dma(dst, src_ap, w=[key], q="pool")

    def rmsnorm_fm(xs, kx, nchunk, n, gain_col, hout, kh, onesm, sq, ksq, rstd, krs, ssbank, eps=EPS):
        P.act(lambda e: e.activation(out=sq[:, 0:nchunk, 0:n], in_=xs[:, 0:nchunk, 0:n], func=AF.Square),
              r=[kx], w=[ksq])
        for c in range(nchunk):
            P.pe(lambda e, c=c: e.matmul(ps[ssbank][:, 0:n], lhsT=onesm[:], rhs=sq[:, c, 0:n],
                                         start=(c == 0), stop=(c == nchunk - 1)),
                 r=[ksq, "ones_b", "ones_q"], w=[PK(ssbank)])
        P.act(lambda e: e.activation(out=rstd[:, 0:n], in_=ps[ssbank][:, 0:n], func=AF.Sqrt, bias=eps, scale=1.0),
              r=[PK(ssbank)], w=[krs])
        P.dve(lambda e: e.reciprocal(out=rstd[:, 0:n], in_=rstd[:, 0:n]), r=[krs], w=[krs])
        for c in range(nchunk):
            P.dve(lambda e, c=c: e.scalar_tensor_tensor(out=hout[:, c, 0:n], in0=xs[:, c, 0:n],
                                                      scalar=cst[:, gain_col + c:gain_col + c + 1],
                                                      in1=rstd[:, 0:n], op0=ALU.mult, op1=ALU.mult),
                r=[kx, krs, "cst"], w=[(kh, c)])

    bankrot = [0]

    def nb():
        b = bankrot[0] % 4
        bankrot[0] += 1
        return b

    def proj_fm(wt, kw, oc0, h, kh, nk, n, bank, m=128):
        for kc in range(nk):
            P.pe(lambda e, kc=kc: e.matmul(ps[bank][0:m, 0:n], lhsT=wt[:, kc, oc0:oc0 + m], rhs=h[:, kc, 0:n],
                                           start=(kc == 0), stop=(kc == nk - 1)),
                 r=[kw, (kh, kc)], w=[PK(bank)])

    rope_mark = P.mark()
    C0s, S0s = rope_tables(pos_seq, T, C_FR0, C_SG0, "r0s")

    wk = P.sb("wk", [128, 8, 1280], BF16)
    wv = P.sb("wv", [128, 8, 512], BF16)
    load_w(wk[:], w0k_d.rearrange("(c p) n -> p c n", p=128), "wk")
    load_w(wv[:], w0v_d.rearrange("(c p) n -> p c n", p=128), "wv")
    xs2 = [P.sb(f"xs{i}", [128, 8, 512], F32) for i in range(2)]
    sq = P.sb("sq", [128, 8, 512], BF16)
    hb = P.sb("hb", [128, 8, 512], BF16)
    rstd = P.sb("rstd", [128, 512], F32)
    t1 = [P.sb(f"t1_{i}", [128, 512], F32) for i in range(2)]
    t2 = [P.sb(f"t2_{i}", [128, 512], F32) for i in range(2)]
    kst = [P.sb(f"kst{i}", [128, 5, 512], BF16) for i in range(2)]
    vst = [P.sb(f"vst{i}", [128, 8, 128], BF16) for i in range(2)]
    for i in range(2):
        P.pool(lambda e, i=i: e.memset(vst[i][:], 1.0), w=[("vst", i)])
    xseq_v = xT_seq.rearrange("(c p) t -> p c t", p=128)
    tcnt = [0]

    def rope_evac(bA, bB, Ct, St, col0, n, dst, kdst, kC="r0sC", kS="r0sS"):
        i = tcnt[0] % 2
        tcnt[0] += 1
        P.dve(lambda e: e.tensor_tensor(out=t1[i][:, 0:n], in0=ps[bA][:, 0:n], in1=Ct[:, col0:col0 + n], op=ALU.mult),
              r=[PK(bA), kC], w=[("t1", i)])
        P.dve(lambda e: e.tensor_tensor(out=t2[i][:, 0:n], in0=ps[bB][:, 0:n], in1=St[:, col0:col0 + n], op=ALU.mult),
              r=[PK(bB), kS], w=[("t2", i)])
        P.pool(lambda e: e.tensor_tensor(out=dst, in0=t1[i][:, 0:n], in1=t2[i][:, 0:n], op=ALU.add),
               r=[("t1", i), ("t2", i)], w=[kdst])

    for tc in range(8):
        xs = xs2[tc % 2]
        kx = ("xs", tc % 2)
        P.dma(xs[:], xseq_v[:, :, tc * 512:(tc + 1) * 512], w=[kx])
        rmsnorm_fm(xs, kx, 8, 512, gcol(0, 0), hb, "hb", ones_b, sq, "sq", rstd, "rstd", 7)
        if tc == 0 and "dbg_d" in dbg:
            dtmp = P.sb("dtmp", [128, 1024], F32)
            P.dma(dbg_d[:, 0:512], xs[:, 0, :], r=[kx], w=["dbg0"])
            P.dma(dbg_d[:, 512:1024], rstd[:], r=["rstd"], w=["dbg1"])
            P.dve(lambda e: e.tensor_copy(out=dtmp[:, 0:512], in_=hb[:, 0, :]), r=[("hb", 0)], w=["dtmp"])
            P.dve(lambda e: e.tensor_copy(out=dtmp[:, 512:1024], in_=sq[:, 0, :]), r=["sq"], w=["dtmp"])
            P.dma(dbg_d[:, 1024:2048], dtmp[:], r=["dtmp"], w=["dbg2"])
            dt2 = P.sb("dt2", [128, 512], F32)
            P.act(lambda e: e.activation(out=dt2[:], in_=ps[7][:, :], func=AF.Copy), r=[PK(7)], w=["dt2"])
            P.dma(dbg_d[:, 2048:2560], dt2[:], r=["dt2"], w=["dbg3"])
            P.dma(dbg_d[:, 2560:3072], S0s[:, 0:512], r=["r0sS"], w=["dbg4"])
            P.dma(dbg_d[:, 3072:3584], C0s[:, 3584:4096], r=["r0sC"], w=["dbg5"])
            P.dma(dbg_d[:, 3584:4096], S0s[:, 3584:4096], r=["r0sS"], w=["dbg6"])
        ks = kst[tc % 2]
        for oc in range(5):
            bA, bB = nb(), nb()
            colA = oc * 128 if oc < 4 else 1024
            colB = 512 + oc * 128 if oc < 4 else 1152
            proj_fm(wk, "wk", colA, hb, "hb", 8, 512, bA)
            proj_fm(wk, "wk", colB, hb, "hb", 8, 512, bB)
            rope_evac(bA, bB, C0s, S0s, tc * 512, 512, ks[:, oc, :], ("kst", tc % 2, oc))
        for oc in range(4):
            P.dma(kT_d[oc, :, tc * 512:(tc + 1) * 512], ks[:, oc, :], r=[("kst", tc % 2, oc)], w=[("kT_d", oc, tc)])
        P.dma(kidxT_d[:, tc * 512:(tc + 1) * 512], ks[:, 4, :], r=[("kst", tc % 2, 4)], w=[("kidxT_d", tc)])
        for tb in range(4):
            kb = tc * 4 + tb
            b = nb()
            for kc in range(8):
                P.pe(lambda e, kc=kc, tb=tb: e.matmul(ps[b][:, :], lhsT=hb[:, kc, tb * 128:(tb + 1) * 128],
                                                      rhs=wv[:, kc, :], start=(kc == 0), stop=(kc == 7)),
                     r=["wv", ("hb", kc)], w=[PK(b)])
            vs = vst[kb % 2]
            pv = ps[b][:, :].rearrange("p (h two d) -> p h two d", two=2, d=64)
            vv = vs[:].rearrange("p (h two) d -> p h two d", two=2)
            P.act(lambda e, pv=pv, vv=vv: e.activation(out=vv[:, :, 0, 0:64], in_=pv[:, :, 0, :], func=AF.Copy),
                  r=[PK(b)], w=[("vst", kb % 2)])
            P.act(lambda e, pv=pv, vv=vv: e.activation(out=vv[:, :, 1, 64:128], in_=pv[:, :, 1, :], func=AF.Copy),
                  r=[PK(b)], w=[("vst", kb % 2)])
            P.dma(vaug_d[kb], vs[:].rearrange("p h d -> p (h d)"), r=[("vst", kb % 2)], w=[("vaug_d", kb)])
    P.barrier()
    P.release(rope_mark)

    for s_ in range(2):
        C0o, S0o = rope_tables(pos_own[s_], NO, C_FR0, C_SG0, "r0o")
        set_mark = P.mark()
        if stop("K"):
            return nc
        wq = P.sb("wq", [128, 8, 4608], BF16)
        wwi = P.sb("wwi", [128, 8, 16], BF16)
        for c in range(8):
            P.dma(wq[:, c, :], w0q_d[c * 128:(c + 1) * 128, :], w=["wq"], q="pool")
        load_w(wwi[:], w0wi_d.rearrange("(c p) n -> p c n", p=128), "wwi")
        uext = P.sb("uext", [128, 4, 16, 130], F32)
        gbs = P.sb("gbs", [128, 4, NO], BF16)
        q_mark = P.mark()
        xs2 = [P.sb("xsq", [128, 8, 512], F32)] * 2
        sq = P.sb("sq", [128, 8, 512], BF16)
        hb = P.sb("hb", [128, 8, 512], BF16)
        rstd = P.sb("rstd", [128, 512], F32)
        t1 = [P.sb(f"t1_{i}", [128, 512], F32) for i in range(2)]
        t2 = [P.sb(f"t2_{i}", [128, 512], F32) for i in range(2)]
        qrot = [P.sb(f"qrot{i}", [128, 512], BF16) for i in range(6)]
        gcs = [P.sb(f"gcs{i}", [128, 512], F32) for i in range(2)]
        qrc = [0]
        xown_v = xT_own[s_].rearrange("(c p) t -> p c t", p=128)
        for tc in range(5):
            n = 512 if tc < 4 else 32
            xs = xs2[0]
            kx = ("xs", 0)
            P.dma(xs[:, :, 0:n], xown_v[:, :, tc * 512:tc * 512 + n], w=[kx])
            rmsnorm_fm(xs, kx, 8, n, gcol(0, 0), hb, "hb", ones_b, sq, "sq", rstd, "rstd", 7)
            if tc < 4:
                for oc in range(12):
                    bA, bB = nb(), nb()
                    colA = oc * 128 if oc < 4 else 1024 + (oc - 4) * 128
                    colB = 512 + oc * 128 if oc < 4 else 2048 + (oc - 4) * 128
                    proj_fm(wq, "wq", colA, hb, "hb", 8, 512, bA)
                    proj_fm(wq, "wq", colB, hb, "hb", 8, 512, bB)
                    qi_ = qrc[0] % 6
                    qrc[0] += 1
                    rope_evac(bA, bB, C0o, S0o, tc * 512, 512, qrot[qi_][:], ("qrot", qi_), "r0oC", "r0oS")
                    if oc < 4:
                        P.dma(qT_d[oc, :, tc * 512:(tc + 1) * 512], qrot[qi_][:], r=[("qrot", qi_)], w=[("qT_d", oc, tc)])
                    else:
                        P.dma(qidxT_d[oc - 4, :, tc * 512:(tc + 1) * 512], qrot[qi_][:], r=[("qrot", qi_)],
                              w=[("qidxT_d", oc - 4, tc)])
                for cc in range(4):
                    b = nb()
                    proj_fm(wq, "wq", 3072 + cc * 128, hb, "hb", 8, 512, b)
                    P.act(lambda e, b=b, cc=cc: e.activation(out=gbs[:, cc, tc * 512:(tc + 1) * 512], in_=ps[b][:, :], func=AF.Copy),
                          r=[PK(b)], w=[("gbs", cc)])
                for tb in range(4):
                    b = nb()
                    for kc in range(8):
                        P.pe(lambda e, kc=kc, tb=tb, b=b: e.matmul(ps[b][:, 0:16], lhsT=hb[:, kc, tb * 128:(tb + 1) * 128],
                                                                   rhs=wwi[:, kc, :], start=(kc == 0), stop=(kc == 7)),
                             r=["wwi", ("hb", kc)], w=[PK(b)])
                    P.act(lambda e, b=b, tb=tb: e.activation(out=widx[:, tc * 4 + tb, :], in_=ps[b][:, 0:16], func=AF.Copy),
                          r=[PK(b)], w=["widx"])
            for cc in range(4):
                bA, bB = nb(), nb()
                proj_fm(wq, "wq", 3584 + cc * 128, hb, "hb", 8, n, bA)
                proj_fm(wq, "wq", 4096 + cc * 128, hb, "hb", 8, n, bB)
                g = gcs[cc % 2]
                P.act(lambda e, g=g, bA=bA: e.activation(out=g[:, 0:n], in_=ps[bA][:, 0:n], func=AF.Copy),
                      r=[PK(bA)], w=[("gcs", cc % 2)])
                if tc < 4:
                    o_ap = uext[:, cc, tc * 4:(tc + 1) * 4, 2:130]
                    i0 = ps[bB][:, :].rearrange("p (b t) -> p b t", t=128)
                    i1 = g[:].rearrange("p (b t) -> p b t", t=128)
                else:
                    o_ap = uext[:, cc, :, 0:2]
                    i0 = ps[bB][:, 0:32].rearrange("p (b t) -> p b t", t=2)
                    i1 = g[:, 0:32].rearrange("p (b t) -> p b t", t=2)
                P.dve(lambda e, o_ap=o_ap, i0=i0, i1=i1: e.tensor_tensor(out=o_ap, in0=i0, in1=i1, op=ALU.mult),
                      r=[PK(bB), ("gcs", cc % 2)], w=[("uext", cc)])
        P.barrier()
        P.release(q_mark)
        cacc = [P.sb(f"cacc{i}", [128, 16, 128], F32) for i in range(2)]
        cvo = [P.sb(f"cvo{i}", [128, NO], BF16) for i in range(2)]
        for cc in range(4):
            a = cacc[cc % 2]
            ka = ("cacc", cc % 2)
            P.dve(lambda e, a=a, cc=cc: e.tensor_scalar(out=a[:], in0=uext[:, cc, :, 2:130],
                                                        scalar1=cst[:, C_CW + 8 + cc:C_CW + 9 + cc], scalar2=None, op0=ALU.mult),
                  r=[("uext", cc), "cst"], w=[ka])
            P.dve(lambda e, a=a, cc=cc: e.scalar_tensor_tensor(out=a[:], in0=uext[:, cc, :, 1:129],
                                                               scalar=cst[:, C_CW + 4 + cc:C_CW + 5 + cc], in1=a[:],
                                                               op0=ALU.mult, op1=ALU.add), r=[("uext", cc), "cst", ka], w=[ka])
            P.dve(lambda e, a=a, cc=cc: e.scalar_tensor_tensor(out=a[:], in0=uext[:, cc, :, 0:128],
                                                               scalar=cst[:, C_CW + cc:C_CW + 1 + cc], in1=a[:],
                                                               op0=ALU.mult, op1=ALU.add), r=[("uext", cc), "cst", ka], w=[ka])
            co = cvo[cc % 2]
            P.dve(lambda e, a=a, cc=cc, co=co: e.tensor_tensor(out=co[:], in0=a[:].rearrange("p b t -> p (b t)"),
                                                               in1=gbs[:, cc, :], op=ALU.mult),
                  r=[ka, ("gbs", cc)], w=[("cvo", cc % 2)])
            P.dma(convT_d[cc], co[:], r=[("cvo", cc % 2)], w=[("convT_d", cc)])
        P.barrier()
        P.release(base_mark)

        if stop("Q"):
            return nc
        kidx = P.sb("kidx", [128, T], BF16)
        qidx = P.sb("qidx", [128, 8, NO], BF16)
        P.dma(kidx[:], kidxT_d, r=[("kidxT_d", t_) for t_ in range(8)], w=["kidx"])
        for oc in range(8):
            P.dma(qidx[:, oc, :], qidxT_d[oc], r=[("qidxT_d", oc, t_) for t_ in range(4)], w=[("qidx", oc)])
        sc4 = [P.sb(f"sc{i}", [128, T], F32) for i in range(4)]
        junk = P.sb("junk", [128, T], BF16)
        m01 = P.sb("m01", [128, T], F32)
        cbt = [P.sb(f"cbt{i}", [128, 1024], F32) for i in range(2)]
        rr = [P.sb(f"rr{i}", [128, 512], BF16) for i in range(6)]
        mTs = [P.sb(f"mTs{i}", [128, 32, 512], BF16) for i in range(1)]
        dg2 = [P.sb(f"dg{i}", [128, 16, 128], BF16) for i in range(2)]
        identb = P.sb("identb", [128, 128], BF16)
        sm2 = [P.sb(f"sm{i}", [128, 16], F32) for i in range(2)]
        stepT2 = [P.sb(f"stepT{i}", [128, 2, NBIS + 1], F32) for i in range(2)]
        P.dve(lambda e: e.tensor_copy(out=identb[:], in_=ident[:]), r=["ident"], w=["identb"])
        rcnt = [0]
        acnt_i = [0]
        mt = mTs[0]

        def acc_half(g, hf, hi):
            nk = 1024 * (g + 1)
            nch = nk // 512
            for jj in range(2):
                j = 2 * hf + jj
                qi = 4 * g + j
                sc = sc4[j]
                ksc = ("sc", j)
                dg = dg2[qi % 2]
                kdg = ("dg", qi % 2)
                for h in range(16):
                    P.pool(lambda e, dg=dg, h=h, qi=qi: e.tensor_scalar(out=dg[:, h, :], in0=identb[:], scalar1=widx[:, qi, h:h + 1],
                                                                        scalar2=None, op0=ALU.mult),
                           r=["identb", "widx"], w=[kdg])
                for ch in range(nch):
                    accb = 4 + (acnt_i[0] % 2)
                    acnt_i[0] += 1
                    pend = []
                    for h in range(16):
                        b = nb()
                        base = (h % 2) * 64
                        P.pe(lambda e, b=b, h=h, base=base, ch=ch, qi=qi: e.matmul(
                            ps[b][:, :], lhsT=qidx[base:base + 64, h // 2, qi * 128:(qi + 1) * 128],
                            rhs=kidx[base:base + 64, ch * 512:(ch + 1) * 512], start=True, stop=True),
                            r=["kidx", ("qidx", h // 2)], w=[PK(b)])
                        ri = rcnt[0] % 6
                        rcnt[0] += 1
                        r_ = rr[ri]
                        P.act(lambda e, b=b, r_=r_: e.activation(out=r_[:], in_=ps[b][:, :], func=AF.Relu),
                              r=[PK(b)], w=[("rr", ri)])
                        pend.append((h, r_, ri))
                        if len(pend) > 2:
                            h0, r0, ri0 = pend.pop(0)
                            P.pe(lambda e, h0=h0, r0=r0, accb=accb, dg=dg: e.matmul(ps[accb][:, :], lhsT=dg[:, h0, :], rhs=r0[:],
                                                                                    start=(h0 == 0), stop=(h0 == 15)),
                                 r=[("rr", ri0), kdg], w=[PK(accb)])
                    for h0, r0, ri0 in pend:
                        P.pe(lambda e, h0=h0, r0=r0, accb=accb, dg=dg: e.matmul(ps[accb][:, :], lhsT=dg[:, h0, :], rhs=r0[:],
                                                                                start=(h0 == 0), stop=(h0 == 15)),
                             r=[("rr", ri0), kdg], w=[PK(accb)])
                    P.act(lambda e, sc=sc, ch=ch, accb=accb: e.activation(out=sc[:, ch * 512:(ch + 1) * 512], in_=ps[accb][:, :], func=AF.Copy),
                          r=[PK(accb)], w=[(ksc, ch)])

        def prep_half(g, hf, hi):
            nk = 1024 * (g + 1)
            nch = nk // 512
            sm = sm2[hi % 2]
            stepT = stepT2[hi % 2]
            for jj in range(2):
                j = 2 * hf + jj
                qi = 4 * g + j
                sc = sc4[j]
                allsc = [(("sc", j), ch) for ch in range(nch)]
                cb = cbt[jj]
                P.dma(cb[:], cb_d[s_, (g // 2) * 4 + j], w=[("cbt", jj)])
                P.dve(lambda e, sc=sc, jj=jj, sm=sm: e.tensor_reduce(out=sm[:, jj:jj + 1], in_=sc[:, 0:nk], axis=AX.X, op=ALU.max,
                                                                     apply_absolute_value=True), r=allsc, w=[("smA", hi % 2, jj)])
                P.dve(lambda e, sc=sc, cb=cb: e.tensor_tensor(out=sc[:, nk - 1024:nk], in0=sc[:, nk - 1024:nk], in1=cb[:],
                                                              op=ALU.add), r=allsc + [("cbt", jj), ("smA", hi % 2, jj)], w=allsc)
            allA = [("smA", hi % 2, jj) for jj in range(2)]
            P.dve(lambda e, sm=sm: e.tensor_single_scalar(out=sm[:, 0:2].bitcast(I32), in_=sm[:, 0:2].bitcast(I32),
                                                          scalar=0x7F800000, op=ALU.bitwise_and), r=allA, w=[("smA2", hi % 2)])
            P.dve(lambda e, sm=sm: e.tensor_scalar(out=sm[:, 0:2], in0=sm[:, 0:2], scalar1=2.0, scalar2=1e-30,
                                                   op0=ALU.mult, op1=ALU.max), r=[("smA2", hi % 2)], w=[("smA2", hi % 2)])
            for i in range(NBIS + 1):
                P.pool(lambda e, i=i, sm=sm, stepT=stepT: e.tensor_scalar(out=stepT[:, :, i], in0=sm[:, 0:2], scalar1=float(2.0 ** -i),
                                                                          scalar2=None, op0=ALU.mult),
                       r=[("smA2", hi % 2)], w=[("stepT", hi % 2, i)])
            P.pool(lambda e, sm=sm: e.memset(sm[:, 2:4], 0.0), w=[("mid", hi % 2)])

        def bis_half(g, hf, hi):
            nk = 1024 * (g + 1)
            nch = nk // 512
            sm = sm2[hi % 2]
            stepT = stepT2[hi % 2]
            kmid = ("mid", hi % 2)
            qi0 = 4 * g + 2 * hf
            kq2 = kq[:, s_ * 16 + qi0:s_ * 16 + qi0 + 2]
            for i in range(NBIS):
                for jj in range(2):
                    j = 2 * hf + jj
                    P.dve(lambda e, j=j, jj=jj, sm=sm: e.tensor_scalar(out=junk[:, 0:nk], in0=sc4[j][:, 0:nk], scalar1=sm[:, 2 + jj:3 + jj],
                                                                       scalar2=None, op0=ALU.is_ge, op1=ALU.add, accum_out=sm[:, 4 + jj:5 + jj]),
                          r=[(("sc", j), ch) for ch in range(nch)] + [kmid], w=[("cnt", hi % 2, jj)])
                P.dve(lambda e, sm=sm: e.tensor_tensor(out=sm[:, 6:8], in0=sm[:, 4:6], in1=kq2, op=ALU.is_ge),
                      r=[("cnt", hi % 2, jj) for jj in range(2)] + ["kq"], w=[("s4", hi % 2)])
                P.dve(lambda e, i=i, sm=sm, stepT=stepT: e.scalar_tensor_tensor(out=sm[:, 8:10], in0=sm[:, 6:8], scalar=0.5, in1=stepT[:, :, i],
                                                                                op0=ALU.subtract, op1=ALU.mult),
                      r=[("s4", hi % 2), ("stepT", hi % 2, i)], w=[("d4", hi % 2)])
                P.dve(lambda e, sm=sm: e.tensor_tensor(out=sm[:, 2:4], in0=sm[:, 2:4], in1=sm[:, 8:10], op=ALU.add),
                      r=[("d4", hi % 2), kmid], w=[kmid])
            P.dve(lambda e, sm=sm, stepT=stepT: e.tensor_tensor(out=sm[:, 10:12], in0=sm[:, 2:4], in1=stepT[:, :, NBIS], op=ALU.subtract),
                  r=[kmid, ("stepT", hi % 2, NBIS)], w=[("thr", hi % 2)])
            for jj in range(2):
                j = 2 * hf + jj
                P.dve(lambda e, j=j, jj=jj, sm=sm: e.tensor_scalar(out=m01[:, 0:nk], in0=sc4[j][:, 0:nk], scalar1=sm[:, 10 + jj:11 + jj], scalar2=None,
                                                                   op0=ALU.is_ge), r=[(("sc", j), ch) for ch in range(nch)] + [("thr", hi % 2)], w=["m01"])
                for k4 in range(nk // 512):
                    b = 6 + (k4 % 2)
                    for kk in range(4):
                        kb = k4 * 4 + kk
                        P.pe(lambda e, b=b, kk=kk, kb=kb: e.transpose(out=ps[b][:, kk * 128:(kk + 1) * 128],
                                                                      in_=m01[:, kb * 128:(kb + 1) * 128], identity=ident[:]),
                             r=["m01", "ident"], w=[PK(b)])
                    P.act(lambda e, b=b, k4=k4, j=j: e.activation(
                        out=mt[:, k4 * 4:(k4 + 1) * 4, j * 128:(j + 1) * 128],
                        in_=ps[b][:, :].rearrange("p (k t) -> p k t", t=128), func=AF.Copy),
                        r=[PK(b)], w=["mTs"])
            if hf == 1:
                P.dma(maskT_d[g, :, 0:nk // 128, :], mt[:, 0:nk // 128, :], r=["mTs"], w=[("maskT_d", g)])

        halves = [(g, hf) for g in range(4) for hf in range(2)]
        for hi, (g, hf) in enumerate(halves):
            acc_half(g, hf, hi)
            if hi > 0:
                bis_half(halves[hi - 1][0], halves[hi - 1][1], hi - 1)
            prep_half(g, hf, hi)
        bis_half(halves[-1][0], halves[-1][1], len(halves) - 1)
        P.barrier()
        P.release(base_mark)

        if stop("I"):
            return nc
        def attention(nheads, hp_loader, st_emit, mask_for, scale, out_d, okey):
            pts = [P.sb(f"pt{i}", [128, 512], BF16) for i in range(6)]
            ident_fill = P.sb("ifill", [128, 128], BF16)
            P.pool(lambda e: e.memset(ident_fill[:], 0.0), w=["ifill"])
            rdn = [P.sb(f"rdn{i}", [128, 512], F32) for i in range(2)]
            ost = [P.sb(f"ost{i}", [128, NO], BF16) for i in range(2)]
            ucnt = [0]
            acnt = [0]
            for hp in range(nheads // 2):
                bufs = hp_loader(hp)
                o_t = ost[hp % 2]
                for hh in range(2):
                    h = hp * 2 + hh
                    for g in range(4):
                        nkb = 8 * (g + 1)
                        accb = 4 + (acnt[0] % 2)
                        acnt[0] += 1
                        pend = []
                        for _f in range(NBURST):
                            P.pe(lambda e: e.matmul(ps[6][:, :], lhsT=ident_fill[:], rhs=pts[0][:], start=True, stop=True))
                        for kb in range(nkb):
                            sb_ = nb()
                            rel_ = kb - (nkb - 8)
                            c0 = 128 * (rel_ // 2) if rel_ > 0 else 0
                            st_emit(bufs, h, hh, g, kb, sb_, c0)
                            for _f in range(NFILL):
                                P.pe(lambda e: e.matmul(ps[6][:, :], lhsT=ident_fill[:], rhs=pts[0][:], start=True, stop=True))
                            pi = ucnt[0] % 6
                            ucnt[0] += 1
                            pt = pts[pi]
                            P.act(lambda e, sb_=sb_, pt=pt, c0=c0: e.activation(out=pt[:, c0:512], in_=ps[sb_][:, c0:512], func=AF.Exp, scale=scale),
                                  r=[PK(sb_)], w=[("pt", pi)])
                            mk = mask_for(bufs, g, kb)
                            if mk is not None:
                                map_, mkey = mk
                                P.dve(lambda e, pt=pt, map_=map_, c0=c0: e.tensor_tensor(out=pt[:, c0:512], in0=pt[:, c0:512], in1=map_[:, c0:512], op=ALU.mult),
                                      r=[("pt", pi), mkey], w=[("pt", pi)])
                            pend.append((kb, pt, pi, c0))
                            if len(pend) > 2:
                                kb0, pt0, pi0, c00 = pend.pop(0)
                                P.pe(lambda e, kb0=kb0, pt0=pt0, accb=accb, c00=c00: e.matmul(
                                    ps[accb][:, c00:512], lhsT=bufs["v"][:, kb0, hh * 128:(hh + 1) * 128], rhs=pt0[:, c00:512],
                                    start=(kb0 == 0), stop=(kb0 == nkb - 1)), r=[("pt", pi0), bufs["kv"]], w=[PK(accb)])
                        for kb0, pt0, pi0, c00 in pend:
                            P.pe(lambda e, kb0=kb0, pt0=pt0, accb=accb, c00=c00: e.matmul(
                                ps[accb][:, c00:512], lhsT=bufs["v"][:, kb0, hh * 128:(hh + 1) * 128], rhs=pt0[:, c00:512],
                                start=(kb0 == 0), stop=(kb0 == nkb - 1)), r=[("pt", pi0), bufs["kv"]], w=[PK(accb)])
                        rd = rdn[acnt[0] % 2]
                        krd = ("rdn", acnt[0] % 2)
                        nlo, dlo = (0, 64) if hh == 0 else (64, 0)
                        P.dve(lambda e, rd=rd, accb=accb, dlo=dlo: e.reciprocal(out=rd[dlo:dlo + 64, :], in_=ps[accb][dlo:dlo + 64, :]),
                              r=[PK(accb)], w=[krd])
                        P.dve(lambda e, rd=rd, accb=accb, dlo=dlo, nlo=nlo, g=g, o_t=o_t: e.tensor_tensor(
                            out=o_t[nlo:nlo + 64, g * 512:(g + 1) * 512], in0=ps[accb][nlo:nlo + 64, :],
                            in1=rd[dlo:dlo + 64, :], op=ALU.mult), r=[PK(accb), krd], w=[("ost", hp % 2)])
                P.dma(out_d[hp], o_t[:], r=[("ost", hp % 2)], w=[(okey, hp)])

        mres = P.sb("mres", [128, 80, 512], BF16)
        goff = [0, 8, 24, 48]
        for g in range(4):
            nkb = 8 * (g + 1)
            P.dma(mres[:, goff[g]:goff[g] + nkb, :], maskT_d[g, :, 0:nkb, :], r=[("maskT_d", g)], w=[("mres", g)])
        kb2 = [P.sb(f"kb2_{i}", [128, T], BF16) for i in range(2)]
        qb2 = [P.sb(f"qb2_{i}", [128, NO], BF16) for i in range(2)]
        vb2 = [P.sb(f"vb2_{i}", [128, 32, 256], BF16) for i in range(2)]

        def hp_loader0(hp):
            i = hp % 2
            key = ("kvq0", i)
            P.dma(kb2[i][:], kT_d[hp], r=[("kT_d", hp, t_) for t_ in range(8)], w=[key])
            P.dma(qb2[i][:], qT_d[hp], r=[("qT_d", hp, t_) for t_ in range(4)], w=[key])
            P.dma(vb2[i][:], vaug_d[:, :, hp * 256:(hp + 1) * 256].rearrange("k p c -> p k c"),
                  r=[("vaug_d", k_) for k_ in range(32)], w=[key])
            return {"k": kb2[i], "q": qb2[i], "v": vb2[i], "kv": key}

        def st_emit0(bufs, h, hh, g, kb, sb_, c0=0):
            base = hh * 64
            P.pe(lambda e: e.matmul(ps[sb_][:, c0:512], lhsT=bufs["k"][base:base + 64, kb * 128:(kb + 1) * 128],
                                    rhs=bufs["q"][base:base + 64, g * 512 + c0:(g + 1) * 512], start=True, stop=True),
                 r=[bufs["kv"]], w=[PK(sb_)])

        def mask_for0(bufs, g, kb):
            return mres[:, goff[g] + kb, :], ("mres", g)

        attention(8, hp_loader0, st_emit0, mask_for0, 0.125, attnT_d, "attnT_d")
        P.barrier()
        P.release(base_mark)

        if stop("A0"):
            return nc
        def out_phase(in_chunks, wd, layer, x_src, xkey_src, x_dst, xkey_dst):
            wo = P.sb("wo", [128, 8, D], BF16)
            load_w(wo[:], wd.rearrange("(c p) n -> p c n", p=128), "wo")
            ain = P.sb("ain", [128, 8, NO], BF16)
            for c, (ap_, rk) in enumerate(in_chunks):
                P.dma(ain[:, c, :], ap_, r=rk, w=[("ain", c)])
            xo2 = [P.sb(f"xo{i}", [128, 8, 512], F32) for i in range(2)]
            mx = P.sb("mx", [128, 8, 512], F32)
            sq_ = P.sb("sqo", [128, 8, 512], BF16)
            rs = P.sb("rso", [128, 512], F32)
            xv = x_src.rearrange("(c p) t -> p c t", p=128)
            xdv = x_dst.rearrange("(c p) t -> p c t", p=128)
            for tc in range(4):
                xo = xo2[tc % 2]
                kxo = ("xo", tc % 2)
                P.dma(xo[:], xv[:, :, tc * 512:(tc + 1) * 512], r=[(xkey_src, tc)], w=[kxo])
                for oc in range(8):
                    b = nb()
                    for kc in range(8):
                        P.pe(lambda e, kc=kc, oc=oc, b=b: e.matmul(ps[b][:, :], lhsT=wo[:, kc, oc * 128:(oc + 1) * 128],
                                                                   rhs=ain[:, kc, tc * 512:(tc + 1) * 512],
                                                                   start=(kc == 0), stop=(kc == 7)),
                             r=["wo", ("ain", kc)], w=[PK(b)])
                    P.act(lambda e, b=b, oc=oc: e.activation(out=mx[:, oc, :], in_=ps[b][:, :], func=AF.Copy),
                          r=[PK(b)], w=[("mx", oc)])
                    P.dve(lambda e, b=b, oc=oc: e.tensor_tensor(out=sq_[:, oc, :], in0=mx[:, oc, :], in1=mx[:, oc, :], op=ALU.mult),
                          r=[("mx", oc)], w=[("sqo", oc)])
                for c in range(8):
                    P.pe(lambda e, c=c: e.matmul(ps[7][:, :], lhsT=ones_b[:], rhs=sq_[:, c, :], start=(c == 0), stop=(c == 7)),
                         r=[("sqo", c), "ones_b"], w=[PK(7)])
                P.act(lambda e: e.activation(out=rs[:], in_=ps[7][:, :], func=AF.Sqrt, bias=EPS, scale=1.0), r=[PK(7)], w=["rso"])
                P.dve(lambda e: e.reciprocal(out=rs[:], in_=rs[:]), r=["rso"], w=["rso"])
                for c in range(8):
                    gc_ = gcol(layer, 1) + c
                    P.dve(lambda e, c=c, gc_=gc_: e.scalar_tensor_tensor(out=mx[:, c, :], in0=mx[:, c, :], scalar=cst[:, gc_:gc_ + 1],
                                                                          in1=rs[:], op0=ALU.mult, op1=ALU.mult),
                           r=[("mx", c), "rso", "cst"], w=[("mx", c)])
                    P.dve(lambda e, c=c, xo=xo: e.tensor_tensor(out=xo[:, c, :], in0=xo[:, c, :], in1=mx[:, c, :], op=ALU.add),
                          r=[("mx", c), kxo], w=[kxo])
                P.dma(xdv[:, :, tc * 512:(tc + 1) * 512], xo[:], r=[kxo], w=[(xkey_dst, tc)])

        in0 = [(attnT_d[c], [("attnT_d", c)]) for c in range(4)] + [(convT_d[c], [("convT_d", c)]) for c in range(4)]
        out_phase(in0, wout_d, 0, xT_own[s_, :, 0:NO], "xown", x1T_d, "x1T_d")
        P.barrier()
        P.release(base_mark)

        def ffn_phase(layer, x_src, xkey_src, x_dst, xkey_dst):
            w1s = P.sb("w1s", [128, 8, 4096], BF16)
            w2s = P.sb("w2s", [128, 32, D], BF16)
            w1v = w1_d[layer].rearrange("(c p) n -> p c n", p=128)
            w2v = w2_d[layer].rearrange("(c p) n -> p c n", p=128)
            for c in range(8):
                P.dma(w1s[:, c, :], w1v[:, c, :], w=[("w1s", c)], q="pool")
            for c in range(0, 32, 4):
                P.dma(w2s[:, c:c + 4, :], w2v[:, c:c + 4, :], w=[("w2s", c // 4)], q="pool")
            xf = P.sb("xf", [128, 8, 512], F32)
            off3 = P.mark()
            yb = P.sb("yb", [128, 8, 512], F32)
            end3 = P.mark()
            P.release(off3)
            sq_f = P.sb("sqf", [128, 8, 512], BF16)
            hf = P.sb("hf", [128, 8, 512], BF16)
            assert P.mark() == end3
            h1 = P.sb("h1", [128, 32, 512], BF16)
            sqy = [P.sb(f"sqy{i}", [128, 512], BF16) for i in range(2)]
            rt = [P.sb(f"rt{i}", [128, 512], F32) for i in range(2)]
            alias_keys = ["sqf"] + [("hf", c) for c in range(8)]
            rsf = P.sb("rsf", [128, 512], F32)
            xv = x_src.rearrange("(c p) t -> p c t", p=128)
            xdv = x_dst.rearrange("(c p) t -> p c t", p=128)
            rc = [0]
            for tc in range(4):
                P.dma(xf[:], xv[:, :, tc * 512:(tc + 1) * 512], r=[(xkey_src, tc)], w=["xf"])
                P.act(lambda e: e.activation(out=rsf[:, 0:1], in_=rsf[:, 0:1], func=AF.Copy), r=["rsf"],
                      w=alias_keys + ["rsf"] + [("yb", c) for c in range(8)])
                rmsnorm_fm(xf, "xf", 8, 512, gcol(layer, 2), hf, "hf", ones_b, sq_f, "sqf", rsf, "rsf", 7)
                for oc in range(32):
                    b = nb()
                    for kc in range(8):
                        P.pe(lambda e, kc=kc, oc=oc, b=b: e.matmul(ps[b][:, :], lhsT=w1s[:, kc, oc * 128:(oc + 1) * 128],
                                                                   rhs=hf[:, kc, :], start=(kc == 0), stop=(kc == 7)),
                             r=[("w1s", kc), ("hf", kc)], w=[PK(b)])
                    ri = rc[0] % 2
                    rc[0] += 1
                    P.act(lambda e, b=b, ri=ri: e.activation(out=rt[ri][:], in_=ps[b][:, :], func=AF.Relu), r=[PK(b)], w=[("rt", ri)])
                    eng = P.dve if oc % 2 == 0 else P.pool
                    eng(lambda e, ri=ri, oc=oc: e.tensor_tensor(out=h1[:, oc, :], in0=rt[ri][:], in1=rt[ri][:], op=ALU.mult),
                        r=[("rt", ri)], w=[("h1", oc)])
                for oc in range(8):
                    b = nb()
                    for kc in range(32):
                        P.pe(lambda e, kc=kc, oc=oc, b=b: e.matmul(ps[b][:, :], lhsT=w2s[:, kc, oc * 128:(oc + 1) * 128],
                                                                   rhs=h1[:, kc, :], start=(kc == 0), stop=(kc == 31)),
                             r=[("w2s", kc // 4), ("h1", kc)], w=[PK(b)])
                    P.act(lambda e, b=b, oc=oc: e.activation(out=yb[:, oc, :], in_=ps[b][:, :], func=AF.Copy), r=[PK(b)],
                          w=[("yb", oc)] + (alias_keys if oc == 0 else []))
                    P.dve(lambda e, oc=oc: e.tensor_tensor(out=sqy[oc % 2][:], in0=yb[:, oc, :], in1=yb[:, oc, :], op=ALU.mult),
                          r=[("yb", oc)], w=[("sqy", oc % 2)])
                    P.pe(lambda e, oc=oc: e.matmul(ps[7][:, :], lhsT=ones_b[:], rhs=sqy[oc % 2][:], start=(oc == 0), stop=(oc == 7)),
                         r=[("sqy", oc % 2), "ones_b"], w=[PK(7)])
                P.act(lambda e: e.activation(out=rsf[:], in_=ps[7][:, :], func=AF.Sqrt, bias=EPS, scale=1.0), r=[PK(7)], w=["rsf"])
                P.dve(lambda e: e.reciprocal(out=rsf[:], in_=rsf[:]), r=["rsf"], w=["rsf"])
                for c in range(8):
                    gc_ = gcol(layer, 3) + c
                    P.dve(lambda e, c=c, gc_=gc_: e.scalar_tensor_tensor(out=yb[:, c, :], in0=yb[:, c, :], scalar=cst[:, gc_:gc_ + 1],
                                                                          in1=rsf[:], op0=ALU.mult, op1=ALU.mult),
                           r=[("yb", c), "rsf", "cst"], w=[("yb", c)])
                    P.dve(lambda e, c=c: e.tensor_tensor(out=yb[:, c, :], in0=yb[:, c, :], in1=xf[:, c, :], op=ALU.add),
                          r=[("yb", c), "xf"], w=[("yb", c)])
                P.dma(xdv[:, :, tc * 512:(tc + 1) * 512], yb[:], r=[("yb", c) for c in range(8)], w=[(xkey_dst, tc)])

        if stop("O0"):
            return nc
        ffn_phase(0, x1T_d, "x1T_d", x2T_d[s_], ("x2T_d", s_))
        P.barrier()
        P.release(base_mark)

    if stop("F0"):
        return nc
    tabs = [rope_tables(pos_own[s1], NO, C_FR1, C_SG1, f"r1o{s1}") for s1 in range(2)]
    C1S1 = [None, None]
    m1 = P.mark()
    wdq = P.sb("wdq", [128, 8, 384], BF16)
    wuq = P.sb("wuq", [128, 3, 2048], BF16)
    wdkv = P.sb("wdkv", [128, 8, 320], BF16)
    load_w(wdq[:], wdq_d.rearrange("(c p) n -> p c n", p=128), "wdq")
    load_w(wuq[:], wuq_d.rearrange("(c p) n -> p c n", p=128), "wuq")
    load_w(wdkv[:], wdkv_d.rearrange("(c p) n -> p c n", p=128), "wdkv")
    xs2 = [P.sb(f"xs{i}", [128, 8, 512], F32) for i in range(2)]
    sq = P.sb("sq", [128, 8, 512], BF16)
    hb = P.sb("hb", [128, 8, 512], BF16)
    rstd = P.sb("rstd", [128, 512], F32)
    t1 = [P.sb(f"t1_{i}", [128, 512], F32) for i in range(2)]
    t2 = [P.sb(f"t2_{i}", [128, 512], F32) for i in range(2)]
    cq = P.sb("cq", [128, 3, 512], F32)
    cqn = P.sb("cqn", [128, 3, 512], BF16)
    ckv = P.sb("ckv", [128, 2, 512], F32)
    ckvn = P.sb("ckvn", [128, 2, 512], F32)
    krs = P.sb("krs", [32, 512], F32)
    qst = [P.sb(f"qst1_{i}", [128, 16, 512], BF16) for i in range(2)]
    sq3 = P.sb("sq3", [128, 3, 512], BF16)
    rs3 = P.sb("rs3", [128, 512], F32)

    def rope_evac1(bA, bB, m, col0, dst, kdst, eng_out="pool"):
        Cc, Ss = C1S1[0], C1S1[1]
        i = tcnt[0] % 2
        tcnt[0] += 1
        P.dve(lambda e: e.tensor_tensor(out=t1[i][0:m, :], in0=ps[bA][0:m, :], in1=Cc[0:m, col0:col0 + 512], op=ALU.mult),
              r=[PK(bA), "r1o0C", "r1o1C"], w=[("t1", i)])
        P.dve(lambda e: e.tensor_tensor(out=t2[i][0:m, :], in0=ps[bB][0:m, :], in1=Ss[0:m, col0:col0 + 512], op=ALU.mult),
              r=[PK(bB), "r1o0S", "r1o1S"], w=[("t2", i)])
        P.pool(lambda e: e.tensor_tensor(out=dst, in0=t1[i][0:m, :], in1=t2[i][0:m, :], op=ALU.add),
               r=[("t1", i), ("t2", i)], w=[kdst])

    for s1 in range(2):
        C1S1[0], C1S1[1] = tabs[s1]
        x2v = x2T_d[s1].rearrange("(c p) t -> p c t", p=128)
        for tc in range(4):
            xs = xs2[tc % 2]
            kx = ("xs", tc % 2)
            P.dma(xs[:], x2v[:, :, tc * 512:(tc + 1) * 512], r=[(("x2T_d", s1), tc)], w=[kx])
            rmsnorm_fm(xs, kx, 8, 512, gcol(1, 0), hb, "hb", ones_b, sq, "sq", rstd, "rstd", 7)
            if s1 == 0:
                for oc in range(3):
                    b = nb()
                    proj_fm(wdq, "wdq", oc * 128, hb, "hb", 8, 512, b)
                    P.act(lambda e, b=b, oc=oc: e.activation(out=cq[:, oc, :], in_=ps[b][:, :], func=AF.Copy), r=[PK(b)], w=["cq"])
                P.act(lambda e: e.activation(out=sq3[:], in_=cq[:], func=AF.Square), r=["cq"], w=["sq3"])
                for c in range(3):
                    P.pe(lambda e, c=c: e.matmul(ps[7][:, :], lhsT=ones_q[:], rhs=sq3[:, c, :], start=(c == 0), stop=(c == 2)),
                         r=["sq3", "ones_q"], w=[PK(7)])
                P.act(lambda e: e.activation(out=rs3[:], in_=ps[7][:, :], func=AF.Sqrt, bias=EPS, scale=256.0 / 384.0), r=[PK(7)], w=["rs3"])
                P.dve(lambda e: e.reciprocal(out=rs3[:], in_=rs3[:]), r=["rs3"], w=["rs3"])
                for c in range(3):
                    P.dve(lambda e, c=c: e.scalar_tensor_tensor(out=cqn[:, c, :], in0=cq[:, c, :], scalar=cst[:, C_QN + c:C_QN + c + 1],
                                                                in1=rs3[:], op0=ALU.mult, op1=ALU.mult), r=["cq", "rs3", "cst"], w=[("cqn", c)])
                qs = qst[tc % 2]
                for oc in range(8):
                    b = nb()
                    proj_fm(wuq, "wuq", oc * 128, cqn, "cqn", 3, 512, b)
                    P.act(lambda e, b=b, oc=oc: e.activation(out=qs[:, oc, :], in_=ps[b][:, :], func=AF.Copy), r=[PK(b)], w=[("qst", tc % 2, oc)])
                    P.dma(qnT_d[oc, :, tc * 512:(tc + 1) * 512], qs[:, oc, :], r=[("qst", tc % 2, oc)], w=[("qnT_d", oc, tc)])
                for oc in range(8):
                    bA, bB = nb(), nb()
                    proj_fm(wuq, "wuq", 1024 + oc * 64, cqn, "cqn", 3, 512, bA, m=64)
                    proj_fm(wuq, "wuq", 1536 + oc * 64, cqn, "cqn", 3, 512, bB, m=64)
                    rope_evac1(bA, bB, 64, tc * 512, qs[0:64, 8 + oc, :], ("qst", tc % 2, 8 + oc))
                    P.dma(qrT_d[oc, :, tc * 512:(tc + 1) * 512], qs[0:64, 8 + oc, :], r=[("qst", tc % 2, 8 + oc)], w=[("qrT_d", oc, tc)])
            for oc in range(2):
                b = nb()
                proj_fm(wdkv, "wdkv", oc * 128, hb, "hb", 8, 512, b)
                P.act(lambda e, b=b, oc=oc: e.activation(out=ckv[:, oc, :], in_=ps[b][:, :], func=AF.Copy), r=[PK(b)], w=["ckv"])
            P.act(lambda e: e.activation(out=sq3[:, 0:2, :], in_=ckv[:], func=AF.Square), r=["ckv"], w=["sq3"])
            for c in range(2):
                P.pe(lambda e, c=c: e.matmul(ps[7][:, :], lhsT=ones_q[:], rhs=sq3[:, c, :], start=(c == 0), stop=(c == 1)),
                     r=["sq3", "ones_q"], w=[PK(7)])
            P.act(lambda e: e.activation(out=rs3[:], in_=ps[7][:, :], func=AF.Sqrt, bias=EPS, scale=1.0), r=[PK(7)], w=["rs3"])
            P.dve(lambda e: e.reciprocal(out=rs3[:], in_=rs3[:]), r=["rs3"], w=["rs3"])
            for c in range(2):
                P.dve(lambda e, c=c: e.scalar_tensor_tensor(out=ckvn[:, c, :], in0=ckv[:, c, :], scalar=cst[:, C_KVN + c:C_KVN + c + 1],
                                                            in1=rs3[:], op0=ALU.mult, op1=ALU.mult), r=["ckv", "rs3", "cst"], w=[("ckvn", c)])
                P.dma(kva_sets_d[s1, c * 128:(c + 1) * 128, tc * 512:(tc + 1) * 512], ckvn[:, c, :], r=[("ckvn", c)], w=[("kva", s1, tc, c)])
            bA, bB = nb(), nb()
            proj_fm(wdkv, "wdkv", 256, hb, "hb", 8, 512, bA, m=32)
            proj_fm(wdkv, "wdkv", 288, hb, "hb", 8, 512, bB, m=32)
            rope_evac1(bA, bB, 32, tc * 512, krs[:, :], "krs")
            P.dma(kva_sets_d[s1, 256:288, tc * 512:(tc + 1) * 512], krs[:, :], r=["krs"], w=[("kva", s1, tc, 2)])
    P.barrier()
    P.release(base_mark)

    if stop("Q1"):
        return nc
    wkk = P.sb("wkk", [128, 2, 1024], BF16)
    wkv = P.sb("wkv", [128, 2, 1024], BF16)
    load_w(wkk[:], wukvk_d.rearrange("(c p) n -> p c n", p=128), "wkk")
    load_w(wkv[:], wukvv_d.rearrange("(c p) n -> p c n", p=128), "wkv")
    ckf = [P.sb(f"ckf{i}", [128, 2, 512], F32) for i in range(2)]
    ckb = [P.sb(f"ckb{i}", [128, 2, 512], BF16) for i in range(2)]
    kns = [P.sb(f"kns{i}", [128, 8, 512], BF16) for i in range(2)]
    vst1 = [P.sb(f"vst1_{i}", [128, 16, 128], BF16) for i in range(2)]
    kr4 = P.sb("kr4", [64, T], BF16)
    krf = P.sb("krf", [64, T], F32)
    for i in range(2):
        P.pool(lambda e, i=i: e.memset(vst1[i][:], 1.0), w=[("vst1", i)])
    for sl in range(32):
        s1, m_ = sl % 2, sl // 2
        for rep_ in range(2):
            P.dma(krf[rep_ * 32:(rep_ + 1) * 32, sl * 128:(sl + 1) * 128],
                  kva_sets_d[s1, 256:288, m_ * 128:(m_ + 1) * 128], w=[("krf", sl // 8)])
    for q4 in range(4):
        P.dve(lambda e, q4=q4: e.tensor_copy(out=kr4[:, q4 * 1024:(q4 + 1) * 1024], in_=krf[:, q4 * 1024:(q4 + 1) * 1024]),
              r=[("krf", q4)], w=["kr4"])
    for tc in range(8):
        cf = ckf[tc % 2]
        cb_ = ckb[tc % 2]
        for tb in range(4):
            sl = tc * 4 + tb
            s1, m_ = sl % 2, sl // 2
            for c in range(2):
                P.dma(cf[:, c, tb * 128:(tb + 1) * 128],
                      kva_sets_d[s1, c * 128:(c + 1) * 128, m_ * 128:(m_ + 1) * 128], w=[("ckf", tc % 2)])
        P.dve(lambda e, cf=cf, cb_=cb_: e.tensor_copy(out=cb_[:], in_=cf[:]), r=[("ckf", tc % 2)], w=[(("ckb", tc % 2), 0), (("ckb", tc % 2), 1)])
        kn = kns[tc % 2]
        for oc in range(8):
            b = nb()
            proj_fm(wkk, "wkk", oc * 128, cb_, ("ckb", tc % 2), 2, 512, b)
            P.act(lambda e, b=b, oc=oc, kn=kn: e.activation(out=kn[:, oc, :], in_=ps[b][:, :], func=AF.Copy), r=[PK(b)], w=[("kns", tc % 2, oc)])
            P.dma(knT_d[oc, :, tc * 512:(tc + 1) * 512], kn[:, oc, :], r=[("kns", tc % 2, oc)], w=[("knT_d", oc, tc)])
        for tb in range(4):
            kb = tc * 4 + tb
            vs = vst1[kb % 2]
            vv = vs[:].rearrange("p (h two) d -> p h two d", two=2)
            for half in range(2):
                b = nb()
                for kc in range(2):
                    P.pe(lambda e, kc=kc, tb=tb, b=b, half=half, cb_=cb_: e.matmul(
                        ps[b][:, :], lhsT=cb_[:, kc, tb * 128:(tb + 1) * 128], rhs=wkv[:, kc, half * 512:(half + 1) * 512],
                        start=(kc == 0), stop=(kc == 1)), r=["wkv", (("ckb", tc % 2), kc)], w=[PK(b)])
                pv = ps[b][:, :].rearrange("p (h two d) -> p h two d", two=2, d=64)
                P.act(lambda e, pv=pv, vv=vv, half=half: e.activation(out=vv[:, half * 4:(half + 1) * 4, 0, 0:64], in_=pv[:, :, 0, :], func=AF.Copy),
                      r=[PK(b)], w=[("vst1", kb % 2)])
                P.act(lambda e, pv=pv, vv=vv, half=half: e.activation(out=vv[:, half * 4:(half + 1) * 4, 1, 64:128], in_=pv[:, :, 1, :], func=AF.Copy),
                      r=[PK(b)], w=[("vst1", kb % 2)])
            P.dma(vaug1_d[kb], vs[:].rearrange("p h d -> p (h d)"), r=[("vst1", kb % 2)], w=[("vaug1_d", kb)])
    P.dma(kr4_d, kr4[:], r=["kr4"], w=["kr4_d"])
    P.barrier()
    P.release(base_mark)

    if stop("K1"):
        return nc
    mTs1 = P.sb("mTs1", [128, 2, 8, 512], BF16)
    for i in range(2):
        P.dma(mTs1[:, i], mT_d[i], w=["mTs1"])
    kh2 = [[P.sb(f"kh_{i}_{hh}", [96, T], BF16) for hh in range(2)] for i in range(2)]
    qh2 = [[P.sb(f"qh_{i}_{hh}", [96, NO], BF16) for hh in range(2)] for i in range(2)]
    vb2 = [P.sb(f"vb21_{i}", [128, 32, 256], BF16) for i in range(2)]

    def hp_loader1(hp):
        i = hp % 2
        key = ("kvq1", i)
        for hh in range(2):
            P.dma(kh2[i][hh][0:64, :], knT_d[hp, hh * 64:(hh + 1) * 64, :], r=[("knT_d", hp, t_) for t_ in range(8)], w=[key])
            P.dma(kh2[i][hh][64:96, :], kr4_d[0:32, :], r=["kr4_d"], w=[key])
            P.dma(qh2[i][hh][0:64, :], qnT_d[hp, hh * 64:(hh + 1) * 64, :], r=[("qnT_d", hp, t_) for t_ in range(4)], w=[key])
            P.dma(qh2[i][hh][64:96, :], qrT_d[hp, hh * 32:(hh + 1) * 32, :], r=[("qrT_d", hp, t_) for t_ in range(4)], w=[key])
        P.dma(vb2[i][:], vaug1_d[:, :, hp * 256:(hp + 1) * 256].rearrange("k p c -> p k c"),
              r=[("vaug1_d", k_) for k_ in range(32)], w=[key])
        return {"k": kh2[i], "q": qh2[i], "v": vb2[i], "kv": key}

    def st_emit1(bufs, h, hh, g, kb, sb_, c0=0):
        P.pe(lambda e: e.matmul(ps[sb_][:, c0:512], lhsT=bufs["k"][hh][0:96, kb * 128:(kb + 1) * 128],
                                rhs=bufs["q"][hh][0:96, g * 512 + c0:(g + 1) * 512], start=True, stop=True),
             r=[bufs["kv"]], w=[PK(sb_)])

    def mask_for1(bufs, g, kb):
        rel = kb - 8 * g
        if rel < 0:
            return None
        return mTs1[:, g // 2, rel, :], "mTs1"

    attention(16, hp_loader1, st_emit1, mask_for1, float(96 ** -0.5), attnT_d, "attnT1_d")
    P.barrier()
    P.release(base_mark)

    if stop("A1"):
        return nc
    in1 = [(attnT_d[c], [("attnT1_d", c)]) for c in range(8)]
    out_phase(in1, wo_d, 1, x2T_d[0], ("x2T_d", 0), x3T_d, "x3T_d")
    P.barrier()
    P.release(base_mark)
    ffn_phase(1, x3T_d, "x3T_d", outT, "outT")
    P.final_wait([("outT", tc) for tc in range(4)])
    P.emit()
    return nc


def _swap_cols(w, head, a, b_):
    n = w.shape[1]
    idx = np.arange(n).reshape(-1, head)
    perm = np.concatenate([idx[:, a:b_], idx[:, 0:a], idx[:, b_:]], axis=1).reshape(-1)
    return w[:, perm]


def prepare_inputs(inp, n_batch=4):
    f32 = np.float32
    x = np.asarray(inp["x"], f32)
    pos = np.asarray(inp["positions"]).astype(np.int32)
    w_in = np.asarray(inp["even_w_in"], f32)[0]
    offs = np.cumsum([0, 512, 512, 512, 1024, 64, 16, 512, 512, 512])
    wq_, wk_, wv_, wqi, wki, wwi, wgb, wgc, wxi = [w_in[:, offs[i]:offs[i + 1]] for i in range(9)]
    w0k = np.concatenate([wk_, _swap_cols(wk_, 64, 8, 16), wki, wki, _swap_cols(wki, 64, 8, 16), _swap_cols(wki, 64, 8, 16)], axis=1)
    w0q = np.concatenate([wq_, _swap_cols(wq_, 64, 8, 16), wqi, _swap_cols(wqi, 64, 8, 16), wgb, wgc, wxi], axis=1)
    w_uq = np.asarray(inp["odd_w_uq"], f32)[0]
    cols = np.arange(1536).reshape(16, 96)
    wuq_n = w_uq[:, cols[:, :64].reshape(-1)]
    wuq_r = w_uq[:, cols[:, 64:].reshape(-1)]
    wuq = np.concatenate([wuq_n, wuq_r, _swap_cols(wuq_r, 32, 16, 32)], axis=1)
    w_dkv = np.asarray(inp["odd_w_dkv"], f32)[0]
    wdkv = np.concatenate([w_dkv[:, :256], w_dkv[:, 256:], _swap_cols(w_dkv[:, 256:], 32, 16, 32)], axis=1)
    w_ukv = np.asarray(inp["odd_w_ukv"], f32)[0]
    c2 = np.arange(2048).reshape(16, 128)
    wukv_k = w_ukv[:, c2[:, :64].reshape(-1)]
    wukv_v = w_ukv[:, c2[:, 64:].reshape(-1)]

    cst = np.zeros((128, NCST), f32)
    kinds = ["norm_mix_pre", "norm_mix_post", "norm_ffn_pre", "norm_ffn_post"]
    for l in range(2):
        for k, nm in enumerate(kinds):
            cst[:, gcol(l, k):gcol(l, k) + 8] = np.asarray(inp[nm], f32)[l].reshape(8, 128).T
    cst[:, C_QN:C_QN + 3] = np.asarray(inp["odd_q_norm"], f32)[0].reshape(3, 128).T
    cst[:, C_KVN:C_KVN + 2] = np.asarray(inp["odd_kv_norm"], f32)[0].reshape(2, 128).T
    cw = np.asarray(inp["even_conv_w"], f32)[0]
    for j in range(3):
        cst[:, C_CW + j * 4:C_CW + j * 4 + 4] = cw[j].reshape(4, 128).T
    theta = 500000.0
    if0 = (theta ** (-np.arange(0, 16, 2, dtype=np.float32) / 16)).astype(f32)
    if1 = (theta ** (-np.arange(0, 32, 2, dtype=np.float32) / 32)).astype(f32)
    for p in range(128):
        r = p % 64
        if r < 16:
            cst[p, C_FR0] = if0[r % 8]
            cst[p, C_SG0] = -1.0 if r < 8 else 1.0
        r = p % 32
        cst[p, C_FR1] = if1[r % 16]
        cst[p, C_SG1] = -1.0 if r < 16 else 1.0

    shared = {
        "cst": cst, "w0k": w0k, "w0v": np.ascontiguousarray(wv_), "w0q": w0q, "w0wi": np.ascontiguousarray(wwi),
        "w_out": np.asarray(inp["even_w_out"], f32)[0], "w1": np.asarray(inp["mlp_w1"], f32),
        "w2": np.asarray(inp["mlp_w2"], f32), "w_dq": np.asarray(inp["odd_w_dq"], f32)[0], "w_uq": wuq,
        "w_dkv": wdkv, "w_ukv_k": np.ascontiguousarray(wukv_k), "w_ukv_v": np.ascontiguousarray(wukv_v),
        "w_o": np.asarray(inp["odd_w_o"], f32)[0],
    }
    shared = {k: np.ascontiguousarray(v, dtype=f32) for k, v in shared.items()}
    in_maps = []
    own_idx_all = []
    qi = np.arange(128)
    s_ = np.arange(1024)
    for b in range(n_batch):
        for par in range(2):
            xb = x[b]
            sets = [blocks_for(par), blocks_for(1 - par)]
            xT_sets, pos_sets, cbs = [], [], []
            kq = np.zeros((128, 32), f32)
            for si, blks in enumerate(sets):
                idx = np.concatenate([np.arange(128 * p, 128 * p + 128) for p in blks])
                halo = np.zeros((32, D), f32)
                for i, p in enumerate(blks):
                    if p > 0:
                        halo[2 * i:2 * i + 2] = xb[128 * p - 2:128 * p]
                xT_sets.append(np.concatenate([xb[idx], halo], axis=0).T)
                pos_sets.append(pos[b][idx][None, :])
                cb = np.zeros((8, 128, 1024), f32)
                for i, p in enumerate(blks):
                    kq[:, si * 16 + i] = np.minimum(256, 128 * p + qi + 1)
                for g in range(4):
                    for j in range(4):
                        rel = blks[4 * g + j] % 8
                        vis = s_[None, :] <= (rel * 128 + qi)[:, None]
                        cb[(g // 2) * 4 + j] = np.where(vis, 0.0, -1e30)
                cbs.append(cb)
            own = np.concatenate([np.arange(128 * p, 128 * p + 128) for p in sets[0]])
            own_idx_all.append(own)
            mT = np.zeros((2, 128, 8, 512), f32)
            for g in (0, 2):
                for j in range(4):
                    pq = sets[0][4 * g + j]
                    qpos = 128 * pq + qi
                    for rel in range(8):
                        sl = 8 * g + rel
                        pk = sets[sl % 2][sl // 2]
                        kpos = 128 * pk + qi
                        mT[g // 2, :, rel, j * 128:(j + 1) * 128] = (kpos[:, None] <= qpos[None, :])
            m = dict(shared)
            m.update({
                "xT_seq": np.ascontiguousarray(xb.T), "xT_own": np.ascontiguousarray(np.stack(xT_sets)),
                "pos_seq": np.ascontiguousarray(pos[b][None, :]), "pos_own": np.ascontiguousarray(np.stack(pos_sets)),
                "kq": kq, "cb": np.stack(cbs), "mT": mT.astype(ml_dtypes.bfloat16),
            })
            in_maps.append(m)
    return in_maps, own_idx_all


_NC_CACHE = {}


def kernel(**inputs):
    in_maps, own_idx = prepare_inputs(inputs, 4)
    if 8 not in _NC_CACHE:
        _NC_CACHE[8] = build_program(8)
    nc = _NC_CACHE[8]
    res = run_bass_kernel_spmd(nc, in_maps, core_ids=list(range(8)))
    out = np.zeros((4, T, D), np.float32)
    for c in range(8):
        b = c // 2
        out[b, own_idx[c], :] = np.asarray(res.results[c]["outT"], np.float32).T
    return out
```

```python
import types
import numpy as np
import ml_dtypes
import concourse.bass as bass
import concourse.mybir as mybir
from concourse.bass_utils import run_bass_kernel_spmd

F32 = mybir.dt.float32
BF16 = mybir.dt.bfloat16
I32 = mybir.dt.int32
AF = mybir.ActivationFunctionType
ALU = mybir.AluOpType
AX = mybir.AxisListType
DT_SIZE = {F32: 4, BF16: 2, I32: 4}

T = 4096
NO = 2048
D = 1024
EPS = 1e-6
NBIS = 18
NFILL = 1
NBURST = 10
NBURST_I = 10
NFILL_I = 0
TWO_PI = float(2 * np.pi)


class Op:
    __slots__ = ("eng", "fn", "reads", "writes", "is_dma", "deps", "needed", "ordinal",
                 "dsem", "dval", "barrier")

    def __init__(self, eng, fn, reads, writes, is_dma):
        self.eng = eng
        self.fn = fn
        self.reads = reads
        self.writes = writes
        self.is_dma = is_dma
        self.deps = []
        self.needed = False
        self.ordinal = None
        self.dsem = None
        self.dval = None
        self.barrier = False


class Prog:
    ENGS = ("pe", "act", "dve", "pool", "sp")
    SB_LIMIT = 228352

    def __init__(self, nc, n_dma_sems=12):
        self.nc = nc
        self.ops = []
        self.sb_off = 16896
        self.sb_max = 0
        self.n_dma_sems = n_dma_sems
        self._uid = 0
        self._bank = 0

    def sb(self, name, shape, dtype):
        nbytes = int(np.prod(shape[1:])) * DT_SIZE[dtype]
        nbytes = (nbytes + 63) // 64 * 64
        self._uid += 1
        t = self.nc.alloc_sbuf_tensor_at(f"{name}_{self._uid}", list(shape), dtype, offset=self.sb_off)
        self.sb_off += nbytes
        self.sb_max = max(self.sb_max, self.sb_off)
        assert self.sb_off <= self.SB_LIMIT, f"SBUF overflow {self.sb_off} at {name}"
        return t

    def mark(self):
        return self.sb_off

    def release(self, m):
        self.sb_off = m

    @staticmethod
    def _freeze(fn):
        if getattr(fn, "__closure__", None) is None:
            return fn
        cells = []
        for c in fn.__closure__:
            try:
                cells.append(types.CellType(c.cell_contents))
            except ValueError:
                cells.append(c)
        return types.FunctionType(fn.__code__, fn.__globals__, fn.__name__, fn.__defaults__, tuple(cells))

    def add(self, eng, fn, r=(), w=(), dma=False):
        fn = self._freeze(fn)
        o = Op(eng, fn, tuple(r), tuple(w), dma)
        self.ops.append(o)
        return o

    def pe(self, fn, r=(), w=()):
        return self.add("pe", fn, r, w)

    def act(self, fn, r=(), w=()):
        return self.add("act", fn, r, w)

    def dve(self, fn, r=(), w=()):
        return self.add("dve", fn, r, w)

    def pool(self, fn, r=(), w=()):
        return self.add("pool", fn, r, w)

    def dma(self, out, in_, r=(), w=(), q="sp", **kw):
        return self.add(q, lambda e: e.dma_start(out=out, in_=in_, **kw), r, w, dma=True)

    def final_wait(self, keys):
        return self.add("sp", lambda e: e.nop(), r=keys, w=())

    def barrier(self):
        o = Op(None, None, (), (), False)
        o.barrier = True
        self.ops.append(o)

    def finalize(self):
        last_w = {}
        readers = {}
        since_barrier = []
        pending_barrier = None
        seen_after = set()
        for o in self.ops:
            if o.barrier:
                summ = []
                lastc = {}
                for p in since_barrier:
                    if p.is_dma:
                        summ.append(p)
                    else:
                        lastc[p.eng] = p
                summ.extend(lastc.values())
                if pending_barrier is not None:
                    summ.extend(pending_barrier)
                pending_barrier = summ
                seen_after = set()
                since_barrier = []
                continue
            deps = []
            if pending_barrier is not None and o.eng not in seen_after:
                deps.extend(pending_barrier)
                seen_after.add(o.eng)
            for k in o.reads:
                if k in last_w:
                    deps.append(last_w[k])
            for k in o.writes:
                if k in last_w:
                    deps.append(last_w[k])
                deps.extend(readers.get(k, ()))
            for k in o.reads:
                readers.setdefault(k, []).append(o)
            for k in o.writes:
                last_w[k] = o
                readers[k] = []
            dd = []
            seen = set()
            for d in deps:
                if d is o or id(d) in seen:
                    continue
                seen.add(id(d))
                if (not d.is_dma) and (not o.is_dma) and d.eng == "pe" and o.eng == "pe":
                    continue
                dd.append(d)
            o.deps = dd
            for d in dd:
                d.needed = True
            since_barrier.append(o)
        cnt = {e: 0 for e in self.ENGS}
        dma_rr = {e: 0 for e in self.ENGS}
        dma_uses = {}
        for o in self.ops:
            if o.barrier:
                continue
            if o.is_dma:
                slot = dma_rr[o.eng] % self.n_dma_sems
                dma_rr[o.eng] += 1
                key = (o.eng, slot)
                dma_uses[key] = dma_uses.get(key, 0) + 1
                o.dsem = key
                o.dval = 16 * dma_uses[key]
            elif o.needed:
                cnt[o.eng] += 1
                o.ordinal = cnt[o.eng]
        self.max_ord = dict(cnt)

    def emit(self):
        nc = self.nc
        self.finalize()
        from contextlib import ExitStack
        es = ExitStack()
        sems = {}
        for e in ("pe", "act", "dve", "pool", "sp"):
            sems[e] = es.enter_context(nc.semaphore(f"c_{e}"))
        dsems = {}
        used = sorted({o.dsem for o in self.ops if (not o.barrier) and o.is_dma})
        for key in used:
            dsems[key] = es.enter_context(nc.semaphore(f"d_{key[0]}_{key[1]}"))
        block = es.enter_context(nc.Block())
        per_eng = {e: [o for o in self.ops if (not o.barrier) and o.eng == e] for e in self.ENGS}

        def body(ename, engine):
            known = {}
            for o in per_eng[ename]:
                waits = {}
                for d in o.deps:
                    if d.is_dma:
                        s, v, k = dsems[d.dsem], d.dval, ("d",) + d.dsem
                    else:
                        s, v, k = sems[d.eng], d.ordinal, ("c", d.eng)
                    if v > waits.get(k, (None, 0))[1]:
                        waits[k] = (s, v)
                if o.is_dma and o.dval > 16:
                    k = ("d",) + o.dsem
                    v = o.dval - 16
                    if v > waits.get(k, (None, 0))[1]:
                        waits[k] = (dsems[o.dsem], v)
                for k, (s, v) in waits.items():
                    if known.get(k, 0) >= v:
                        continue
                    engine.wait_ge(s, v)
                    known[k] = v
                ins = o.fn(engine)
                if o.is_dma:
                    ins.then_inc(dsems[o.dsem], 16)
                elif o.needed:
                    ins.then_inc(sems[ename], 1)

        @block.tensor
        def _(e):
            body("pe", e)

        @block.scalar
        def _(e):
            body("act", e)

        @block.vector
        def _(e):
            body("dve", e)

        @block.gpsimd
        def _(e):
            body("pool", e)

        @block.sync
        def _(e):
            body("sp", e)

        es.close()


def blocks_for(par):
    lo = list(range(par, 16, 2))
    hi = sorted(31 - j for j in lo)
    return lo + hi


C_G = 0
C_QN = 64
C_KVN = 67
C_CW = 69
C_FR0 = 81
C_SG0 = 82
C_FR1 = 83
C_SG1 = 84
NCST = 96


def gcol(layer, kind):
    return C_G + (layer * 4 + kind) * 8


def build_program(n_cores, dbg=(), no_cc=False, stop_after=None):
    nc = bass.Bass("TRN2", target_bir_lowering=False)
    P = Prog(nc)

    def stop(name):
        if stop_after == name:
            P.barrier()
            P.final_wait([])
            P.emit()
            return True
        return False

    def din(name, shape, dt=F32):
        return nc.dram_tensor(name, list(shape), dt, kind="ExternalInput").ap()

    def dscr(name, shape, dt):
        kind = "ExternalOutput" if name in dbg else "Internal"
        return nc.dram_tensor(name, list(shape), dt, kind=kind).ap()

    xT_seq = din("xT_seq", [D, T])
    xT_own = din("xT_own", [2, D, NO + 32])
    pos_seq = din("pos_seq", [1, T], I32)
    pos_own = din("pos_own", [2, 1, NO], I32)
    cst_d = din("cst", [128, NCST])
    kq_d = din("kq", [128, 32])
    cb_d = din("cb", [2, 8, 128, 1024])
    mT_d = din("mT", [2, 128, 8, 512], BF16)
    w0k_d = din("w0k", [D, 1280])
    w0v_d = din("w0v", [D, 512])
    w0q_d = din("w0q", [D, 4608])
    w0wi_d = din("w0wi", [D, 16])
    wout_d = din("w_out", [D, D])
    w1_d = din("w1", [2, D, 4096])
    w2_d = din("w2", [2, 4096, D])
    wdq_d = din("w_dq", [D, 384])
    wuq_d = din("w_uq", [384, 2048])
    wdkv_d = din("w_dkv", [D, 320])
    wukvk_d = din("w_ukv_k", [256, 1024])
    wukvv_d = din("w_ukv_v", [256, 1024])
    wo_d = din("w_o", [D, D])
    outT = nc.dram_tensor("outT", [D, NO], F32, kind="ExternalOutput").ap()

    kT_d = dscr("kT_d", [4, 128, T], BF16)
    kidxT_d = dscr("kidxT_d", [128, T], BF16)
    vaug_d = dscr("vaug_d", [32, 128, 1024], BF16)
    qT_d = dscr("qT_d", [4, 128, NO], BF16)
    qidxT_d = dscr("qidxT_d", [8, 128, NO], BF16)
    convT_d = dscr("convT_d", [4, 128, NO], BF16)
    maskT_d = dscr("maskT_d", [4, 128, 32, 512], BF16)
    attnT_d = dscr("attnT_d", [8, 128, NO], BF16)
    x1T_d = dscr("x1T_d", [D, NO], F32)
    x2T_d = dscr("x2T_d", [2, D, NO], F32)
    x3T_d = dscr("x3T_d", [D, NO], F32)
    kva_sets_d = dscr("kva_sets_d", [2, 288, NO], F32)
    qnT_d = dscr("qnT_d", [8, 128, NO], BF16)
    qrT_d = dscr("qrT_d", [8, 64, NO], BF16)
    knT_d = dscr("knT_d", [8, 128, T], BF16)
    vaug1_d = dscr("vaug1_d", [32, 128, 2048], BF16)
    dbg_d = dscr("dbg_d", [128, 4096], F32)
    kr4_d = dscr("kr4_d", [64, T], BF16)

    ps = [nc.alloc_psum_tensor(f"ps{i}", [128, 512], F32) for i in range(8)]

    def PK(i):
        return ("ps", i)

    cst = P.sb("cst", [128, NCST], F32)
    kq = P.sb("kq", [128, 32], F32)
    widx = P.sb("widx", [128, 16, 16], F32)
    ones_b = P.sb("ones_b", [128, 128], BF16)
    ones_q = P.sb("ones_q", [128, 128], BF16)
    ident = P.sb("ident", [128, 128], F32)
    P.dma(cst[:], cst_d, w=["cst"])
    P.dma(kq[:], kq_d, w=["kq"])
    P.pool(lambda e: e.memset(ones_b[:], 1.0 / 1024), w=["ones_b"])
    P.pool(lambda e: e.memset(ones_q[:], 1.0 / 256), w=["ones_q"])
    P.pool(lambda e: e.memset(ident[:], 1.0), w=["ident"])
    P.pool(lambda e: e.affine_select(out=ident[:], in_=ident[:], pattern=[[-1, 128]], compare_op=ALU.is_equal,
                                     fill=0.0, base=0, channel_multiplier=1), r=["ident"], w=["ident"])
    base_mark = P.mark()

    uid = [0]

    def U(s):
        uid[0] += 1
        return f"{s}#{uid[0]}"

    def rope_tables(pos_d, n, fr_col, sg_col, tag):
        C = P.sb(tag + "C", [128, n], F32)
        S = P.sb(tag + "S", [128, n], F32)
        m = P.mark()
        pi_ = P.sb("posi", [128, n], I32)
        pf = P.sb("posf", [128, n], F32)
        tmp = P.sb("rtmp", [128, n], F32)
        ki = P.sb("rki", [128, n], I32)
        kpi, kpf, kt, kk = U("posi"), U("posf"), U("rtmp"), U("rki")
        kC, kS = tag + "C", tag + "S"
        P.dma(pi_[:], pos_d.to_broadcast([128, n]), w=[kpi])
        P.dve(lambda e: e.tensor_copy(out=pf[:], in_=pi_[:]), r=[kpi], w=[kpf])
        P.dve(lambda e: e.tensor_scalar(out=pf[:], in0=pf[:], scalar1=cst[:, fr_col:fr_col + 1], scalar2=None,
                                        op0=ALU.mult), r=[kpf, "cst"], w=[kpf])
        for which, dst, kd in (("s", S, kS), ("c", C, kC)):
            off = 0.0 if which == "s" else float(np.pi / 2)
            P.dve(lambda e, off=off: e.tensor_scalar(out=tmp[:], in0=pf[:], scalar1=off, scalar2=1.0 / TWO_PI,
                                                     op0=ALU.add, op1=ALU.mult), r=[kpf], w=[kt])
            P.dve(lambda e: e.tensor_copy(out=ki[:], in_=tmp[:]), r=[kt], w=[kk])
            P.dve(lambda e: e.tensor_copy(out=tmp[:], in_=ki[:]), r=[kk], w=[kt])
            P.dve(lambda e: e.scalar_tensor_tensor(out=tmp[:], in0=tmp[:], scalar=-TWO_PI, in1=pf[:],
                                                   op0=ALU.mult, op1=ALU.add), r=[kt, kpf], w=[kt])
            P.dve(lambda e, off=off: e.tensor_scalar(out=tmp[:], in0=tmp[:], scalar1=off, scalar2=None,
                                                     op0=ALU.add), r=[kt], w=[kt])
            P.dve(lambda e, dst=dst: e.tensor_scalar(out=dst[:], in0=tmp[:], scalar1=float(np.pi), scalar2=-TWO_PI,
                                                     op0=ALU.is_gt, op1=ALU.mult), r=[kt], w=[kd])
            P.dve(lambda e, dst=dst: e.tensor_tensor(out=tmp[:], in0=tmp[:], in1=dst[:], op=ALU.add), r=[kt, kd], w=[kt])
            P.dve(lambda e, dst=dst: e.tensor_scalar(out=dst[:], in0=tmp[:], scalar1=-float(np.pi), scalar2=TWO_PI,
                                                     op0=ALU.is_lt, op1=ALU.mult), r=[kt], w=[kd])
            P.dve(lambda e, dst=dst: e.tensor_tensor(out=tmp[:], in0=tmp[:], in1=dst[:], op=ALU.add), r=[kt, kd], w=[kt])
            P.dve(lambda e: e.tensor_scalar(out=tmp[:], in0=tmp[:], scalar1=-3.14159, scalar2=3.14159,
                                            op0=ALU.max, op1=ALU.min), r=[kt], w=[kt])
            P.act(lambda e, dst=dst: e.activation(out=dst[:], in_=tmp[:], func=AF.Sin), r=[kt], w=[kd])
        P.dve(lambda e: e.tensor_scalar(out=S[:], in0=S[:], scalar1=cst[:, sg_col:sg_col + 1], scalar2=None,
                                        op0=ALU.mult), r=[kS, "cst"], w=[kS])
        P.barrier()
        P.release(m)
        return C, S

    def load_w(dst, src_ap, key, nsplit=1):
        P.dma(dst, src_ap, w=[key], q="pool")

    def rmsnorm_fm(xs, kx, nchunk, n, gain_col, hout, kh, onesm, sq, ksq, rstd, krs, ssbank, eps=EPS):
        P.act(lambda e: e.activation(out=sq[:, 0:nchunk, 0:n], in_=xs[:, 0:nchunk, 0:n], func=AF.Square),
              r=[kx], w=[ksq])
        for c in range(nchunk):
            P.pe(lambda e, c=c: e.matmul(ps[ssbank][:, 0:n], lhsT=onesm[:], rhs=sq[:, c, 0:n],
                                         start=(c == 0), stop=(c == nchunk - 1)),
                 r=[ksq, "ones_b", "ones_q"], w=[PK(ssbank)])
        P.act(lambda e: e.activation(out=rstd[:, 0:n], in_=ps[ssbank][:, 0:n], func=AF.Sqrt, bias=eps, scale=1.0),
              r=[PK(ssbank)], w=[krs])
        P.dve(lambda e: e.reciprocal(out=rstd[:, 0:n], in_=rstd[:, 0:n]), r=[krs], w=[krs])
        for c in range(nchunk):
            P.dve(lambda e, c=c: e.scalar_tensor_tensor(out=hout[:, c, 0:n], in0=xs[:, c, 0:n],
                                                      scalar=cst[:, gain_col + c:gain_col + c + 1],
                                                      in1=rstd[:, 0:n], op0=ALU.mult, op1=ALU.mult),
                r=[kx, krs, "cst"], w=[(kh, c)])

    bankrot = [0]

    def nb():
        b = bankrot[0] % 4
        bankrot[0] += 1
        return b

    def proj_fm(wt, kw, oc0, h, kh, nk, n, bank, m=128):
        for kc in range(nk):
            P.pe(lambda e, kc=kc: e.matmul(ps[bank][0:m, 0:n], lhsT=wt[:, kc, oc0:oc0 + m], rhs=h[:, kc, 0:n],
                                           start=(kc == 0), stop=(kc == nk - 1)),
                 r=[kw, (kh, kc)], w=[PK(bank)])

    rope_mark = P.mark()
    C0s, S0s = rope_tables(pos_seq, T, C_FR0, C_SG0, "r0s")

    wk = P.sb("wk", [128, 8, 1280], BF16)
    wv = P.sb("wv", [128, 8, 512], BF16)
    load_w(wk[:], w0k_d.rearrange("(c p) n -> p c n", p=128), "wk")
    load_w(wv[:], w0v_d.rearrange("(c p) n -> p c n", p=128), "wv")
    xs2 = [P.sb(f"xs{i}", [128, 8, 512], F32) for i in range(2)]
    sq = P.sb("sq", [128, 8, 512], BF16)
    hb = P.sb("hb", [128, 8, 512], BF16)
    rstd = P.sb("rstd", [128, 512], F32)
    t1 = [P.sb(f"t1_{i}", [128, 512], F32) for i in range(2)]
    t2 = [P.sb(f"t2_{i}", [128, 512], F32) for i in range(2)]
    kst = [P.sb(f"kst{i}", [128, 5, 512], BF16) for i in range(2)]
    vst = [P.sb(f"vst{i}", [128, 8, 128], BF16) for i in range(2)]
    for i in range(2):
        P.pool(lambda e, i=i: e.memset(vst[i][:], 1.0), w=[("vst", i)])
    xseq_v = xT_seq.rearrange("(c p) t -> p c t", p=128)
    tcnt = [0]

    def rope_evac(bA, bB, Ct, St, col0, n, dst, kdst, kC="r0sC", kS="r0sS"):
        i = tcnt[0] % 2
        tcnt[0] += 1
        P.dve(lambda e: e.tensor_tensor(out=t1[i][:, 0:n], in0=ps[bA][:, 0:n], in1=Ct[:, col0:col0 + n], op=ALU.mult),
              r=[PK(bA), kC], w=[("t1", i)])
        P.dve(lambda e: e.tensor_tensor(out=t2[i][:, 0:n], in0=ps[bB][:, 0:n], in1=St[:, col0:col0 + n], op=ALU.mult),
              r=[PK(bB), kS], w=[("t2", i)])
        P.pool(lambda e: e.tensor_tensor(out=dst, in0=t1[i][:, 0:n], in1=t2[i][:, 0:n], op=ALU.add),
               r=[("t1", i), ("t2", i)], w=[kdst])

    for tc in range(8):
        xs = xs2[tc % 2]
        kx = ("xs", tc % 2)
        P.dma(xs[:], xseq_v[:, :, tc * 512:(tc + 1) * 512], w=[kx])
        rmsnorm_fm(xs, kx, 8, 512, gcol(0, 0), hb, "hb", ones_b, sq, "sq", rstd, "rstd", 7)
        if tc == 0 and "dbg_d" in dbg:
            dtmp = P.sb("dtmp", [128, 1024], F32)
            P.dma(dbg_d[:, 0:512], xs[:, 0, :], r=[kx], w=["dbg0"])
            P.dma(dbg_d[:, 512:1024], rstd[:], r=["rstd"], w=["dbg1"])
            P.dve(lambda e: e.tensor_copy(out=dtmp[:, 0:512], in_=hb[:, 0, :]), r=[("hb", 0)], w=["dtmp"])
            P.dve(lambda e: e.tensor_copy(out=dtmp[:, 512:1024], in_=sq[:, 0, :]), r=["sq"], w=["dtmp"])
            P.dma(dbg_d[:, 1024:2048], dtmp[:], r=["dtmp"], w=["dbg2"])
            dt2 = P.sb("dt2", [128, 512], F32)
            P.act(lambda e: e.activation(out=dt2[:], in_=ps[7][:, :], func=AF.Copy), r=[PK(7)], w=["dt2"])
            P.dma(dbg_d[:, 2048:2560], dt2[:], r=["dt2"], w=["dbg3"])
            P.dma(dbg_d[:, 2560:3072], S0s[:, 0:512], r=["r0sS"], w=["dbg4"])
            P.dma(dbg_d[:, 3072:3584], C0s[:, 3584:4096], r=["r0sC"], w=["dbg5"])
            P.dma(dbg_d[:, 3584:4096], S0s[:, 3584:4096], r=["r0sS"], w=["dbg6"])
        ks = kst[tc % 2]
        for oc in range(5):
            bA, bB = nb(), nb()
            colA = oc * 128 if oc < 4 else 1024
            colB = 512 + oc * 128 if oc < 4 else 1152
            proj_fm(wk, "wk", colA, hb, "hb", 8, 512, bA)
            proj_fm(wk, "wk", colB, hb, "hb", 8, 512, bB)
            rope_evac(bA, bB, C0s, S0s, tc * 512, 512, ks[:, oc, :], ("kst", tc % 2, oc))
        for oc in range(4):
            P.dma(kT_d[oc, :, tc * 512:(tc + 1) * 512], ks[:, oc, :], r=[("kst", tc % 2, oc)], w=[("kT_d", oc, tc)])
        P.dma(kidxT_d[:, tc * 512:(tc + 1) * 512], ks[:, 4, :], r=[("kst", tc % 2, 4)], w=[("kidxT_d", tc)])
        for tb in range(4):
            kb = tc * 4 + tb
            b = nb()
            for kc in range(8):
                P.pe(lambda e, kc=kc, tb=tb: e.matmul(ps[b][:, :], lhsT=hb[:, kc, tb * 128:(tb + 1) * 128],
                                                      rhs=wv[:, kc, :], start=(kc == 0), stop=(kc == 7)),
                     r=["wv", ("hb", kc)], w=[PK(b)])
            vs = vst[kb % 2]
            pv = ps[b][:, :].rearrange("p (h two d) -> p h two d", two=2, d=64)
            vv = vs[:].rearrange("p (h two) d -> p h two d", two=2)
            P.act(lambda e, pv=pv, vv=vv: e.activation(out=vv[:, :, 0, 0:64], in_=pv[:, :, 0, :], func=AF.Copy),
                  r=[PK(b)], w=[("vst", kb % 2)])
            P.act(lambda e, pv=pv, vv=vv: e.activation(out=vv[:, :, 1, 64:128], in_=pv[:, :, 1, :], func=AF.Copy),
                  r=[PK(b)], w=[("vst", kb % 2)])
            P.dma(vaug_d[kb], vs[:].rearrange("p h d -> p (h d)"), r=[("vst", kb % 2)], w=[("vaug_d", kb)])
    P.barrier()
    P.release(rope_mark)

    for s_ in range(2):
        C0o, S0o = rope_tables(pos_own[s_], NO, C_FR0, C_SG0, "r0o")
        set_mark = P.mark()
        if stop("K"):
            return nc
        wq = P.sb("wq", [128, 8, 4608], BF16)
        wwi = P.sb("wwi", [128, 8, 16], BF16)
        for c in range(8):
            P.dma(wq[:, c, :], w0q_d[c * 128:(c + 1) * 128, :], w=["wq"], q="pool")
        load_w(wwi[:], w0wi_d.rearrange("(c p) n -> p c n", p=128), "wwi")
        uext = P.sb("uext", [128, 4, 16, 130], F32)
        gbs = P.sb("gbs", [128, 4, NO], BF16)
        q_mark = P.mark()
        xs2 = [P.sb("xsq", [128, 8, 512], F32)] * 2
        sq = P.sb("sq", [128, 8, 512], BF16)
        hb = P.sb("hb", [128, 8, 512], BF16)
        rstd = P.sb("rstd", [128, 512], F32)
        t1 = [P.sb(f"t1_{i}", [128, 512], F32) for i in range(2)]
        t2 = [P.sb(f"t2_{i}", [128, 512], F32) for i in range(2)]
        qrot = [P.sb(f"qrot{i}", [128, 512], BF16) for i in range(6)]
        gcs = [P.sb(f"gcs{i}", [128, 512], F32) for i in range(2)]
        qrc = [0]
        xown_v = xT_own[s_].rearrange("(c p) t -> p c t", p=128)
        for tc in range(5):
            n = 512 if tc < 4 else 32
            xs = xs2[0]
            kx = ("xs", 0)
            P.dma(xs[:, :, 0:n], xown_v[:, :, tc * 512:tc * 512 + n], w=[kx])
            rmsnorm_fm(xs, kx, 8, n, gcol(0, 0), hb, "hb", ones_b, sq, "sq", rstd, "rstd", 7)
            if tc < 4:
                for oc in range(12):
                    bA, bB = nb(), nb()
                    colA = oc * 128 if oc < 4 else 1024 + (oc - 4) * 128
                    colB = 512 + oc * 128 if oc < 4 else 2048 + (oc - 4) * 128
                    proj_fm(wq, "wq", colA, hb, "hb", 8, 512, bA)
                    proj_fm(wq, "wq", colB, hb, "hb", 8, 512, bB)
                    qi_ = qrc[0] % 6
                    qrc[0] += 1
                    rope_evac(bA, bB, C0o, S0o, tc * 512, 512, qrot[qi_][:], ("qrot", qi_), "r0oC", "r0oS")
                    if oc < 4:
                        P.dma(qT_d[oc, :, tc * 512:(tc + 1) * 512], qrot[qi_][:], r=[("qrot", qi_)], w=[("qT_d", oc, tc)])
                    else:
                        P.dma(qidxT_d[oc - 4, :, tc * 512:(tc + 1) * 512], qrot[qi_][:], r=[("qrot", qi_)],
                              w=[("qidxT_d", oc - 4, tc)])
                for cc in range(4):
                    b = nb()
                    proj_fm(wq, "wq", 3072 + cc * 128, hb, "hb", 8, 512, b)
                    P.act(lambda e, b=b, cc=cc: e.activation(out=gbs[:, cc, tc * 512:(tc + 1) * 512], in_=ps[b][:, :], func=AF.Copy),
                          r=[PK(b)], w=[("gbs", cc)])
                for tb in range(4):
                    b = nb()
                    for kc in range(8):
                        P.pe(lambda e, kc=kc, tb=tb, b=b: e.matmul(ps[b][:, 0:16], lhsT=hb[:, kc, tb * 128:(tb + 1) * 128],
                                                                   rhs=wwi[:, kc, :], start=(kc == 0), stop=(kc == 7)),
                             r=["wwi", ("hb", kc)], w=[PK(b)])
                    P.act(lambda e, b=b, tb=tb: e.activation(out=widx[:, tc * 4 + tb, :], in_=ps[b][:, 0:16], func=AF.Copy),
                          r=[PK(b)], w=["widx"])
            for cc in range(4):
                bA, bB = nb(), nb()
                proj_fm(wq, "wq", 3584 + cc * 128, hb, "hb", 8, n, bA)
                proj_fm(wq, "wq", 4096 + cc * 128, hb, "hb", 8, n, bB)
                g = gcs[cc % 2]
                P.act(lambda e, g=g, bA=bA: e.activation(out=g[:, 0:n], in_=ps[bA][:, 0:n], func=AF.Copy),
                      r=[PK(bA)], w=[("gcs", cc % 2)])
                if tc < 4:
                    o_ap = uext[:, cc, tc * 4:(tc + 1) * 4, 2:130]
                    i0 = ps[bB][:, :].rearrange("p (b t) -> p b t", t=128)
                    i1 = g[:].rearrange("p (b t) -> p b t", t=128)
                else:
                    o_ap = uext[:, cc, :, 0:2]
                    i0 = ps[bB][:, 0:32].rearrange("p (b t) -> p b t", t=2)
                    i1 = g[:, 0:32].rearrange("p (b t) -> p b t", t=2)
                P.dve(lambda e, o_ap=o_ap, i0=i0, i1=i1: e.tensor_tensor(out=o_ap, in0=i0, in1=i1, op=ALU.mult),
                      r=[PK(bB), ("gcs", cc % 2)], w=[("uext", cc)])
        P.barrier()
        P.release(q_mark)
        cacc = [P.sb(f"cacc{i}", [128, 16, 128], F32) for i in range(2)]
        cvo = [P.sb(f"cvo{i}", [128, NO], BF16) for i in range(2)]
        for cc in range(4):
            a = cacc[cc % 2]
            ka = ("cacc", cc % 2)
            P.dve(lambda e, a=a, cc=cc: e.tensor_scalar(out=a[:], in0=uext[:, cc, :, 2:130],
                                                        scalar1=cst[:, C_CW + 8 + cc:C_CW + 9 + cc], scalar2=None, op0=ALU.mult),
                  r=[("uext", cc), "cst"], w=[ka])
            P.dve(lambda e, a=a, cc=cc: e.scalar_tensor_tensor(out=a[:], in0=uext[:, cc, :, 1:129],
                                                               scalar=cst[:, C_CW + 4 + cc:C_CW + 5 + cc], in1=a[:],
                                                               op0=ALU.mult, op1=ALU.add), r=[("uext", cc), "cst", ka], w=[ka])
            P.dve(lambda e, a=a, cc=cc: e.scalar_tensor_tensor(out=a[:], in0=uext[:, cc, :, 0:128],
                                                               scalar=cst[:, C_CW + cc:C_CW + 1 + cc], in1=a[:],
                                                               op0=ALU.mult, op1=ALU.add), r=[("uext", cc), "cst", ka], w=[ka])
            co = cvo[cc % 2]
            P.dve(lambda e, a=a, cc=cc, co=co: e.tensor_tensor(out=co[:], in0=a[:].rearrange("p b t -> p (b t)"),
                                                               in1=gbs[:, cc, :], op=ALU.mult),
                  r=[ka, ("gbs", cc)], w=[("cvo", cc % 2)])
            P.dma(convT_d[cc], co[:], r=[("cvo", cc % 2)], w=[("convT_d", cc)])
        P.barrier()
        P.release(base_mark)

        if stop("Q"):
            return nc
        kidx = P.sb("kidx", [128, T], BF16)
        qidx = P.sb("qidx", [128, 8, NO], BF16)
        P.dma(kidx[:], kidxT_d, r=[("kidxT_d", t_) for t_ in range(8)], w=["kidx"])
        for oc in range(8):
            P.dma(qidx[:, oc, :], qidxT_d[oc], r=[("qidxT_d", oc, t_) for t_ in range(4)], w=[("qidx", oc)])
        sc4 = [P.sb(f"sc{i}", [128, T], F32) for i in range(4)]
        junk = P.sb("junk", [128, T], BF16)
        m01 = P.sb("m01", [128, T], F32)
        cbt = [P.sb(f"cbt{i}", [128, 1024], F32) for i in range(2)]
        rr = [P.sb(f"rr{i}", [128, 512], BF16) for i in range(6)]
        mTs = [P.sb(f"mTs{i}", [128, 32, 512], BF16) for i in range(1)]
        dg2 = [P.sb(f"dg{i}", [128, 16, 128], BF16) for i in range(2)]
        identb = P.sb("identb", [128, 128], BF16)
        sm2 = [P.sb(f"sm{i}", [128, 16], F32) for i in range(2)]
        stepT2 = [P.sb(f"stepT{i}", [128, 2, NBIS + 1], F32) for i in range(2)]
        P.dve(lambda e: e.tensor_copy(out=identb[:], in_=ident[:]), r=["ident"], w=["identb"])
        rcnt = [0]
        acnt_i = [0]
        mt = mTs[0]

        def acc_half(g, hf, hi):
            nk = 1024 * (g + 1)
            nch = nk // 512
            for jj in range(2):
                j = 2 * hf + jj
                qi = 4 * g + j
                sc = sc4[j]
                ksc = ("sc", j)
                dg = dg2[qi % 2]
                kdg = ("dg", qi % 2)
                for h in range(16):
                    P.pool(lambda e, dg=dg, h=h, qi=qi: e.tensor_scalar(out=dg[:, h, :], in0=identb[:], scalar1=widx[:, qi, h:h + 1],
                                                                        scalar2=None, op0=ALU.mult),
                           r=["identb", "widx"], w=[kdg])
                for ch in range(nch):
                    accb = 4 + (acnt_i[0] % 2)
                    acnt_i[0] += 1
                    pend = []
                    for _f in range(NBURST_I):
                        P.pe(lambda e: e.matmul(ps[7][:, :], lhsT=identb[:], rhs=junk[:, 0:512], start=True, stop=True))
                    for h in range(16):
                        b = nb()
                        base = (h % 2) * 64
                        if NFILL_I and h % 2 == 1:
                            P.pe(lambda e: e.matmul(ps[7][:, :], lhsT=identb[:], rhs=junk[:, 0:512], start=True, stop=True))
                        P.pe(lambda e, b=b, h=h, base=base, ch=ch, qi=qi: e.matmul(
                            ps[b][:, :], lhsT=qidx[base:base + 64, h // 2, qi * 128:(qi + 1) * 128],
                            rhs=kidx[base:base + 64, ch * 512:(ch + 1) * 512], start=True, stop=True),
                            r=["kidx", ("qidx", h // 2)], w=[PK(b)])
                        ri = rcnt[0] % 6
                        rcnt[0] += 1
                        r_ = rr[ri]
                        P.act(lambda e, b=b, r_=r_: e.activation(out=r_[:], in_=ps[b][:, :], func=AF.Relu),
                              r=[PK(b)], w=[("rr", ri)])
                        pend.append((h, r_, ri))
                        if len(pend) > 2:
                            h0, r0, ri0 = pend.pop(0)
                            P.pe(lambda e, h0=h0, r0=r0, accb=accb, dg=dg: e.matmul(ps[accb][:, :], lhsT=dg[:, h0, :], rhs=r0[:],
                                                                                    start=(h0 == 0), stop=(h0 == 15)),
                                 r=[("rr", ri0), kdg], w=[PK(accb)])
                    for h0, r0, ri0 in pend:
                        P.pe(lambda e, h0=h0, r0=r0, accb=accb, dg=dg: e.matmul(ps[accb][:, :], lhsT=dg[:, h0, :], rhs=r0[:],
                                                                                start=(h0 == 0), stop=(h0 == 15)),
                             r=[("rr", ri0), kdg], w=[PK(accb)])
                    P.act(lambda e, sc=sc, ch=ch, accb=accb: e.activation(out=sc[:, ch * 512:(ch + 1) * 512], in_=ps[accb][:, :], func=AF.Copy),
                          r=[PK(accb)], w=[(ksc, ch)])

        def prep_half(g, hf, hi):
            nk = 1024 * (g + 1)
            nch = nk // 512
            sm = sm2[hi % 2]
            stepT = stepT2[hi % 2]
            for jj in range(2):
                j = 2 * hf + jj
                qi = 4 * g + j
                sc = sc4[j]
                allsc = [(("sc", j), ch) for ch in range(nch)]
                cb = cbt[jj]
                P.dma(cb[:], cb_d[s_, (g // 2) * 4 + j], w=[("cbt", jj)])
                P.dve(lambda e, sc=sc, jj=jj, sm=sm: e.tensor_reduce(out=sm[:, jj:jj + 1], in_=sc[:, 0:nk], axis=AX.X, op=ALU.max,
                                                                     apply_absolute_value=True), r=allsc, w=[("smA", hi % 2, jj)])
                P.dve(lambda e, sc=sc, cb=cb: e.tensor_tensor(out=sc[:, nk - 1024:nk], in0=sc[:, nk - 1024:nk], in1=cb[:],
                                                              op=ALU.add), r=allsc + [("cbt", jj), ("smA", hi % 2, jj)], w=allsc)
            allA = [("smA", hi % 2, jj) for jj in range(2)]
            P.dve(lambda e, sm=sm: e.tensor_single_scalar(out=sm[:, 0:2].bitcast(I32), in_=sm[:, 0:2].bitcast(I32),
                                                          scalar=0x7F800000, op=ALU.bitwise_and), r=allA, w=[("smA2", hi % 2)])
            P.dve(lambda e, sm=sm: e.tensor_scalar(out=sm[:, 0:2], in0=sm[:, 0:2], scalar1=2.0, scalar2=1e-30,
                                                   op0=ALU.mult, op1=ALU.max), r=[("smA2", hi % 2)], w=[("smA2", hi % 2)])
            for i in range(NBIS + 1):
                P.pool(lambda e, i=i, sm=sm, stepT=stepT: e.tensor_scalar(out=stepT[:, :, i], in0=sm[:, 0:2], scalar1=float(2.0 ** -i),
                                                                          scalar2=None, op0=ALU.mult),
                       r=[("smA2", hi % 2)], w=[("stepT", hi % 2, i)])
            P.pool(lambda e, sm=sm: e.memset(sm[:, 2:4], 0.0), w=[("mid", hi % 2)])

        def bis_half(g, hf, hi):
            nk = 1024 * (g + 1)
            nch = nk // 512
            sm = sm2[hi % 2]
            stepT = stepT2[hi % 2]
            kmid = ("mid", hi % 2)
            qi0 = 4 * g + 2 * hf
            kq2 = kq[:, s_ * 16 + qi0:s_ * 16 + qi0 + 2]
            for i in range(NBIS):
                for jj in range(2):
                    j = 2 * hf + jj
                    P.dve(lambda e, j=j, jj=jj, sm=sm: e.tensor_scalar(out=junk[:, 0:nk], in0=sc4[j][:, 0:nk], scalar1=sm[:, 2 + jj:3 + jj],
                                                                       scalar2=None, op0=ALU.is_ge, op1=ALU.add, accum_out=sm[:, 4 + jj:5 + jj]),
                          r=[(("sc", j), ch) for ch in range(nch)] + [kmid], w=[("cnt", hi % 2, jj)])
                P.dve(lambda e, sm=sm: e.tensor_tensor(out=sm[:, 6:8], in0=sm[:, 4:6], in1=kq2, op=ALU.is_ge),
                      r=[("cnt", hi % 2, jj) for jj in range(2)] + ["kq"], w=[("s4", hi % 2)])
                P.dve(lambda e, i=i, sm=sm, stepT=stepT: e.scalar_tensor_tensor(out=sm[:, 8:10], in0=sm[:, 6:8], scalar=0.5, in1=stepT[:, :, i],
                                                                                op0=ALU.subtract, op1=ALU.mult),
                      r=[("s4", hi % 2), ("stepT", hi % 2, i)], w=[("d4", hi % 2)])
                P.dve(lambda e, sm=sm: e.tensor_tensor(out=sm[:, 2:4], in0=sm[:, 2:4], in1=sm[:, 8:10], op=ALU.add),
                      r=[("d4", hi % 2), kmid], w=[kmid])
            P.dve(lambda e, sm=sm, stepT=stepT: e.tensor_tensor(out=sm[:, 10:12], in0=sm[:, 2:4], in1=stepT[:, :, NBIS], op=ALU.subtract),
                  r=[kmid, ("stepT", hi % 2, NBIS)], w=[("thr", hi % 2)])
            for jj in range(2):
                j = 2 * hf + jj
                P.dve(lambda e, j=j, jj=jj, sm=sm: e.tensor_scalar(out=m01[:, 0:nk], in0=sc4[j][:, 0:nk], scalar1=sm[:, 10 + jj:11 + jj], scalar2=None,
                                                                   op0=ALU.is_ge), r=[(("sc", j), ch) for ch in range(nch)] + [("thr", hi % 2)], w=["m01"])
                for k4 in range(nk // 512):
                    b = 6
                    for kk in range(4):
                        kb = k4 * 4 + kk
                        P.pe(lambda e, b=b, kk=kk, kb=kb: e.transpose(out=ps[b][:, kk * 128:(kk + 1) * 128],
                                                                      in_=m01[:, kb * 128:(kb + 1) * 128], identity=ident[:]),
                             r=["m01", "ident"], w=[PK(b)])
                    P.act(lambda e, b=b, k4=k4, j=j: e.activation(
                        out=mt[:, k4 * 4:(k4 + 1) * 4, j * 128:(j + 1) * 128],
                        in_=ps[b][:, :].rearrange("p (k t) -> p k t", t=128), func=AF.Copy),
                        r=[PK(b)], w=["mTs"])
            if hf == 1:
                P.dma(maskT_d[g, :, 0:nk // 128, :], mt[:, 0:nk // 128, :], r=["mTs"], w=[("maskT_d", g)])

        halves = [(g, hf) for g in range(4) for hf in range(2)]
        for hi, (g, hf) in enumerate(halves):
            acc_half(g, hf, hi)
            if hi > 0:
                bis_half(halves[hi - 1][0], halves[hi - 1][1], hi - 1)
            prep_half(g, hf, hi)
        bis_half(halves[-1][0], halves[-1][1], len(halves) - 1)
        P.barrier()
        P.release(base_mark)

        if stop("I"):
            return nc
        def attention(nheads, hp_loader, st_emit, mask_for, scale, out_d, okey):
            pts = [P.sb(f"pt{i}", [128, 512], BF16) for i in range(6)]
            ident_fill = P.sb("ifill", [128, 128], BF16)
            P.pool(lambda e: e.memset(ident_fill[:], 0.0), w=["ifill"])
            rdn = [P.sb(f"rdn{i}", [128, 512], F32) for i in range(2)]
            ost = [P.sb(f"ost{i}", [128, NO], BF16) for i in range(2)]
            ucnt = [0]
            acnt = [0]
            for hp in range(nheads // 2):
                bufs = hp_loader(hp)
                o_t = ost[hp % 2]
                for hh in range(2):
                    h = hp * 2 + hh
                    for g in range(4):
                        nkb = 8 * (g + 1)
                        accb = 4 + (acnt[0] % 2)
                        acnt[0] += 1
                        pend = []
                        for _f in range(NBURST):
                            P.pe(lambda e: e.matmul(ps[6][:, :], lhsT=ident_fill[:], rhs=pts[0][:], start=True, stop=True))
                        for kb in range(nkb):
                            sb_ = nb()
                            rel_ = kb - (nkb - 8)
                            c0 = 128 * (rel_ // 2) if rel_ > 0 else 0
                            st_emit(bufs, h, hh, g, kb, sb_, c0)
                            for _f in range(NFILL):
                                P.pe(lambda e: e.matmul(ps[6][:, :], lhsT=ident_fill[:], rhs=pts[0][:], start=True, stop=True))
                            pi = ucnt[0] % 6
                            ucnt[0] += 1
                            pt = pts[pi]
                            P.act(lambda e, sb_=sb_, pt=pt, c0=c0: e.activation(out=pt[:, c0:512], in_=ps[sb_][:, c0:512], func=AF.Exp, scale=scale),
                                  r=[PK(sb_)], w=[("pt", pi)])
                            mk = mask_for(bufs, g, kb)
                            if mk is not None:
                                map_, mkey = mk
                                P.dve(lambda e, pt=pt, map_=map_, c0=c0: e.tensor_tensor(out=pt[:, c0:512], in0=pt[:, c0:512], in1=map_[:, c0:512], op=ALU.mult),
                                      r=[("pt", pi), mkey], w=[("pt", pi)])
                            pend.append((kb, pt, pi, c0))
                            if len(pend) > 2:
                                kb0, pt0, pi0, c00 = pend.pop(0)
                                P.pe(lambda e, kb0=kb0, pt0=pt0, accb=accb, c00=c00: e.matmul(
                                    ps[accb][:, c00:512], lhsT=bufs["v"][:, kb0, hh * 128:(hh + 1) * 128], rhs=pt0[:, c00:512],
                                    start=(kb0 == 0), stop=(kb0 == nkb - 1)), r=[("pt", pi0), bufs["kv"]], w=[PK(accb)])
                        for kb0, pt0, pi0, c00 in pend:
                            P.pe(lambda e, kb0=kb0, pt0=pt0, accb=accb, c00=c00: e.matmul(
                                ps[accb][:, c00:512], lhsT=bufs["v"][:, kb0, hh * 128:(hh + 1) * 128], rhs=pt0[:, c00:512],
                                start=(kb0 == 0), stop=(kb0 == nkb - 1)), r=[("pt", pi0), bufs["kv"]], w=[PK(accb)])
                        rd = rdn[acnt[0] % 2]
                        krd = ("rdn", acnt[0] % 2)
                        nlo, dlo = (0, 64) if hh == 0 else (64, 0)
                        P.dve(lambda e, rd=rd, accb=accb, dlo=dlo: e.reciprocal(out=rd[dlo:dlo + 64, :], in_=ps[accb][dlo:dlo + 64, :]),
                              r=[PK(accb)], w=[krd])
                        P.dve(lambda e, rd=rd, accb=accb, dlo=dlo, nlo=nlo, g=g, o_t=o_t: e.tensor_tensor(
                            out=o_t[nlo:nlo + 64, g * 512:(g + 1) * 512], in0=ps[accb][nlo:nlo + 64, :],
                            in1=rd[dlo:dlo + 64, :], op=ALU.mult), r=[PK(accb), krd], w=[("ost", hp % 2)])
                P.dma(out_d[hp], o_t[:], r=[("ost", hp % 2)], w=[(okey, hp)])

        mres = P.sb("mres", [128, 80, 512], BF16)
        goff = [0, 8, 24, 48]
        for g in range(4):
            nkb = 8 * (g + 1)
            P.dma(mres[:, goff[g]:goff[g] + nkb, :], maskT_d[g, :, 0:nkb, :], r=[("maskT_d", g)], w=[("mres", g)])
        kb2 = [P.sb(f"kb2_{i}", [128, T], BF16) for i in range(2)]
        qb2 = [P.sb(f"qb2_{i}", [128, NO], BF16) for i in range(2)]
        vb2 = [P.sb(f"vb2_{i}", [128, 32, 256], BF16) for i in range(2)]

        def hp_loader0(hp):
            i = hp % 2
            key = ("kvq0", i)
            P.dma(kb2[i][:], kT_d[hp], r=[("kT_d", hp, t_) for t_ in range(8)], w=[key])
            P.dma(qb2[i][:], qT_d[hp], r=[("qT_d", hp, t_) for t_ in range(4)], w=[key])
            P.dma(vb2[i][:], vaug_d[:, :, hp * 256:(hp + 1) * 256].rearrange("k p c -> p k c"),
                  r=[("vaug_d", k_) for k_ in range(32)], w=[key])
            return {"k": kb2[i], "q": qb2[i], "v": vb2[i], "kv": key}

        def st_emit0(bufs, h, hh, g, kb, sb_, c0=0):
            base = hh * 64
            P.pe(lambda e: e.matmul(ps[sb_][:, c0:512], lhsT=bufs["k"][base:base + 64, kb * 128:(kb + 1) * 128],
                                    rhs=bufs["q"][base:base + 64, g * 512 + c0:(g + 1) * 512], start=True, stop=True),
                 r=[bufs["kv"]], w=[PK(sb_)])

        def mask_for0(bufs, g, kb):
            return mres[:, goff[g] + kb, :], ("mres", g)

        attention(8, hp_loader0, st_emit0, mask_for0, 0.125, attnT_d, "attnT_d")
        P.barrier()
        P.release(base_mark)

        if stop("A0"):
            return nc
        def out_phase(in_chunks, wd, layer, x_src, xkey_src, x_dst, xkey_dst):
            wo = P.sb("wo", [128, 8, D], BF16)
            load_w(wo[:], wd.rearrange("(c p) n -> p c n", p=128), "wo")
            ain = P.sb("ain", [128, 8, NO], BF16)
            for c, (ap_, rk) in enumerate(in_chunks):
                P.dma(ain[:, c, :], ap_, r=rk, w=[("ain", c)])
            xo2 = [P.sb(f"xo{i}", [128, 8, 512], F32) for i in range(2)]
            mx = P.sb("mx", [128, 8, 512], F32)
            sq_ = P.sb("sqo", [128, 8, 512], BF16)
            rs = P.sb("rso", [128, 512], F32)
            xv = x_src.rearrange("(c p) t -> p c t", p=128)
            xdv = x_dst.rearrange("(c p) t -> p c t", p=128)
            for tc in range(4):
                xo = xo2[tc % 2]
                kxo = ("xo", tc % 2)
                P.dma(xo[:], xv[:, :, tc * 512:(tc + 1) * 512], r=[(xkey_src, tc)], w=[kxo])
                for oc in range(8):
                    b = nb()
                    for kc in range(8):
                        P.pe(lambda e, kc=kc, oc=oc, b=b: e.matmul(ps[b][:, :], lhsT=wo[:, kc, oc * 128:(oc + 1) * 128],
                                                                   rhs=ain[:, kc, tc * 512:(tc + 1) * 512],
                                                                   start=(kc == 0), stop=(kc == 7)),
                             r=["wo", ("ain", kc)], w=[PK(b)])
                    P.act(lambda e, b=b, oc=oc: e.activation(out=mx[:, oc, :], in_=ps[b][:, :], func=AF.Copy),
                          r=[PK(b)], w=[("mx", oc)])
                    P.dve(lambda e, b=b, oc=oc: e.tensor_tensor(out=sq_[:, oc, :], in0=mx[:, oc, :], in1=mx[:, oc, :], op=ALU.mult),
                          r=[("mx", oc)], w=[("sqo", oc)])
                for c in range(8):
                    P.pe(lambda e, c=c: e.matmul(ps[7][:, :], lhsT=ones_b[:], rhs=sq_[:, c, :], start=(c == 0), stop=(c == 7)),
                         r=[("sqo", c), "ones_b"], w=[PK(7)])
                P.act(lambda e: e.activation(out=rs[:], in_=ps[7][:, :], func=AF.Sqrt, bias=EPS, scale=1.0), r=[PK(7)], w=["rso"])
                P.dve(lambda e: e.reciprocal(out=rs[:], in_=rs[:]), r=["rso"], w=["rso"])
                for c in range(8):
                    gc_ = gcol(layer, 1) + c
                    P.dve(lambda e, c=c, gc_=gc_: e.scalar_tensor_tensor(out=mx[:, c, :], in0=mx[:, c, :], scalar=cst[:, gc_:gc_ + 1],
                                                                          in1=rs[:], op0=ALU.mult, op1=ALU.mult),
                           r=[("mx", c), "rso", "cst"], w=[("mx", c)])
                    P.dve(lambda e, c=c, xo=xo: e.tensor_tensor(out=xo[:, c, :], in0=xo[:, c, :], in1=mx[:, c, :], op=ALU.add),
                          r=[("mx", c), kxo], w=[kxo])
                P.dma(xdv[:, :, tc * 512:(tc + 1) * 512], xo[:], r=[kxo], w=[(xkey_dst, tc)])

        in0 = [(attnT_d[c], [("attnT_d", c)]) for c in range(4)] + [(convT_d[c], [("convT_d", c)]) for c in range(4)]
        out_phase(in0, wout_d, 0, xT_own[s_, :, 0:NO], "xown", x1T_d, "x1T_d")
        P.barrier()
        P.release(base_mark)

        def ffn_phase(layer, x_src, xkey_src, x_dst, xkey_dst):
            w1s = P.sb("w1s", [128, 8, 4096], BF16)
            w2s = P.sb("w2s", [128, 32, D], BF16)
            w1v = w1_d[layer].rearrange("(c p) n -> p c n", p=128)
            w2v = w2_d[layer].rearrange("(c p) n -> p c n", p=128)
            for c in range(8):
                P.dma(w1s[:, c, :], w1v[:, c, :], w=[("w1s", c)], q="pool")
            for c in range(0, 32, 4):
                P.dma(w2s[:, c:c + 4, :], w2v[:, c:c + 4, :], w=[("w2s", c // 4)], q="pool")
            xf = P.sb("xf", [128, 8, 512], F32)
            off3 = P.mark()
            yb = P.sb("yb", [128, 8, 512], F32)
            end3 = P.mark()
            P.release(off3)
            sq_f = P.sb("sqf", [128, 8, 512], BF16)
            hf = P.sb("hf", [128, 8, 512], BF16)
            assert P.mark() == end3
            h1 = P.sb("h1", [128, 32, 512], BF16)
            sqy = [P.sb(f"sqy{i}", [128, 512], BF16) for i in range(2)]
            rt = [P.sb(f"rt{i}", [128, 512], F32) for i in range(2)]
            alias_keys = ["sqf"] + [("hf", c) for c in range(8)]
            rsf = P.sb("rsf", [128, 512], F32)
            xv = x_src.rearrange("(c p) t -> p c t", p=128)
            xdv = x_dst.rearrange("(c p) t -> p c t", p=128)
            rc = [0]
            for tc in range(4):
                P.dma(xf[:], xv[:, :, tc * 512:(tc + 1) * 512], r=[(xkey_src, tc)], w=["xf"])
                P.act(lambda e: e.activation(out=rsf[:, 0:1], in_=rsf[:, 0:1], func=AF.Copy), r=["rsf"],
                      w=alias_keys + ["rsf"] + [("yb", c) for c in range(8)])
                rmsnorm_fm(xf, "xf", 8, 512, gcol(layer, 2), hf, "hf", ones_b, sq_f, "sqf", rsf, "rsf", 7)
                for oc in range(32):
                    b = nb()
                    for kc in range(8):
                        P.pe(lambda e, kc=kc, oc=oc, b=b: e.matmul(ps[b][:, :], lhsT=w1s[:, kc, oc * 128:(oc + 1) * 128],
                                                                   rhs=hf[:, kc, :], start=(kc == 0), stop=(kc == 7)),
                             r=[("w1s", kc), ("hf", kc)], w=[PK(b)])
                    ri = rc[0] % 2
                    rc[0] += 1
                    P.act(lambda e, b=b, ri=ri: e.activation(out=rt[ri][:], in_=ps[b][:, :], func=AF.Relu), r=[PK(b)], w=[("rt", ri)])
                    eng = P.dve if oc % 2 == 0 else P.pool
                    eng(lambda e, ri=ri, oc=oc: e.tensor_tensor(out=h1[:, oc, :], in0=rt[ri][:], in1=rt[ri][:], op=ALU.mult),
                        r=[("rt", ri)], w=[("h1", oc)])
                for oc in range(8):
                    b = nb()
                    for kc in range(32):
                        P.pe(lambda e, kc=kc, oc=oc, b=b: e.matmul(ps[b][:, :], lhsT=w2s[:, kc, oc * 128:(oc + 1) * 128],
                                                                   rhs=h1[:, kc, :], start=(kc == 0), stop=(kc == 31)),
                             r=[("w2s", kc // 4), ("h1", kc)], w=[PK(b)])
                    P.act(lambda e, b=b, oc=oc: e.activation(out=yb[:, oc, :], in_=ps[b][:, :], func=AF.Copy), r=[PK(b)],
                          w=[("yb", oc)] + (alias_keys if oc == 0 else []))
                    P.dve(lambda e, oc=oc: e.tensor_tensor(out=sqy[oc % 2][:], in0=yb[:, oc, :], in1=yb[:, oc, :], op=ALU.mult),
                          r=[("yb", oc)], w=[("sqy", oc % 2)])
                    P.pe(lambda e, oc=oc: e.matmul(ps[7][:, :], lhsT=ones_b[:], rhs=sqy[oc % 2][:], start=(oc == 0), stop=(oc == 7)),
                         r=[("sqy", oc % 2), "ones_b"], w=[PK(7)])
                P.act(lambda e: e.activation(out=rsf[:], in_=ps[7][:, :], func=AF.Sqrt, bias=EPS, scale=1.0), r=[PK(7)], w=["rsf"])
                P.dve(lambda e: e.reciprocal(out=rsf[:], in_=rsf[:]), r=["rsf"], w=["rsf"])
                for c in range(8):
                    gc_ = gcol(layer, 3) + c
                    P.dve(lambda e, c=c, gc_=gc_: e.scalar_tensor_tensor(out=yb[:, c, :], in0=yb[:, c, :], scalar=cst[:, gc_:gc_ + 1],
                                                                          in1=rsf[:], op0=ALU.mult, op1=ALU.mult),
                           r=[("yb", c), "rsf", "cst"], w=[("yb", c)])
                    P.dve(lambda e, c=c: e.tensor_tensor(out=yb[:, c, :], in0=yb[:, c, :], in1=xf[:, c, :], op=ALU.add),
                          r=[("yb", c), "xf"], w=[("yb", c)])
                P.dma(xdv[:, :, tc * 512:(tc + 1) * 512], yb[:], r=[("yb", c) for c in range(8)], w=[(xkey_dst, tc)])

        if stop("O0"):
            return nc
        ffn_phase(0, x1T_d, "x1T_d", x2T_d[s_], ("x2T_d", s_))
        P.barrier()
        P.release(base_mark)

    if stop("F0"):
        return nc
    tabs = [rope_tables(pos_own[s1], NO, C_FR1, C_SG1, f"r1o{s1}") for s1 in range(2)]
    C1S1 = [None, None]
    m1 = P.mark()
    wdq = P.sb("wdq", [128, 8, 384], BF16)
    wuq = P.sb("wuq", [128, 3, 2048], BF16)
    wdkv = P.sb("wdkv", [128, 8, 320], BF16)
    load_w(wdq[:], wdq_d.rearrange("(c p) n -> p c n", p=128), "wdq")
    load_w(wuq[:], wuq_d.rearrange("(c p) n -> p c n", p=128), "wuq")
    load_w(wdkv[:], wdkv_d.rearrange("(c p) n -> p c n", p=128), "wdkv")
    xs2 = [P.sb(f"xs{i}", [128, 8, 512], F32) for i in range(2)]
    sq = P.sb("sq", [128, 8, 512], BF16)
    hb = P.sb("hb", [128, 8, 512], BF16)
    rstd = P.sb("rstd", [128, 512], F32)
    t1 = [P.sb(f"t1_{i}", [128, 512], F32) for i in range(2)]
    t2 = [P.sb(f"t2_{i}", [128, 512], F32) for i in range(2)]
    cq = P.sb("cq", [128, 3, 512], F32)
    cqn = P.sb("cqn", [128, 3, 512], BF16)
    ckv = P.sb("ckv", [128, 2, 512], F32)
    ckvn = P.sb("ckvn", [128, 2, 512], F32)
    krs = P.sb("krs", [32, 512], F32)
    qst = [P.sb(f"qst1_{i}", [128, 16, 512], BF16) for i in range(2)]
    sq3 = P.sb("sq3", [128, 3, 512], BF16)
    rs3 = P.sb("rs3", [128, 512], F32)

    def rope_evac1(bA, bB, m, col0, dst, kdst, eng_out="pool"):
        Cc, Ss = C1S1[0], C1S1[1]
        i = tcnt[0] % 2
        tcnt[0] += 1
        P.dve(lambda e: e.tensor_tensor(out=t1[i][0:m, :], in0=ps[bA][0:m, :], in1=Cc[0:m, col0:col0 + 512], op=ALU.mult),
              r=[PK(bA), "r1o0C", "r1o1C"], w=[("t1", i)])
        P.dve(lambda e: e.tensor_tensor(out=t2[i][0:m, :], in0=ps[bB][0:m, :], in1=Ss[0:m, col0:col0 + 512], op=ALU.mult),
              r=[PK(bB), "r1o0S", "r1o1S"], w=[("t2", i)])
        P.pool(lambda e: e.tensor_tensor(out=dst, in0=t1[i][0:m, :], in1=t2[i][0:m, :], op=ALU.add),
               r=[("t1", i), ("t2", i)], w=[kdst])

    for s1 in range(2):
        C1S1[0], C1S1[1] = tabs[s1]
        x2v = x2T_d[s1].rearrange("(c p) t -> p c t", p=128)
        for tc in range(4):
            xs = xs2[tc % 2]
            kx = ("xs", tc % 2)
            P.dma(xs[:], x2v[:, :, tc * 512:(tc + 1) * 512], r=[(("x2T_d", s1), tc)], w=[kx])
            rmsnorm_fm(xs, kx, 8, 512, gcol(1, 0), hb, "hb", ones_b, sq, "sq", rstd, "rstd", 7)
            if s1 == 0:
                for oc in range(3):
                    b = nb()
                    proj_fm(wdq, "wdq", oc * 128, hb, "hb", 8, 512, b)
                    P.act(lambda e, b=b, oc=oc: e.activation(out=cq[:, oc, :], in_=ps[b][:, :], func=AF.Copy), r=[PK(b)], w=["cq"])
                P.act(lambda e: e.activation(out=sq3[:], in_=cq[:], func=AF.Square), r=["cq"], w=["sq3"])
                for c in range(3):
                    P.pe(lambda e, c=c: e.matmul(ps[7][:, :], lhsT=ones_q[:], rhs=sq3[:, c, :], start=(c == 0), stop=(c == 2)),
                         r=["sq3", "ones_q"], w=[PK(7)])
                P.act(lambda e: e.activation(out=rs3[:], in_=ps[7][:, :], func=AF.Sqrt, bias=EPS, scale=256.0 / 384.0), r=[PK(7)], w=["rs3"])
                P.dve(lambda e: e.reciprocal(out=rs3[:], in_=rs3[:]), r=["rs3"], w=["rs3"])
                for c in range(3):
                    P.dve(lambda e, c=c: e.scalar_tensor_tensor(out=cqn[:, c, :], in0=cq[:, c, :], scalar=cst[:, C_QN + c:C_QN + c + 1],
                                                                in1=rs3[:], op0=ALU.mult, op1=ALU.mult), r=["cq", "rs3", "cst"], w=[("cqn", c)])
                qs = qst[tc % 2]
                for oc in range(8):
                    b = nb()
                    proj_fm(wuq, "wuq", oc * 128, cqn, "cqn", 3, 512, b)
                    P.act(lambda e, b=b, oc=oc: e.activation(out=qs[:, oc, :], in_=ps[b][:, :], func=AF.Copy), r=[PK(b)], w=[("qst", tc % 2, oc)])
                    P.dma(qnT_d[oc, :, tc * 512:(tc + 1) * 512], qs[:, oc, :], r=[("qst", tc % 2, oc)], w=[("qnT_d", oc, tc)])
                for oc in range(8):
                    bA, bB = nb(), nb()
                    proj_fm(wuq, "wuq", 1024 + oc * 64, cqn, "cqn", 3, 512, bA, m=64)
                    proj_fm(wuq, "wuq", 1536 + oc * 64, cqn, "cqn", 3, 512, bB, m=64)
                    rope_evac1(bA, bB, 64, tc * 512, qs[0:64, 8 + oc, :], ("qst", tc % 2, 8 + oc))
                    P.dma(qrT_d[oc, :, tc * 512:(tc + 1) * 512], qs[0:64, 8 + oc, :], r=[("qst", tc % 2, 8 + oc)], w=[("qrT_d", oc, tc)])
            for oc in range(2):
                b = nb()
                proj_fm(wdkv, "wdkv", oc * 128, hb, "hb", 8, 512, b)
                P.act(lambda e, b=b, oc=oc: e.activation(out=ckv[:, oc, :], in_=ps[b][:, :], func=AF.Copy), r=[PK(b)], w=["ckv"])
            P.act(lambda e: e.activation(out=sq3[:, 0:2, :], in_=ckv[:], func=AF.Square), r=["ckv"], w=["sq3"])
            for c in range(2):
                P.pe(lambda e, c=c: e.matmul(ps[7][:, :], lhsT=ones_q[:], rhs=sq3[:, c, :], start=(c == 0), stop=(c == 1)),
                     r=["sq3", "ones_q"], w=[PK(7)])
            P.act(lambda e: e.activation(out=rs3[:], in_=ps[7][:, :], func=AF.Sqrt, bias=EPS, scale=1.0), r=[PK(7)], w=["rs3"])
            P.dve(lambda e: e.reciprocal(out=rs3[:], in_=rs3[:]), r=["rs3"], w=["rs3"])
            for c in range(2):
                P.dve(lambda e, c=c: e.scalar_tensor_tensor(out=ckvn[:, c, :], in0=ckv[:, c, :], scalar=cst[:, C_KVN + c:C_KVN + c + 1],
                                                            in1=rs3[:], op0=ALU.mult, op1=ALU.mult), r=["ckv", "rs3", "cst"], w=[("ckvn", c)])
                P.dma(kva_sets_d[s1, c * 128:(c + 1) * 128, tc * 512:(tc + 1) * 512], ckvn[:, c, :], r=[("ckvn", c)], w=[("kva", s1, tc, c)])
            bA, bB = nb(), nb()
            proj_fm(wdkv, "wdkv", 256, hb, "hb", 8, 512, bA, m=32)
            proj_fm(wdkv, "wdkv", 288, hb, "hb", 8, 512, bB, m=32)
            rope_evac1(bA, bB, 32, tc * 512, krs[:, :], "krs")
            P.dma(kva_sets_d[s1, 256:288, tc * 512:(tc + 1) * 512], krs[:, :], r=["krs"], w=[("kva", s1, tc, 2)])
    P.barrier()
    P.release(base_mark)

    if stop("Q1"):
        return nc
    wkk = P.sb("wkk", [128, 2, 1024], BF16)
    wkv = P.sb("wkv", [128, 2, 1024], BF16)
    load_w(wkk[:], wukvk_d.rearrange("(c p) n -> p c n", p=128), "wkk")
    load_w(wkv[:], wukvv_d.rearrange("(c p) n -> p c n", p=128), "wkv")
    ckf = [P.sb(f"ckf{i}", [128, 2, 512], F32) for i in range(2)]
    ckb = [P.sb(f"ckb{i}", [128, 2, 512], BF16) for i in range(2)]
    kns = [P.sb(f"kns{i}", [128, 8, 512], BF16) for i in range(2)]
    vst1 = [P.sb(f"vst1_{i}", [128, 16, 128], BF16) for i in range(2)]
    kr4 = P.sb("kr4", [64, T], BF16)
    krf = P.sb("krf", [64, T], F32)
    for i in range(2):
        P.pool(lambda e, i=i: e.memset(vst1[i][:], 1.0), w=[("vst1", i)])
    for sl in range(32):
        s1, m_ = sl % 2, sl // 2
        for rep_ in range(2):
            P.dma(krf[rep_ * 32:(rep_ + 1) * 32, sl * 128:(sl + 1) * 128],
                  kva_sets_d[s1, 256:288, m_ * 128:(m_ + 1) * 128], w=[("krf", sl // 8)])
    for q4 in range(4):
        P.dve(lambda e, q4=q4: e.tensor_copy(out=kr4[:, q4 * 1024:(q4 + 1) * 1024], in_=krf[:, q4 * 1024:(q4 + 1) * 1024]),
              r=[("krf", q4)], w=["kr4"])
    for tc in range(8):
        cf = ckf[tc % 2]
        cb_ = ckb[tc % 2]
        for tb in range(4):
            sl = tc * 4 + tb
            s1, m_ = sl % 2, sl // 2
            for c in range(2):
                P.dma(cf[:, c, tb * 128:(tb + 1) * 128],
                      kva_sets_d[s1, c * 128:(c + 1) * 128, m_ * 128:(m_ + 1) * 128], w=[("ckf", tc % 2)])
        P.dve(lambda e, cf=cf, cb_=cb_: e.tensor_copy(out=cb_[:], in_=cf[:]), r=[("ckf", tc % 2)], w=[(("ckb", tc % 2), 0), (("ckb", tc % 2), 1)])
        kn = kns[tc % 2]
        for oc in range(8):
            b = nb()
            proj_fm(wkk, "wkk", oc * 128, cb_, ("ckb", tc % 2), 2, 512, b)
            P.act(lambda e, b=b, oc=oc, kn=kn: e.activation(out=kn[:, oc, :], in_=ps[b][:, :], func=AF.Copy), r=[PK(b)], w=[("kns", tc % 2, oc)])
            P.dma(knT_d[oc, :, tc * 512:(tc + 1) * 512], kn[:, oc, :], r=[("kns", tc % 2, oc)], w=[("knT_d", oc, tc)])
        for tb in range(4):
            kb = tc * 4 + tb
            vs = vst1[kb % 2]
            vv = vs[:].rearrange("p (h two) d -> p h two d", two=2)
            for half in range(2):
                b = nb()
                for kc in range(2):
                    P.pe(lambda e, kc=kc, tb=tb, b=b, half=half, cb_=cb_: e.matmul(
                        ps[b][:, :], lhsT=cb_[:, kc, tb * 128:(tb + 1) * 128], rhs=wkv[:, kc, half * 512:(half + 1) * 512],
                        start=(kc == 0), stop=(kc == 1)), r=["wkv", (("ckb", tc % 2), kc)], w=[PK(b)])
                pv = ps[b][:, :].rearrange("p (h two d) -> p h two d", two=2, d=64)
                P.act(lambda e, pv=pv, vv=vv, half=half: e.activation(out=vv[:, half * 4:(half + 1) * 4, 0, 0:64], in_=pv[:, :, 0, :], func=AF.Copy),
                      r=[PK(b)], w=[("vst1", kb % 2)])
                P.act(lambda e, pv=pv, vv=vv, half=half: e.activation(out=vv[:, half * 4:(half + 1) * 4, 1, 64:128], in_=pv[:, :, 1, :], func=AF.Copy),
                      r=[PK(b)], w=[("vst1", kb % 2)])
            P.dma(vaug1_d[kb], vs[:].rearrange("p h d -> p (h d)"), r=[("vst1", kb % 2)], w=[("vaug1_d", kb)])
    P.dma(kr4_d, kr4[:], r=["kr4"], w=["kr4_d"])
    P.barrier()
    P.release(base_mark)

    if stop("K1"):
        return nc
    mTs1 = P.sb("mTs1", [128, 2, 8, 512], BF16)
    for i in range(2):
        P.dma(mTs1[:, i], mT_d[i], w=["mTs1"])
    kh2 = [[P.sb(f"kh_{i}_{hh}", [96, T], BF16) for hh in range(2)] for i in range(2)]
    qh2 = [[P.sb(f"qh_{i}_{hh}", [96, NO], BF16) for hh in range(2)] for i in range(2)]
    vb2 = [P.sb(f"vb21_{i}", [128, 32, 256], BF16) for i in range(2)]

    def hp_loader1(hp):
        i = hp % 2
        key = ("kvq1", i)
        for hh in range(2):
            P.dma(kh2[i][hh][0:64, :], knT_d[hp, hh * 64:(hh + 1) * 64, :], r=[("knT_d", hp, t_) for t_ in range(8)], w=[key])
            P.dma(kh2[i][hh][64:96, :], kr4_d[0:32, :], r=["kr4_d"], w=[key])
            P.dma(qh2[i][hh][0:64, :], qnT_d[hp, hh * 64:(hh + 1) * 64, :], r=[("qnT_d", hp, t_) for t_ in range(4)], w=[key])
            P.dma(qh2[i][hh][64:96, :], qrT_d[hp, hh * 32:(hh + 1) * 32, :], r=[("qrT_d", hp, t_) for t_ in range(4)], w=[key])
        P.dma(vb2[i][:], vaug1_d[:, :, hp * 256:(hp + 1) * 256].rearrange("k p c -> p k c"),
              r=[("vaug1_d", k_) for k_ in range(32)], w=[key])
        return {"k": kh2[i], "q": qh2[i], "v": vb2[i], "kv": key}

    def st_emit1(bufs, h, hh, g, kb, sb_, c0=0):
        P.pe(lambda e: e.matmul(ps[sb_][:, c0:512], lhsT=bufs["k"][hh][0:96, kb * 128:(kb + 1) * 128],
                                rhs=bufs["q"][hh][0:96, g * 512 + c0:(g + 1) * 512], start=True, stop=True),
             r=[bufs["kv"]], w=[PK(sb_)])

    def mask_for1(bufs, g, kb):
        rel = kb - 8 * g
        if rel < 0:
            return None
        return mTs1[:, g // 2, rel, :], "mTs1"

    attention(16, hp_loader1, st_emit1, mask_for1, float(96 ** -0.5), attnT_d, "attnT1_d")
    P.barrier()
    P.release(base_mark)

    if stop("A1"):
        return nc
    in1 = [(attnT_d[c], [("attnT1_d", c)]) for c in range(8)]
    out_phase(in1, wo_d, 1, x2T_d[0], ("x2T_d", 0), x3T_d, "x3T_d")
    P.barrier()
    P.release(base_mark)
    ffn_phase(1, x3T_d, "x3T_d", outT, "outT")
    P.final_wait([("outT", tc) for tc in range(4)])
    P.emit()
    return nc


def _swap_cols(w, head, a, b_):
    n = w.shape[1]
    idx = np.arange(n).reshape(-1, head)
    perm = np.concatenate([idx[:, a:b_], idx[:, 0:a], idx[:, b_:]], axis=1).reshape(-1)
    return w[:, perm]


def prepare_inputs(inp, n_batch=4):
    f32 = np.float32
    x = np.asarray(inp["x"], f32)
    pos = np.asarray(inp["positions"]).astype(np.int32)
    w_in = np.asarray(inp["even_w_in"], f32)[0]
    offs = np.cumsum([0, 512, 512, 512, 1024, 64, 16, 512, 512, 512])
    wq_, wk_, wv_, wqi, wki, wwi, wgb, wgc, wxi = [w_in[:, offs[i]:offs[i + 1]] for i in range(9)]
    w0k = np.concatenate([wk_, _swap_cols(wk_, 64, 8, 16), wki, wki, _swap_cols(wki, 64, 8, 16), _swap_cols(wki, 64, 8, 16)], axis=1)
    w0q = np.concatenate([wq_, _swap_cols(wq_, 64, 8, 16), wqi, _swap_cols(wqi, 64, 8, 16), wgb, wgc, wxi], axis=1)
    w_uq = np.asarray(inp["odd_w_uq"], f32)[0]
    cols = np.arange(1536).reshape(16, 96)
    wuq_n = w_uq[:, cols[:, :64].reshape(-1)]
    wuq_r = w_uq[:, cols[:, 64:].reshape(-1)]
    wuq = np.concatenate([wuq_n, wuq_r, _swap_cols(wuq_r, 32, 16, 32)], axis=1)
    w_dkv = np.asarray(inp["odd_w_dkv"], f32)[0]
    wdkv = np.concatenate([w_dkv[:, :256], w_dkv[:, 256:], _swap_cols(w_dkv[:, 256:], 32, 16, 32)], axis=1)
    w_ukv = np.asarray(inp["odd_w_ukv"], f32)[0]
    c2 = np.arange(2048).reshape(16, 128)
    wukv_k = w_ukv[:, c2[:, :64].reshape(-1)]
    wukv_v = w_ukv[:, c2[:, 64:].reshape(-1)]

    cst = np.zeros((128, NCST), f32)
    kinds = ["norm_mix_pre", "norm_mix_post", "norm_ffn_pre", "norm_ffn_post"]
    for l in range(2):
        for k, nm in enumerate(kinds):
            cst[:, gcol(l, k):gcol(l, k) + 8] = np.asarray(inp[nm], f32)[l].reshape(8, 128).T
    cst[:, C_QN:C_QN + 3] = np.asarray(inp["odd_q_norm"], f32)[0].reshape(3, 128).T
    cst[:, C_KVN:C_KVN + 2] = np.asarray(inp["odd_kv_norm"], f32)[0].reshape(2, 128).T
    cw = np.asarray(inp["even_conv_w"], f32)[0]
    for j in range(3):
        cst[:, C_CW + j * 4:C_CW + j * 4 + 4] = cw[j].reshape(4, 128).T
    theta = 500000.0
    if0 = (theta ** (-np.arange(0, 16, 2, dtype=np.float32) / 16)).astype(f32)
    if1 = (theta ** (-np.arange(0, 32, 2, dtype=np.float32) / 32)).astype(f32)
    for p in range(128):
        r = p % 64
        if r < 16:
            cst[p, C_FR0] = if0[r % 8]
            cst[p, C_SG0] = -1.0 if r < 8 else 1.0
        r = p % 32
        cst[p, C_FR1] = if1[r % 16]
        cst[p, C_SG1] = -1.0 if r < 16 else 1.0

    shared = {
        "cst": cst, "w0k": w0k, "w0v": np.ascontiguousarray(wv_), "w0q": w0q, "w0wi": np.ascontiguousarray(wwi),
        "w_out": np.asarray(inp["even_w_out"], f32)[0], "w1": np.asarray(inp["mlp_w1"], f32),
        "w2": np.asarray(inp["mlp_w2"], f32), "w_dq": np.asarray(inp["odd_w_dq"], f32)[0], "w_uq": wuq,
        "w_dkv": wdkv, "w_ukv_k": np.ascontiguousarray(wukv_k), "w_ukv_v": np.ascontiguousarray(wukv_v),
        "w_o": np.asarray(inp["odd_w_o"], f32)[0],
    }
    shared = {k: np.ascontiguousarray(v, dtype=f32) for k, v in shared.items()}
    in_maps = []
    own_idx_all = []
    qi = np.arange(128)
    s_ = np.arange(1024)
    for b in range(n_batch):
        for par in range(2):
            xb = x[b]
            sets = [blocks_for(par), blocks_for(1 - par)]
            xT_sets, pos_sets, cbs = [], [], []
            kq = np.zeros((128, 32), f32)
            for si, blks in enumerate(sets):
                idx = np.concatenate([np.arange(128 * p, 128 * p + 128) for p in blks])
                halo = np.zeros((32, D), f32)
                for i, p in enumerate(blks):
                    if p > 0:
                        halo[2 * i:2 * i + 2] = xb[128 * p - 2:128 * p]
                xT_sets.append(np.concatenate([xb[idx], halo], axis=0).T)
                pos_sets.append(pos[b][idx][None, :])
                cb = np.zeros((8, 128, 1024), f32)
                for i, p in enumerate(blks):
                    kq[:, si * 16 + i] = np.minimum(256, 128 * p + qi + 1)
                for g in range(4):
                    for j in range(4):
                        rel = blks[4 * g + j] % 8
                        vis = s_[None, :] <= (rel * 128 + qi)[:, None]
                        cb[(g // 2) * 4 + j] = np.where(vis, 0.0, -1e30)
                cbs.append(cb)
            own = np.concatenate([np.arange(128 * p, 128 * p + 128) for p in sets[0]])
            own_idx_all.append(own)
            mT = np.zeros((2, 128, 8, 512), f32)
            for g in (0, 2):
                for j in range(4):
                    pq = sets[0][4 * g + j]
                    qpos = 128 * pq + qi
                    for rel in range(8):
                        sl = 8 * g + rel
                        pk = sets[sl % 2][sl // 2]
                        kpos = 128 * pk + qi
                        mT[g // 2, :, rel, j * 128:(j + 1) * 128] = (kpos[:, None] <= qpos[None, :])
            m = dict(shared)
            m.update({
                "xT_seq": np.ascontiguousarray(xb.T), "xT_own": np.ascontiguousarray(np.stack(xT_sets)),
                "pos_seq": np.ascontiguousarray(pos[b][None, :]), "pos_own": np.ascontiguousarray(np.stack(pos_sets)),
                "kq": kq, "cb": np.stack(cbs), "mT": mT.astype(ml_dtypes.bfloat16),
            })
            in_maps.append(m)
    return in_maps, own_idx_all


_NC_CACHE = {}


def kernel(**inputs):
    in_maps, own_idx = prepare_inputs(inputs, 4)
    if 8 not in _NC_CACHE:
        _NC_CACHE[8] = build_program(8)
    nc = _NC_CACHE[8]
    res = run_bass_kernel_spmd(nc, in_maps, core_ids=list(range(8)))
    out = np.zeros((4, T, D), np.float32)
    for c in range(8):
        b = c // 2
        out[b, own_idx[c], :] = np.asarray(res.results[c]["outT"], np.float32).T
    return out
```

```python
import types
import numpy as np
import ml_dtypes
import concourse.bass as bass
import concourse.mybir as mybir
from concourse.bass_utils import run_bass_kernel_spmd

F32 = mybir.dt.float32
BF16 = mybir.dt.bfloat16
I32 = mybir.dt.int32
AF = mybir.ActivationFunctionType
ALU = mybir.AluOpType
AX = mybir.AxisListType
DT_SIZE = {F32: 4, BF16: 2, I32: 4}

T = 4096
NO = 2048
D = 1024
EPS = 1e-6
NBIS = 18
NFILL = 1
NBURST = 10
NBURST_I = 10
NFILL_I = 0
TWO_PI = float(2 * np.pi)


class Op:
    __slots__ = ("eng", "fn", "reads", "writes", "is_dma", "deps", "needed", "ordinal",
                 "dsem", "dval", "barrier")

    def __init__(self, eng, fn, reads, writes, is_dma):
        self.eng = eng
        self.fn = fn
        self.reads = reads
        self.writes = writes
        self.is_dma = is_dma
        self.deps = []
        self.needed = False
        self.ordinal = None
        self.dsem = None
        self.dval = None
        self.barrier = False


class Prog:
    ENGS = ("pe", "act", "dve", "pool", "sp")
    SB_LIMIT = 228352

    def __init__(self, nc, n_dma_sems=12):
        self.nc = nc
        self.ops = []
        self.sb_off = 16896
        self.sb_max = 0
        self.n_dma_sems = n_dma_sems
        self._uid = 0
        self._bank = 0

    def sb(self, name, shape, dtype):
        nbytes = int(np.prod(shape[1:])) * DT_SIZE[dtype]
        nbytes = (nbytes + 63) // 64 * 64
        self._uid += 1
        t = self.nc.alloc_sbuf_tensor_at(f"{name}_{self._uid}", list(shape), dtype, offset=self.sb_off)
        self.sb_off += nbytes
        self.sb_max = max(self.sb_max, self.sb_off)
        assert self.sb_off <= self.SB_LIMIT, f"SBUF overflow {self.sb_off} at {name}"
        return t

    def mark(self):
        return self.sb_off

    def release(self, m):
        self.sb_off = m

    @staticmethod
    def _freeze(fn):
        if getattr(fn, "__closure__", None) is None:
            return fn
        cells = []
        for c in fn.__closure__:
            try:
                cells.append(types.CellType(c.cell_contents))
            except ValueError:
                cells.append(c)
        return types.FunctionType(fn.__code__, fn.__globals__, fn.__name__, fn.__defaults__, tuple(cells))

    def add(self, eng, fn, r=(), w=(), dma=False):
        fn = self._freeze(fn)
        o = Op(eng, fn, tuple(r), tuple(w), dma)
        self.ops.append(o)
        return o

    def pe(self, fn, r=(), w=()):
        return self.add("pe", fn, r, w)

    def act(self, fn, r=(), w=()):
        return self.add("act", fn, r, w)

    def dve(self, fn, r=(), w=()):
        return self.add("dve", fn, r, w)

    def pool(self, fn, r=(), w=()):
        return self.add("pool", fn, r, w)

    def dma(self, out, in_, r=(), w=(), q="sp", **kw):
        return self.add(q, lambda e: e.dma_start(out=out, in_=in_, **kw), r, w, dma=True)

    def final_wait(self, keys):
        return self.add("sp", lambda e: e.nop(), r=keys, w=())

    def barrier(self):
        o = Op(None, None, (), (), False)
        o.barrier = True
        self.ops.append(o)

    def finalize(self):
        last_w = {}
        readers = {}
        since_barrier = []
        pending_barrier = None
        seen_after = set()
        for o in self.ops:
            if o.barrier:
                summ = []
                lastc = {}
                for p in since_barrier:
                    if p.is_dma:
                        summ.append(p)
                    else:
                        lastc[p.eng] = p
                summ.extend(lastc.values())
                if pending_barrier is not None:
                    summ.extend(pending_barrier)
                pending_barrier = summ
                seen_after = set()
                since_barrier = []
                continue
            deps = []
            if pending_barrier is not None and o.eng not in seen_after:
                deps.extend(pending_barrier)
                seen_after.add(o.eng)
            for k in o.reads:
                if k in last_w:
                    deps.append(last_w[k])
            for k in o.writes:
                if k in last_w:
                    deps.append(last_w[k])
                deps.extend(readers.get(k, ()))
            for k in o.reads:
                readers.setdefault(k, []).append(o)
            for k in o.writes:
                last_w[k] = o
                readers[k] = []
            dd = []
            seen = set()
            for d in deps:
                if d is o or id(d) in seen:
                    continue
                seen.add(id(d))
                if (not d.is_dma) and (not o.is_dma) and d.eng == "pe" and o.eng == "pe":
                    continue
                dd.append(d)
            o.deps = dd
            for d in dd:
                d.needed = True
            since_barrier.append(o)
        cnt = {e: 0 for e in self.ENGS}
        dma_rr = {e: 0 for e in self.ENGS}
        dma_uses = {}
        for o in self.ops:
            if o.barrier:
                continue
            if o.is_dma:
                slot = dma_rr[o.eng] % self.n_dma_sems
                dma_rr[o.eng] += 1
                key = (o.eng, slot)
                dma_uses[key] = dma_uses.get(key, 0) + 1
                o.dsem = key
                o.dval = 16 * dma_uses[key]
            elif o.needed:
                cnt[o.eng] += 1
                o.ordinal = cnt[o.eng]
        self.max_ord = dict(cnt)

    def emit(self):
        nc = self.nc
        self.finalize()
        from contextlib import ExitStack
        es = ExitStack()
        sems = {}
        for e in ("pe", "act", "dve", "pool", "sp"):
            sems[e] = es.enter_context(nc.semaphore(f"c_{e}"))
        dsems = {}
        used = sorted({o.dsem for o in self.ops if (not o.barrier) and o.is_dma})
        for key in used:
            dsems[key] = es.enter_context(nc.semaphore(f"d_{key[0]}_{key[1]}"))
        block = es.enter_context(nc.Block())
        per_eng = {e: [o for o in self.ops if (not o.barrier) and o.eng == e] for e in self.ENGS}

        def body(ename, engine):
            known = {}
            for o in per_eng[ename]:
                waits = {}
                for d in o.deps:
                    if d.is_dma:
                        s, v, k = dsems[d.dsem], d.dval, ("d",) + d.dsem
                    else:
                        s, v, k = sems[d.eng], d.ordinal, ("c", d.eng)
                    if v > waits.get(k, (None, 0))[1]:
                        waits[k] = (s, v)
                if o.is_dma and o.dval > 16:
                    k = ("d",) + o.dsem
                    v = o.dval - 16
                    if v > waits.get(k, (None, 0))[1]:
                        waits[k] = (dsems[o.dsem], v)
                for k, (s, v) in waits.items():
                    if known.get(k, 0) >= v:
                        continue
                    engine.wait_ge(s, v)
                    known[k] = v
                ins = o.fn(engine)
                if o.is_dma:
                    ins.then_inc(dsems[o.dsem], 16)
                elif o.needed:
                    ins.then_inc(sems[ename], 1)

        @block.tensor
        def _(e):
            body("pe", e)

        @block.scalar
        def _(e):
            body("act", e)

        @block.vector
        def _(e):
            body("dve", e)

        @block.gpsimd
        def _(e):
            body("pool", e)

        @block.sync
        def _(e):
            body("sp", e)

        es.close()


def blocks_for(par):
    lo = list(range(par, 16, 2))
    hi = sorted(31 - j for j in lo)
    return lo + hi


C_G = 0
C_QN = 64
C_KVN = 67
C_CW = 69
C_FR0 = 81
C_SG0 = 82
C_FR1 = 83
C_SG1 = 84
NCST = 96


def gcol(layer, kind):
    return C_G + (layer * 4 + kind) * 8


def build_program(n_cores, dbg=(), no_cc=False, stop_after=None):
    nc = bass.Bass("TRN2", target_bir_lowering=False)
    P = Prog(nc)

    def stop(name):
        if stop_after == name:
            P.barrier()
            P.final_wait([])
            P.emit()
            return True
        return False

    def din(name, shape, dt=F32):
        return nc.dram_tensor(name, list(shape), dt, kind="ExternalInput").ap()

    def dscr(name, shape, dt):
        kind = "ExternalOutput" if name in dbg else "Internal"
        return nc.dram_tensor(name, list(shape), dt, kind=kind).ap()

    xT_seq = din("xT_seq", [D, T])
    xT_own = din("xT_own", [2, D, NO + 32])
    pos_seq = din("pos_seq", [1, T], I32)
    pos_own = din("pos_own", [2, 1, NO], I32)
    cst_d = din("cst", [128, NCST])
    kq_d = din("kq", [128, 32])
    cb_d = din("cb", [2, 8, 128, 1024])
    mT_d = din("mT", [2, 128, 8, 512], BF16)
    w0k_d = din("w0k", [D, 1280])
    w0v_d = din("w0v", [D, 512])
    w0q_d = din("w0q", [D, 4608])
    w0wi_d = din("w0wi", [D, 16])
    wout_d = din("w_out", [D, D])
    w1_d = din("w1", [2, D, 4096])
    w2_d = din("w2", [2, 4096, D])
    wdq_d = din("w_dq", [D, 384])
    wuq_d = din("w_uq", [384, 2048])
    wdkv_d = din("w_dkv", [D, 320])
    wukvk_d = din("w_ukv_k", [256, 1024])
    wukvv_d = din("w_ukv_v", [256, 1024])
    wo_d = din("w_o", [D, D])
    outT = nc.dram_tensor("outT", [D, NO], F32, kind="ExternalOutput").ap()

    kT_d = dscr("kT_d", [4, 128, T], BF16)
    kidxT_d = dscr("kidxT_d", [128, T], BF16)
    vaug_d = dscr("vaug_d", [32, 128, 1024], BF16)
    qT_d = dscr("qT_d", [4, 128, NO], BF16)
    qidxT_d = dscr("qidxT_d", [8, 128, NO], BF16)
    convT_d = dscr("convT_d", [4, 128, NO], BF16)
    maskT_d = dscr("maskT_d", [4, 128, 32, 512], BF16)
    attnT_d = dscr("attnT_d", [8, 128, NO], BF16)
    x1T_d = dscr("x1T_d", [D, NO], F32)
    x2T_d = dscr("x2T_d", [2, D, NO], F32)
    x3T_d = dscr("x3T_d", [D, NO], F32)
    kva_sets_d = dscr("kva_sets_d", [2, 288, NO], F32)
    qnT_d = dscr("qnT_d", [8, 128, NO], BF16)
    qrT_d = dscr("qrT_d", [8, 64, NO], BF16)
    knT_d = dscr("knT_d", [8, 128, T], BF16)
    vaug1_d = dscr("vaug1_d", [32, 128, 2048], BF16)
    dbg_d = dscr("dbg_d", [128, 4096], F32)
    kr4_d = dscr("kr4_d", [64, T], BF16)

    ps = [nc.alloc_psum_tensor(f"ps{i}", [128, 512], F32) for i in range(8)]

    def PK(i):
        return ("ps", i)

    cst = P.sb("cst", [128, NCST], F32)
    kq = P.sb("kq", [128, 32], F32)
    widx = P.sb("widx", [128, 16, 16], F32)
    ones_b = P.sb("ones_b", [128, 128], BF16)
    ones_q = P.sb("ones_q", [128, 128], BF16)
    ident = P.sb("ident", [128, 128], F32)
    P.dma(cst[:], cst_d, w=["cst"])
    P.dma(kq[:], kq_d, w=["kq"])
    P.pool(lambda e: e.memset(ones_b[:], 1.0 / 1024), w=["ones_b"])
    P.pool(lambda e: e.memset(ones_q[:], 1.0 / 256), w=["ones_q"])
    P.pool(lambda e: e.memset(ident[:], 1.0), w=["ident"])
    P.pool(lambda e: e.affine_select(out=ident[:], in_=ident[:], pattern=[[-1, 128]], compare_op=ALU.is_equal,
                                     fill=0.0, base=0, channel_multiplier=1), r=["ident"], w=["ident"])
    base_mark = P.mark()

    uid = [0]
    W1OFF = [None]

    def U(s):
        uid[0] += 1
        return f"{s}#{uid[0]}"

    def rope_tables(pos_d, n, fr_col, sg_col, tag):
        C = P.sb(tag + "C", [128, n], F32)
        S = P.sb(tag + "S", [128, n], F32)
        m = P.mark()
        pi_ = P.sb("posi", [128, n], I32)
        pf = P.sb("posf", [128, n], F32)
        tmp = P.sb("rtmp", [128, n], F32)
        ki = P.sb("rki", [128, n], I32)
        kpi, kpf, kt, kk = U("posi"), U("posf"), U("rtmp"), U("rki")
        kC, kS = tag + "C", tag + "S"
        P.dma(pi_[:], pos_d.to_broadcast([128, n]), w=[kpi])
        P.dve(lambda e: e.tensor_copy(out=pf[:], in_=pi_[:]), r=[kpi], w=[kpf])
        P.dve(lambda e: e.tensor_scalar(out=pf[:], in0=pf[:], scalar1=cst[:, fr_col:fr_col + 1], scalar2=None,
                                        op0=ALU.mult), r=[kpf, "cst"], w=[kpf])
        for which, dst, kd in (("s", S, kS), ("c", C, kC)):
            off = 0.0 if which == "s" else float(np.pi / 2)
            P.dve(lambda e, off=off: e.tensor_scalar(out=tmp[:], in0=pf[:], scalar1=off, scalar2=1.0 / TWO_PI,
                                                     op0=ALU.add, op1=ALU.mult), r=[kpf], w=[kt])
            P.dve(lambda e: e.tensor_copy(out=ki[:], in_=tmp[:]), r=[kt], w=[kk])
            P.dve(lambda e: e.tensor_copy(out=tmp[:], in_=ki[:]), r=[kk], w=[kt])
            P.dve(lambda e: e.scalar_tensor_tensor(out=tmp[:], in0=tmp[:], scalar=-TWO_PI, in1=pf[:],
                                                   op0=ALU.mult, op1=ALU.add), r=[kt, kpf], w=[kt])
            P.dve(lambda e, off=off: e.tensor_scalar(out=tmp[:], in0=tmp[:], scalar1=off, scalar2=None,
                                                     op0=ALU.add), r=[kt], w=[kt])
            P.dve(lambda e, dst=dst: e.tensor_scalar(out=dst[:], in0=tmp[:], scalar1=float(np.pi), scalar2=-TWO_PI,
                                                     op0=ALU.is_gt, op1=ALU.mult), r=[kt], w=[kd])
            P.dve(lambda e, dst=dst: e.tensor_tensor(out=tmp[:], in0=tmp[:], in1=dst[:], op=ALU.add), r=[kt, kd], w=[kt])
            P.dve(lambda e, dst=dst: e.tensor_scalar(out=dst[:], in0=tmp[:], scalar1=-float(np.pi), scalar2=TWO_PI,
                                                     op0=ALU.is_lt, op1=ALU.mult), r=[kt], w=[kd])
            P.dve(lambda e, dst=dst: e.tensor_tensor(out=tmp[:], in0=tmp[:], in1=dst[:], op=ALU.add), r=[kt, kd], w=[kt])
            P.dve(lambda e: e.tensor_scalar(out=tmp[:], in0=tmp[:], scalar1=-3.14159, scalar2=3.14159,
                                            op0=ALU.max, op1=ALU.min), r=[kt], w=[kt])
            P.act(lambda e, dst=dst: e.activation(out=dst[:], in_=tmp[:], func=AF.Sin), r=[kt], w=[kd])
        P.dve(lambda e: e.tensor_scalar(out=S[:], in0=S[:], scalar1=cst[:, sg_col:sg_col + 1], scalar2=None,
                                        op0=ALU.mult), r=[kS, "cst"], w=[kS])
        P.barrier()
        P.release(m)
        return C, S

    def load_w(dst, src_ap, key, nsplit=1):
        P.dma(dst, src_ap, w=[key], q="pool")

    def rmsnorm_fm(xs, kx, nchunk, n, gain_col, hout, kh, onesm, sq, ksq, rstd, krs, ssbank, eps=EPS):
        P.act(lambda e: e.activation(out=sq[:, 0:nchunk, 0:n], in_=xs[:, 0:nchunk, 0:n], func=AF.Square),
              r=[kx], w=[ksq])
        for c in range(nchunk):
            P.pe(lambda e, c=c: e.matmul(ps[ssbank][:, 0:n], lhsT=onesm[:], rhs=sq[:, c, 0:n],
                                         start=(c == 0), stop=(c == nchunk - 1)),
                 r=[ksq, "ones_b", "ones_q"], w=[PK(ssbank)])
        P.act(lambda e: e.activation(out=rstd[:, 0:n], in_=ps[ssbank][:, 0:n], func=AF.Sqrt, bias=eps, scale=1.0),
              r=[PK(ssbank)], w=[krs])
        P.dve(lambda e: e.reciprocal(out=rstd[:, 0:n], in_=rstd[:, 0:n]), r=[krs], w=[krs])
        for c in range(nchunk):
            P.dve(lambda e, c=c: e.scalar_tensor_tensor(out=hout[:, c, 0:n], in0=xs[:, c, 0:n],
                                                      scalar=cst[:, gain_col + c:gain_col + c + 1],
                                                      in1=rstd[:, 0:n], op0=ALU.mult, op1=ALU.mult),
                r=[kx, krs, "cst"], w=[(kh, c)])

    bankrot = [0]

    def nb():
        b = bankrot[0] % 4
        bankrot[0] += 1
        return b

    def proj_fm(wt, kw, oc0, h, kh, nk, n, bank, m=128):
        for kc in range(nk):
            P.pe(lambda e, kc=kc: e.matmul(ps[bank][0:m, 0:n], lhsT=wt[:, kc, oc0:oc0 + m], rhs=h[:, kc, 0:n],
                                           start=(kc == 0), stop=(kc == nk - 1)),
                 r=[kw, (kh, kc)], w=[PK(bank)])

    rope_mark = P.mark()
    C0s, S0s = rope_tables(pos_seq, T, C_FR0, C_SG0, "r0s")

    wk = P.sb("wk", [128, 8, 1280], BF16)
    wv = P.sb("wv", [128, 8, 512], BF16)
    load_w(wk[:], w0k_d.rearrange("(c p) n -> p c n", p=128), "wk")
    load_w(wv[:], w0v_d.rearrange("(c p) n -> p c n", p=128), "wv")
    xs2 = [P.sb(f"xs{i}", [128, 8, 512], F32) for i in range(2)]
    sq = P.sb("sq", [128, 8, 512], BF16)
    hb2 = [P.sb(f"hb{i}", [128, 8, 512], BF16) for i in range(2)]
    rstd = P.sb("rstd", [128, 512], F32)
    t1 = [P.sb(f"t1_{i}", [128, 512], F32) for i in range(2)]
    t2 = [P.sb(f"t2_{i}", [128, 512], F32) for i in range(2)]
    kst = [P.sb(f"kst{i}", [128, 5, 512], BF16) for i in range(2)]
    vst = [P.sb(f"vst{i}", [128, 8, 128], BF16) for i in range(2)]
    for i in range(2):
        P.pool(lambda e, i=i: e.memset(vst[i][:], 1.0), w=[("vst", i)])
    xseq_v = xT_seq.rearrange("(c p) t -> p c t", p=128)
    tcnt = [0]

    def rope_evac(bA, bB, Ct, St, col0, n, dst, kdst, kC="r0sC", kS="r0sS"):
        i = tcnt[0] % 2
        tcnt[0] += 1
        P.dve(lambda e: e.tensor_tensor(out=t1[i][:, 0:n], in0=ps[bA][:, 0:n], in1=Ct[:, col0:col0 + n], op=ALU.mult),
              r=[PK(bA), kC], w=[("t1", i)])
        P.dve(lambda e: e.tensor_tensor(out=t2[i][:, 0:n], in0=ps[bB][:, 0:n], in1=St[:, col0:col0 + n], op=ALU.mult),
              r=[PK(bB), kS], w=[("t2", i)])
        P.pool(lambda e: e.tensor_tensor(out=dst, in0=t1[i][:, 0:n], in1=t2[i][:, 0:n], op=ALU.add),
               r=[("t1", i), ("t2", i)], w=[kdst])

    for tc in range(8):
        xs = xs2[tc % 2]
        kx = ("xs", tc % 2)
        hbc = hb2[tc % 2]
        khb = ("hb", tc % 2)
        P.dma(xs[:], xseq_v[:, :, tc * 512:(tc + 1) * 512], w=[kx])
        rmsnorm_fm(xs, kx, 8, 512, gcol(0, 0), hbc, khb, ones_b, sq, "sq", rstd, "rstd", 7)
        if tc == 0 and "dbg_d" in dbg:
            dtmp = P.sb("dtmp", [128, 1024], F32)
            P.dma(dbg_d[:, 0:512], xs[:, 0, :], r=[kx], w=["dbg0"])
            P.dma(dbg_d[:, 512:1024], rstd[:], r=["rstd"], w=["dbg1"])
            P.dve(lambda e: e.tensor_copy(out=dtmp[:, 0:512], in_=hbc[:, 0, :]), r=[(khb, 0)], w=["dtmp"])
            P.dve(lambda e: e.tensor_copy(out=dtmp[:, 512:1024], in_=sq[:, 0, :]), r=["sq"], w=["dtmp"])
            P.dma(dbg_d[:, 1024:2048], dtmp[:], r=["dtmp"], w=["dbg2"])
            dt2 = P.sb("dt2", [128, 512], F32)
            P.act(lambda e: e.activation(out=dt2[:], in_=ps[7][:, :], func=AF.Copy), r=[PK(7)], w=["dt2"])
            P.dma(dbg_d[:, 2048:2560], dt2[:], r=["dt2"], w=["dbg3"])
            P.dma(dbg_d[:, 2560:3072], S0s[:, 0:512], r=["r0sS"], w=["dbg4"])
            P.dma(dbg_d[:, 3072:3584], C0s[:, 3584:4096], r=["r0sC"], w=["dbg5"])
            P.dma(dbg_d[:, 3584:4096], S0s[:, 3584:4096], r=["r0sS"], w=["dbg6"])
        ks = kst[tc % 2]
        for oc in range(5):
            bA, bB = nb(), nb()
            colA = oc * 128 if oc < 4 else 1024
            colB = 512 + oc * 128 if oc < 4 else 1152
            proj_fm(wk, "wk", colA, hbc, khb, 8, 512, bA)
            proj_fm(wk, "wk", colB, hbc, khb, 8, 512, bB)
            rope_evac(bA, bB, C0s, S0s, tc * 512, 512, ks[:, oc, :], ("kst", tc % 2, oc))
        for oc in range(4):
            P.dma(kT_d[oc, :, tc * 512:(tc + 1) * 512], ks[:, oc, :], r=[("kst", tc % 2, oc)], w=[("kT_d", oc, tc)])
        P.dma(kidxT_d[:, tc * 512:(tc + 1) * 512], ks[:, 4, :], r=[("kst", tc % 2, 4)], w=[("kidxT_d", tc)])
        for tb in range(4):
            kb = tc * 4 + tb
            b = nb()
            for kc in range(8):
                P.pe(lambda e, kc=kc, tb=tb: e.matmul(ps[b][:, :], lhsT=hbc[:, kc, tb * 128:(tb + 1) * 128],
                                                      rhs=wv[:, kc, :], start=(kc == 0), stop=(kc == 7)),
                     r=["wv", (khb, kc)], w=[PK(b)])
            vs = vst[kb % 2]
            pv = ps[b][:, :].rearrange("p (h two d) -> p h two d", two=2, d=64)
            vv = vs[:].rearrange("p (h two) d -> p h two d", two=2)
            P.act(lambda e, pv=pv, vv=vv: e.activation(out=vv[:, :, 0, 0:64], in_=pv[:, :, 0, :], func=AF.Copy),
                  r=[PK(b)], w=[("vst", kb % 2)])
            P.act(lambda e, pv=pv, vv=vv: e.activation(out=vv[:, :, 1, 64:128], in_=pv[:, :, 1, :], func=AF.Copy),
                  r=[PK(b)], w=[("vst", kb % 2)])
            P.dma(vaug_d[kb], vs[:].rearrange("p h d -> p (h d)"), r=[("vst", kb % 2)], w=[("vaug_d", kb)])
    P.barrier()
    P.release(rope_mark)

    for s_ in range(2):
        C0o, S0o = rope_tables(pos_own[s_], NO, C_FR0, C_SG0, "r0o")
        set_mark = P.mark()
        if stop("K"):
            return nc
        wq = P.sb("wq", [128, 8, 4608], BF16)
        wwi = P.sb("wwi", [128, 8, 16], BF16)
        for c in range(8):
            P.dma(wq[:, c, :], w0q_d[c * 128:(c + 1) * 128, :], w=["wq"], q="pool")
        load_w(wwi[:], w0wi_d.rearrange("(c p) n -> p c n", p=128), "wwi")
        uext = P.sb("uext", [128, 4, 16, 130], F32)
        gbs = P.sb("gbs", [128, 4, NO], BF16)
        q_mark = P.mark()
        xs2 = [P.sb("xsq", [128, 8, 512], F32)] * 2
        sq = P.sb("sq", [128, 8, 512], BF16)
        hb2 = [P.sb(f"hb{i}", [128, 8, 512], BF16) for i in range(2)]
        rstd = P.sb("rstd", [128, 512], F32)
        t1 = [P.sb(f"t1_{i}", [128, 512], F32) for i in range(2)]
        t2 = [P.sb(f"t2_{i}", [128, 512], F32) for i in range(2)]
        qrot = [P.sb(f"qrot{i}", [128, 512], BF16) for i in range(6)]
        gcs = [P.sb(f"gcs{i}", [128, 512], F32) for i in range(2)]
        qrc = [0]
        xown_v = xT_own[s_].rearrange("(c p) t -> p c t", p=128)
        for tc in range(5):
            n = 512 if tc < 4 else 32
            xs = xs2[0]
            kx = ("xs", 0)
            hbc = hb2[tc % 2]
            khb = ("hb", tc % 2)
            P.dma(xs[:, :, 0:n], xown_v[:, :, tc * 512:tc * 512 + n], w=[kx])
            rmsnorm_fm(xs, kx, 8, n, gcol(0, 0), hbc, khb, ones_b, sq, "sq", rstd, "rstd", 7)
            if tc < 4:
                for oc in range(12):
                    bA, bB = nb(), nb()
                    colA = oc * 128 if oc < 4 else 1024 + (oc - 4) * 128
                    colB = 512 + oc * 128 if oc < 4 else 2048 + (oc - 4) * 128
                    proj_fm(wq, "wq", colA, hbc, khb, 8, 512, bA)
                    proj_fm(wq, "wq", colB, hbc, khb, 8, 512, bB)
                    qi_ = qrc[0] % 6
                    qrc[0] += 1
                    rope_evac(bA, bB, C0o, S0o, tc * 512, 512, qrot[qi_][:], ("qrot", qi_), "r0oC", "r0oS")
                    if oc < 4:
                        P.dma(qT_d[oc, :, tc * 512:(tc + 1) * 512], qrot[qi_][:], r=[("qrot", qi_)], w=[("qT_d", oc, tc)])
                    else:
                        P.dma(qidxT_d[oc - 4, :, tc * 512:(tc + 1) * 512], qrot[qi_][:], r=[("qrot", qi_)],
                              w=[("qidxT_d", oc - 4, tc)])
                for cc in range(4):
                    b = nb()
                    proj_fm(wq, "wq", 3072 + cc * 128, hbc, khb, 8, 512, b)
                    P.act(lambda e, b=b, cc=cc: e.activation(out=gbs[:, cc, tc * 512:(tc + 1) * 512], in_=ps[b][:, :], func=AF.Copy),
                          r=[PK(b)], w=[("gbs", cc)])
                for tb in range(4):
                    b = nb()
                    for kc in range(8):
                        P.pe(lambda e, kc=kc, tb=tb, b=b: e.matmul(ps[b][:, 0:16], lhsT=hbc[:, kc, tb * 128:(tb + 1) * 128],
                                                                   rhs=wwi[:, kc, :], start=(kc == 0), stop=(kc == 7)),
                             r=["wwi", (khb, kc)], w=[PK(b)])
                    P.act(lambda e, b=b, tb=tb: e.activation(out=widx[:, tc * 4 + tb, :], in_=ps[b][:, 0:16], func=AF.Copy),
                          r=[PK(b)], w=["widx"])
            for cc in range(4):
                bA, bB = nb(), nb()
                proj_fm(wq, "wq", 3584 + cc * 128, hbc, khb, 8, n, bA)
                proj_fm(wq, "wq", 4096 + cc * 128, hbc, khb, 8, n, bB)
                g = gcs[cc % 2]
                P.act(lambda e, g=g, bA=bA: e.activation(out=g[:, 0:n], in_=ps[bA][:, 0:n], func=AF.Copy),
                      r=[PK(bA)], w=[("gcs", cc % 2)])
                if tc < 4:
                    o_ap = uext[:, cc, tc * 4:(tc + 1) * 4, 2:130]
                    i0 = ps[bB][:, :].rearrange("p (b t) -> p b t", t=128)
                    i1 = g[:].rearrange("p (b t) -> p b t", t=128)
                else:
                    o_ap = uext[:, cc, :, 0:2]
                    i0 = ps[bB][:, 0:32].rearrange("p (b t) -> p b t", t=2)
                    i1 = g[:, 0:32].rearrange("p (b t) -> p b t", t=2)
                P.dve(lambda e, o_ap=o_ap, i0=i0, i1=i1: e.tensor_tensor(out=o_ap, in0=i0, in1=i1, op=ALU.mult),
                      r=[PK(bB), ("gcs", cc % 2)], w=[("uext", cc)])
        P.barrier()
        P.release(q_mark)
        cacc = [P.sb(f"cacc{i}", [128, 16, 128], F32) for i in range(2)]
        cvo = [P.sb(f"cvo{i}", [128, NO], BF16) for i in range(2)]
        for cc in range(4):
            a = cacc[cc % 2]
            ka = ("cacc", cc % 2)
            P.dve(lambda e, a=a, cc=cc: e.tensor_scalar(out=a[:], in0=uext[:, cc, :, 2:130],
                                                        scalar1=cst[:, C_CW + 8 + cc:C_CW + 9 + cc], scalar2=None, op0=ALU.mult),
                  r=[("uext", cc), "cst"], w=[ka])
            P.dve(lambda e, a=a, cc=cc: e.scalar_tensor_tensor(out=a[:], in0=uext[:, cc, :, 1:129],
                                                               scalar=cst[:, C_CW + 4 + cc:C_CW + 5 + cc], in1=a[:],
                                                               op0=ALU.mult, op1=ALU.add), r=[("uext", cc), "cst", ka], w=[ka])
            P.dve(lambda e, a=a, cc=cc: e.scalar_tensor_tensor(out=a[:], in0=uext[:, cc, :, 0:128],
                                                               scalar=cst[:, C_CW + cc:C_CW + 1 + cc], in1=a[:],
                                                               op0=ALU.mult, op1=ALU.add), r=[("uext", cc), "cst", ka], w=[ka])
            co = cvo[cc % 2]
            P.dve(lambda e, a=a, cc=cc, co=co: e.tensor_tensor(out=co[:], in0=a[:].rearrange("p b t -> p (b t)"),
                                                               in1=gbs[:, cc, :], op=ALU.mult),
                  r=[ka, ("gbs", cc)], w=[("cvo", cc % 2)])
            P.dma(convT_d[cc], co[:], r=[("cvo", cc % 2)], w=[("convT_d", cc)])
        P.barrier()
        P.release(base_mark)

        if stop("Q"):
            return nc
        kidx = P.sb("kidx", [128, T], BF16)
        qidx = P.sb("qidx", [128, 8, NO], BF16)
        P.dma(kidx[:], kidxT_d, r=[("kidxT_d", t_) for t_ in range(8)], w=["kidx"])
        for oc in range(8):
            P.dma(qidx[:, oc, :], qidxT_d[oc], r=[("qidxT_d", oc, t_) for t_ in range(4)], w=[("qidx", oc)])
        sc4 = [P.sb(f"sc{i}", [128, T], F32) for i in range(4)]
        junk = P.sb("junk", [128, T], BF16)
        m01 = P.sb("m01", [128, T], F32)
        cbt = [P.sb(f"cbt{i}", [128, 1024], F32) for i in range(2)]
        rr = [P.sb(f"rr{i}", [128, 512], BF16) for i in range(6)]
        mTs = [P.sb(f"mTs{i}", [128, 32, 512], BF16) for i in range(1)]
        dg2 = [P.sb(f"dg{i}", [128, 16, 128], BF16) for i in range(2)]
        identb = P.sb("identb", [128, 128], BF16)
        sm2 = [P.sb(f"sm{i}", [128, 16], F32) for i in range(2)]
        stepT2 = [P.sb(f"stepT{i}", [128, 2, NBIS + 1], F32) for i in range(2)]
        P.dve(lambda e: e.tensor_copy(out=identb[:], in_=ident[:]), r=["ident"], w=["identb"])
        rcnt = [0]
        acnt_i = [0]
        mt = mTs[0]

        def acc_half(g, hf, hi):
            nk = 1024 * (g + 1)
            nch = nk // 512
            for jj in range(2):
                j = 2 * hf + jj
                qi = 4 * g + j
                sc = sc4[j]
                ksc = ("sc", j)
                dg = dg2[qi % 2]
                kdg = ("dg", qi % 2)
                for h in range(16):
                    P.pool(lambda e, dg=dg, h=h, qi=qi: e.tensor_scalar(out=dg[:, h, :], in0=identb[:], scalar1=widx[:, qi, h:h + 1],
                                                                        scalar2=None, op0=ALU.mult),
                           r=["identb", "widx"], w=[kdg])
                for ch in range(nch):
                    accb = 4 + (acnt_i[0] % 2)
                    acnt_i[0] += 1
                    pend = []
                    for _f in range(NBURST_I):
                        P.pe(lambda e: e.matmul(ps[7][:, :], lhsT=identb[:], rhs=junk[:, 0:512], start=True, stop=True))
                    for h in range(16):
                        b = nb()
                        base = (h % 2) * 64
                        if NFILL_I and h % 2 == 1:
                            P.pe(lambda e: e.matmul(ps[7][:, :], lhsT=identb[:], rhs=junk[:, 0:512], start=True, stop=True))
                        P.pe(lambda e, b=b, h=h, base=base, ch=ch, qi=qi: e.matmul(
                            ps[b][:, :], lhsT=qidx[base:base + 64, h // 2, qi * 128:(qi + 1) * 128],
                            rhs=kidx[base:base + 64, ch * 512:(ch + 1) * 512], start=True, stop=True),
                            r=["kidx", ("qidx", h // 2)], w=[PK(b)])
                        ri = rcnt[0] % 6
                        rcnt[0] += 1
                        r_ = rr[ri]
                        P.act(lambda e, b=b, r_=r_: e.activation(out=r_[:], in_=ps[b][:, :], func=AF.Relu),
                              r=[PK(b)], w=[("rr", ri)])
                        pend.append((h, r_, ri))
                        if len(pend) > 2:
                            h0, r0, ri0 = pend.pop(0)
                            P.pe(lambda e, h0=h0, r0=r0, accb=accb, dg=dg: e.matmul(ps[accb][:, :], lhsT=dg[:, h0, :], rhs=r0[:],
                                                                                    start=(h0 == 0), stop=(h0 == 15)),
                                 r=[("rr", ri0), kdg], w=[PK(accb)])
                    for h0, r0, ri0 in pend:
                        P.pe(lambda e, h0=h0, r0=r0, accb=accb, dg=dg: e.matmul(ps[accb][:, :], lhsT=dg[:, h0, :], rhs=r0[:],
                                                                                start=(h0 == 0), stop=(h0 == 15)),
                             r=[("rr", ri0), kdg], w=[PK(accb)])
                    P.act(lambda e, sc=sc, ch=ch, accb=accb: e.activation(out=sc[:, ch * 512:(ch + 1) * 512], in_=ps[accb][:, :], func=AF.Copy),
                          r=[PK(accb)], w=[(ksc, ch)])

        def prep_half(g, hf, hi):
            nk = 1024 * (g + 1)
            nch = nk // 512
            sm = sm2[hi % 2]
            stepT = stepT2[hi % 2]
            for jj in range(2):
                j = 2 * hf + jj
                qi = 4 * g + j
                sc = sc4[j]
                allsc = [(("sc", j), ch) for ch in range(nch)]
                cb = cbt[jj]
                P.dma(cb[:], cb_d[s_, (g // 2) * 4 + j], w=[("cbt", jj)])
                P.dve(lambda e, sc=sc, jj=jj, sm=sm: e.tensor_reduce(out=sm[:, jj:jj + 1], in_=sc[:, 0:nk], axis=AX.X, op=ALU.max,
                                                                     apply_absolute_value=True), r=allsc, w=[("smA", hi % 2, jj)])
                P.dve(lambda e, sc=sc, cb=cb: e.tensor_tensor(out=sc[:, nk - 1024:nk], in0=sc[:, nk - 1024:nk], in1=cb[:],
                                                              op=ALU.add), r=allsc + [("cbt", jj), ("smA", hi % 2, jj)], w=allsc)
            allA = [("smA", hi % 2, jj) for jj in range(2)]
            P.dve(lambda e, sm=sm: e.tensor_single_scalar(out=sm[:, 0:2].bitcast(I32), in_=sm[:, 0:2].bitcast(I32),
                                                          scalar=0x7F800000, op=ALU.bitwise_and), r=allA, w=[("smA2", hi % 2)])
            P.dve(lambda e, sm=sm: e.tensor_scalar(out=sm[:, 0:2], in0=sm[:, 0:2], scalar1=2.0, scalar2=1e-30,
                                                   op0=ALU.mult, op1=ALU.max), r=[("smA2", hi % 2)], w=[("smA2", hi % 2)])
            for i in range(NBIS + 1):
                P.pool(lambda e, i=i, sm=sm, stepT=stepT: e.tensor_scalar(out=stepT[:, :, i], in0=sm[:, 0:2], scalar1=float(2.0 ** -i),
                                                                          scalar2=None, op0=ALU.mult),
                       r=[("smA2", hi % 2)], w=[("stepT", hi % 2, i)])
            P.pool(lambda e, sm=sm: e.memset(sm[:, 2:4], 0.0), w=[("mid", hi % 2)])

        def bis_half(g, hf, hi):
            nk = 1024 * (g + 1)
            nch = nk // 512
            sm = sm2[hi % 2]
            stepT = stepT2[hi % 2]
            kmid = ("mid", hi % 2)
            qi0 = 4 * g + 2 * hf
            kq2 = kq[:, s_ * 16 + qi0:s_ * 16 + qi0 + 2]
            for i in range(NBIS):
                for jj in range(2):
                    j = 2 * hf + jj
                    P.dve(lambda e, j=j, jj=jj, sm=sm: e.tensor_scalar(out=junk[:, 0:nk], in0=sc4[j][:, 0:nk], scalar1=sm[:, 2 + jj:3 + jj],
                                                                       scalar2=None, op0=ALU.is_ge, op1=ALU.add, accum_out=sm[:, 4 + jj:5 + jj]),
                          r=[(("sc", j), ch) for ch in range(nch)] + [kmid], w=[("cnt", hi % 2, jj)])
                P.dve(lambda e, sm=sm: e.tensor_tensor(out=sm[:, 6:8], in0=sm[:, 4:6], in1=kq2, op=ALU.is_ge),
                      r=[("cnt", hi % 2, jj) for jj in range(2)] + ["kq"], w=[("s4", hi % 2)])
                P.dve(lambda e, i=i, sm=sm, stepT=stepT: e.scalar_tensor_tensor(out=sm[:, 8:10], in0=sm[:, 6:8], scalar=0.5, in1=stepT[:, :, i],
                                                                                op0=ALU.subtract, op1=ALU.mult),
                      r=[("s4", hi % 2), ("stepT", hi % 2, i)], w=[("d4", hi % 2)])
                P.dve(lambda e, sm=sm: e.tensor_tensor(out=sm[:, 2:4], in0=sm[:, 2:4], in1=sm[:, 8:10], op=ALU.add),
                      r=[("d4", hi % 2), kmid], w=[kmid])
            P.dve(lambda e, sm=sm, stepT=stepT: e.tensor_tensor(out=sm[:, 10:12], in0=sm[:, 2:4], in1=stepT[:, :, NBIS], op=ALU.subtract),
                  r=[kmid, ("stepT", hi % 2, NBIS)], w=[("thr", hi % 2)])
            for jj in range(2):
                j = 2 * hf + jj
                P.dve(lambda e, j=j, jj=jj, sm=sm: e.tensor_scalar(out=m01[:, 0:nk], in0=sc4[j][:, 0:nk], scalar1=sm[:, 10 + jj:11 + jj], scalar2=None,
                                                                   op0=ALU.is_ge), r=[(("sc", j), ch) for ch in range(nch)] + [("thr", hi % 2)], w=["m01"])
                for k4 in range(nk // 512):
                    b = 6
                    for kk in range(4):
                        kb = k4 * 4 + kk
                        P.pe(lambda e, b=b, kk=kk, kb=kb: e.transpose(out=ps[b][:, kk * 128:(kk + 1) * 128],
                                                                      in_=m01[:, kb * 128:(kb + 1) * 128], identity=ident[:]),
                             r=["m01", "ident"], w=[PK(b)])
                    P.act(lambda e, b=b, k4=k4, j=j: e.activation(
                        out=mt[:, k4 * 4:(k4 + 1) * 4, j * 128:(j + 1) * 128],
                        in_=ps[b][:, :].rearrange("p (k t) -> p k t", t=128), func=AF.Copy),
                        r=[PK(b)], w=["mTs"])
            if hf == 1:
                P.dma(maskT_d[g, :, 0:nk // 128, :], mt[:, 0:nk // 128, :], r=["mTs"], w=[("maskT_d", g)])

        halves = [(g, hf) for g in range(4) for hf in range(2)]
        for hi, (g, hf) in enumerate(halves):
            acc_half(g, hf, hi)
            if hi > 0:
                bis_half(halves[hi - 1][0], halves[hi - 1][1], hi - 1)
            prep_half(g, hf, hi)
        bis_half(halves[-1][0], halves[-1][1], len(halves) - 1)
        P.barrier()
        P.release(base_mark)

        if stop("I"):
            return nc
        def attention(nheads, hp_loader, st_emit, mask_for, scale, out_d, okey):
            pts = [P.sb(f"pt{i}", [128, 512], BF16) for i in range(6)]
            ident_fill = P.sb("ifill", [128, 128], BF16)
            P.pool(lambda e: e.memset(ident_fill[:], 0.0), w=["ifill"])
            rdn = [P.sb(f"rdn{i}", [128, 512], F32) for i in range(2)]
            ost = [P.sb(f"ost{i}", [128, NO], BF16) for i in range(2)]
            ucnt = [0]
            acnt = [0]
            for hp in range(nheads // 2):
                bufs = hp_loader(hp)
                o_t = ost[hp % 2]
                for hh in range(2):
                    h = hp * 2 + hh
                    for g in range(4):
                        nkb = 8 * (g + 1)
                        accb = 4 + (acnt[0] % 2)
                        acnt[0] += 1
                        pend = []
                        for _f in range(NBURST):
                            P.pe(lambda e: e.matmul(ps[6][:, :], lhsT=ident_fill[:], rhs=pts[0][:], start=True, stop=True))
                        for kb in range(nkb):
                            sb_ = nb()
                            rel_ = kb - (nkb - 8)
                            c0 = 128 * (rel_ // 2) if rel_ > 0 else 0
                            st_emit(bufs, h, hh, g, kb, sb_, c0)
                            for _f in range(NFILL):
                                P.pe(lambda e: e.matmul(ps[6][:, :], lhsT=ident_fill[:], rhs=pts[0][:], start=True, stop=True))
                            pi = ucnt[0] % 6
                            ucnt[0] += 1
                            pt = pts[pi]
                            P.act(lambda e, sb_=sb_, pt=pt, c0=c0: e.activation(out=pt[:, c0:512], in_=ps[sb_][:, c0:512], func=AF.Exp, scale=scale),
                                  r=[PK(sb_)], w=[("pt", pi)])
                            mk = mask_for(bufs, g, kb)
                            if mk is not None:
                                map_, mkey = mk
                                P.dve(lambda e, pt=pt, map_=map_, c0=c0: e.tensor_tensor(out=pt[:, c0:512], in0=pt[:, c0:512], in1=map_[:, c0:512], op=ALU.mult),
                                      r=[("pt", pi), mkey], w=[("pt", pi)])
                            pend.append((kb, pt, pi, c0))
                            if len(pend) > 2:
                                kb0, pt0, pi0, c00 = pend.pop(0)
                                P.pe(lambda e, kb0=kb0, pt0=pt0, accb=accb, c00=c00: e.matmul(
                                    ps[accb][:, c00:512], lhsT=bufs["v"][:, kb0, hh * 128:(hh + 1) * 128], rhs=pt0[:, c00:512],
                                    start=(kb0 == 0), stop=(kb0 == nkb - 1)), r=[("pt", pi0), bufs["kv"]], w=[PK(accb)])
                        for kb0, pt0, pi0, c00 in pend:
                            P.pe(lambda e, kb0=kb0, pt0=pt0, accb=accb, c00=c00: e.matmul(
                                ps[accb][:, c00:512], lhsT=bufs["v"][:, kb0, hh * 128:(hh + 1) * 128], rhs=pt0[:, c00:512],
                                start=(kb0 == 0), stop=(kb0 == nkb - 1)), r=[("pt", pi0), bufs["kv"]], w=[PK(accb)])
                        rd = rdn[acnt[0] % 2]
                        krd = ("rdn", acnt[0] % 2)
                        nlo, dlo = (0, 64) if hh == 0 else (64, 0)
                        P.dve(lambda e, rd=rd, accb=accb, dlo=dlo: e.reciprocal(out=rd[dlo:dlo + 64, :], in_=ps[accb][dlo:dlo + 64, :]),
                              r=[PK(accb)], w=[krd])
                        P.dve(lambda e, rd=rd, accb=accb, dlo=dlo, nlo=nlo, g=g, o_t=o_t: e.tensor_tensor(
                            out=o_t[nlo:nlo + 64, g * 512:(g + 1) * 512], in0=ps[accb][nlo:nlo + 64, :],
                            in1=rd[dlo:dlo + 64, :], op=ALU.mult), r=[PK(accb), krd], w=[("ost", hp % 2)])
                P.dma(out_d[hp], o_t[:], r=[("ost", hp % 2)], w=[(okey, hp)])

        mres = P.sb("mres", [128, 80, 512], BF16)
        goff = [0, 8, 24, 48]
        for g in range(4):
            nkb = 8 * (g + 1)
            P.dma(mres[:, goff[g]:goff[g] + nkb, :], maskT_d[g, :, 0:nkb, :], r=[("maskT_d", g)], w=[("mres", g)])
        kb2 = [P.sb(f"kb2_{i}", [128, T], BF16) for i in range(2)]
        qb2 = [P.sb(f"qb2_{i}", [128, NO], BF16) for i in range(2)]
        vb2 = [P.sb(f"vb2_{i}", [128, 32, 256], BF16) for i in range(2)]

        def hp_loader0(hp):
            i = hp % 2
            key = ("kvq0", i)
            P.dma(kb2[i][:], kT_d[hp], r=[("kT_d", hp, t_) for t_ in range(8)], w=[key])
            P.dma(qb2[i][:], qT_d[hp], r=[("qT_d", hp, t_) for t_ in range(4)], w=[key])
            P.dma(vb2[i][:], vaug_d[:, :, hp * 256:(hp + 1) * 256].rearrange("k p c -> p k c"),
                  r=[("vaug_d", k_) for k_ in range(32)], w=[key])
            return {"k": kb2[i], "q": qb2[i], "v": vb2[i], "kv": key}

        def st_emit0(bufs, h, hh, g, kb, sb_, c0=0):
            base = hh * 64
            P.pe(lambda e: e.matmul(ps[sb_][:, c0:512], lhsT=bufs["k"][base:base + 64, kb * 128:(kb + 1) * 128],
                                    rhs=bufs["q"][base:base + 64, g * 512 + c0:(g + 1) * 512], start=True, stop=True),
                 r=[bufs["kv"]], w=[PK(sb_)])

        def mask_for0(bufs, g, kb):
            return mres[:, goff[g] + kb, :], ("mres", g)

        attention(8, hp_loader0, st_emit0, mask_for0, 0.125, attnT_d, "attnT_d")
        P.barrier()
        P.release(base_mark)

        if stop("A0"):
            return nc
        def out_phase(in_chunks, wd, layer, x_src, xkey_src, x_dst, xkey_dst):
            W1OFF[0] = P.mark()
            w1pre = P.sb("w1s", [128, 8, 4096], BF16)
            wo = P.sb("wo", [128, 8, D], BF16)
            load_w(wo[:], wd.rearrange("(c p) n -> p c n", p=128), "wo")
            w1v_ = w1_d[layer].rearrange("(c p) n -> p c n", p=128)
            for c in range(8):
                P.dma(w1pre[:, c, :], w1v_[:, c, :], w=[("w1s", c)], q="pool")
            ain = P.sb("ain", [128, 8, NO], BF16)
            for c, (ap_, rk) in enumerate(in_chunks):
                P.dma(ain[:, c, :], ap_, r=rk, w=[("ain", c)])
            xo2 = [P.sb(f"xo{i}", [128, 8, 512], F32) for i in range(2)]
            mx = P.sb("mx", [128, 8, 512], F32)
            sq_ = P.sb("sqo", [128, 8, 512], BF16)
            rs = P.sb("rso", [128, 512], F32)
            xv = x_src.rearrange("(c p) t -> p c t", p=128)
            xdv = x_dst.rearrange("(c p) t -> p c t", p=128)
            for tc in range(4):
                xo = xo2[tc % 2]
                kxo = ("xo", tc % 2)
                P.dma(xo[:], xv[:, :, tc * 512:(tc + 1) * 512], r=[(xkey_src, tc)], w=[kxo])
                for oc in range(8):
                    b = nb()
                    for kc in range(8):
                        P.pe(lambda e, kc=kc, oc=oc, b=b: e.matmul(ps[b][:, :], lhsT=wo[:, kc, oc * 128:(oc + 1) * 128],
                                                                   rhs=ain[:, kc, tc * 512:(tc + 1) * 512],
                                                                   start=(kc == 0), stop=(kc == 7)),
                             r=["wo", ("ain", kc)], w=[PK(b)])
                    P.act(lambda e, b=b, oc=oc: e.activation(out=mx[:, oc, :], in_=ps[b][:, :], func=AF.Copy),
                          r=[PK(b)], w=[("mx", oc)])
                    P.dve(lambda e, b=b, oc=oc: e.tensor_tensor(out=sq_[:, oc, :], in0=mx[:, oc, :], in1=mx[:, oc, :], op=ALU.mult),
                          r=[("mx", oc)], w=[("sqo", oc)])
                for c in range(8):
                    P.pe(lambda e, c=c: e.matmul(ps[7][:, :], lhsT=ones_b[:], rhs=sq_[:, c, :], start=(c == 0), stop=(c == 7)),
                         r=[("sqo", c), "ones_b"], w=[PK(7)])
                P.act(lambda e: e.activation(out=rs[:], in_=ps[7][:, :], func=AF.Sqrt, bias=EPS, scale=1.0), r=[PK(7)], w=["rso"])
                P.dve(lambda e: e.reciprocal(out=rs[:], in_=rs[:]), r=["rso"], w=["rso"])
                for c in range(8):
                    gc_ = gcol(layer, 1) + c
                    P.dve(lambda e, c=c, gc_=gc_: e.scalar_tensor_tensor(out=mx[:, c, :], in0=mx[:, c, :], scalar=cst[:, gc_:gc_ + 1],
                                                                          in1=rs[:], op0=ALU.mult, op1=ALU.mult),
                           r=[("mx", c), "rso", "cst"], w=[("mx", c)])
                    P.dve(lambda e, c=c, xo=xo: e.tensor_tensor(out=xo[:, c, :], in0=xo[:, c, :], in1=mx[:, c, :], op=ALU.add),
                          r=[("mx", c), kxo], w=[kxo])
                P.dma(xdv[:, :, tc * 512:(tc + 1) * 512], xo[:], r=[kxo], w=[(xkey_dst, tc)])

        in0 = [(attnT_d[c], [("attnT_d", c)]) for c in range(4)] + [(convT_d[c], [("convT_d", c)]) for c in range(4)]
        out_phase(in0, wout_d, 0, xT_own[s_, :, 0:NO], "xown", x1T_d, "x1T_d")
        P.barrier()
        P.release(base_mark)

        def ffn_phase(layer, x_src, xkey_src, x_dst, xkey_dst):
            assert P.mark() == W1OFF[0], (P.mark(), W1OFF[0])
            w1s = P.sb("w1s", [128, 8, 4096], BF16)
            w2s = P.sb("w2s", [128, 32, D], BF16)
            w1v = w1_d[layer].rearrange("(c p) n -> p c n", p=128)
            w2v = w2_d[layer].rearrange("(c p) n -> p c n", p=128)
            for c in range(0, 32, 4):
                P.dma(w2s[:, c:c + 4, :], w2v[:, c:c + 4, :], w=[("w2s", c // 4)], q="pool")
            xf = P.sb("xf", [128, 8, 512], F32)
            off3 = P.mark()
            yb = P.sb("yb", [128, 8, 512], F32)
            end3 = P.mark()
            P.release(off3)
            sq_f = P.sb("sqf", [128, 8, 512], BF16)
            hf = P.sb("hf", [128, 8, 512], BF16)
            assert P.mark() == end3
            h1 = P.sb("h1", [128, 32, 512], BF16)
            sqy = [P.sb(f"sqy{i}", [128, 512], BF16) for i in range(2)]
            rt = [P.sb(f"rt{i}", [128, 512], F32) for i in range(2)]
            alias_keys = ["sqf"] + [("hf", c) for c in range(8)]
            rsf = P.sb("rsf", [128, 512], F32)
            xv = x_src.rearrange("(c p) t -> p c t", p=128)
            xdv = x_dst.rearrange("(c p) t -> p c t", p=128)
            rc = [0]
            for tc in range(4):
                P.dma(xf[:], xv[:, :, tc * 512:(tc + 1) * 512], r=[(xkey_src, tc)], w=["xf"])
                P.act(lambda e: e.activation(out=rsf[:, 0:1], in_=rsf[:, 0:1], func=AF.Copy), r=["rsf"],
                      w=alias_keys + ["rsf"] + [("yb", c) for c in range(8)])
                rmsnorm_fm(xf, "xf", 8, 512, gcol(layer, 2), hf, "hf", ones_b, sq_f, "sqf", rsf, "rsf", 7)
                for oc in range(32):
                    b = nb()
                    for kc in range(8):
                        P.pe(lambda e, kc=kc, oc=oc, b=b: e.matmul(ps[b][:, :], lhsT=w1s[:, kc, oc * 128:(oc + 1) * 128],
                                                                   rhs=hf[:, kc, :], start=(kc == 0), stop=(kc == 7)),
                             r=[("w1s", kc), ("hf", kc)], w=[PK(b)])
                    ri = rc[0] % 2
                    rc[0] += 1
                    P.act(lambda e, b=b, ri=ri: e.activation(out=rt[ri][:], in_=ps[b][:, :], func=AF.Relu), r=[PK(b)], w=[("rt", ri)])
                    eng = P.dve if oc % 2 == 0 else P.pool
                    eng(lambda e, ri=ri, oc=oc: e.tensor_tensor(out=h1[:, oc, :], in0=rt[ri][:], in1=rt[ri][:], op=ALU.mult),
                        r=[("rt", ri)], w=[("h1", oc)])
                for oc in range(8):
                    b = nb()
                    for kc in range(32):
                        P.pe(lambda e, kc=kc, oc=oc, b=b: e.matmul(ps[b][:, :], lhsT=w2s[:, kc, oc * 128:(oc + 1) * 128],
                                                                   rhs=h1[:, kc, :], start=(kc == 0), stop=(kc == 31)),
                             r=[("w2s", kc // 4), ("h1", kc)], w=[PK(b)])
                    P.act(lambda e, b=b, oc=oc: e.activation(out=yb[:, oc, :], in_=ps[b][:, :], func=AF.Copy), r=[PK(b)],
                          w=[("yb", oc)] + (alias_keys if oc == 0 else []))
                    P.dve(lambda e, oc=oc: e.tensor_tensor(out=sqy[oc % 2][:], in0=yb[:, oc, :], in1=yb[:, oc, :], op=ALU.mult),
                          r=[("yb", oc)], w=[("sqy", oc % 2)])
                    P.pe(lambda e, oc=oc: e.matmul(ps[7][:, :], lhsT=ones_b[:], rhs=sqy[oc % 2][:], start=(oc == 0), stop=(oc == 7)),
                         r=[("sqy", oc % 2), "ones_b"], w=[PK(7)])
                P.act(lambda e: e.activation(out=rsf[:], in_=ps[7][:, :], func=AF.Sqrt, bias=EPS, scale=1.0), r=[PK(7)], w=["rsf"])
                P.dve(lambda e: e.reciprocal(out=rsf[:], in_=rsf[:]), r=["rsf"], w=["rsf"])
                for c in range(8):
                    gc_ = gcol(layer, 3) + c
                    P.dve(lambda e, c=c, gc_=gc_: e.scalar_tensor_tensor(out=yb[:, c, :], in0=yb[:, c, :], scalar=cst[:, gc_:gc_ + 1],
                                                                          in1=rsf[:], op0=ALU.mult, op1=ALU.mult),
                           r=[("yb", c), "rsf", "cst"], w=[("yb", c)])
                    P.dve(lambda e, c=c: e.tensor_tensor(out=yb[:, c, :], in0=yb[:, c, :], in1=xf[:, c, :], op=ALU.add),
                          r=[("yb", c), "xf"], w=[("yb", c)])
                P.dma(xdv[:, :, tc * 512:(tc + 1) * 512], yb[:], r=[("yb", c) for c in range(8)], w=[(xkey_dst, tc)])

        if stop("O0"):
            return nc
        ffn_phase(0, x1T_d, "x1T_d", x2T_d[s_], ("x2T_d", s_))
        P.barrier()
        P.release(base_mark)

    if stop("F0"):
        return nc
    tabs = [rope_tables(pos_own[s1], NO, C_FR1, C_SG1, f"r1o{s1}") for s1 in range(2)]
    C1S1 = [None, None]
    m1 = P.mark()
    wdq = P.sb("wdq", [128, 8, 384], BF16)
    wuq = P.sb("wuq", [128, 3, 2048], BF16)
    wdkv = P.sb("wdkv", [128, 8, 320], BF16)
    load_w(wdq[:], wdq_d.rearrange("(c p) n -> p c n", p=128), "wdq")
    load_w(wuq[:], wuq_d.rearrange("(c p) n -> p c n", p=128), "wuq")
    load_w(wdkv[:], wdkv_d.rearrange("(c p) n -> p c n", p=128), "wdkv")
    xs2 = [P.sb(f"xs{i}", [128, 8, 512], F32) for i in range(2)]
    sq = P.sb("sq", [128, 8, 512], BF16)
    hb2 = [P.sb(f"hb{i}", [128, 8, 512], BF16) for i in range(2)]
    rstd = P.sb("rstd", [128, 512], F32)
    t1 = [P.sb(f"t1_{i}", [128, 512], F32) for i in range(2)]
    t2 = [P.sb(f"t2_{i}", [128, 512], F32) for i in range(2)]
    cq = P.sb("cq", [128, 3, 512], F32)
    cqn = P.sb("cqn", [128, 3, 512], BF16)
    ckv = P.sb("ckv", [128, 2, 512], F32)
    ckvn = P.sb("ckvn", [128, 2, 512], F32)
    krs = P.sb("krs", [32, 512], F32)
    qst = [P.sb(f"qst1_{i}", [128, 16, 512], BF16) for i in range(2)]
    sq3 = P.sb("sq3", [128, 3, 512], BF16)
    rs3 = P.sb("rs3", [128, 512], F32)

    def rope_evac1(bA, bB, m, col0, dst, kdst, eng_out="pool"):
        Cc, Ss = C1S1[0], C1S1[1]
        i = tcnt[0] % 2
        tcnt[0] += 1
        P.dve(lambda e: e.tensor_tensor(out=t1[i][0:m, :], in0=ps[bA][0:m, :], in1=Cc[0:m, col0:col0 + 512], op=ALU.mult),
              r=[PK(bA), "r1o0C", "r1o1C"], w=[("t1", i)])
        P.dve(lambda e: e.tensor_tensor(out=t2[i][0:m, :], in0=ps[bB][0:m, :], in1=Ss[0:m, col0:col0 + 512], op=ALU.mult),
              r=[PK(bB), "r1o0S", "r1o1S"], w=[("t2", i)])
        P.pool(lambda e: e.tensor_tensor(out=dst, in0=t1[i][0:m, :], in1=t2[i][0:m, :], op=ALU.add),
               r=[("t1", i), ("t2", i)], w=[kdst])

    for s1 in range(2):
        C1S1[0], C1S1[1] = tabs[s1]
        x2v = x2T_d[s1].rearrange("(c p) t -> p c t", p=128)
        for tc in range(4):
            xs = xs2[tc % 2]
            kx = ("xs", tc % 2)
            hbc = hb2[tc % 2]
            khb = ("hb", tc % 2)
            P.dma(xs[:], x2v[:, :, tc * 512:(tc + 1) * 512], r=[(("x2T_d", s1), tc)], w=[kx])
            rmsnorm_fm(xs, kx, 8, 512, gcol(1, 0), hbc, khb, ones_b, sq, "sq", rstd, "rstd", 7)
            if s1 == 0:
                for oc in range(3):
                    b = nb()
                    proj_fm(wdq, "wdq", oc * 128, hbc, khb, 8, 512, b)
                    P.act(lambda e, b=b, oc=oc: e.activation(out=cq[:, oc, :], in_=ps[b][:, :], func=AF.Copy), r=[PK(b)], w=["cq"])
                P.act(lambda e: e.activation(out=sq3[:], in_=cq[:], func=AF.Square), r=["cq"], w=["sq3"])
                for c in range(3):
                    P.pe(lambda e, c=c: e.matmul(ps[7][:, :], lhsT=ones_q[:], rhs=sq3[:, c, :], start=(c == 0), stop=(c == 2)),
                         r=["sq3", "ones_q"], w=[PK(7)])
                P.act(lambda e: e.activation(out=rs3[:], in_=ps[7][:, :], func=AF.Sqrt, bias=EPS, scale=256.0 / 384.0), r=[PK(7)], w=["rs3"])
                P.dve(lambda e: e.reciprocal(out=rs3[:], in_=rs3[:]), r=["rs3"], w=["rs3"])
                for c in range(3):
                    P.dve(lambda e, c=c: e.scalar_tensor_tensor(out=cqn[:, c, :], in0=cq[:, c, :], scalar=cst[:, C_QN + c:C_QN + c + 1],
                                                                in1=rs3[:], op0=ALU.mult, op1=ALU.mult), r=["cq", "rs3", "cst"], w=[("cqn", c)])
                qs = qst[tc % 2]
                for oc in range(8):
                    b = nb()
                    proj_fm(wuq, "wuq", oc * 128, cqn, "cqn", 3, 512, b)
                    P.act(lambda e, b=b, oc=oc: e.activation(out=qs[:, oc, :], in_=ps[b][:, :], func=AF.Copy), r=[PK(b)], w=[("qst", tc % 2, oc)])
                    P.dma(qnT_d[oc, :, tc * 512:(tc + 1) * 512], qs[:, oc, :], r=[("qst", tc % 2, oc)], w=[("qnT_d", oc, tc)])
                for oc in range(8):
                    bA, bB = nb(), nb()
                    proj_fm(wuq, "wuq", 1024 + oc * 64, cqn, "cqn", 3, 512, bA, m=64)
                    proj_fm(wuq, "wuq", 1536 + oc * 64, cqn, "cqn", 3, 512, bB, m=64)
                    rope_evac1(bA, bB, 64, tc * 512, qs[0:64, 8 + oc, :], ("qst", tc % 2, 8 + oc))
                    P.dma(qrT_d[oc, :, tc * 512:(tc + 1) * 512], qs[0:64, 8 + oc, :], r=[("qst", tc % 2, 8 + oc)], w=[("qrT_d", oc, tc)])
            for oc in range(2):
                b = nb()
                proj_fm(wdkv, "wdkv", oc * 128, hbc, khb, 8, 512, b)
                P.act(lambda e, b=b, oc=oc: e.activation(out=ckv[:, oc, :], in_=ps[b][:, :], func=AF.Copy), r=[PK(b)], w=["ckv"])
            P.act(lambda e: e.activation(out=sq3[:, 0:2, :], in_=ckv[:], func=AF.Square), r=["ckv"], w=["sq3"])
            for c in range(2):
                P.pe(lambda e, c=c: e.matmul(ps[7][:, :], lhsT=ones_q[:], rhs=sq3[:, c, :], start=(c == 0), stop=(c == 1)),
                     r=["sq3", "ones_q"], w=[PK(7)])
            P.act(lambda e: e.activation(out=rs3[:], in_=ps[7][:, :], func=AF.Sqrt, bias=EPS, scale=1.0), r=[PK(7)], w=["rs3"])
            P.dve(lambda e: e.reciprocal(out=rs3[:], in_=rs3[:]), r=["rs3"], w=["rs3"])
            for c in range(2):
                P.dve(lambda e, c=c: e.scalar_tensor_tensor(out=ckvn[:, c, :], in0=ckv[:, c, :], scalar=cst[:, C_KVN + c:C_KVN + c + 1],
                                                            in1=rs3[:], op0=ALU.mult, op1=ALU.mult), r=["ckv", "rs3", "cst"], w=[("ckvn", c)])
                P.dma(kva_sets_d[s1, c * 128:(c + 1) * 128, tc * 512:(tc + 1) * 512], ckvn[:, c, :], r=[("ckvn", c)], w=[("kva", s1, tc, c)])
            bA, bB = nb(), nb()
            proj_fm(wdkv, "wdkv", 256, hbc, khb, 8, 512, bA, m=32)
            proj_fm(wdkv, "wdkv", 288, hbc, khb, 8, 512, bB, m=32)
            rope_evac1(bA, bB, 32, tc * 512, krs[:, :], "krs")
            P.dma(kva_sets_d[s1, 256:288, tc * 512:(tc + 1) * 512], krs[:, :], r=["krs"], w=[("kva", s1, tc, 2)])
    P.barrier()
    P.release(base_mark)

    if stop("Q1"):
        return nc
    wkk = P.sb("wkk", [128, 2, 1024], BF16)
    wkv = P.sb("wkv", [128, 2, 1024], BF16)
    load_w(wkk[:], wukvk_d.rearrange("(c p) n -> p c n", p=128), "wkk")
    load_w(wkv[:], wukvv_d.rearrange("(c p) n -> p c n", p=128), "wkv")
    ckf = [P.sb(f"ckf{i}", [128, 2, 512], F32) for i in range(2)]
    ckb = [P.sb(f"ckb{i}", [128, 2, 512], BF16) for i in range(2)]
    kns = [P.sb(f"kns{i}", [128, 8, 512], BF16) for i in range(2)]
    vst1 = [P.sb(f"vst1_{i}", [128, 16, 128], BF16) for i in range(2)]
    kr4 = P.sb("kr4", [64, T], BF16)
    krf = P.sb("krf", [64, T], F32)
    for i in range(2):
        P.pool(lambda e, i=i: e.memset(vst1[i][:], 1.0), w=[("vst1", i)])
    for sl in range(32):
        s1, m_ = sl % 2, sl // 2
        for rep_ in range(2):
            P.dma(krf[rep_ * 32:(rep_ + 1) * 32, sl * 128:(sl + 1) * 128],
                  kva_sets_d[s1, 256:288, m_ * 128:(m_ + 1) * 128], w=[("krf", sl // 8)])
    for q4 in range(4):
        P.dve(lambda e, q4=q4: e.tensor_copy(out=kr4[:, q4 * 1024:(q4 + 1) * 1024], in_=krf[:, q4 * 1024:(q4 + 1) * 1024]),
              r=[("krf", q4)], w=["kr4"])
    for tc in range(8):
        cf = ckf[tc % 2]
        cb_ = ckb[tc % 2]
        for tb in range(4):
            sl = tc * 4 + tb
            s1, m_ = sl % 2, sl // 2
            for c in range(2):
                P.dma(cf[:, c, tb * 128:(tb + 1) * 128],
                      kva_sets_d[s1, c * 128:(c + 1) * 128, m_ * 128:(m_ + 1) * 128], w=[("ckf", tc % 2)])
        P.dve(lambda e, cf=cf, cb_=cb_: e.tensor_copy(out=cb_[:], in_=cf[:]), r=[("ckf", tc % 2)], w=[(("ckb", tc % 2), 0), (("ckb", tc % 2), 1)])
        kn = kns[tc % 2]
        for oc in range(8):
            b = nb()
            proj_fm(wkk, "wkk", oc * 128, cb_, ("ckb", tc % 2), 2, 512, b)
            P.act(lambda e, b=b, oc=oc, kn=kn: e.activation(out=kn[:, oc, :], in_=ps[b][:, :], func=AF.Copy), r=[PK(b)], w=[("kns", tc % 2, oc)])
            P.dma(knT_d[oc, :, tc * 512:(tc + 1) * 512], kn[:, oc, :], r=[("kns", tc % 2, oc)], w=[("knT_d", oc, tc)])
        for tb in range(4):
            kb = tc * 4 + tb
            vs = vst1[kb % 2]
            vv = vs[:].rearrange("p (h two) d -> p h two d", two=2)
            for half in range(2):
                b = nb()
                for kc in range(2):
                    P.pe(lambda e, kc=kc, tb=tb, b=b, half=half, cb_=cb_: e.matmul(
                        ps[b][:, :], lhsT=cb_[:, kc, tb * 128:(tb + 1) * 128], rhs=wkv[:, kc, half * 512:(half + 1) * 512],
                        start=(kc == 0), stop=(kc == 1)), r=["wkv", (("ckb", tc % 2), kc)], w=[PK(b)])
                pv = ps[b][:, :].rearrange("p (h two d) -> p h two d", two=2, d=64)
                P.act(lambda e, pv=pv, vv=vv, half=half: e.activation(out=vv[:, half * 4:(half + 1) * 4, 0, 0:64], in_=pv[:, :, 0, :], func=AF.Copy),
                      r=[PK(b)], w=[("vst1", kb % 2)])
                P.act(lambda e, pv=pv, vv=vv, half=half: e.activation(out=vv[:, half * 4:(half + 1) * 4, 1, 64:128], in_=pv[:, :, 1, :], func=AF.Copy),
                      r=[PK(b)], w=[("vst1", kb % 2)])
            P.dma(vaug1_d[kb], vs[:].rearrange("p h d -> p (h d)"), r=[("vst1", kb % 2)], w=[("vaug1_d", kb)])
    P.dma(kr4_d, kr4[:], r=["kr4"], w=["kr4_d"])
    P.barrier()
    P.release(base_mark)

    if stop("K1"):
        return nc
    mTs1 = P.sb("mTs1", [128, 2, 8, 512], BF16)
    for i in range(2):
        P.dma(mTs1[:, i], mT_d[i], w=["mTs1"])
    kh2 = [[P.sb(f"kh_{i}_{hh}", [96, T], BF16) for hh in range(2)] for i in range(2)]
    qh2 = [[P.sb(f"qh_{i}_{hh}", [96, NO], BF16) for hh in range(2)] for i in range(2)]
    vb2 = [P.sb(f"vb21_{i}", [128, 32, 256], BF16) for i in range(2)]

    def hp_loader1(hp):
        i = hp % 2
        key = ("kvq1", i)
        for hh in range(2):
            P.dma(kh2[i][hh][0:64, :], knT_d[hp, hh * 64:(hh + 1) * 64, :], r=[("knT_d", hp, t_) for t_ in range(8)], w=[key])
            P.dma(kh2[i][hh][64:96, :], kr4_d[0:32, :], r=["kr4_d"], w=[key])
            P.dma(qh2[i][hh][0:64, :], qnT_d[hp, hh * 64:(hh + 1) * 64, :], r=[("qnT_d", hp, t_) for t_ in range(4)], w=[key])
            P.dma(qh2[i][hh][64:96, :], qrT_d[hp, hh * 32:(hh + 1) * 32, :], r=[("qrT_d", hp, t_) for t_ in range(4)], w=[key])
        P.dma(vb2[i][:], vaug1_d[:, :, hp * 256:(hp + 1) * 256].rearrange("k p c -> p k c"),
              r=[("vaug1_d", k_) for k_ in range(32)], w=[key])
        return {"k": kh2[i], "q": qh2[i], "v": vb2[i], "kv": key}

    def st_emit1(bufs, h, hh, g, kb, sb_, c0=0):
        P.pe(lambda e: e.matmul(ps[sb_][:, c0:512], lhsT=bufs["k"][hh][0:96, kb * 128:(kb + 1) * 128],
                                rhs=bufs["q"][hh][0:96, g * 512 + c0:(g + 1) * 512], start=True, stop=True),
             r=[bufs["kv"]], w=[PK(sb_)])

    def mask_for1(bufs, g, kb):
        rel = kb - 8 * g
        if rel < 0:
            return None
        return mTs1[:, g // 2, rel, :], "mTs1"

    attention(16, hp_loader1, st_emit1, mask_for1, float(96 ** -0.5), attnT_d, "attnT1_d")
    P.barrier()
    P.release(base_mark)

    if stop("A1"):
        return nc
    in1 = [(attnT_d[c], [("attnT1_d", c)]) for c in range(8)]
    out_phase(in1, wo_d, 1, x2T_d[0], ("x2T_d", 0), x3T_d, "x3T_d")
    P.barrier()
    P.release(base_mark)
    ffn_phase(1, x3T_d, "x3T_d", outT, "outT")
    P.final_wait([("outT", tc) for tc in range(4)])
    P.emit()
    return nc


def _swap_cols(w, head, a, b_):
    n = w.shape[1]
    idx = np.arange(n).reshape(-1, head)
    perm = np.concatenate([idx[:, a:b_], idx[:, 0:a], idx[:, b_:]], axis=1).reshape(-1)
    return w[:, perm]


def prepare_inputs(inp, n_batch=4):
    f32 = np.float32
    x = np.asarray(inp["x"], f32)
    pos = np.asarray(inp["positions"]).astype(np.int32)
    w_in = np.asarray(inp["even_w_in"], f32)[0]
    offs = np.cumsum([0, 512, 512, 512, 1024, 64, 16, 512, 512, 512])
    wq_, wk_, wv_, wqi, wki, wwi, wgb, wgc, wxi = [w_in[:, offs[i]:offs[i + 1]] for i in range(9)]
    w0k = np.concatenate([wk_, _swap_cols(wk_, 64, 8, 16), wki, wki, _swap_cols(wki, 64, 8, 16), _swap_cols(wki, 64, 8, 16)], axis=1)
    w0q = np.concatenate([wq_, _swap_cols(wq_, 64, 8, 16), wqi, _swap_cols(wqi, 64, 8, 16), wgb, wgc, wxi], axis=1)
    w_uq = np.asarray(inp["odd_w_uq"], f32)[0]
    cols = np.arange(1536).reshape(16, 96)
    wuq_n = w_uq[:, cols[:, :64].reshape(-1)]
    wuq_r = w_uq[:, cols[:, 64:].reshape(-1)]
    wuq = np.concatenate([wuq_n, wuq_r, _swap_cols(wuq_r, 32, 16, 32)], axis=1)
    w_dkv = np.asarray(inp["odd_w_dkv"], f32)[0]
    wdkv = np.concatenate([w_dkv[:, :256], w_dkv[:, 256:], _swap_cols(w_dkv[:, 256:], 32, 16, 32)], axis=1)
    w_ukv = np.asarray(inp["odd_w_ukv"], f32)[0]
    c2 = np.arange(2048).reshape(16, 128)
    wukv_k = w_ukv[:, c2[:, :64].reshape(-1)]
    wukv_v = w_ukv[:, c2[:, 64:].reshape(-1)]

    cst = np.zeros((128, NCST), f32)
    kinds = ["norm_mix_pre", "norm_mix_post", "norm_ffn_pre", "norm_ffn_post"]
    for l in range(2):
        for k, nm in enumerate(kinds):
            cst[:, gcol(l, k):gcol(l, k) + 8] = np.asarray(inp[nm], f32)[l].reshape(8, 128).T
    cst[:, C_QN:C_QN + 3] = np.asarray(inp["odd_q_norm"], f32)[0].reshape(3, 128).T
    cst[:, C_KVN:C_KVN + 2] = np.asarray(inp["odd_kv_norm"], f32)[0].reshape(2, 128).T
    cw = np.asarray(inp["even_conv_w"], f32)[0]
    for j in range(3):
        cst[:, C_CW + j * 4:C_CW + j * 4 + 4] = cw[j].reshape(4, 128).T
    theta = 500000.0
    if0 = (theta ** (-np.arange(0, 16, 2, dtype=np.float32) / 16)).astype(f32)
    if1 = (theta ** (-np.arange(0, 32, 2, dtype=np.float32) / 32)).astype(f32)
    for p in range(128):
        r = p % 64
        if r < 16:
            cst[p, C_FR0] = if0[r % 8]
            cst[p, C_SG0] = -1.0 if r < 8 else 1.0
        r = p % 32
        cst[p, C_FR1] = if1[r % 16]
        cst[p, C_SG1] = -1.0 if r < 16 else 1.0

    shared = {
        "cst": cst, "w0k": w0k, "w0v": np.ascontiguousarray(wv_), "w0q": w0q, "w0wi": np.ascontiguousarray(wwi),
        "w_out": np.asarray(inp["even_w_out"], f32)[0], "w1": np.asarray(inp["mlp_w1"], f32),
        "w2": np.asarray(inp["mlp_w2"], f32), "w_dq": np.asarray(inp["odd_w_dq"], f32)[0], "w_uq": wuq,
        "w_dkv": wdkv, "w_ukv_k": np.ascontiguousarray(wukv_k), "w_ukv_v": np.ascontiguousarray(wukv_v),
        "w_o": np.asarray(inp["odd_w_o"], f32)[0],
    }
    shared = {k: np.ascontiguousarray(v, dtype=f32) for k, v in shared.items()}
    in_maps = []
    own_idx_all = []
    qi = np.arange(128)
    s_ = np.arange(1024)
    for b in range(n_batch):
        for par in range(2):
            xb = x[b]
            sets = [blocks_for(par), blocks_for(1 - par)]
            xT_sets, pos_sets, cbs = [], [], []
            kq = np.zeros((128, 32), f32)
            for si, blks in enumerate(sets):
                idx = np.concatenate([np.arange(128 * p, 128 * p + 128) for p in blks])
                halo = np.zeros((32, D), f32)
                for i, p in enumerate(blks):
                    if p > 0:
                        halo[2 * i:2 * i + 2] = xb[128 * p - 2:128 * p]
                xT_sets.append(np.concatenate([xb[idx], halo], axis=0).T)
                pos_sets.append(pos[b][idx][None, :])
                cb = np.zeros((8, 128, 1024), f32)
                for i, p in enumerate(blks):
                    kq[:, si * 16 + i] = np.minimum(256, 128 * p + qi + 1)
                for g in range(4):
                    for j in range(4):
                        rel = blks[4 * g + j] % 8
                        vis = s_[None, :] <= (rel * 128 + qi)[:, None]
                        cb[(g // 2) * 4 + j] = np.where(vis, 0.0, -1e30)
                cbs.append(cb)
            own = np.concatenate([np.arange(128 * p, 128 * p + 128) for p in sets[0]])
            own_idx_all.append(own)
            mT = np.zeros((2, 128, 8, 512), f32)
            for g in (0, 2):
                for j in range(4):
                    pq = sets[0][4 * g + j]
                    qpos = 128 * pq + qi
                    for rel in range(8):
                        sl = 8 * g + rel
                        pk = sets[sl % 2][sl // 2]
                        kpos = 128 * pk + qi
                        mT[g // 2, :, rel, j * 128:(j + 1) * 128] = (kpos[:, None] <= qpos[None, :])
            m = dict(shared)
            m.update({
                "xT_seq": np.ascontiguousarray(xb.T), "xT_own": np.ascontiguousarray(np.stack(xT_sets)),
                "pos_seq": np.ascontiguousarray(pos[b][None, :]), "pos_own": np.ascontiguousarray(np.stack(pos_sets)),
                "kq": kq, "cb": np.stack(cbs), "mT": mT.astype(ml_dtypes.bfloat16),
            })
            in_maps.append(m)
    return in_maps, own_idx_all


_NC_CACHE = {}


def kernel(**inputs):
    in_maps, own_idx = prepare_inputs(inputs, 4)
    if 8 not in _NC_CACHE:
        _NC_CACHE[8] = build_program(8)
    nc = _NC_CACHE[8]
    res = run_bass_kernel_spmd(nc, in_maps, core_ids=list(range(8)))
    out = np.zeros((4, T, D), np.float32)
    for c in range(8):
        b = c // 2
        out[b, own_idx[c], :] = np.asarray(res.results[c]["outT"], np.float32).T
    return out
```

```python
import types
import numpy as np
import ml_dtypes
import concourse.bass as bass
import concourse.mybir as mybir
from concourse.bass_utils import run_bass_kernel_spmd

F32 = mybir.dt.float32
BF16 = mybir.dt.bfloat16
I32 = mybir.dt.int32
AF = mybir.ActivationFunctionType
ALU = mybir.AluOpType
AX = mybir.AxisListType
DT_SIZE = {F32: 4, BF16: 2, I32: 4}

T = 4096
NO = 2048
D = 1024
EPS = 1e-6
NBIS = 18
NFILL = 1
NBURST = 10
NBURST_I = 10
NFILL_I = 0
TWO_PI = float(2 * np.pi)


class Op:
    __slots__ = ("eng", "fn", "reads", "writes", "is_dma", "deps", "needed", "ordinal",
                 "dsem", "dval", "barrier")

    def __init__(self, eng, fn, reads, writes, is_dma):
        self.eng = eng
        self.fn = fn
        self.reads = reads
        self.writes = writes
        self.is_dma = is_dma
        self.deps = []
        self.needed = False
        self.ordinal = None
        self.dsem = None
        self.dval = None
        self.barrier = False


class Prog:
    ENGS = ("pe", "act", "dve", "pool", "sp")
    SB_LIMIT = 228352

    def __init__(self, nc, n_dma_sems=12):
        self.nc = nc
        self.ops = []
        self.sb_off = 16896
        self.sb_max = 0
        self.n_dma_sems = n_dma_sems
        self._uid = 0
        self._bank = 0

    def sb(self, name, shape, dtype):
        nbytes = int(np.prod(shape[1:])) * DT_SIZE[dtype]
        nbytes = (nbytes + 63) // 64 * 64
        self._uid += 1
        t = self.nc.alloc_sbuf_tensor_at(f"{name}_{self._uid}", list(shape), dtype, offset=self.sb_off)
        self.sb_off += nbytes
        self.sb_max = max(self.sb_max, self.sb_off)
        assert self.sb_off <= self.SB_LIMIT, f"SBUF overflow {self.sb_off} at {name}"
        return t

    def mark(self):
        return self.sb_off

    def release(self, m):
        self.sb_off = m

    @staticmethod
    def _freeze(fn):
        if getattr(fn, "__closure__", None) is None:
            return fn
        cells = []
        for c in fn.__closure__:
            try:
                cells.append(types.CellType(c.cell_contents))
            except ValueError:
                cells.append(c)
        return types.FunctionType(fn.__code__, fn.__globals__, fn.__name__, fn.__defaults__, tuple(cells))

    def add(self, eng, fn, r=(), w=(), dma=False):
        fn = self._freeze(fn)
        o = Op(eng, fn, tuple(r), tuple(w), dma)
        self.ops.append(o)
        return o

    def pe(self, fn, r=(), w=()):
        return self.add("pe", fn, r, w)

    def act(self, fn, r=(), w=()):
        return self.add("act", fn, r, w)

    def dve(self, fn, r=(), w=()):
        return self.add("dve", fn, r, w)

    def pool(self, fn, r=(), w=()):
        return self.add("pool", fn, r, w)

    def dma(self, out, in_, r=(), w=(), q="sp", **kw):
        return self.add(q, lambda e: e.dma_start(out=out, in_=in_, **kw), r, w, dma=True)

    def final_wait(self, keys):
        return self.add("sp", lambda e: e.nop(), r=keys, w=())

    def barrier(self):
        o = Op(None, None, (), (), False)
        o.barrier = True
        self.ops.append(o)

    def finalize(self):
        last_w = {}
        readers = {}
        since_barrier = []
        pending_barrier = None
        seen_after = set()
        for o in self.ops:
            if o.barrier:
                summ = []
                lastc = {}
                for p in since_barrier:
                    if p.is_dma:
                        summ.append(p)
                    else:
                        lastc[p.eng] = p
                summ.extend(lastc.values())
                if pending_barrier is not None:
                    summ.extend(pending_barrier)
                pending_barrier = summ
                seen_after = set()
                since_barrier = []
                continue
            deps = []
            if pending_barrier is not None and o.eng not in seen_after:
                deps.extend(pending_barrier)
                seen_after.add(o.eng)
            for k in o.reads:
                if k in last_w:
                    deps.append(last_w[k])
            for k in o.writes:
                if k in last_w:
                    deps.append(last_w[k])
                deps.extend(readers.get(k, ()))
            for k in o.reads:
                readers.setdefault(k, []).append(o)
            for k in o.writes:
                last_w[k] = o
                readers[k] = []
            dd = []
            seen = set()
            for d in deps:
                if d is o or id(d) in seen:
                    continue
                seen.add(id(d))
                if (not d.is_dma) and (not o.is_dma) and d.eng == "pe" and o.eng == "pe":
                    continue
                dd.append(d)
            o.deps = dd
            for d in dd:
                d.needed = True
            since_barrier.append(o)
        cnt = {e: 0 for e in self.ENGS}
        dma_rr = {e: 0 for e in self.ENGS}
        dma_uses = {}
        for o in self.ops:
            if o.barrier:
                continue
            if o.is_dma:
                slot = dma_rr[o.eng] % self.n_dma_sems
                dma_rr[o.eng] += 1
                key = (o.eng, slot)
                dma_uses[key] = dma_uses.get(key, 0) + 1
                o.dsem = key
                o.dval = 16 * dma_uses[key]
            elif o.needed:
                cnt[o.eng] += 1
                o.ordinal = cnt[o.eng]
        self.max_ord = dict(cnt)

    def emit(self):
        nc = self.nc
        self.finalize()
        from contextlib import ExitStack
        es = ExitStack()
        sems = {}
        for e in ("pe", "act", "dve", "pool", "sp"):
            sems[e] = es.enter_context(nc.semaphore(f"c_{e}"))
        dsems = {}
        used = sorted({o.dsem for o in self.ops if (not o.barrier) and o.is_dma})
        for key in used:
            dsems[key] = es.enter_context(nc.semaphore(f"d_{key[0]}_{key[1]}"))
        block = es.enter_context(nc.Block())
        per_eng = {e: [o for o in self.ops if (not o.barrier) and o.eng == e] for e in self.ENGS}

        def body(ename, engine):
            known = {}
            for o in per_eng[ename]:
                waits = {}
                for d in o.deps:
                    if d.is_dma:
                        s, v, k = dsems[d.dsem], d.dval, ("d",) + d.dsem
                    else:
                        s, v, k = sems[d.eng], d.ordinal, ("c", d.eng)
                    if v > waits.get(k, (None, 0))[1]:
                        waits[k] = (s, v)
                if o.is_dma and o.dval > 16:
                    k = ("d",) + o.dsem
                    v = o.dval - 16
                    if v > waits.get(k, (None, 0))[1]:
                        waits[k] = (dsems[o.dsem], v)
                for k, (s, v) in waits.items():
                    if known.get(k, 0) >= v:
                        continue
                    engine.wait_ge(s, v)
                    known[k] = v
                ins = o.fn(engine)
                if o.is_dma:
                    ins.then_inc(dsems[o.dsem], 16)
                elif o.needed:
                    ins.then_inc(sems[ename], 1)

        @block.tensor
        def _(e):
            body("pe", e)

        @block.scalar
        def _(e):
            body("act", e)

        @block.vector
        def _(e):
            body("dve", e)

        @block.gpsimd
        def _(e):
            body("pool", e)

        @block.sync
        def _(e):
            body("sp", e)

        es.close()


def blocks_for(par):
    lo = list(range(par, 16, 2))
    hi = sorted(31 - j for j in lo)
    return lo + hi


C_G = 0
C_QN = 64
C_KVN = 67
C_CW = 69
C_FR0 = 81
C_SG0 = 82
C_FR1 = 83
C_SG1 = 84
NCST = 96


def gcol(layer, kind):
    return C_G + (layer * 4 + kind) * 8


def build_program(n_cores, dbg=(), no_cc=False, stop_after=None):
    nc = bass.Bass("TRN2", target_bir_lowering=False)
    P = Prog(nc)

    def stop(name):
        if stop_after == name:
            P.barrier()
            P.final_wait([])
            P.emit()
            return True
        return False

    def din(name, shape, dt=F32):
        return nc.dram_tensor(name, list(shape), dt, kind="ExternalInput").ap()

    def dscr(name, shape, dt):
        kind = "ExternalOutput" if name in dbg else "Internal"
        return nc.dram_tensor(name, list(shape), dt, kind=kind).ap()

    xT_seq = din("xT_seq", [D, T])
    xT_own = din("xT_own", [2, D, NO + 32])
    pos_seq = din("pos_seq", [1, T], I32)
    pos_own = din("pos_own", [2, 1, NO], I32)
    cst_d = din("cst", [128, NCST])
    kq_d = din("kq", [128, 32])
    cb_d = din("cb", [2, 8, 128, 1024])
    mT_d = din("mT", [2, 128, 8, 512], BF16)
    w0k_d = din("w0k", [D, 1280])
    w0v_d = din("w0v", [D, 512])
    w0q_d = din("w0q", [D, 4608])
    w0wi_d = din("w0wi", [D, 16])
    wout_d = din("w_out", [D, D])
    w1_d = din("w1", [2, D, 4096])
    w2_d = din("w2", [2, 4096, D])
    wdq_d = din("w_dq", [D, 384])
    wuq_d = din("w_uq", [384, 2048])
    wdkv_d = din("w_dkv", [D, 320])
    wukvk_d = din("w_ukv_k", [256, 1024])
    wukvv_d = din("w_ukv_v", [256, 1024])
    wo_d = din("w_o", [D, D])
    outT = nc.dram_tensor("outT", [D, NO], F32, kind="ExternalOutput").ap()

    kT_d = dscr("kT_d", [4, 128, T], BF16)
    kidxT_d = dscr("kidxT_d", [128, T], BF16)
    vaug_d = dscr("vaug_d", [32, 128, 1024], BF16)
    qT_d = dscr("qT_d", [4, 128, NO], BF16)
    qidxT_d = dscr("qidxT_d", [8, 128, NO], BF16)
    convT_d = dscr("convT_d", [4, 128, NO], BF16)
    maskT_d = dscr("maskT_d", [4, 128, 32, 512], BF16)
    attnT_d = dscr("attnT_d", [8, 128, NO], BF16)
    x1T_d = dscr("x1T_d", [D, NO], F32)
    x2T_d = dscr("x2T_d", [2, D, NO], F32)
    x3T_d = dscr("x3T_d", [D, NO], F32)
    kva_sets_d = dscr("kva_sets_d", [2, 288, NO], F32)
    qnT_d = dscr("qnT_d", [8, 128, NO], BF16)
    qrT_d = dscr("qrT_d", [8, 64, NO], BF16)
    knT_d = dscr("knT_d", [8, 128, T], BF16)
    vaug1_d = dscr("vaug1_d", [32, 128, 2048], BF16)
    dbg_d = dscr("dbg_d", [128, 4096], F32)
    kr4_d = dscr("kr4_d", [64, T], BF16)

    ps = [nc.alloc_psum_tensor(f"ps{i}", [128, 512], F32) for i in range(8)]

    def PK(i):
        return ("ps", i)

    cst = P.sb("cst", [128, NCST], F32)
    kq = P.sb("kq", [128, 32], F32)
    widx = P.sb("widx", [128, 16, 16], F32)
    ones_b = P.sb("ones_b", [128, 128], BF16)
    ones_q = P.sb("ones_q", [128, 128], BF16)
    ident = P.sb("ident", [128, 128], F32)
    P.dma(cst[:], cst_d, w=["cst"])
    P.dma(kq[:], kq_d, w=["kq"])
    P.pool(lambda e: e.memset(ones_b[:], 1.0 / 1024), w=["ones_b"])
    P.pool(lambda e: e.memset(ones_q[:], 1.0 / 256), w=["ones_q"])
    P.pool(lambda e: e.memset(ident[:], 1.0), w=["ident"])
    P.pool(lambda e: e.affine_select(out=ident[:], in_=ident[:], pattern=[[-1, 128]], compare_op=ALU.is_equal,
                                     fill=0.0, base=0, channel_multiplier=1), r=["ident"], w=["ident"])
    base_mark = P.mark()

    uid = [0]
    W1OFF = [None]

    def U(s):
        uid[0] += 1
        return f"{s}#{uid[0]}"

    def rope_tables(pos_d, n, fr_col, sg_col, tag):
        C = P.sb(tag + "C", [128, n], F32)
        S = P.sb(tag + "S", [128, n], F32)
        m = P.mark()
        pi_ = P.sb("posi", [128, n], I32)
        pf = P.sb("posf", [128, n], F32)
        tmp = P.sb("rtmp", [128, n], F32)
        ki = P.sb("rki", [128, n], I32)
        kpi, kpf, kt, kk = U("posi"), U("posf"), U("rtmp"), U("rki")
        kC, kS = tag + "C", tag + "S"
        P.dma(pi_[:], pos_d.to_broadcast([128, n]), w=[kpi])
        P.dve(lambda e: e.tensor_copy(out=pf[:], in_=pi_[:]), r=[kpi], w=[kpf])
        P.dve(lambda e: e.tensor_scalar(out=pf[:], in0=pf[:], scalar1=cst[:, fr_col:fr_col + 1], scalar2=None,
                                        op0=ALU.mult), r=[kpf, "cst"], w=[kpf])
        for which, dst, kd in (("s", S, kS), ("c", C, kC)):
            off = 0.0 if which == "s" else float(np.pi / 2)
            P.dve(lambda e, off=off: e.tensor_scalar(out=tmp[:], in0=pf[:], scalar1=off, scalar2=1.0 / TWO_PI,
                                                     op0=ALU.add, op1=ALU.mult), r=[kpf], w=[kt])
            P.dve(lambda e: e.tensor_copy(out=ki[:], in_=tmp[:]), r=[kt], w=[kk])
            P.dve(lambda e: e.tensor_copy(out=tmp[:], in_=ki[:]), r=[kk], w=[kt])
            P.dve(lambda e: e.scalar_tensor_tensor(out=tmp[:], in0=tmp[:], scalar=-TWO_PI, in1=pf[:],
                                                   op0=ALU.mult, op1=ALU.add), r=[kt, kpf], w=[kt])
            P.dve(lambda e, off=off: e.tensor_scalar(out=tmp[:], in0=tmp[:], scalar1=off, scalar2=None,
                                                     op0=ALU.add), r=[kt], w=[kt])
            P.dve(lambda e, dst=dst: e.tensor_scalar(out=dst[:], in0=tmp[:], scalar1=float(np.pi), scalar2=-TWO_PI,
                                                     op0=ALU.is_gt, op1=ALU.mult), r=[kt], w=[kd])
            P.dve(lambda e, dst=dst: e.tensor_tensor(out=tmp[:], in0=tmp[:], in1=dst[:], op=ALU.add), r=[kt, kd], w=[kt])
            P.dve(lambda e, dst=dst: e.tensor_scalar(out=dst[:], in0=tmp[:], scalar1=-float(np.pi), scalar2=TWO_PI,
                                                     op0=ALU.is_lt, op1=ALU.mult), r=[kt], w=[kd])
            P.dve(lambda e, dst=dst: e.tensor_tensor(out=tmp[:], in0=tmp[:], in1=dst[:], op=ALU.add), r=[kt, kd], w=[kt])
            P.dve(lambda e: e.tensor_scalar(out=tmp[:], in0=tmp[:], scalar1=-3.14159, scalar2=3.14159,
                                            op0=ALU.max, op1=ALU.min), r=[kt], w=[kt])
            P.act(lambda e, dst=dst: e.activation(out=dst[:], in_=tmp[:], func=AF.Sin), r=[kt], w=[kd])
        P.dve(lambda e: e.tensor_scalar(out=S[:], in0=S[:], scalar1=cst[:, sg_col:sg_col + 1], scalar2=None,
                                        op0=ALU.mult), r=[kS, "cst"], w=[kS])
        P.barrier()
        P.release(m)
        return C, S

    def load_w(dst, src_ap, key, nsplit=1):
        P.dma(dst, src_ap, w=[key], q="pool")

    def rmsnorm_fm(xs, kx, nchunk, n, gain_col, hout, kh, onesm, sq, ksq, rstd, krs, ssbank, eps=EPS):
        P.act(lambda e: e.activation(out=sq[:, 0:nchunk, 0:n], in_=xs[:, 0:nchunk, 0:n], func=AF.Square),
              r=[kx], w=[ksq])
        for c in range(nchunk):
            P.pe(lambda e, c=c: e.matmul(ps[ssbank][:, 0:n], lhsT=onesm[:], rhs=sq[:, c, 0:n],
                                         start=(c == 0), stop=(c == nchunk - 1)),
                 r=[ksq, "ones_b", "ones_q"], w=[PK(ssbank)])
        P.act(lambda e: e.activation(out=rstd[:, 0:n], in_=ps[ssbank][:, 0:n], func=AF.Sqrt, bias=eps, scale=1.0),
              r=[PK(ssbank)], w=[krs])
        P.dve(lambda e: e.reciprocal(out=rstd[:, 0:n], in_=rstd[:, 0:n]), r=[krs], w=[krs])
        for c in range(nchunk):
            P.dve(lambda e, c=c: e.scalar_tensor_tensor(out=hout[:, c, 0:n], in0=xs[:, c, 0:n],
                                                      scalar=cst[:, gain_col + c:gain_col + c + 1],
                                                      in1=rstd[:, 0:n], op0=ALU.mult, op1=ALU.mult),
                r=[kx, krs, "cst"], w=[(kh, c)])

    bankrot = [0]

    def nb():
        b = bankrot[0] % 4
        bankrot[0] += 1
        return b

    def proj_fm(wt, kw, oc0, h, kh, nk, n, bank, m=128):
        for kc in range(nk):
            P.pe(lambda e, kc=kc: e.matmul(ps[bank][0:m, 0:n], lhsT=wt[:, kc, oc0:oc0 + m], rhs=h[:, kc, 0:n],
                                           start=(kc == 0), stop=(kc == nk - 1)),
                 r=[kw, (kh, kc)], w=[PK(bank)])

    rope_mark = P.mark()
    C0s, S0s = rope_tables(pos_seq, T, C_FR0, C_SG0, "r0s")

    wk = P.sb("wk", [128, 8, 1280], BF16)
    wv = P.sb("wv", [128, 8, 512], BF16)
    load_w(wk[:], w0k_d.rearrange("(c p) n -> p c n", p=128), "wk")
    load_w(wv[:], w0v_d.rearrange("(c p) n -> p c n", p=128), "wv")
    xs2 = [P.sb(f"xs{i}", [128, 8, 512], F32) for i in range(2)]
    sq = P.sb("sq", [128, 8, 512], BF16)
    hb2 = [P.sb(f"hb{i}", [128, 8, 512], BF16) for i in range(2)]
    rstd = P.sb("rstd", [128, 512], F32)
    t1 = [P.sb(f"t1_{i}", [128, 512], F32) for i in range(2)]
    t2 = [P.sb(f"t2_{i}", [128, 512], F32) for i in range(2)]
    kst = [P.sb(f"kst{i}", [128, 5, 512], BF16) for i in range(2)]
    vst = [P.sb(f"vst{i}", [128, 8, 128], BF16) for i in range(2)]
    for i in range(2):
        P.pool(lambda e, i=i: e.memset(vst[i][:], 1.0), w=[("vst", i)])
    xseq_v = xT_seq.rearrange("(c p) t -> p c t", p=128)
    tcnt = [0]

    def rope_evac(bA, bB, Ct, St, col0, n, dst, kdst, kC="r0sC", kS="r0sS"):
        i = tcnt[0] % 2
        tcnt[0] += 1
        P.dve(lambda e: e.tensor_tensor(out=t1[i][:, 0:n], in0=ps[bA][:, 0:n], in1=Ct[:, col0:col0 + n], op=ALU.mult),
              r=[PK(bA), kC], w=[("t1", i)])
        P.dve(lambda e: e.tensor_tensor(out=t2[i][:, 0:n], in0=ps[bB][:, 0:n], in1=St[:, col0:col0 + n], op=ALU.mult),
              r=[PK(bB), kS], w=[("t2", i)])
        P.pool(lambda e: e.tensor_tensor(out=dst, in0=t1[i][:, 0:n], in1=t2[i][:, 0:n], op=ALU.add),
               r=[("t1", i), ("t2", i)], w=[kdst])

    for tc in range(8):
        xs = xs2[tc % 2]
        kx = ("xs", tc % 2)
        hbc = hb2[tc % 2]
        khb = ("hb", tc % 2)
        P.dma(xs[:], xseq_v[:, :, tc * 512:(tc + 1) * 512], w=[kx])
        rmsnorm_fm(xs, kx, 8, 512, gcol(0, 0), hbc, khb, ones_b, sq, "sq", rstd, "rstd", 7)
        if tc == 0 and "dbg_d" in dbg:
            dtmp = P.sb("dtmp", [128, 1024], F32)
            P.dma(dbg_d[:, 0:512], xs[:, 0, :], r=[kx], w=["dbg0"])
            P.dma(dbg_d[:, 512:1024], rstd[:], r=["rstd"], w=["dbg1"])
            P.dve(lambda e: e.tensor_copy(out=dtmp[:, 0:512], in_=hbc[:, 0, :]), r=[(khb, 0)], w=["dtmp"])
            P.dve(lambda e: e.tensor_copy(out=dtmp[:, 512:1024], in_=sq[:, 0, :]), r=["sq"], w=["dtmp"])
            P.dma(dbg_d[:, 1024:2048], dtmp[:], r=["dtmp"], w=["dbg2"])
            dt2 = P.sb("dt2", [128, 512], F32)
            P.act(lambda e: e.activation(out=dt2[:], in_=ps[7][:, :], func=AF.Copy), r=[PK(7)], w=["dt2"])
            P.dma(dbg_d[:, 2048:2560], dt2[:], r=["dt2"], w=["dbg3"])
            P.dma(dbg_d[:, 2560:3072], S0s[:, 0:512], r=["r0sS"], w=["dbg4"])
            P.dma(dbg_d[:, 3072:3584], C0s[:, 3584:4096], r=["r0sC"], w=["dbg5"])
            P.dma(dbg_d[:, 3584:4096], S0s[:, 3584:4096], r=["r0sS"], w=["dbg6"])
        ks = kst[tc % 2]
        for oc in range(5):
            bA, bB = nb(), nb()
            colA = oc * 128 if oc < 4 else 1024
            colB = 512 + oc * 128 if oc < 4 else 1152
            proj_fm(wk, "wk", colA, hbc, khb, 8, 512, bA)
            proj_fm(wk, "wk", colB, hbc, khb, 8, 512, bB)
            rope_evac(bA, bB, C0s, S0s, tc * 512, 512, ks[:, oc, :], ("kst", tc % 2, oc))
        for oc in range(4):
            P.dma(kT_d[oc, :, tc * 512:(tc + 1) * 512], ks[:, oc, :], r=[("kst", tc % 2, oc)], w=[("kT_d", oc, tc)])
        P.dma(kidxT_d[:, tc * 512:(tc + 1) * 512], ks[:, 4, :], r=[("kst", tc % 2, 4)], w=[("kidxT_d", tc)])
        for tb in range(4):
            kb = tc * 4 + tb
            b = nb()
            for kc in range(8):
                P.pe(lambda e, kc=kc, tb=tb: e.matmul(ps[b][:, :], lhsT=hbc[:, kc, tb * 128:(tb + 1) * 128],
                                                      rhs=wv[:, kc, :], start=(kc == 0), stop=(kc == 7)),
                     r=["wv", (khb, kc)], w=[PK(b)])
            vs = vst[kb % 2]
            pv = ps[b][:, :].rearrange("p (h two d) -> p h two d", two=2, d=64)
            vv = vs[:].rearrange("p (h two) d -> p h two d", two=2)
            P.act(lambda e, pv=pv, vv=vv: e.activation(out=vv[:, :, 0, 0:64], in_=pv[:, :, 0, :], func=AF.Copy),
                  r=[PK(b)], w=[("vst", kb % 2)])
            P.act(lambda e, pv=pv, vv=vv: e.activation(out=vv[:, :, 1, 64:128], in_=pv[:, :, 1, :], func=AF.Copy),
                  r=[PK(b)], w=[("vst", kb % 2)])
            P.dma(vaug_d[kb], vs[:].rearrange("p h d -> p (h d)"), r=[("vst", kb % 2)], w=[("vaug_d", kb)])
    P.barrier()
    P.release(rope_mark)

    for s_ in range(2):
        C0o, S0o = rope_tables(pos_own[s_], NO, C_FR0, C_SG0, "r0o")
        set_mark = P.mark()
        if stop("K"):
            return nc
        wq = P.sb("wq", [128, 8, 4608], BF16)
        wwi = P.sb("wwi", [128, 8, 16], BF16)
        for c in range(8):
            P.dma(wq[:, c, :], w0q_d[c * 128:(c + 1) * 128, :], w=["wq"], q="pool")
        load_w(wwi[:], w0wi_d.rearrange("(c p) n -> p c n", p=128), "wwi")
        uext = P.sb("uext", [128, 4, 16, 130], F32)
        gbs = P.sb("gbs", [128, 4, NO], BF16)
        q_mark = P.mark()
        xs2 = [P.sb("xsq", [128, 8, 512], F32)] * 2
        sq = P.sb("sq", [128, 8, 512], BF16)
        hb2 = [P.sb(f"hb{i}", [128, 8, 512], BF16) for i in range(2)]
        rstd = P.sb("rstd", [128, 512], F32)
        t1 = [P.sb(f"t1_{i}", [128, 512], F32) for i in range(2)]
        t2 = [P.sb(f"t2_{i}", [128, 512], F32) for i in range(2)]
        qrot = [P.sb(f"qrot{i}", [128, 512], BF16) for i in range(6)]
        gcs = [P.sb(f"gcs{i}", [128, 512], F32) for i in range(2)]
        qrc = [0]
        xown_v = xT_own[s_].rearrange("(c p) t -> p c t", p=128)
        for tc in range(5):
            n = 512 if tc < 4 else 32
            xs = xs2[0]
            kx = ("xs", 0)
            hbc = hb2[tc % 2]
            khb = ("hb", tc % 2)
            P.dma(xs[:, :, 0:n], xown_v[:, :, tc * 512:tc * 512 + n], w=[kx])
            rmsnorm_fm(xs, kx, 8, n, gcol(0, 0), hbc, khb, ones_b, sq, "sq", rstd, "rstd", 7)
            if tc < 4:
                for oc in range(12):
                    bA, bB = nb(), nb()
                    colA = oc * 128 if oc < 4 else 1024 + (oc - 4) * 128
                    colB = 512 + oc * 128 if oc < 4 else 2048 + (oc - 4) * 128
                    proj_fm(wq, "wq", colA, hbc, khb, 8, 512, bA)
                    proj_fm(wq, "wq", colB, hbc, khb, 8, 512, bB)
                    qi_ = qrc[0] % 6
                    qrc[0] += 1
                    rope_evac(bA, bB, C0o, S0o, tc * 512, 512, qrot[qi_][:], ("qrot", qi_), "r0oC", "r0oS")
                    if oc < 4:
                        P.dma(qT_d[oc, :, tc * 512:(tc + 1) * 512], qrot[qi_][:], r=[("qrot", qi_)], w=[("qT_d", oc, tc)])
                    else:
                        P.dma(qidxT_d[oc - 4, :, tc * 512:(tc + 1) * 512], qrot[qi_][:], r=[("qrot", qi_)],
                              w=[("qidxT_d", oc - 4, tc)])
                for cc in range(4):
                    b = nb()
                    proj_fm(wq, "wq", 3072 + cc * 128, hbc, khb, 8, 512, b)
                    P.act(lambda e, b=b, cc=cc: e.activation(out=gbs[:, cc, tc * 512:(tc + 1) * 512], in_=ps[b][:, :], func=AF.Copy),
                          r=[PK(b)], w=[("gbs", cc)])
                for tb in range(4):
                    b = nb()
                    for kc in range(8):
                        P.pe(lambda e, kc=kc, tb=tb, b=b: e.matmul(ps[b][:, 0:16], lhsT=hbc[:, kc, tb * 128:(tb + 1) * 128],
                                                                   rhs=wwi[:, kc, :], start=(kc == 0), stop=(kc == 7)),
                             r=["wwi", (khb, kc)], w=[PK(b)])
                    P.act(lambda e, b=b, tb=tb: e.activation(out=widx[:, tc * 4 + tb, :], in_=ps[b][:, 0:16], func=AF.Copy),
                          r=[PK(b)], w=["widx"])
            for cc in range(4):
                bA, bB = nb(), nb()
                proj_fm(wq, "wq", 3584 + cc * 128, hbc, khb, 8, n, bA)
                proj_fm(wq, "wq", 4096 + cc * 128, hbc, khb, 8, n, bB)
                g = gcs[cc % 2]
                P.act(lambda e, g=g, bA=bA: e.activation(out=g[:, 0:n], in_=ps[bA][:, 0:n], func=AF.Copy),
                      r=[PK(bA)], w=[("gcs", cc % 2)])
                if tc < 4:
                    o_ap = uext[:, cc, tc * 4:(tc + 1) * 4, 2:130]
                    i0 = ps[bB][:, :].rearrange("p (b t) -> p b t", t=128)
                    i1 = g[:].rearrange("p (b t) -> p b t", t=128)
                else:
                    o_ap = uext[:, cc, :, 0:2]
                    i0 = ps[bB][:, 0:32].rearrange("p (b t) -> p b t", t=2)
                    i1 = g[:, 0:32].rearrange("p (b t) -> p b t", t=2)
                P.dve(lambda e, o_ap=o_ap, i0=i0, i1=i1: e.tensor_tensor(out=o_ap, in0=i0, in1=i1, op=ALU.mult),
                      r=[PK(bB), ("gcs", cc % 2)], w=[("uext", cc)])
        P.barrier()
        P.release(q_mark)
        cacc = [P.sb(f"cacc{i}", [128, 16, 128], F32) for i in range(2)]
        cvo = [P.sb(f"cvo{i}", [128, NO], BF16) for i in range(2)]
        for cc in range(4):
            a = cacc[cc % 2]
            ka = ("cacc", cc % 2)
            P.dve(lambda e, a=a, cc=cc: e.tensor_scalar(out=a[:], in0=uext[:, cc, :, 2:130],
                                                        scalar1=cst[:, C_CW + 8 + cc:C_CW + 9 + cc], scalar2=None, op0=ALU.mult),
                  r=[("uext", cc), "cst"], w=[ka])
            P.dve(lambda e, a=a, cc=cc: e.scalar_tensor_tensor(out=a[:], in0=uext[:, cc, :, 1:129],
                                                               scalar=cst[:, C_CW + 4 + cc:C_CW + 5 + cc], in1=a[:],
                                                               op0=ALU.mult, op1=ALU.add), r=[("uext", cc), "cst", ka], w=[ka])
            P.dve(lambda e, a=a, cc=cc: e.scalar_tensor_tensor(out=a[:], in0=uext[:, cc, :, 0:128],
                                                               scalar=cst[:, C_CW + cc:C_CW + 1 + cc], in1=a[:],
                                                               op0=ALU.mult, op1=ALU.add), r=[("uext", cc), "cst", ka], w=[ka])
            co = cvo[cc % 2]
            P.dve(lambda e, a=a, cc=cc, co=co: e.tensor_tensor(out=co[:], in0=a[:].rearrange("p b t -> p (b t)"),
                                                               in1=gbs[:, cc, :], op=ALU.mult),
                  r=[ka, ("gbs", cc)], w=[("cvo", cc % 2)])
            P.dma(convT_d[cc], co[:], r=[("cvo", cc % 2)], w=[("convT_d", cc)])
        P.barrier()
        P.release(base_mark)

        if stop("Q"):
            return nc
        kidx = P.sb("kidx", [128, T], BF16)
        qidx = P.sb("qidx", [128, 8, NO], BF16)
        P.dma(kidx[:], kidxT_d, r=[("kidxT_d", t_) for t_ in range(8)], w=["kidx"])
        for oc in range(8):
            P.dma(qidx[:, oc, :], qidxT_d[oc], r=[("qidxT_d", oc, t_) for t_ in range(4)], w=[("qidx", oc)])
        sc4 = [P.sb(f"sc{i}", [128, T], F32) for i in range(4)]
        junk = P.sb("junk", [128, T], BF16)
        m01 = P.sb("m01", [128, T], F32)
        cbt = [P.sb(f"cbt{i}", [128, 1024], F32) for i in range(2)]
        rr = [P.sb(f"rr{i}", [128, 512], BF16) for i in range(6)]
        mTs = [P.sb(f"mTs{i}", [128, 32, 512], BF16) for i in range(1)]
        dg2 = [P.sb(f"dg{i}", [128, 16, 128], BF16) for i in range(2)]
        identb = P.sb("identb", [128, 128], BF16)
        sm2 = [P.sb(f"sm{i}", [128, 16], F32) for i in range(2)]
        stepT2 = [P.sb(f"stepT{i}", [128, 2, NBIS + 1], F32) for i in range(2)]
        P.dve(lambda e: e.tensor_copy(out=identb[:], in_=ident[:]), r=["ident"], w=["identb"])
        fz = P.sb("fz", [128, 512], BF16)
        P.pool(lambda e: e.memset(fz[:], 0.0), w=["fz"])
        rcnt = [0]
        acnt_i = [0]
        mt = mTs[0]

        def acc_half(g, hf, hi):
            nk = 1024 * (g + 1)
            nch = nk // 512
            for jj in range(2):
                j = 2 * hf + jj
                qi = 4 * g + j
                sc = sc4[j]
                ksc = ("sc", j)
                dg = dg2[qi % 2]
                kdg = ("dg", qi % 2)
                for h in range(16):
                    P.pool(lambda e, dg=dg, h=h, qi=qi: e.tensor_scalar(out=dg[:, h, :], in0=identb[:], scalar1=widx[:, qi, h:h + 1],
                                                                        scalar2=None, op0=ALU.mult),
                           r=["identb", "widx"], w=[kdg])
                for ch in range(nch):
                    accb = 4 + (acnt_i[0] % 2)
                    acnt_i[0] += 1
                    pend = []
                    for _f in range(NBURST_I):
                        P.pe(lambda e: e.matmul(ps[7][:, :], lhsT=identb[:], rhs=fz[:], start=True, stop=True), r=["identb", "fz"])
                    for h in range(16):
                        b = nb()
                        base = (h % 2) * 64
                        if NFILL_I and h % 2 == 1:
                            P.pe(lambda e: e.matmul(ps[7][:, :], lhsT=identb[:], rhs=fz[:], start=True, stop=True), r=["identb", "fz"])
                        P.pe(lambda e, b=b, h=h, base=base, ch=ch, qi=qi: e.matmul(
                            ps[b][:, :], lhsT=qidx[base:base + 64, h // 2, qi * 128:(qi + 1) * 128],
                            rhs=kidx[base:base + 64, ch * 512:(ch + 1) * 512], start=True, stop=True),
                            r=["kidx", ("qidx", h // 2)], w=[PK(b)])
                        ri = rcnt[0] % 6
                        rcnt[0] += 1
                        r_ = rr[ri]
                        P.act(lambda e, b=b, r_=r_: e.activation(out=r_[:], in_=ps[b][:, :], func=AF.Relu),
                              r=[PK(b)], w=[("rr", ri)])
                        pend.append((h, r_, ri))
                        if len(pend) > 2:
                            h0, r0, ri0 = pend.pop(0)
                            P.pe(lambda e, h0=h0, r0=r0, accb=accb, dg=dg: e.matmul(ps[accb][:, :], lhsT=dg[:, h0, :], rhs=r0[:],
                                                                                    start=(h0 == 0), stop=(h0 == 15)),
                                 r=[("rr", ri0), kdg], w=[PK(accb)])
                    for h0, r0, ri0 in pend:
                        P.pe(lambda e, h0=h0, r0=r0, accb=accb, dg=dg: e.matmul(ps[accb][:, :], lhsT=dg[:, h0, :], rhs=r0[:],
                                                                                start=(h0 == 0), stop=(h0 == 15)),
                             r=[("rr", ri0), kdg], w=[PK(accb)])
                    P.act(lambda e, sc=sc, ch=ch, accb=accb: e.activation(out=sc[:, ch * 512:(ch + 1) * 512], in_=ps[accb][:, :], func=AF.Copy),
                          r=[PK(accb)], w=[(ksc, ch)])

        def prep_half(g, hf, hi):
            nk = 1024 * (g + 1)
            nch = nk // 512
            sm = sm2[hi % 2]
            stepT = stepT2[hi % 2]
            for jj in range(2):
                j = 2 * hf + jj
                qi = 4 * g + j
                sc = sc4[j]
                allsc = [(("sc", j), ch) for ch in range(nch)]
                cb = cbt[jj]
                P.dma(cb[:], cb_d[s_, (g // 2) * 4 + j], w=[("cbt", jj)])
                P.dve(lambda e, sc=sc, jj=jj, sm=sm: e.tensor_reduce(out=sm[:, jj:jj + 1], in_=sc[:, 0:nk], axis=AX.X, op=ALU.max,
                                                                     apply_absolute_value=True), r=allsc, w=[("smA", hi % 2, jj)])
                P.dve(lambda e, sc=sc, cb=cb: e.tensor_tensor(out=sc[:, nk - 1024:nk], in0=sc[:, nk - 1024:nk], in1=cb[:],
                                                              op=ALU.add), r=allsc + [("cbt", jj), ("smA", hi % 2, jj)], w=allsc)
            allA = [("smA", hi % 2, jj) for jj in range(2)]
            P.dve(lambda e, sm=sm: e.tensor_single_scalar(out=sm[:, 0:2].bitcast(I32), in_=sm[:, 0:2].bitcast(I32),
                                                          scalar=0x7F800000, op=ALU.bitwise_and), r=allA, w=[("smA2", hi % 2)])
            P.dve(lambda e, sm=sm: e.tensor_scalar(out=sm[:, 0:2], in0=sm[:, 0:2], scalar1=2.0, scalar2=1e-30,
                                                   op0=ALU.mult, op1=ALU.max), r=[("smA2", hi % 2)], w=[("smA2", hi % 2)])
            for i in range(NBIS + 1):
                P.pool(lambda e, i=i, sm=sm, stepT=stepT: e.tensor_scalar(out=stepT[:, :, i], in0=sm[:, 0:2], scalar1=float(2.0 ** -i),
                                                                          scalar2=None, op0=ALU.mult),
                       r=[("smA2", hi % 2)], w=[("stepT", hi % 2, i)])
            P.pool(lambda e, sm=sm: e.memset(sm[:, 2:4], 0.0), w=[("mid", hi % 2)])

        def bis_half(g, hf, hi):
            nk = 1024 * (g + 1)
            nch = nk // 512
            sm = sm2[hi % 2]
            stepT = stepT2[hi % 2]
            kmid = ("mid", hi % 2)
            qi0 = 4 * g + 2 * hf
            kq2 = kq[:, s_ * 16 + qi0:s_ * 16 + qi0 + 2]
            for i in range(NBIS):
                for jj in range(2):
                    j = 2 * hf + jj
                    P.dve(lambda e, j=j, jj=jj, sm=sm: e.tensor_scalar(out=junk[:, 0:nk], in0=sc4[j][:, 0:nk], scalar1=sm[:, 2 + jj:3 + jj],
                                                                       scalar2=None, op0=ALU.is_ge, op1=ALU.add, accum_out=sm[:, 4 + jj:5 + jj]),
                          r=[(("sc", j), ch) for ch in range(nch)] + [kmid], w=[("cnt", hi % 2, jj)])
                P.dve(lambda e, sm=sm: e.tensor_tensor(out=sm[:, 6:8], in0=sm[:, 4:6], in1=kq2, op=ALU.is_ge),
                      r=[("cnt", hi % 2, jj) for jj in range(2)] + ["kq"], w=[("s4", hi % 2)])
                P.dve(lambda e, i=i, sm=sm, stepT=stepT: e.scalar_tensor_tensor(out=sm[:, 8:10], in0=sm[:, 6:8], scalar=0.5, in1=stepT[:, :, i],
                                                                                op0=ALU.subtract, op1=ALU.mult),
                      r=[("s4", hi % 2), ("stepT", hi % 2, i)], w=[("d4", hi % 2)])
                P.dve(lambda e, sm=sm: e.tensor_tensor(out=sm[:, 2:4], in0=sm[:, 2:4], in1=sm[:, 8:10], op=ALU.add),
                      r=[("d4", hi % 2), kmid], w=[kmid])
            P.dve(lambda e, sm=sm, stepT=stepT: e.tensor_tensor(out=sm[:, 10:12], in0=sm[:, 2:4], in1=stepT[:, :, NBIS], op=ALU.subtract),
                  r=[kmid, ("stepT", hi % 2, NBIS)], w=[("thr", hi % 2)])
            for jj in range(2):
                j = 2 * hf + jj
                P.dve(lambda e, j=j, jj=jj, sm=sm: e.tensor_scalar(out=m01[:, 0:nk], in0=sc4[j][:, 0:nk], scalar1=sm[:, 10 + jj:11 + jj], scalar2=None,
                                                                   op0=ALU.is_ge), r=[(("sc", j), ch) for ch in range(nch)] + [("thr", hi % 2)], w=["m01"])
                for k4 in range(nk // 512):
                    b = 6
                    for kk in range(4):
                        kb = k4 * 4 + kk
                        P.pe(lambda e, b=b, kk=kk, kb=kb: e.transpose(out=ps[b][:, kk * 128:(kk + 1) * 128],
                                                                      in_=m01[:, kb * 128:(kb + 1) * 128], identity=ident[:]),
                             r=["m01", "ident"], w=[PK(b)])
                    P.act(lambda e, b=b, k4=k4, j=j: e.activation(
                        out=mt[:, k4 * 4:(k4 + 1) * 4, j * 128:(j + 1) * 128],
                        in_=ps[b][:, :].rearrange("p (k t) -> p k t", t=128), func=AF.Copy),
                        r=[PK(b)], w=["mTs"])
            if hf == 1:
                P.dma(maskT_d[g, :, 0:nk // 128, :], mt[:, 0:nk // 128, :], r=["mTs"], w=[("maskT_d", g)])

        halves = [(g, hf) for g in range(4) for hf in range(2)]
        for hi, (g, hf) in enumerate(halves):
            acc_half(g, hf, hi)
            if hi > 0:
                bis_half(halves[hi - 1][0], halves[hi - 1][1], hi - 1)
            prep_half(g, hf, hi)
        bis_half(halves[-1][0], halves[-1][1], len(halves) - 1)
        P.barrier()
        P.release(base_mark)

        if stop("I"):
            return nc
        def attention(nheads, hp_loader, st_emit, mask_for, scale, out_d, okey):
            pts = [P.sb(f"pt{i}", [128, 512], BF16) for i in range(6)]
            ident_fill = P.sb("ifill", [128, 128], BF16)
            P.pool(lambda e: e.memset(ident_fill[:], 0.0), w=["ifill"])
            fill_rhs = P.sb("fillz", [128, 512], BF16)
            P.pool(lambda e: e.memset(fill_rhs[:], 0.0), w=["fillz"])
            rdn = [P.sb(f"rdn{i}", [128, 512], F32) for i in range(2)]
            ost = [P.sb(f"ost{i}", [128, NO], BF16) for i in range(2)]
            ucnt = [0]
            acnt = [0]
            for hp in range(nheads // 2):
                bufs = hp_loader(hp)
                o_t = ost[hp % 2]
                for hh in range(2):
                    h = hp * 2 + hh
                    for g in range(4):
                        nkb = 8 * (g + 1)
                        accb = 4 + (acnt[0] % 2)
                        acnt[0] += 1
                        pend = []
                        for _f in range(NBURST):
                            P.pe(lambda e: e.matmul(ps[6][:, :], lhsT=ident_fill[:], rhs=fill_rhs[:], start=True, stop=True), r=["ifill", "fillz"])
                        for kb in range(nkb):
                            sb_ = nb()
                            rel_ = kb - (nkb - 8)
                            c0 = 128 * (rel_ // 2) if rel_ > 0 else 0
                            st_emit(bufs, h, hh, g, kb, sb_, c0)
                            for _f in range(NFILL):
                                P.pe(lambda e: e.matmul(ps[6][:, :], lhsT=ident_fill[:], rhs=fill_rhs[:], start=True, stop=True), r=["ifill", "fillz"])
                            pi = ucnt[0] % 6
                            ucnt[0] += 1
                            pt = pts[pi]
                            P.act(lambda e, sb_=sb_, pt=pt, c0=c0: e.activation(out=pt[:, c0:512], in_=ps[sb_][:, c0:512], func=AF.Exp, scale=scale),
                                  r=[PK(sb_)], w=[("pt", pi)])
                            mk = mask_for(bufs, g, kb)
                            if mk is not None:
                                map_, mkey = mk
                                P.dve(lambda e, pt=pt, map_=map_, c0=c0: e.tensor_tensor(out=pt[:, c0:512], in0=pt[:, c0:512], in1=map_[:, c0:512], op=ALU.mult),
                                      r=[("pt", pi), mkey], w=[("pt", pi)])
                            pend.append((kb, pt, pi, c0))
                            if len(pend) > 2:
                                kb0, pt0, pi0, c00 = pend.pop(0)
                                P.pe(lambda e, kb0=kb0, pt0=pt0, accb=accb, c00=c00: e.matmul(
                                    ps[accb][:, c00:512], lhsT=bufs["v"][:, kb0, hh * 128:(hh + 1) * 128], rhs=pt0[:, c00:512],
                                    start=(kb0 == 0), stop=(kb0 == nkb - 1)), r=[("pt", pi0), bufs["kv"]], w=[PK(accb)])
                        for kb0, pt0, pi0, c00 in pend:
                            P.pe(lambda e, kb0=kb0, pt0=pt0, accb=accb, c00=c00: e.matmul(
                                ps[accb][:, c00:512], lhsT=bufs["v"][:, kb0, hh * 128:(hh + 1) * 128], rhs=pt0[:, c00:512],
                                start=(kb0 == 0), stop=(kb0 == nkb - 1)), r=[("pt", pi0), bufs["kv"]], w=[PK(accb)])
                        rd = rdn[acnt[0] % 2]
                        krd = ("rdn", acnt[0] % 2)
                        nlo, dlo = (0, 64) if hh == 0 else (64, 0)
                        P.dve(lambda e, rd=rd, accb=accb, dlo=dlo: e.reciprocal(out=rd[dlo:dlo + 64, :], in_=ps[accb][dlo:dlo + 64, :]),
                              r=[PK(accb)], w=[krd])
                        P.dve(lambda e, rd=rd, accb=accb, dlo=dlo, nlo=nlo, g=g, o_t=o_t: e.tensor_tensor(
                            out=o_t[nlo:nlo + 64, g * 512:(g + 1) * 512], in0=ps[accb][nlo:nlo + 64, :],
                            in1=rd[dlo:dlo + 64, :], op=ALU.mult), r=[PK(accb), krd], w=[("ost", hp % 2)])
                P.dma(out_d[hp], o_t[:], r=[("ost", hp % 2)], w=[(okey, hp)])

        mres = P.sb("mres", [128, 80, 512], BF16)
        goff = [0, 8, 24, 48]
        for g in range(4):
            nkb = 8 * (g + 1)
            P.dma(mres[:, goff[g]:goff[g] + nkb, :], maskT_d[g, :, 0:nkb, :], r=[("maskT_d", g)], w=[("mres", g)])
        kb2 = [P.sb(f"kb2_{i}", [128, T], BF16) for i in range(2)]
        qb2 = [P.sb(f"qb2_{i}", [128, NO], BF16) for i in range(2)]
        vb2 = [P.sb(f"vb2_{i}", [128, 32, 256], BF16) for i in range(2)]

        def hp_loader0(hp):
            i = hp % 2
            key = ("kvq0", i)
            P.dma(kb2[i][:], kT_d[hp], r=[("kT_d", hp, t_) for t_ in range(8)], w=[key])
            P.dma(qb2[i][:], qT_d[hp], r=[("qT_d", hp, t_) for t_ in range(4)], w=[key])
            P.dma(vb2[i][:], vaug_d[:, :, hp * 256:(hp + 1) * 256].rearrange("k p c -> p k c"),
                  r=[("vaug_d", k_) for k_ in range(32)], w=[key])
            return {"k": kb2[i], "q": qb2[i], "v": vb2[i], "kv": key}

        def st_emit0(bufs, h, hh, g, kb, sb_, c0=0):
            base = hh * 64
            P.pe(lambda e: e.matmul(ps[sb_][:, c0:512], lhsT=bufs["k"][base:base + 64, kb * 128:(kb + 1) * 128],
                                    rhs=bufs["q"][base:base + 64, g * 512 + c0:(g + 1) * 512], start=True, stop=True),
                 r=[bufs["kv"]], w=[PK(sb_)])

        def mask_for0(bufs, g, kb):
            return mres[:, goff[g] + kb, :], ("mres", g)

        attention(8, hp_loader0, st_emit0, mask_for0, 0.125, attnT_d, "attnT_d")
        P.barrier()
        P.release(base_mark)

        if stop("A0"):
            return nc
        def out_phase(in_chunks, wd, layer, x_src, xkey_src, x_dst, xkey_dst):
            W1OFF[0] = P.mark()
            w1pre = P.sb("w1s", [128, 8, 4096], BF16)
            wo = P.sb("wo", [128, 8, D], BF16)
            load_w(wo[:], wd.rearrange("(c p) n -> p c n", p=128), "wo")
            w1v_ = w1_d[layer].rearrange("(c p) n -> p c n", p=128)
            for c in range(8):
                P.dma(w1pre[:, c, :], w1v_[:, c, :], w=[("w1s", c)], q="pool")
            ain = P.sb("ain", [128, 8, NO], BF16)
            for c, (ap_, rk) in enumerate(in_chunks):
                P.dma(ain[:, c, :], ap_, r=rk, w=[("ain", c)])
            xo2 = [P.sb(f"xo{i}", [128, 8, 512], F32) for i in range(2)]
            mx = P.sb("mx", [128, 8, 512], F32)
            sq_ = P.sb("sqo", [128, 8, 512], BF16)
            rs = P.sb("rso", [128, 512], F32)
            xv = x_src.rearrange("(c p) t -> p c t", p=128)
            xdv = x_dst.rearrange("(c p) t -> p c t", p=128)
            for tc in range(4):
                xo = xo2[tc % 2]
                kxo = ("xo", tc % 2)
                P.dma(xo[:], xv[:, :, tc * 512:(tc + 1) * 512], r=[(xkey_src, tc)], w=[kxo])
                for oc in range(8):
                    b = nb()
                    for kc in range(8):
                        P.pe(lambda e, kc=kc, oc=oc, b=b: e.matmul(ps[b][:, :], lhsT=wo[:, kc, oc * 128:(oc + 1) * 128],
                                                                   rhs=ain[:, kc, tc * 512:(tc + 1) * 512],
                                                                   start=(kc == 0), stop=(kc == 7)),
                             r=["wo", ("ain", kc)], w=[PK(b)])
                    P.act(lambda e, b=b, oc=oc: e.activation(out=mx[:, oc, :], in_=ps[b][:, :], func=AF.Copy),
                          r=[PK(b)], w=[("mx", oc)])
                    P.dve(lambda e, b=b, oc=oc: e.tensor_tensor(out=sq_[:, oc, :], in0=mx[:, oc, :], in1=mx[:, oc, :], op=ALU.mult),
                          r=[("mx", oc)], w=[("sqo", oc)])
                for c in range(8):
                    P.pe(lambda e, c=c: e.matmul(ps[7][:, :], lhsT=ones_b[:], rhs=sq_[:, c, :], start=(c == 0), stop=(c == 7)),
                         r=[("sqo", c), "ones_b"], w=[PK(7)])
                P.act(lambda e: e.activation(out=rs[:], in_=ps[7][:, :], func=AF.Sqrt, bias=EPS, scale=1.0), r=[PK(7)], w=["rso"])
                P.dve(lambda e: e.reciprocal(out=rs[:], in_=rs[:]), r=["rso"], w=["rso"])
                for c in range(8):
                    gc_ = gcol(layer, 1) + c
                    P.dve(lambda e, c=c, gc_=gc_: e.scalar_tensor_tensor(out=mx[:, c, :], in0=mx[:, c, :], scalar=cst[:, gc_:gc_ + 1],
                                                                          in1=rs[:], op0=ALU.mult, op1=ALU.mult),
                           r=[("mx", c), "rso", "cst"], w=[("mx", c)])
                    P.dve(lambda e, c=c, xo=xo: e.tensor_tensor(out=xo[:, c, :], in0=xo[:, c, :], in1=mx[:, c, :], op=ALU.add),
                          r=[("mx", c), kxo], w=[kxo])
                P.dma(xdv[:, :, tc * 512:(tc + 1) * 512], xo[:], r=[kxo], w=[(xkey_dst, tc)])

        in0 = [(attnT_d[c], [("attnT_d", c)]) for c in range(4)] + [(convT_d[c], [("convT_d", c)]) for c in range(4)]
        out_phase(in0, wout_d, 0, xT_own[s_, :, 0:NO], "xown", x1T_d, "x1T_d")
        P.barrier()
        P.release(base_mark)

        def ffn_phase(layer, x_src, xkey_src, x_dst, xkey_dst):
            assert P.mark() == W1OFF[0], (P.mark(), W1OFF[0])
            w1s = P.sb("w1s", [128, 8, 4096], BF16)
            w2s = P.sb("w2s", [128, 32, D], BF16)
            w1v = w1_d[layer].rearrange("(c p) n -> p c n", p=128)
            w2v = w2_d[layer].rearrange("(c p) n -> p c n", p=128)
            for c in range(0, 32, 4):
                P.dma(w2s[:, c:c + 4, :], w2v[:, c:c + 4, :], w=[("w2s", c // 4)], q="pool")
            xf = P.sb("xf", [128, 8, 512], F32)
            off3 = P.mark()
            yb = P.sb("yb", [128, 8, 512], F32)
            end3 = P.mark()
            P.release(off3)
            sq_f = P.sb("sqf", [128, 8, 512], BF16)
            hf = P.sb("hf", [128, 8, 512], BF16)
            assert P.mark() == end3
            h1 = P.sb("h1", [128, 32, 512], BF16)
            sqy = [P.sb(f"sqy{i}", [128, 512], BF16) for i in range(2)]
            rt = [P.sb(f"rt{i}", [128, 512], F32) for i in range(2)]
            alias_keys = ["sqf"] + [("hf", c) for c in range(8)]
            rsf = P.sb("rsf", [128, 512], F32)
            xv = x_src.rearrange("(c p) t -> p c t", p=128)
            xdv = x_dst.rearrange("(c p) t -> p c t", p=128)
            rc = [0]
            for tc in range(4):
                P.dma(xf[:], xv[:, :, tc * 512:(tc + 1) * 512], r=[(xkey_src, tc)], w=["xf"])
                P.act(lambda e: e.activation(out=rsf[:, 0:1], in_=rsf[:, 0:1], func=AF.Copy), r=["rsf"],
                      w=alias_keys + ["rsf"] + [("yb", c) for c in range(8)])
                rmsnorm_fm(xf, "xf", 8, 512, gcol(layer, 2), hf, "hf", ones_b, sq_f, "sqf", rsf, "rsf", 7)
                for oc in range(32):
                    b = nb()
                    for kc in range(8):
                        P.pe(lambda e, kc=kc, oc=oc, b=b: e.matmul(ps[b][:, :], lhsT=w1s[:, kc, oc * 128:(oc + 1) * 128],
                                                                   rhs=hf[:, kc, :], start=(kc == 0), stop=(kc == 7)),
                             r=[("w1s", kc), ("hf", kc)], w=[PK(b)])
                    ri = rc[0] % 2
                    rc[0] += 1
                    P.act(lambda e, b=b, ri=ri: e.activation(out=rt[ri][:], in_=ps[b][:, :], func=AF.Relu), r=[PK(b)], w=[("rt", ri)])
                    eng = P.dve if oc % 2 == 0 else P.pool
                    eng(lambda e, ri=ri, oc=oc: e.tensor_tensor(out=h1[:, oc, :], in0=rt[ri][:], in1=rt[ri][:], op=ALU.mult),
                        r=[("rt", ri)], w=[("h1", oc)])
                for oc in range(8):
                    b = nb()
                    for kc in range(32):
                        P.pe(lambda e, kc=kc, oc=oc, b=b: e.matmul(ps[b][:, :], lhsT=w2s[:, kc, oc * 128:(oc + 1) * 128],
                                                                   rhs=h1[:, kc, :], start=(kc == 0), stop=(kc == 31)),
                             r=[("w2s", kc // 4), ("h1", kc)], w=[PK(b)])
                    P.act(lambda e, b=b, oc=oc: e.activation(out=yb[:, oc, :], in_=ps[b][:, :], func=AF.Copy), r=[PK(b)],
                          w=[("yb", oc)] + (alias_keys if oc == 0 else []))
                    P.dve(lambda e, oc=oc: e.tensor_tensor(out=sqy[oc % 2][:], in0=yb[:, oc, :], in1=yb[:, oc, :], op=ALU.mult),
                          r=[("yb", oc)], w=[("sqy", oc % 2)])
                    P.pe(lambda e, oc=oc: e.matmul(ps[7][:, :], lhsT=ones_b[:], rhs=sqy[oc % 2][:], start=(oc == 0), stop=(oc == 7)),
                         r=[("sqy", oc % 2), "ones_b"], w=[PK(7)])
                P.act(lambda e: e.activation(out=rsf[:], in_=ps[7][:, :], func=AF.Sqrt, bias=EPS, scale=1.0), r=[PK(7)], w=["rsf"])
                P.dve(lambda e: e.reciprocal(out=rsf[:], in_=rsf[:]), r=["rsf"], w=["rsf"])
                for c in range(8):
                    gc_ = gcol(layer, 3) + c
                    P.dve(lambda e, c=c, gc_=gc_: e.scalar_tensor_tensor(out=yb[:, c, :], in0=yb[:, c, :], scalar=cst[:, gc_:gc_ + 1],
                                                                          in1=rsf[:], op0=ALU.mult, op1=ALU.mult),
                           r=[("yb", c), "rsf", "cst"], w=[("yb", c)])
                    P.dve(lambda e, c=c: e.tensor_tensor(out=yb[:, c, :], in0=yb[:, c, :], in1=xf[:, c, :], op=ALU.add),
                          r=[("yb", c), "xf"], w=[("yb", c)])
                P.dma(xdv[:, :, tc * 512:(tc + 1) * 512], yb[:], r=[("yb", c) for c in range(8)], w=[(xkey_dst, tc)])

        if stop("O0"):
            return nc
        ffn_phase(0, x1T_d, "x1T_d", x2T_d[s_], ("x2T_d", s_))
        P.barrier()
        P.release(base_mark)

    if stop("F0"):
        return nc
    tabs = [rope_tables(pos_own[s1], NO, C_FR1, C_SG1, f"r1o{s1}") for s1 in range(2)]
    C1S1 = [None, None]
    m1 = P.mark()
    wdq = P.sb("wdq", [128, 8, 384], BF16)
    wuq = P.sb("wuq", [128, 3, 2048], BF16)
    wdkv = P.sb("wdkv", [128, 8, 320], BF16)
    load_w(wdq[:], wdq_d.rearrange("(c p) n -> p c n", p=128), "wdq")
    load_w(wuq[:], wuq_d.rearrange("(c p) n -> p c n", p=128), "wuq")
    load_w(wdkv[:], wdkv_d.rearrange("(c p) n -> p c n", p=128), "wdkv")
    xs2 = [P.sb(f"xs{i}", [128, 8, 512], F32) for i in range(2)]
    sq = P.sb("sq", [128, 8, 512], BF16)
    hb2 = [P.sb(f"hb{i}", [128, 8, 512], BF16) for i in range(2)]
    rstd = P.sb("rstd", [128, 512], F32)
    t1 = [P.sb(f"t1_{i}", [128, 512], F32) for i in range(2)]
    t2 = [P.sb(f"t2_{i}", [128, 512], F32) for i in range(2)]
    cq = P.sb("cq", [128, 3, 512], F32)
    cqn = P.sb("cqn", [128, 3, 512], BF16)
    ckv = P.sb("ckv", [128, 2, 512], F32)
    ckvn = P.sb("ckvn", [128, 2, 512], F32)
    krs = P.sb("krs", [32, 512], F32)
    qst = [P.sb(f"qst1_{i}", [128, 16, 512], BF16) for i in range(2)]
    sq3 = P.sb("sq3", [128, 3, 512], BF16)
    rs3 = P.sb("rs3", [128, 512], F32)

    def rope_evac1(bA, bB, m, col0, dst, kdst, eng_out="pool"):
        Cc, Ss = C1S1[0], C1S1[1]
        i = tcnt[0] % 2
        tcnt[0] += 1
        P.dve(lambda e: e.tensor_tensor(out=t1[i][0:m, :], in0=ps[bA][0:m, :], in1=Cc[0:m, col0:col0 + 512], op=ALU.mult),
              r=[PK(bA), "r1o0C", "r1o1C"], w=[("t1", i)])
        P.dve(lambda e: e.tensor_tensor(out=t2[i][0:m, :], in0=ps[bB][0:m, :], in1=Ss[0:m, col0:col0 + 512], op=ALU.mult),
              r=[PK(bB), "r1o0S", "r1o1S"], w=[("t2", i)])
        P.pool(lambda e: e.tensor_tensor(out=dst, in0=t1[i][0:m, :], in1=t2[i][0:m, :], op=ALU.add),
               r=[("t1", i), ("t2", i)], w=[kdst])

    for s1 in range(2):
        C1S1[0], C1S1[1] = tabs[s1]
        x2v = x2T_d[s1].rearrange("(c p) t -> p c t", p=128)
        for tc in range(4):
            xs = xs2[tc % 2]
            kx = ("xs", tc % 2)
            hbc = hb2[tc % 2]
            khb = ("hb", tc % 2)
            P.dma(xs[:], x2v[:, :, tc * 512:(tc + 1) * 512], r=[(("x2T_d", s1), tc)], w=[kx])
            rmsnorm_fm(xs, kx, 8, 512, gcol(1, 0), hbc, khb, ones_b, sq, "sq", rstd, "rstd", 7)
            if s1 == 0:
                for oc in range(3):
                    b = nb()
                    proj_fm(wdq, "wdq", oc * 128, hbc, khb, 8, 512, b)
                    P.act(lambda e, b=b, oc=oc: e.activation(out=cq[:, oc, :], in_=ps[b][:, :], func=AF.Copy), r=[PK(b)], w=["cq"])
                P.act(lambda e: e.activation(out=sq3[:], in_=cq[:], func=AF.Square), r=["cq"], w=["sq3"])
                for c in range(3):
                    P.pe(lambda e, c=c: e.matmul(ps[7][:, :], lhsT=ones_q[:], rhs=sq3[:, c, :], start=(c == 0), stop=(c == 2)),
                         r=["sq3", "ones_q"], w=[PK(7)])
                P.act(lambda e: e.activation(out=rs3[:], in_=ps[7][:, :], func=AF.Sqrt, bias=EPS, scale=256.0 / 384.0), r=[PK(7)], w=["rs3"])
                P.dve(lambda e: e.reciprocal(out=rs3[:], in_=rs3[:]), r=["rs3"], w=["rs3"])
                for c in range(3):
                    P.dve(lambda e, c=c: e.scalar_tensor_tensor(out=cqn[:, c, :], in0=cq[:, c, :], scalar=cst[:, C_QN + c:C_QN + c + 1],
                                                                in1=rs3[:], op0=ALU.mult, op1=ALU.mult), r=["cq", "rs3", "cst"], w=[("cqn", c)])
                qs = qst[tc % 2]
                for oc in range(8):
                    b = nb()
                    proj_fm(wuq, "wuq", oc * 128, cqn, "cqn", 3, 512, b)
                    P.act(lambda e, b=b, oc=oc: e.activation(out=qs[:, oc, :], in_=ps[b][:, :], func=AF.Copy), r=[PK(b)], w=[("qst", tc % 2, oc)])
                    P.dma(qnT_d[oc, :, tc * 512:(tc + 1) * 512], qs[:, oc, :], r=[("qst", tc % 2, oc)], w=[("qnT_d", oc, tc)])
                for oc in range(8):
                    bA, bB = nb(), nb()
                    proj_fm(wuq, "wuq", 1024 + oc * 64, cqn, "cqn", 3, 512, bA, m=64)
                    proj_fm(wuq, "wuq", 1536 + oc * 64, cqn, "cqn", 3, 512, bB, m=64)
                    rope_evac1(bA, bB, 64, tc * 512, qs[0:64, 8 + oc, :], ("qst", tc % 2, 8 + oc))
                    P.dma(qrT_d[oc, :, tc * 512:(tc + 1) * 512], qs[0:64, 8 + oc, :], r=[("qst", tc % 2, 8 + oc)], w=[("qrT_d", oc, tc)])
            for oc in range(2):
                b = nb()
                proj_fm(wdkv, "wdkv", oc * 128, hbc, khb, 8, 512, b)
                P.act(lambda e, b=b, oc=oc: e.activation(out=ckv[:, oc, :], in_=ps[b][:, :], func=AF.Copy), r=[PK(b)], w=["ckv"])
            P.act(lambda e: e.activation(out=sq3[:, 0:2, :], in_=ckv[:], func=AF.Square), r=["ckv"], w=["sq3"])
            for c in range(2):
                P.pe(lambda e, c=c: e.matmul(ps[7][:, :], lhsT=ones_q[:], rhs=sq3[:, c, :], start=(c == 0), stop=(c == 1)),
                     r=["sq3", "ones_q"], w=[PK(7)])
            P.act(lambda e: e.activation(out=rs3[:], in_=ps[7][:, :], func=AF.Sqrt, bias=EPS, scale=1.0), r=[PK(7)], w=["rs3"])
            P.dve(lambda e: e.reciprocal(out=rs3[:], in_=rs3[:]), r=["rs3"], w=["rs3"])
            for c in range(2):
                P.dve(lambda e, c=c: e.scalar_tensor_tensor(out=ckvn[:, c, :], in0=ckv[:, c, :], scalar=cst[:, C_KVN + c:C_KVN + c + 1],
                                                            in1=rs3[:], op0=ALU.mult, op1=ALU.mult), r=["ckv", "rs3", "cst"], w=[("ckvn", c)])
                P.dma(kva_sets_d[s1, c * 128:(c + 1) * 128, tc * 512:(tc + 1) * 512], ckvn[:, c, :], r=[("ckvn", c)], w=[("kva", s1, tc, c)])
            bA, bB = nb(), nb()
            proj_fm(wdkv, "wdkv", 256, hbc, khb, 8, 512, bA, m=32)
            proj_fm(wdkv, "wdkv", 288, hbc, khb, 8, 512, bB, m=32)
            rope_evac1(bA, bB, 32, tc * 512, krs[:, :], "krs")
            P.dma(kva_sets_d[s1, 256:288, tc * 512:(tc + 1) * 512], krs[:, :], r=["krs"], w=[("kva", s1, tc, 2)])
    P.barrier()
    P.release(base_mark)

    if stop("Q1"):
        return nc
    wkk = P.sb("wkk", [128, 2, 1024], BF16)
    wkv = P.sb("wkv", [128, 2, 1024], BF16)
    load_w(wkk[:], wukvk_d.rearrange("(c p) n -> p c n", p=128), "wkk")
    load_w(wkv[:], wukvv_d.rearrange("(c p) n -> p c n", p=128), "wkv")
    ckf = [P.sb(f"ckf{i}", [128, 2, 512], F32) for i in range(2)]
    ckb = [P.sb(f"ckb{i}", [128, 2, 512], BF16) for i in range(2)]
    kns = [P.sb(f"kns{i}", [128, 8, 512], BF16) for i in range(2)]
    vst1 = [P.sb(f"vst1_{i}", [128, 16, 128], BF16) for i in range(2)]
    kr4 = P.sb("kr4", [64, T], BF16)
    krf = P.sb("krf", [64, T], F32)
    for i in range(2):
        P.pool(lambda e, i=i: e.memset(vst1[i][:], 1.0), w=[("vst1", i)])
    for sl in range(32):
        s1, m_ = sl % 2, sl // 2
        for rep_ in range(2):
            P.dma(krf[rep_ * 32:(rep_ + 1) * 32, sl * 128:(sl + 1) * 128],
                  kva_sets_d[s1, 256:288, m_ * 128:(m_ + 1) * 128], w=[("krf", sl // 8)])
    for q4 in range(4):
        P.dve(lambda e, q4=q4: e.tensor_copy(out=kr4[:, q4 * 1024:(q4 + 1) * 1024], in_=krf[:, q4 * 1024:(q4 + 1) * 1024]),
              r=[("krf", q4)], w=["kr4"])
    for tc in range(8):
        cf = ckf[tc % 2]
        cb_ = ckb[tc % 2]
        for tb in range(4):
            sl = tc * 4 + tb
            s1, m_ = sl % 2, sl // 2
            for c in range(2):
                P.dma(cf[:, c, tb * 128:(tb + 1) * 128],
                      kva_sets_d[s1, c * 128:(c + 1) * 128, m_ * 128:(m_ + 1) * 128], w=[("ckf", tc % 2)])
        P.dve(lambda e, cf=cf, cb_=cb_: e.tensor_copy(out=cb_[:], in_=cf[:]), r=[("ckf", tc % 2)], w=[(("ckb", tc % 2), 0), (("ckb", tc % 2), 1)])
        kn = kns[tc % 2]
        for oc in range(8):
            b = nb()
            proj_fm(wkk, "wkk", oc * 128, cb_, ("ckb", tc % 2), 2, 512, b)
            P.act(lambda e, b=b, oc=oc, kn=kn: e.activation(out=kn[:, oc, :], in_=ps[b][:, :], func=AF.Copy), r=[PK(b)], w=[("kns", tc % 2, oc)])
            P.dma(knT_d[oc, :, tc * 512:(tc + 1) * 512], kn[:, oc, :], r=[("kns", tc % 2, oc)], w=[("knT_d", oc, tc)])
        for tb in range(4):
            kb = tc * 4 + tb
            vs = vst1[kb % 2]
            vv = vs[:].rearrange("p (h two) d -> p h two d", two=2)
            for half in range(2):
                b = nb()
                for kc in range(2):
                    P.pe(lambda e, kc=kc, tb=tb, b=b, half=half, cb_=cb_: e.matmul(
                        ps[b][:, :], lhsT=cb_[:, kc, tb * 128:(tb + 1) * 128], rhs=wkv[:, kc, half * 512:(half + 1) * 512],
                        start=(kc == 0), stop=(kc == 1)), r=["wkv", (("ckb", tc % 2), kc)], w=[PK(b)])
                pv = ps[b][:, :].rearrange("p (h two d) -> p h two d", two=2, d=64)
                P.act(lambda e, pv=pv, vv=vv, half=half: e.activation(out=vv[:, half * 4:(half + 1) * 4, 0, 0:64], in_=pv[:, :, 0, :], func=AF.Copy),
                      r=[PK(b)], w=[("vst1", kb % 2)])
                P.act(lambda e, pv=pv, vv=vv, half=half: e.activation(out=vv[:, half * 4:(half + 1) * 4, 1, 64:128], in_=pv[:, :, 1, :], func=AF.Copy),
                      r=[PK(b)], w=[("vst1", kb % 2)])
            P.dma(vaug1_d[kb], vs[:].rearrange("p h d -> p (h d)"), r=[("vst1", kb % 2)], w=[("vaug1_d", kb)])
    P.dma(kr4_d, kr4[:], r=["kr4"], w=["kr4_d"])
    P.barrier()
    P.release(base_mark)

    if stop("K1"):
        return nc
    mTs1 = P.sb("mTs1", [128, 2, 8, 512], BF16)
    for i in range(2):
        P.dma(mTs1[:, i], mT_d[i], w=["mTs1"])
    kh2 = [[P.sb(f"kh_{i}_{hh}", [96, T], BF16) for hh in range(2)] for i in range(2)]
    qh2 = [[P.sb(f"qh_{i}_{hh}", [96, NO], BF16) for hh in range(2)] for i in range(2)]
    vb2 = [P.sb(f"vb21_{i}", [128, 32, 256], BF16) for i in range(2)]

    def hp_loader1(hp):
        i = hp % 2
        key = ("kvq1", i)
        for hh in range(2):
            P.dma(kh2[i][hh][0:64, :], knT_d[hp, hh * 64:(hh + 1) * 64, :], r=[("knT_d", hp, t_) for t_ in range(8)], w=[key])
            P.dma(kh2[i][hh][64:96, :], kr4_d[0:32, :], r=["kr4_d"], w=[key])
            P.dma(qh2[i][hh][0:64, :], qnT_d[hp, hh * 64:(hh + 1) * 64, :], r=[("qnT_d", hp, t_) for t_ in range(4)], w=[key])
            P.dma(qh2[i][hh][64:96, :], qrT_d[hp, hh * 32:(hh + 1) * 32, :], r=[("qrT_d", hp, t_) for t_ in range(4)], w=[key])
        P.dma(vb2[i][:], vaug1_d[:, :, hp * 256:(hp + 1) * 256].rearrange("k p c -> p k c"),
              r=[("vaug1_d", k_) for k_ in range(32)], w=[key])
        return {"k": kh2[i], "q": qh2[i], "v": vb2[i], "kv": key}

    def st_emit1(bufs, h, hh, g, kb, sb_, c0=0):
        P.pe(lambda e: e.matmul(ps[sb_][:, c0:512], lhsT=bufs["k"][hh][0:96, kb * 128:(kb + 1) * 128],
                                rhs=bufs["q"][hh][0:96, g * 512 + c0:(g + 1) * 512], start=True, stop=True),
             r=[bufs["kv"]], w=[PK(sb_)])

    def mask_for1(bufs, g, kb):
        rel = kb - 8 * g
        if rel < 0:
            return None
        return mTs1[:, g // 2, rel, :], "mTs1"

    attention(16, hp_loader1, st_emit1, mask_for1, float(96 ** -0.5), attnT_d, "attnT1_d")
    P.barrier()
    P.release(base_mark)

    if stop("A1"):
        return nc
    in1 = [(attnT_d[c], [("attnT1_d", c)]) for c in range(8)]
    out_phase(in1, wo_d, 1, x2T_d[0], ("x2T_d", 0), x3T_d, "x3T_d")
    P.barrier()
    P.release(base_mark)
    ffn_phase(1, x3T_d, "x3T_d", outT, "outT")
    P.final_wait([("outT", tc) for tc in range(4)])
    P.emit()
    return nc


def _swap_cols(w, head, a, b_):
    n = w.shape[1]
    idx = np.arange(n).reshape(-1, head)
    perm = np.concatenate([idx[:, a:b_], idx[:, 0:a], idx[:, b_:]], axis=1).reshape(-1)
    return w[:, perm]


def prepare_inputs(inp, n_batch=4):
    f32 = np.float32
    x = np.asarray(inp["x"], f32)
    pos = np.asarray(inp["positions"]).astype(np.int32)
    w_in = np.asarray(inp["even_w_in"], f32)[0]
    offs = np.cumsum([0, 512, 512, 512, 1024, 64, 16, 512, 512, 512])
    wq_, wk_, wv_, wqi, wki, wwi, wgb, wgc, wxi = [w_in[:, offs[i]:offs[i + 1]] for i in range(9)]
    w0k = np.concatenate([wk_, _swap_cols(wk_, 64, 8, 16), wki, wki, _swap_cols(wki, 64, 8, 16), _swap_cols(wki, 64, 8, 16)], axis=1)
    w0q = np.concatenate([wq_, _swap_cols(wq_, 64, 8, 16), wqi, _swap_cols(wqi, 64, 8, 16), wgb, wgc, wxi], axis=1)
    w_uq = np.asarray(inp["odd_w_uq"], f32)[0]
    cols = np.arange(1536).reshape(16, 96)
    wuq_n = w_uq[:, cols[:, :64].reshape(-1)]
    wuq_r = w_uq[:, cols[:, 64:].reshape(-1)]
    wuq = np.concatenate([wuq_n, wuq_r, _swap_cols(wuq_r, 32, 16, 32)], axis=1)
    w_dkv = np.asarray(inp["odd_w_dkv"], f32)[0]
    wdkv = np.concatenate([w_dkv[:, :256], w_dkv[:, 256:], _swap_cols(w_dkv[:, 256:], 32, 16, 32)], axis=1)
    w_ukv = np.asarray(inp["odd_w_ukv"], f32)[0]
    c2 = np.arange(2048).reshape(16, 128)
    wukv_k = w_ukv[:, c2[:, :64].reshape(-1)]
    wukv_v = w_ukv[:, c2[:, 64:].reshape(-1)]

    cst = np.zeros((128, NCST), f32)
    kinds = ["norm_mix_pre", "norm_mix_post", "norm_ffn_pre", "norm_ffn_post"]
    for l in range(2):
        for k, nm in enumerate(kinds):
            cst[:, gcol(l, k):gcol(l, k) + 8] = np.asarray(inp[nm], f32)[l].reshape(8, 128).T
    cst[:, C_QN:C_QN + 3] = np.asarray(inp["odd_q_norm"], f32)[0].reshape(3, 128).T
    cst[:, C_KVN:C_KVN + 2] = np.asarray(inp["odd_kv_norm"], f32)[0].reshape(2, 128).T
    cw = np.asarray(inp["even_conv_w"], f32)[0]
    for j in range(3):
        cst[:, C_CW + j * 4:C_CW + j * 4 + 4] = cw[j].reshape(4, 128).T
    theta = 500000.0
    if0 = (theta ** (-np.arange(0, 16, 2, dtype=np.float32) / 16)).astype(f32)
    if1 = (theta ** (-np.arange(0, 32, 2, dtype=np.float32) / 32)).astype(f32)
    for p in range(128):
        r = p % 64
        if r < 16:
            cst[p, C_FR0] = if0[r % 8]
            cst[p, C_SG0] = -1.0 if r < 8 else 1.0
        r = p % 32
        cst[p, C_FR1] = if1[r % 16]
        cst[p, C_SG1] = -1.0 if r < 16 else 1.0

    shared = {
        "cst": cst, "w0k": w0k, "w0v": np.ascontiguousarray(wv_), "w0q": w0q, "w0wi": np.ascontiguousarray(wwi),
        "w_out": np.asarray(inp["even_w_out"], f32)[0], "w1": np.asarray(inp["mlp_w1"], f32),
        "w2": np.asarray(inp["mlp_w2"], f32), "w_dq": np.asarray(inp["odd_w_dq"], f32)[0], "w_uq": wuq,
        "w_dkv": wdkv, "w_ukv_k": np.ascontiguousarray(wukv_k), "w_ukv_v": np.ascontiguousarray(wukv_v),
        "w_o": np.asarray(inp["odd_w_o"], f32)[0],
    }
    shared = {k: np.ascontiguousarray(v, dtype=f32) for k, v in shared.items()}
    in_maps = []
    own_idx_all = []
    qi = np.arange(128)
    s_ = np.arange(1024)
    for b in range(n_batch):
        for par in range(2):
            xb = x[b]
            sets = [blocks_for(par), blocks_for(1 - par)]
            xT_sets, pos_sets, cbs = [], [], []
            kq = np.zeros((128, 32), f32)
            for si, blks in enumerate(sets):
                idx = np.concatenate([np.arange(128 * p, 128 * p + 128) for p in blks])
                halo = np.zeros((32, D), f32)
                for i, p in enumerate(blks):
                    if p > 0:
                        halo[2 * i:2 * i + 2] = xb[128 * p - 2:128 * p]
                xT_sets.append(np.concatenate([xb[idx], halo], axis=0).T)
                pos_sets.append(pos[b][idx][None, :])
                cb = np.zeros((8, 128, 1024), f32)
                for i, p in enumerate(blks):
                    kq[:, si * 16 + i] = np.minimum(256, 128 * p + qi + 1)
                for g in range(4):
                    for j in range(4):
                        rel = blks[4 * g + j] % 8
                        vis = s_[None, :] <= (rel * 128 + qi)[:, None]
                        cb[(g // 2) * 4 + j] = np.where(vis, 0.0, -1e30)
                cbs.append(cb)
            own = np.concatenate([np.arange(128 * p, 128 * p + 128) for p in sets[0]])
            own_idx_all.append(own)
            mT = np.zeros((2, 128, 8, 512), f32)
            for g in (0, 2):
                for j in range(4):
                    pq = sets[0][4 * g + j]
                    qpos = 128 * pq + qi
                    for rel in range(8):
                        sl = 8 * g + rel
                        pk = sets[sl % 2][sl // 2]
                        kpos = 128 * pk + qi
                        mT[g // 2, :, rel, j * 128:(j + 1) * 128] = (kpos[:, None] <= qpos[None, :])
            m = dict(shared)
            m.update({
                "xT_seq": np.ascontiguousarray(xb.T), "xT_own": np.ascontiguousarray(np.stack(xT_sets)),
                "pos_seq": np.ascontiguousarray(pos[b][None, :]), "pos_own": np.ascontiguousarray(np.stack(pos_sets)),
                "kq": kq, "cb": np.stack(cbs), "mT": mT.astype(ml_dtypes.bfloat16),
            })
            in_maps.append(m)
    return in_maps, own_idx_all


_NC_CACHE = {}


def kernel(**inputs):
    in_maps, own_idx = prepare_inputs(inputs, 4)
    if 8 not in _NC_CACHE:
        _NC_CACHE[8] = build_program(8)
    nc = _NC_CACHE[8]
    res = run_bass_kernel_spmd(nc, in_maps, core_ids=list(range(8)))
    out = np.zeros((4, T, D), np.float32)
    for c in range(8):
        b = c // 2
        out[b, own_idx[c], :] = np.asarray(res.results[c]["outT"], np.float32).T
    return out
```
